# Optimizing a Trainium2 kernel written in Bass

```python
import math
import jax, jax.numpy as jnp
from jax import lax
import numpy as np

D_MODEL = 1024
BATCH = 8
SEQ = 8192
DEPTH = 1

GRID_W = 64
CTX_LEN = 256
MIX_WIDTH = D_MODEL
HY_WIDTH = MIX_WIDTH // 2
RET_WIDTH = MIX_WIDTH - HY_WIDTH
RET_HEADS = 8
RET_HEAD_DIM = RET_WIDTH // RET_HEADS
RET_CHUNK = 128
HY_ORDER = 2
HY_PROJ = (HY_ORDER + 1) * HY_WIDTH
IN_COLS = HY_PROJ + 4 * RET_WIDTH
SHORT_CONV_W = 3
HY_BANDS = 16
HY_EMB = 1 + 2 * HY_BANDS
HY_FILT_HID = 64
HY_DECAY_TARGET = 1e-2
HY_FAST_PCT = 0.3
HY_SLOW_PCT = 1.5
D_FF = 4 * D_MODEL
ROPE_BASE = 10000.0
ROPE_PAIRS_AXIS = RET_HEAD_DIM // 4
NORM_EPS = 1e-6

kernel_name = "hymba_hyena_retnet_dit_block"

F32 = jnp.float32


def _rmsnorm(x, g):
    xf = x.astype(F32)
    y = xf * lax.rsqrt(jnp.mean(jnp.square(xf), axis=-1, keepdims=True) + NORM_EPS)
    return (y * g.astype(F32)).astype(x.dtype)


def _modulate(h, shift, scale):
    return h * (1.0 + scale) + shift


def _rope_2d(L):
    rows = L // GRID_W
    r, col = jnp.meshgrid(jnp.arange(rows, dtype=F32), jnp.arange(GRID_W, dtype=F32), indexing="ij")
    inv = ROPE_BASE ** (-jnp.arange(ROPE_PAIRS_AXIS, dtype=F32) / ROPE_PAIRS_AXIS)
    ang = jnp.concatenate([r.reshape(-1, 1) * inv, col.reshape(-1, 1) * inv], axis=-1)
    return jnp.cos(ang), jnp.sin(ang)


def _apply_rope(t, rope):
    cos, sin = rope
    t2 = t.reshape(t.shape[:-1] + (RET_HEAD_DIM // 2, 2))
    a, b = t2[..., 0], t2[..., 1]
    cs, sn = cos[None, :, None, :], sin[None, :, None, :]
    return jnp.stack([a * cs - b * sn, a * sn + b * cs], axis=-1).reshape(t.shape)


def _short_conv(u, w, b):
    up = jnp.pad(u, ((0, 0), (1, 1), (0, 0)))
    return up[:, :-2] * w[0] + up[:, 1:-1] * w[1] + up[:, 2:] * w[2] + b


def _hyena_filters(L, w1, b1, fr1, w2, b2, fr2, w3):
    t = jnp.linspace(0.0, 1.0, L, dtype=F32)[:, None]
    w = (2.0 * math.pi / L) * jnp.arange(L, dtype=F32)[:, None]
    bands = jnp.linspace(1e-4, HY_BANDS - 1, HY_BANDS, dtype=F32)[None, :]
    z = jnp.concatenate([t, jnp.cos(bands * w), -jnp.sin(bands * w)], axis=-1)
    h = jnp.sin(fr1.astype(F32) * (z @ w1.astype(F32) + b1.astype(F32)))
    h = jnp.sin(fr2.astype(F32) * (h @ w2.astype(F32) + b2.astype(F32)))
    h = (h @ w3.astype(F32)).reshape(L, 2, HY_WIDTH)
    deltas = jnp.abs(jnp.linspace(math.log(HY_DECAY_TARGET) / HY_SLOW_PCT,
                                  math.log(HY_DECAY_TARGET) / HY_FAST_PCT, HY_WIDTH, dtype=F32))
    h = h * jnp.exp(-t * deltas)[:, None, :]
    return h / (jnp.sum(jnp.abs(h), axis=(0, 1), keepdims=True) + 1e-6)


def _bidir_long_conv(v, h, bias):
    L = v.shape[1]
    k = jnp.concatenate([h[:, 0], jnp.zeros((1, HY_WIDTH), F32), h[:0:-1, 1]], axis=0)
    vf = jnp.fft.rfft(v.astype(F32), n=2 * L, axis=1)
    kf = jnp.fft.rfft(k, n=2 * L, axis=0)
    y = jnp.fft.irfft(vf * kf[None], n=2 * L, axis=1)[:, :L]
    return (y + v.astype(F32) * bias.astype(F32)).astype(v.dtype)


def _ret_heads(t):
    B, L, _ = t.shape
    return t.reshape(B, L, RET_HEADS, RET_HEAD_DIM)


def _ret_kv(u):
    _, k, v, _ = jnp.split(u[..., HY_PROJ:], 4, axis=-1)
    k = _ret_heads(k).astype(F32) * (RET_HEAD_DIM ** -0.5)
    v = _ret_heads(v).astype(F32)
    return k.transpose(0, 2, 1, 3), v.transpose(0, 2, 1, 3)


def _ret_final_state(k, v, log_gamma):
    Lc = k.shape[2]
    w = jnp.exp(log_gamma[:, None] * (Lc - 1 - jnp.arange(Lc, dtype=F32))[None, :])
    return jnp.einsum("bhmd,bhme->bhde", k * w[None, :, :, None], v)


def _retention_chunkwise(q, k, v, log_gamma, s0):
    B, H, L, dk = q.shape
    dv = v.shape[-1]
    C = RET_CHUNK
    N = L // C
    qc = q.reshape(B, H, N, C, dk)
    kc = k.reshape(B, H, N, C, dk)
    vc = v.reshape(B, H, N, C, dv)
    idx = jnp.arange(C, dtype=F32)
    lg = log_gamma[:, None]
    diff = idx[:, None] - idx[None, :]
    dmask = jnp.where(diff >= 0, jnp.exp(lg[:, :, None] * jnp.maximum(diff, 0.0)[None]), 0.0)
    scores = jnp.einsum("bhncd,bhnmd->bhncm", qc, kc) * dmask[None, :, None]
    out_inner = jnp.einsum("bhncm,bhnme->bhnce", scores, vc)
    w_k = jnp.exp(lg * (C - 1 - idx)[None, :])
    t = jnp.einsum("bhnmd,bhnme->bhnde", kc * w_k[None, :, None, :, None], vc)
    decay_chunk = jnp.exp(log_gamma * C)[None, :, None, None]

    def step(s, t_n):
        return decay_chunk * s + t_n, s

    _, s_prev = lax.scan(step, s0, jnp.moveaxis(t, 2, 0))
    s_prev = jnp.moveaxis(s_prev, 0, 2)
    w_q = jnp.exp(lg * (idx + 1.0)[None, :])
    out_cross = jnp.einsum("bhncd,bhnde->bhnce", qc * w_q[None, :, None, :, None], s_prev)
    return (out_inner + out_cross).reshape(B, H, L, dv)


def _token_mixers(u, rope, s0_f, s0_b, conv_w, conv_b, f_w1, f_b1, f_fr1, f_w2, f_b2, f_fr2, f_w3,
                  hy_bias, log_gamma, gn_g):
    B, L, _ = u.shape
    uh = _short_conv(u[..., :HY_PROJ], conv_w, conv_b)
    x0, x1, v = jnp.split(uh, 3, axis=-1)
    h = _hyena_filters(L, f_w1, f_b1, f_fr1, f_w2, f_b2, f_fr2, f_w3)
    y_hy = _bidir_long_conv(v * x1, h, hy_bias) * x0
    q, k, vr, g = jnp.split(u[..., HY_PROJ:], 4, axis=-1)
    q = _ret_heads(q)
    k = _ret_heads(k) * (RET_HEAD_DIM ** -0.5)
    if rope is not None:
        q = _apply_rope(q, rope)
        k = _apply_rope(k, rope)
    q = q.astype(F32).transpose(0, 2, 1, 3)
    k = k.astype(F32).transpose(0, 2, 1, 3)
    vr = _ret_heads(vr).astype(F32).transpose(0, 2, 1, 3)
    o_f = _retention_chunkwise(q, k, vr, log_gamma[0], s0_f)
    o_b = jnp.flip(_retention_chunkwise(jnp.flip(q, 2), jnp.flip(k, 2), jnp.flip(vr, 2), log_gamma[1], s0_b), 2)
    o = (o_f + o_b).transpose(0, 2, 1, 3)
    mu = jnp.mean(o, axis=-1, keepdims=True)
    var = jnp.mean(jnp.square(o - mu), axis=-1, keepdims=True)
    o = ((o - mu) * lax.rsqrt(var + NORM_EPS)).reshape(B, L, RET_WIDTH) * gn_g.astype(F32)
    y_ret = (o * jax.nn.silu(g.astype(F32))).astype(u.dtype)
    return jnp.concatenate([y_hy, y_ret], axis=-1)


def _sq_relu_mlp(h, w1, w2):
    return jnp.square(jax.nn.relu(h @ w1)) @ w2


def setup_inputs(seed: int = 0) -> dict:
    key = jax.random.key(seed)
    ks = jax.random.split(key, 28)

    def nrm(k, shape, scale):
        return scale * jax.random.normal(k, shape, F32)

    g0 = 1.0 - 2.0 ** (-5.0 - np.arange(RET_HEADS))
    logit0 = jnp.asarray(np.log(g0) - np.log1p(-g0), dtype=F32)
    return {
        "x": nrm(ks[0], (BATCH, SEQ, D_MODEL), 1.0),
        "c": nrm(ks[1], (BATCH, D_MODEL), 1.0),
        "ctx": nrm(ks[2], (BATCH, CTX_LEN, D_MODEL), 1.0),
        "c_ctx": nrm(ks[3], (D_MODEL,), 1.0),
        "w_ada": nrm(ks[4], (DEPTH, D_MODEL, 6 * D_MODEL), 0.2 * D_MODEL ** -0.5),
        "b_ada": nrm(ks[5], (DEPTH, 6 * D_MODEL), 0.02),
        "norm1_g": 1.0 + nrm(ks[6], (DEPTH, D_MODEL), 0.02),
        "w_in": nrm(ks[7], (DEPTH, D_MODEL, IN_COLS), D_MODEL ** -0.5),
        "hy_conv_w": nrm(ks[8], (DEPTH, SHORT_CONV_W, HY_PROJ), SHORT_CONV_W ** -0.5),
        "hy_conv_b": nrm(ks[9], (DEPTH, HY_PROJ), 0.02),
        "hy_f_w1": nrm(ks[10], (DEPTH, HY_EMB, HY_FILT_HID), HY_EMB ** -0.5),
        "hy_f_b1": nrm(ks[11], (DEPTH, HY_FILT_HID), 0.02),
        "hy_f_freq1": 1.0 + nrm(ks[12], (DEPTH, HY_FILT_HID), 0.02),
        "hy_f_w2": nrm(ks[13], (DEPTH, HY_FILT_HID, HY_FILT_HID), HY_FILT_HID ** -0.5),
        "hy_f_b2": nrm(ks[14], (DEPTH, HY_FILT_HID), 0.02),
        "hy_f_freq2": 1.0 + nrm(ks[15], (DEPTH, HY_FILT_HID), 0.02),
        "hy_f_w3": nrm(ks[16], (DEPTH, HY_FILT_HID, 2 * HY_WIDTH), HY_FILT_HID ** -0.5),
        "hy_bias": nrm(ks[17], (DEPTH, HY_WIDTH), 0.5),
        "ret_decay_logit": logit0[None, None, :] + nrm(ks[18], (DEPTH, 2, RET_HEADS), 0.1),
        "ret_gn_g": 1.0 + nrm(ks[19], (DEPTH, RET_WIDTH), 0.02),
        "w_out": nrm(ks[20], (DEPTH, MIX_WIDTH, D_MODEL), MIX_WIDTH ** -0.5),
        "norm2_g": 1.0 + nrm(ks[21], (DEPTH, D_MODEL), 0.02),
        "w_mlp1": nrm(ks[22], (DEPTH, D_MODEL, D_FF), D_MODEL ** -0.5),
        "w_mlp2": nrm(ks[23], (DEPTH, D_FF, D_MODEL), D_FF ** -0.5),
        "norm_f_g": 1.0 + nrm(ks[24], (D_MODEL,), 0.02),
    }


def reference(x, c, ctx, c_ctx, w_ada, b_ada, norm1_g, w_in, hy_conv_w, hy_conv_b, hy_f_w1, hy_f_b1,
              hy_f_freq1, hy_f_w2, hy_f_b2, hy_f_freq2, hy_f_w3, hy_bias, ret_decay_logit, ret_gn_g,
              w_out, norm2_g, w_mlp1, w_mlp2, norm_f_g):
    L = x.shape[1]
    rope = _rope_2d(L)
    for l in range(DEPTH):
        mx = jnp.split(jax.nn.silu(c) @ w_ada[l] + b_ada[l], 6, axis=-1)
        mc = jnp.split(jax.nn.silu(c_ctx) @ w_ada[l] + b_ada[l], 6, axis=-1)
        log_gamma = jax.nn.log_sigmoid(ret_decay_logit[l].astype(F32))
        mixer_w = (hy_conv_w[l], hy_conv_b[l], hy_f_w1[l], hy_f_b1[l], hy_f_freq1[l], hy_f_w2[l],
                   hy_f_b2[l], hy_f_freq2[l], hy_f_w3[l], hy_bias[l], log_gamma, ret_gn_g[l])
        hx = _modulate(_rmsnorm(x, norm1_g[l]), mx[0][:, None], mx[1][:, None])
        hc = _modulate(_rmsnorm(ctx, norm1_g[l]), mc[0], mc[1])
        ux = hx @ w_in[l]
        uc = hc @ w_in[l]
        kc, vc = _ret_kv(uc)
        s_f = _ret_final_state(kc, vc, log_gamma[0])
        s_b = _ret_final_state(jnp.flip(kc, 2), jnp.flip(vc, 2), log_gamma[1])
        mix_x = _token_mixers(ux, rope, s_f, s_b, *mixer_w)
        x_new = x + mx[2][:, None] * (mix_x @ w_out[l])
        hx2 = _modulate(_rmsnorm(x_new, norm2_g[l]), mx[3][:, None], mx[4][:, None])
        x_new = x_new + mx[5][:, None] * _sq_relu_mlp(hx2, w_mlp1[l], w_mlp2[l])
        if l < DEPTH - 1:
            zeros = jnp.zeros_like(s_f)
            mix_c = _token_mixers(uc, None, zeros, zeros, *mixer_w)
            ctx = ctx + mc[2] * (mix_c @ w_out[l])
            hc2 = _modulate(_rmsnorm(ctx, norm2_g[l]), mc[3], mc[4])
            ctx = ctx + mc[5] * _sq_relu_mlp(hc2, w_mlp1[l], w_mlp2[l])
        x = x_new
    return _rmsnorm(x, norm_f_g)
```

```python
import contextlib
import math
import numpy as np
import ml_dtypes
import concourse.bass as bass
import concourse.mybir as mybir
from concourse.bass_utils import run_bass_kernel_spmd

F32 = mybir.dt.float32
BF16 = mybir.dt.bfloat16
AF = mybir.ActivationFunctionType
ALU = mybir.AluOpType
AX = mybir.AxisListType

L = 8192
D = 1024
NT = 64
NFFT = 16384
ENGS = ("sync", "scalar", "vector", "gpsimd", "tensor")


class Dep:
    __slots__ = ("w", "r")

    def __init__(self):
        self.w = None
        self.r = []


class Sched:
    def __init__(self, nc, stack):
        self.nc = nc
        self.stack = stack
        self.streams = {e: [] for e in ENGS}
        self.esem = {}
        self.ecnt = {}
        for e in ("scalar", "vector", "gpsimd", "tensor"):
            self.esem[e] = stack.enter_context(nc.semaphore("es_" + e))
            self.ecnt[e] = 0
        self.dsem = {}
        self.dpool = []
        self.nds = 0
        self.waited = {e: {} for e in ENGS}

    def _wait(self, eng, ev, waits):
        if ev is None:
            return
        sem, val, src = ev
        if eng == "tensor" and src == "tensor":
            return
        key = id(sem)
        if self.waited[eng].get(key, 0) >= val:
            return
        self.waited[eng][key] = val
        waits.append((sem, val))

    def op(self, eng, fn, reads=(), writes=(), dma=None):
        waits = []
        for d in reads:
            self._wait(eng, d.w, waits)
        for d in writes:
            self._wait(eng, d.w, waits)
            for ev in d.r:
                self._wait(eng, ev, waits)
        if dma is not None:
            if dma not in self.dsem:
                if self.dpool:
                    self.dsem[dma] = self.dpool.pop()
                else:
                    self.nds += 1
                    self.dsem[dma] = [self.stack.enter_context(self.nc.semaphore("ds%d" % self.nds)), 0]
            ent = self.dsem[dma]
            ent[1] += 16
            ev = (ent[0], ent[1], "dma")
            inc = (ent[0], 16)
        else:
            self.ecnt[eng] += 1
            ev = (self.esem[eng], self.ecnt[eng], eng)
            inc = (self.esem[eng], 1)
        for d in reads:
            d.r.append(ev)
        for d in writes:
            d.w = ev
            d.r = []
        self.streams[eng].append((waits, fn, inc))
        return ev

    def barrier(self):
        evs = [(self.esem[e], self.ecnt[e], "x") for e in self.esem if self.ecnt[e] > 0]
        evs += [(v[0], v[1], "dma") for v in self.dsem.values() if v[1] > 0]
        for eng in ENGS:
            waits = []
            for ev in evs:
                key = id(ev[0])
                if self.waited[eng].get(key, 0) >= ev[1]:
                    continue
                self.waited[eng][key] = ev[1]
                waits.append((ev[0], ev[1]))
            if waits:
                self.streams[eng].append((waits, None, None))
        self.dpool.extend(self.dsem.values())
        self.dsem = {}

    def emit(self):
        nc = self.nc
        streams = self.streams

        def run(name, eng):
            for waits, fn, inc in streams[name]:
                for sem, val in waits:
                    eng.wait_ge(sem, val)
                if fn is not None:
                    fn(eng).then_inc(inc[0], inc[1])

        with nc.Block() as block:
            @block.sync
            def _(e):
                run("sync", e)

            @block.scalar
            def _(e):
                run("scalar", e)

            @block.vector
            def _(e):
                run("vector", e)

            @block.gpsimd
            def _(e):
                run("gpsimd", e)

            @block.tensor
            def _(e):
                run("tensor", e)


def sap(t, p0, pn, f0, dims):
    shp = list(t.shape)
    Fsz = int(np.prod(shp[1:]))
    return bass.AP(t, p0 * Fsz + f0, [[Fsz, pn]] + [[int(s), int(c)] for s, c in dims])


def dap(t, off, dims):
    return bass.AP(t.tensor, int(off), [[int(s), int(c)] for s, c in dims])


def _bf(a):
    return np.ascontiguousarray(a.astype(np.float32)).astype(ml_dtypes.bfloat16)


_CONSTS = None


def host_consts():
    global _CONSTS
    if _CONSTS is not None:
        return _CONSTS
    n1 = np.arange(64, dtype=np.float64)[:, None]
    k1 = np.arange(64, dtype=np.float64)[None, :]
    th1 = 2 * np.pi * n1 * (k1 + 0.5) / 128.0
    M1 = np.zeros((128, 128)); M1[:64, :64] = np.cos(th1); M1[:64, 64:] = -np.sin(th1)
    M1c = np.zeros((128, 128)); M1c[:64, :64] = np.cos(th1); M1c[:64, 64:] = np.sin(th1)
    n2 = np.arange(128, dtype=np.float64)[:, None]
    tht = 2 * np.pi * n2 * (k1 + 0.5) / NFFT
    k2 = np.arange(128, dtype=np.float64)[None, :]
    th2 = 2 * np.pi * n2 * k2 / 128.0
    C2 = np.cos(th2); S2 = np.sin(th2)
    sc = 2.0 / NFFT
    BDC = np.zeros((128, 128)); BDnS = np.zeros((128, 128))
    for c in range(2):
        BDC[c * 64:(c + 1) * 64, c * 64:(c + 1) * 64] = sc * np.cos(th1).T
        BDnS[c * 64:(c + 1) * 64, c * 64:(c + 1) * 64] = -sc * np.sin(th1).T
    fconst = np.concatenate([M1, M1c, C2, S2, -S2, C2, S2, -S2, C2, BDC, BDnS], axis=1)
    TC = np.concatenate([np.cos(tht), np.cos(tht)], axis=1)
    TS = np.concatenate([np.sin(tht), np.sin(tht)], axis=1)
    ct = np.cos(tht).T; st_ = np.sin(tht).T
    ITC = np.tile(np.concatenate([ct, ct], axis=1), (2, 1))
    ITS = np.tile(np.concatenate([st_, st_], axis=1), (2, 1))
    tconst = np.concatenate([TC, TS, ITC, ITS], axis=1).astype(np.float32)
    t = np.arange(L)
    r = (t // 64).astype(np.float32); col = (t % 64).astype(np.float32)
    inv = (10000.0 ** (-np.arange(16, dtype=np.float32) / 16)).astype(np.float32)
    ang = np.concatenate([r[:, None] * inv, col[:, None] * inv], axis=-1).astype(np.float32)
    cosr = np.cos(ang).astype(np.float32); sinr = np.sin(ang).astype(np.float32)
    def tl(a):
        return np.ascontiguousarray(a.reshape(64, 128, 32).transpose(1, 0, 2))
    rope = np.stack([tl(cosr), tl(sinr)], axis=1).astype(np.float32)
    tt = np.linspace(0.0, 1.0, L, dtype=np.float32)[:, None]
    w = ((2.0 * math.pi / L) * np.arange(L, dtype=np.float32))[:, None].astype(np.float32)
    bands = np.linspace(1e-4, 15, 16, dtype=np.float32)[None, :]
    z = np.concatenate([tt, np.cos(bands * w), -np.sin(bands * w)], axis=-1).astype(np.float32)
    order = (128 * np.arange(64)[None, :] + np.arange(128)[:, None]).reshape(-1)
    zT = np.ascontiguousarray(z[order].T).astype(np.float32)
    deltas = np.abs(np.linspace(math.log(1e-2) / 1.5, math.log(1e-2) / 0.3, 512, dtype=np.float32))
    E = np.exp(-tt * deltas[None, :]).astype(np.float32)
    E4 = E.reshape(64, 32, 4, 8, 64)
    edec = np.ascontiguousarray(E4.transpose(3, 1, 0, 2, 4)).reshape(8, 32, 64, 256).astype(np.float32)
    m = np.arange(128)[:, None]; c = np.arange(128)[None, :]
    pd = np.stack([np.maximum(c - m, 0), (c >= m), np.maximum(m - c, 0), (m >= c)], axis=1).astype(np.float32)
    p = np.arange(128, dtype=np.float32)
    pcols = np.stack([p + 1, 128 - p, 127 - p, p, 255 - p, p, 127 - p, 128 + p], axis=1).astype(np.float32)
    ident = np.eye(128, dtype=np.float32)
    _CONSTS = dict(fconst=_bf(fconst), tconst=tconst, rope=rope, zT=zT, edec=edec, pd=np.ascontiguousarray(pd),
                   pcols=pcols, ident_bf=_bf(ident), ident_f=ident)
    return _CONSTS


def build(debug=False, stop_after=99):
    nc = bass.Bass("TRN2", target_bir_lowering=False)

    def din(name, shape, dt=F32):
        return nc.dram_tensor(name, list(shape), dt, kind="ExternalInput").ap()

    def dscr(name, shape, dt):
        if debug:
            return nc.dram_tensor(name, list(shape), dt, kind="ExternalOutput").ap()
        return nc.dram_tensor(name, list(shape), dt).ap()

    x = din("x", [L, D]); ctx = din("ctx", [256, D]); cc = din("cc", [2, D])
    w_ada = din("w_ada", [D, 6 * D]); b_ada = din("b_ada", [1, 6 * D]); norm1_g = din("norm1_g", [1, D])
    w_in = din("w_in", [D, 3584]); conv_w = din("hy_conv_w", [3, 1536]); conv_b = din("hy_conv_b", [1, 1536])
    f_w1 = din("hy_f_w1", [33, 64]); f_b1 = din("hy_f_b1", [1, 64]); f_fr1 = din("hy_f_freq1", [1, 64])
    f_w2 = din("hy_f_w2", [64, 64]); f_b2 = din("hy_f_b2", [1, 64]); f_fr2 = din("hy_f_freq2", [1, 64])
    f_w3 = din("hy_f_w3", [64, 1024]); hy_bias = din("hy_bias", [1, 512]); logit = din("ret_decay_logit", [1, 16])
    gn_g = din("ret_gn_g", [1, 512]); w_out = din("w_out", [D, D]); norm2_g = din("norm2_g", [1, D])
    w_mlp1 = din("w_mlp1", [D, 4 * D]); w_mlp2 = din("w_mlp2", [4 * D, D]); norm_f_g = din("norm_f_g", [1, D])
    fconst_d = din("fconst", [128, 1408], BF16); tconst_d = din("tconst", [128, 768])
    rope_d = din("rope", [128, 2, 64, 32]); zT_d = din("zT", [33, L]); edec_d = din("edec", [8, 32, 64, 256])
    pd_d = din("pd", [128, 4, 128]); pcols_d = din("pcols", [128, 8])
    identb_d = din("ident_bf", [128, 128], BF16); identf_d = din("ident_f", [128, 128])
    out = nc.dram_tensor("out", [L, D], F32, kind="ExternalOutput").ap()

    modscr = dscr("modscr", [2, 6 * D], F32)
    vxscr = dscr("vxscr", [512, L], BF16); x0scr = dscr("x0scr", [512, L], BF16); yscr = dscr("yscr", [512, L], BF16)
    qscr = dscr("qscr", [L, 512], BF16); kscr = dscr("kscr", [L, 512], BF16)
    vscr = dscr("vscr", [L, 512], BF16); gscr = dscr("gscr", [L, 512], BF16)
    Tscr = dscr("Tscr", [NT, 128, 512], F32); Sscr = dscr("Sscr", [NT, 128, 512], BF16)
    kspec = dscr("kspec", [2, 128, 512 * 64], BF16) if debug else None

    final_events = []

    with contextlib.ExitStack() as gst:
        S = Sched(nc, gst)
        uid = [0]

        def sb(st, shape, dt, name=None):
            uid[0] += 1
            t = st.enter_context(nc.sbuf_tensor(name or ("t%d" % uid[0]), list(shape), dt))
            return t, Dep()

        def ps(st, shape, dt, name=None):
            uid[0] += 1
            t = st.enter_context(nc.psum_tensor(name or ("p%d" % uid[0]), list(shape), dt))
            return t, Dep()

        def load(eng, key, dst_ap, src_ap, dst_dep, src_deps=(), slow=False):
            if slow:
                return S.op(eng, lambda e: e.dma_start(out=dst_ap, in_=src_ap, allow_slow_non_contiguous=True), reads=list(src_deps), writes=[dst_dep], dma=key)
            return S.op(eng, lambda e: e.dma_start(out=dst_ap, in_=src_ap), reads=list(src_deps), writes=[dst_dep], dma=key)

        def store(eng, key, dst_ap, src_ap, src_dep, dst_deps=(), slow=False):
            if slow:
                return S.op(eng, lambda e: e.dma_start(out=dst_ap, in_=src_ap, allow_slow_non_contiguous=True), reads=[src_dep], writes=list(dst_deps), dma=key)
            return S.op(eng, lambda e: e.dma_start(out=dst_ap, in_=src_ap), reads=[src_dep], writes=list(dst_deps), dma=key)

        def row_bc(ap_dram, off, n):
            return dap(ap_dram, off, [(0, 128), (1, n)])

        identb, d_identb = sb(gst, [128, 128], BF16)
        load("sync", "c_idb", identb[:], identb_d, d_identb)
        pcols, d_pcols = sb(gst, [128, 8], F32)
        load("sync", "c_pc", pcols[:], pcols_d, d_pcols)
        gs2, d_gs2 = sb(gst, [128, D], F32); gate2, d_gate2 = sb(gst, [128, D], F32)
        gate5, d_gate5 = sb(gst, [128, D], F32); gF, d_gF = sb(gst, [128, D], F32)
        colx, d_colx = sb(gst, [128, 6, 8], F32)
        lgt, d_lgt = sb(gst, [128, 16], F32)
        lgsel, d_lgsel = sb(gst, [128, 8], F32)
        DT, d_DT = sb(gst, [128, 8, 128], F32)
        Wq, d_Wq = sb(gst, [128, 8, 2], F32)
        Dec, d_Dec = sb(gst, [128, 8], F32)
        S0, d_S0 = sb(gst, [128, 512], F32)
        a1st = gst.enter_context(contextlib.ExitStack())
        gs1, d_gs1 = sb(a1st, [128, D], F32); gs1c, d_gs1c = sb(a1st, [128, D], F32)
        colc, d_colc = sb(a1st, [128, 2, 8], F32)
        Wk, d_Wk = sb(a1st, [128, 8, 2], F32)
        Wkc, d_Wkc = sb(a1st, [128, 2, 8, 2], F32)

        with contextlib.ExitStack() as st:
            ccT, d_ccT = sb(st, [128, 8, 2], F32)
            for r_ in range(2):
                load("sync", "a_cc", ccT[:, :, r_:r_ + 1], dap(cc, r_ * D, [(1, 128), (128, 8), (1, 1)]), d_ccT, slow=True)
            scT, d_scT = sb(st, [128, 8, 2], F32)
            S.op("scalar", lambda e: e.activation(out=scT[:], in_=ccT[:], func=AF.Silu), reads=[d_ccT], writes=[d_scT])
            bada, d_bada = sb(st, [2, 6 * D], F32)
            load("sync", "a_bada", bada[:], dap(b_ada, 0, [(0, 2), (1, 6 * D)]), d_bada)
            modsb, d_modsb = sb(st, [2, 6 * D], F32)
            wab = [sb(st, [128, 8, 512], F32) for _ in range(2)]
            pM = [ps(st, [128, 512], F32) for _ in range(2)]
            for cb in range(12):
                wa, d_wa = wab[cb % 2]
                load("sync", "a_wa%d" % (cb % 2), wa[:],
                     w_ada[:, cb * 512:(cb + 1) * 512].rearrange("(k p) n -> p k n", p=128), d_wa)
                pm, d_pm = pM[cb % 2]
                for k in range(8):
                    S.op("tensor", lambda e, pm=pm, wa=wa, k=k: e.matmul(pm[0:2, :], lhsT=scT[:, k, :], rhs=wa[:, k, :],
                                                                        start=(k == 0), stop=(k == 7)),
                         reads=[d_scT, d_wa], writes=[d_pm])
                S.op("vector", lambda e, pm=pm, cb=cb: e.tensor_tensor(out=modsb[:, cb * 512:(cb + 1) * 512], in0=pm[0:2, :],
                                                                      in1=bada[:, cb * 512:(cb + 1) * 512], op=ALU.add),
                     reads=[d_pm, d_bada], writes=[d_modsb])
            d_modscr = Dep()
            store("sync", "a_modst", modscr, modsb[:], d_modsb, [d_modscr])
            for j_ in range(6):
                load("sync", "a_colx", colx[:, j_, :].rearrange("p (k o) -> p k o", o=1),
                     dap(modscr, j_ * D, [(1, 128), (128, 8), (1, 1)]), d_colx, [d_modscr], slow=True)
            for j_ in range(2):
                load("sync", "a_colc", colc[:, j_, :].rearrange("p (k o) -> p k o", o=1),
                     dap(modscr, 6 * D + j_ * D, [(1, 128), (128, 8), (1, 1)]), d_colc, [d_modscr], slow=True)
            tmpA, d_tmpA = sb(st, [128, D], F32)
            tmpB, d_tmpB = sb(st, [128, D], F32)

            def make_gs(dst, d_dst, scale_off, g_dram, tagn):
                load("sync", "a_tA", tmpA[:], row_bc(modscr, scale_off, D), d_tmpA, [d_modscr])
                load("sync", "a_tB", tmpB[:], row_bc(g_dram, 0, D), d_tmpB)
                S.op("vector", lambda e: e.scalar_tensor_tensor(out=dst[:], in0=tmpA[:], scalar=1.0, op0=ALU.add,
                                                                in1=tmpB[:], op1=ALU.mult),
                     reads=[d_tmpA, d_tmpB], writes=[d_dst])
            make_gs(gs1, d_gs1, 1 * D, norm1_g, 0)
            make_gs(gs1c, d_gs1c, 6 * D + 1 * D, norm1_g, 1)
            make_gs(gs2, d_gs2, 4 * D, norm2_g, 2)
            load("sync", "a_g2", gate2[:], row_bc(modscr, 2 * D, D), d_gate2, [d_modscr])
            load("sync", "a_g5", gate5[:], row_bc(modscr, 5 * D, D), d_gate5, [d_modscr])
            load("sync", "a_gF", gF[:], row_bc(norm_f_g, 0, D), d_gF)

            lraw, d_lraw = sb(st, [128, 16], F32)
            load("sync", "a_lg", lraw[:], row_bc(logit, 0, 16), d_lraw)
            S.op("scalar", lambda e: e.activation(out=lgt[:], in_=lraw[:], func=AF.Exp, scale=-1.0), reads=[d_lraw], writes=[d_lgt])
            S.op("scalar", lambda e: e.activation(out=lgt[:], in_=lgt[:], func=AF.Ln, bias=1.0), reads=[d_lgt], writes=[d_lgt])
            S.op("scalar", lambda e: e.mul(lgt[:], lgt[:], -1.0), reads=[d_lgt], writes=[d_lgt])
            S.op("vector", lambda e: e.tensor_copy(out=lgsel[0:64, :], in_=lgt[0:64, 0:8]), reads=[d_lgt], writes=[d_lgsel])
            S.op("vector", lambda e: e.tensor_copy(out=lgsel[64:128, :], in_=lgt[64:128, 8:16]), reads=[d_lgt], writes=[d_lgsel])
            S.op("scalar", lambda e: e.activation(out=Dec[:], in_=lgsel[:], func=AF.Exp, scale=128.0), reads=[d_lgsel], writes=[d_Dec])
            for (dst, d_dst, cf, cbk) in ((Wq, d_Wq, 0, 1), (Wk, d_Wk, 2, 3)):
                S.op("scalar", lambda e, dst=dst, cf=cf: e.activation(out=dst[:, :, 0], in_=lgt[:, 0:8], func=AF.Exp, scale=pcols[:, cf:cf + 1]),
                     reads=[d_lgt, d_pcols], writes=[d_dst])
                S.op("scalar", lambda e, dst=dst, cbk=cbk: e.activation(out=dst[:, :, 1], in_=lgt[:, 8:16], func=AF.Exp, scale=pcols[:, cbk:cbk + 1]),
                     reads=[d_lgt, d_pcols], writes=[d_dst])
            S.op("vector", lambda e: e.tensor_scalar(out=Wk[:], in0=Wk[:], scalar1=0.125, scalar2=None, op0=ALU.mult), reads=[d_Wk], writes=[d_Wk])
            for tI in range(2):
                S.op("scalar", lambda e, tI=tI: e.activation(out=Wkc[:, tI, :, 0], in_=lgt[:, 0:8], func=AF.Exp, scale=pcols[:, 4 + 2 * tI:5 + 2 * tI]),
                     reads=[d_lgt, d_pcols], writes=[d_Wkc])
                S.op("scalar", lambda e, tI=tI: e.activation(out=Wkc[:, tI, :, 1], in_=lgt[:, 8:16], func=AF.Exp, scale=pcols[:, 5 + 2 * tI:6 + 2 * tI]),
                     reads=[d_lgt, d_pcols], writes=[d_Wkc])
            pdt, d_pdt = sb(st, [128, 4, 128], F32)
            load("sync", "a_pd", pdt[:], pd_d, d_pdt)
            ef, d_ef = sb(st, [128, 128], F32); eb, d_eb = sb(st, [128, 128], F32)
            for h in range(8):
                S.op("scalar", lambda e, h=h: e.activation(out=ef[:], in_=pdt[:, 0, :], func=AF.Exp, scale=lgt[:, h:h + 1]),
                     reads=[d_pdt, d_lgt], writes=[d_ef])
                S.op("scalar", lambda e, h=h: e.activation(out=eb[:], in_=pdt[:, 2, :], func=AF.Exp, scale=lgt[:, 8 + h:9 + h]),
                     reads=[d_pdt, d_lgt], writes=[d_eb])
                S.op("vector", lambda e: e.scalar_tensor_tensor(out=ef[:], in0=ef[:], scalar=0.125, op0=ALU.mult, in1=pdt[:, 1, :], op1=ALU.mult), reads=[d_ef, d_pdt], writes=[d_ef])
                S.op("vector", lambda e: e.scalar_tensor_tensor(out=eb[:], in0=eb[:], scalar=0.125, op0=ALU.mult, in1=pdt[:, 3, :], op1=ALU.mult), reads=[d_eb, d_pdt], writes=[d_eb])
                S.op("vector", lambda e, h=h: e.tensor_tensor(out=DT[:, h, :], in0=ef[:], in1=eb[:], op=ALU.add), reads=[d_ef, d_eb], writes=[d_DT])
            S.barrier()
        if stop_after <= 0:
            S.emit()
            return nc

        d_vx = Dep(); d_x0 = Dep(); d_q = Dep(); d_k = Dep(); d_v = Dep(); d_g = Dep(); d_T = Dep()
        with contextlib.ExitStack() as st:
            Win, d_Win = sb(st, [128, 8, 3584], BF16)
            for k in range(8):
                S.op("gpsimd", lambda e, k=k: e.dma_start(out=Win[:, k, :], in_=w_in[k * 128:(k + 1) * 128, :]), writes=[d_Win], dma="p1_win%d" % k)
            ropeT, d_rope = sb(st, [128, 2, 64, 32], F32)
            load("sync", "p1_rope", ropeT[:], rope_d, d_rope)
            cw, d_cw = sb(st, [128, 12, 4], F32)
            for j in range(3):
                load("sync", "p1_cw", cw[:, :, j:j + 1], dap(conv_w, j * 1536, [(1, 128), (128, 12), (1, 1)]), d_cw, slow=True)
            load("sync", "p1_cw", cw[:, :, 3:4], dap(conv_b, 0, [(1, 128), (128, 12), (1, 1)]), d_cw, slow=True)

            xb = [sb(st, [128, D], F32) for _ in range(3)]
            junk, d_junk = sb(st, [128, D], BF16)
            ssq = [sb(st, [128, 1], F32) for _ in range(3)]
            xm = [sb(st, [128, D], BF16) for _ in range(2)]
            hxT = [sb(st, [128, 8, 512], BF16) for _ in range(2)]
            U = [sb(st, [128, 514], F32) for _ in range(12)]
            cv1 = [sb(st, [128, 512], F32) for _ in range(2)]
            cv2 = [sb(st, [128, 512], F32) for _ in range(2)]
            cvx1 = [sb(st, [128, 512], F32) for _ in range(4)]
            cv3, d_cv3 = sb(st, [128, 512], F32)
            hyo = [sb(st, [128, 512], BF16) for _ in range(4)]
            P1, d_P1 = sb(st, [128, 512], F32); P2, d_P2 = sb(st, [128, 512], F32)
            qo = [sb(st, [128, 512], BF16) for _ in range(2)]
            ko = [sb(st, [128, 512], BF16) for _ in range(2)]
            vo = [sb(st, [128, 512], BF16) for _ in range(2)]
            go = [sb(st, [128, 512], BF16) for _ in range(2)]
            kw, d_kw = sb(st, [128, 8, 2, 64], BF16)
            Tsb = [sb(st, [128, 512], F32) for _ in range(2)]
            pT = [ps(st, [128, 8, 128], BF16) for _ in range(2)]
            pU = [ps(st, [128, 512], F32) for _ in range(2)]
            pR = [ps(st, [128, 512], F32) for _ in range(2)]
            pTs = [ps(st, [128, 512], F32) for _ in range(2)]
            for ft in range(12):
                S.op("gpsimd", lambda e, ft=ft: e.memset(U[ft][0][:, 0:2], 0.0), writes=[U[ft][1]])

            srcs = [ctx[0:128, :], ctx[128:256, :]] + [x[i * 128:(i + 1) * 128, :] for i in range(NT)]

            def xload(s_):
                if s_ < len(srcs):
                    load("sync", "p1_x%d" % (s_ % 3), xb[s_ % 3][0][:], srcs[s_], xb[s_ % 3][1])

            xload(0)

            def norm_transpose(i, gsrow, d_gsrow, shcol_fn, d_shcol, hx_tile, d_hx, tok0):
                xload(i + 1)
                xt, d_xt = xb[i % 3]
                sq, d_sq = ssq[i % 3]
                S.op("scalar", lambda e: e.activation(out=junk[:], in_=xt[:], func=AF.Square, accum_out=sq[:]),
                     reads=[d_xt], writes=[d_junk, d_sq])
                S.op("scalar", lambda e: e.activation(out=sq[:], in_=sq[:], func=AF.Sqrt, scale=1.0 / D, bias=1e-6),
                     reads=[d_sq], writes=[d_sq])
                S.op("vector", lambda e: e.reciprocal(out=sq[:], in_=sq[:]), reads=[d_sq], writes=[d_sq])
                xmt, d_xmt = xm[i % 2]
                S.op("vector", lambda e: e.scalar_tensor_tensor(out=xmt[:], in0=xt[:], scalar=sq[:, 0:1], op0=ALU.mult,
                                                                in1=gsrow[:], op1=ALU.mult),
                     reads=[d_xt, d_sq, d_gsrow], writes=[d_xmt])
                pt, d_pt = pT[i % 2]
                for k in range(8):
                    S.op("tensor", lambda e, k=k: e.transpose(out=pt[:, k, :], in_=xmt[:, k * 128:(k + 1) * 128], identity=identb[:]),
                         reads=[d_xmt, d_identb], writes=[d_pt])
                for k in range(8):
                    S.op("scalar", lambda e, k=k: e.activation(out=hx_tile[:, k, tok0:tok0 + 128], in_=pt[:, k, :], func=AF.Identity,
                                                               bias=shcol_fn(k)),
                         reads=[d_pt, d_shcol], writes=[d_hx])

            def proj_tok(hx_tile, d_hx, tok0, col0, pr, d_pr):
                for k in range(8):
                    S.op("tensor", lambda e, k=k: e.matmul(pr[:], lhsT=hx_tile[:, k, tok0:tok0 + 128], rhs=Win[:, k, col0:col0 + 512],
                                                           start=(k == 0), stop=(k == 7)),
                         reads=[d_hx, d_Win], writes=[d_pr])

            hxc, d_hxc = hxT[0]
            kc = []; vc = []
            for tI in range(2):
                norm_transpose(tI, gs1c, d_gs1c, lambda k: colc[:, 0, k:k + 1], d_colc, hxc, d_hxc, tI * 128)
                pr, d_pr = pR[0]
                proj_tok(hxc, d_hxc, tI * 128, 1536 + 512, pr, d_pr)
                kt, d_kt = ko[tI]
                S.op("scalar", lambda e, kt=kt, pr=pr: e.mul(kt[:], pr[:], 0.125), reads=[d_pr], writes=[d_kt])
                pr2, d_pr2 = pR[1]
                proj_tok(hxc, d_hxc, tI * 128, 1536 + 1024, pr2, d_pr2)
                vt, d_vt = vo[tI]
                S.op("scalar", lambda e, vt=vt, pr2=pr2: e.copy(vt[:], pr2[:]), reads=[d_pr2], writes=[d_vt])
                kc.append((kt, d_kt)); vc.append((vt, d_vt))
            kwc = [sb(st, [128, 8, 2, 64], BF16) for _ in range(2)]
            for tI in range(2):
                kt, d_kt = kc[tI]
                kwt, d_kwt = kwc[tI]
                S.op("vector", lambda e, kt=kt, kwt=kwt, tI=tI: e.tensor_tensor(
                    out=kwt[:], in0=sap(kt, 0, 128, 0, [(64, 8), (0, 2), (1, 64)]),
                    in1=sap(Wkc, 0, 128, tI * 16, [(2, 8), (1, 2), (0, 64)]), op=ALU.mult),
                    reads=[d_kt, d_Wkc], writes=[d_kwt])
            pS0, d_pS0 = pTs[0]
            for h in range(8):
                for tI in range(2):
                    S.op("tensor", lambda e, h=h, tI=tI: e.matmul(pS0[:, h * 64:(h + 1) * 64], lhsT=kwc[tI][0][:, h, :, :],
                                                                 rhs=vc[tI][0][:, h * 64:(h + 1) * 64], start=(tI == 0), stop=(tI == 1)),
                         reads=[kwc[tI][1], vc[tI][1]], writes=[d_pS0])
            S.op("vector", lambda e: e.tensor_copy(out=S0[:], in_=pS0[:]), reads=[d_pS0], writes=[d_S0])

            for i in range(NT):
                B, ii = divmod(i, 4)
                hx_tile, d_hx = hxT[B % 2]
                norm_transpose(i + 2, gs1, d_gs1, lambda k: colx[:, 0, k:k + 1], d_colx, hx_tile, d_hx, ii * 128)
                for cbk in range(4):
                    pr, d_pr = pR[cbk % 2]
                    proj_tok(hx_tile, d_hx, ii * 128, 1536 + cbk * 512, pr, d_pr)
                    if cbk < 2:
                        ot, d_ot = (qo if cbk == 0 else ko)[i % 2]
                        S.op("vector", lambda e, pr=pr, i=i: e.tensor_tensor(
                            out=P1[:].rearrange("p (h j t) -> p h j t", h=8, t=2), in0=pr[:].rearrange("p (h j t) -> p h j t", h=8, t=2),
                            in1=sap(ropeT, 0, 128, (0 * 64 + i) * 32, [(0, 8), (1, 32), (0, 2)]), op=ALU.mult),
                            reads=[d_pr, d_rope], writes=[d_P1])
                        S.op("vector", lambda e, pr=pr, i=i: e.tensor_tensor(
                            out=P2[:].rearrange("p (h j t) -> p h j t", h=8, t=2), in0=pr[:].rearrange("p (h j t) -> p h j t", h=8, t=2),
                            in1=sap(ropeT, 0, 128, (1 * 64 + i) * 32, [(0, 8), (1, 32), (0, 2)]), op=ALU.mult),
                            reads=[d_pr, d_rope], writes=[d_P2])
                        S.op("gpsimd", lambda e, ot=ot: e.tensor_tensor(out=sap(ot, 0, 128, 0, [(2, 256)]), in0=sap(P1, 0, 128, 0, [(2, 256)]),
                                                                       in1=sap(P2, 0, 128, 1, [(2, 256)]), op=ALU.subtract),
                             reads=[d_P1, d_P2], writes=[d_ot])
                        S.op("gpsimd", lambda e, ot=ot: e.tensor_tensor(out=sap(ot, 0, 128, 1, [(2, 256)]), in0=sap(P2, 0, 128, 0, [(2, 256)]),
                                                                       in1=sap(P1, 0, 128, 1, [(2, 256)]), op=ALU.add),
                             reads=[d_P1, d_P2], writes=[d_ot])
                        scr = qscr if cbk == 0 else kscr
                        store("sync", ("p1_q%d" if cbk == 0 else "p1_k%d") % (i % 2), scr[i * 128:(i + 1) * 128, :], ot[:], d_ot)
                        if cbk == 1:
                            S.op("vector", lambda e, ot=ot: e.tensor_tensor(
                                out=kw[:], in0=sap(ot, 0, 128, 0, [(64, 8), (0, 2), (1, 64)]),
                                in1=sap(Wk, 0, 128, 0, [(2, 8), (1, 2), (0, 64)]), op=ALU.mult),
                                reads=[d_ot, d_Wk], writes=[d_kw])
                    elif cbk == 2:
                        vt, d_vt = vo[i % 2]
                        S.op("scalar", lambda e, vt=vt, pr=pr: e.copy(vt[:], pr[:]), reads=[d_pr], writes=[d_vt])
                        store("sync", "p1_v%d" % (i % 2), vscr[i * 128:(i + 1) * 128, :], vt[:], d_vt)
                    else:
                        gt, d_gt = go[i % 2]
                        S.op("scalar", lambda e, gt=gt, pr=pr: e.activation(out=gt[:], in_=pr[:], func=AF.Silu), reads=[d_pr], writes=[d_gt])
                        store("sync", "p1_g%d" % (i % 2), gscr[i * 128:(i + 1) * 128, :], gt[:], d_gt)
                pts, d_pts = pTs[i % 2]
                vt, d_vt = vo[i % 2]
                for h in range(8):
                    S.op("tensor", lambda e, h=h, pts=pts, vt=vt: e.matmul(pts[:, h * 64:(h + 1) * 64], lhsT=kw[:, h, :, :],
                                                                          rhs=vt[:, h * 64:(h + 1) * 64], start=True, stop=True),
                         reads=[d_kw, d_vt], writes=[d_pts])
                tsb, d_tsb = Tsb[i % 2]
                S.op("scalar", lambda e, tsb=tsb, pts=pts: e.copy(tsb[:], pts[:]), reads=[d_pts], writes=[d_tsb])
                store("sync", "p1_T%d" % (i % 2), Tscr[i], tsb[:], d_tsb)
                if ii == 3:
                    s0 = 1 if B == 0 else 0
                    tok_lo = 512 * B - 1 + s0
                    for ft in (4, 5, 6, 7, 8, 9, 10, 11, 0, 1, 2, 3):
                        pu, d_pu = pU[ft % 2]
                        for k in range(8):
                            S.op("tensor", lambda e, k=k, ft=ft, pu=pu, hx_tile=hx_tile: e.matmul(pu[:], lhsT=Win[:, k, ft * 128:(ft + 1) * 128], rhs=hx_tile[:, k, :],
                                                                                start=(k == 0), stop=(k == 7)),
                                 reads=[d_Win, d_hx], writes=[d_pu])
                        u, d_u = U[ft]
                        S.op("scalar", lambda e, u=u, pu=pu: e.copy(u[:, 2:514], pu[:]), reads=[d_pu], writes=[d_u])
                        c1, d_c1 = cv1[ft % 2]; c2, d_c2 = cv2[ft % 2]
                        S.op("gpsimd", lambda e, u=u, c1=c1, ft=ft: e.tensor_scalar(out=c1[:], in0=u[:, 0:512], scalar1=cw[:, ft, 0:1], scalar2=cw[:, ft, 3:4],
                                                                                   op0=ALU.mult, op1=ALU.add),
                             reads=[d_u, d_cw], writes=[d_c1])
                        S.op("vector", lambda e, u=u, c1=c1, c2=c2, ft=ft: e.scalar_tensor_tensor(out=c2[:], in0=u[:, 1:513], scalar=cw[:, ft, 1:2], op0=ALU.mult,
                                                                                                 in1=c1[:], op1=ALU.add),
                             reads=[d_u, d_cw, d_c1], writes=[d_c2])
                        ct = ft % 4
                        if ft < 4:
                            ho, d_ho = hyo[ct]
                            S.op("vector", lambda e, u=u, c2=c2, ho=ho, ft=ft: e.scalar_tensor_tensor(out=ho[:], in0=u[:, 2:514], scalar=cw[:, ft, 2:3], op0=ALU.mult,
                                                                                                     in1=c2[:], op1=ALU.add),
                                 reads=[d_u, d_cw, d_c2], writes=[d_ho])
                            store("sync", "p1_hy%d" % ct, x0scr[ct * 128:(ct + 1) * 128, tok_lo:512 * B + 511], ho[:, s0:512], d_ho)
                        elif ft < 8:
                            cx, d_cx = cvx1[ct]
                            S.op("vector", lambda e, u=u, c2=c2, cx=cx, ft=ft: e.scalar_tensor_tensor(out=cx[:], in0=u[:, 2:514], scalar=cw[:, ft, 2:3], op0=ALU.mult,
                                                                                                     in1=c2[:], op1=ALU.add),
                                 reads=[d_u, d_cw, d_c2], writes=[d_cx])
                        else:
                            cx, d_cx = cvx1[ct]
                            S.op("vector", lambda e, u=u, c2=c2, ft=ft: e.scalar_tensor_tensor(out=cv3[:], in0=u[:, 2:514], scalar=cw[:, ft, 2:3], op0=ALU.mult,
                                                                                              in1=c2[:], op1=ALU.add),
                                 reads=[d_u, d_cw, d_c2], writes=[d_cv3])
                            ho, d_ho = hyo[ct]
                            S.op("gpsimd", lambda e, cx=cx, ho=ho: e.tensor_tensor(out=ho[:], in0=cv3[:], in1=cx[:], op=ALU.mult),
                                 reads=[d_cv3, d_cx], writes=[d_ho])
                            store("sync", "p1_hy%d" % ct, vxscr[ct * 128:(ct + 1) * 128, tok_lo:512 * B + 511], ho[:, s0:512], d_ho)
                        S.op("gpsimd", lambda e, u=u: e.tensor_copy(out=u[:, 0:2], in_=u[:, 512:514]), reads=[d_u], writes=[d_u])
            tl, d_tl = sb(st, [128, 12], F32)
            tlb, d_tlb = sb(st, [128, 8], BF16)
            for ft in range(12):
                u, d_u = U[ft]
                S.op("vector", lambda e, u=u, ft=ft: e.tensor_scalar(out=tl[:, ft:ft + 1], in0=u[:, 0:1], scalar1=cw[:, ft, 0:1], scalar2=cw[:, ft, 3:4],
                                                                    op0=ALU.mult, op1=ALU.add), reads=[d_u, d_cw], writes=[d_tl])
                S.op("vector", lambda e, u=u, ft=ft: e.scalar_tensor_tensor(out=tl[:, ft:ft + 1], in0=u[:, 1:2], scalar=cw[:, ft, 1:2], op0=ALU.mult,
                                                                           in1=tl[:, ft:ft + 1], op1=ALU.add), reads=[d_u, d_cw, d_tl], writes=[d_tl])
            S.op("vector", lambda e: e.tensor_copy(out=tlb[:, 0:4], in_=tl[:, 0:4]), reads=[d_tl], writes=[d_tlb])
            S.op("vector", lambda e: e.tensor_tensor(out=tlb[:, 4:8], in0=tl[:, 4:8], in1=tl[:, 8:12], op=ALU.mult), reads=[d_tl], writes=[d_tlb])
            for ct in range(4):
                store("sync", "p1_tl%d" % ct, dap(x0scr, ct * 128 * L + L - 1, [(L, 128), (1, 1)]), tlb[:, ct:ct + 1], d_tlb, slow=True)
                store("sync", "p1_tv%d" % ct, dap(vxscr, ct * 128 * L + L - 1, [(L, 128), (1, 1)]), tlb[:, 4 + ct:5 + ct], d_tlb, slow=True)
            S.barrier()
        a1st.close()
        if stop_after <= 1:
            S.emit()
            return nc
        with contextlib.ExitStack() as st:
            fc, d_fc = sb(st, [128, 1408], BF16)
            load("sync", "f_fc", fc[:], fconst_d, d_fc)
            tcs, d_tcs = sb(st, [128, 768], F32)
            load("sync", "f_tc", tcs[:], tconst_d, d_tcs)
            M1o, M1co, C2o, S2o, nS2o, C2S2o, nS2C2o, BDCo, BDnSo = 0, 128, 256, 384, 512, 640, 896, 1152, 1280
            w1t, d_w1t = sb(st, [33, 64], F32); load("sync", "f_w1", w1t[:], f_w1, d_w1t)
            w2t, d_w2t = sb(st, [64, 64], F32); load("sync", "f_w2", w2t[:], f_w2, d_w2t)
            w3b, d_w3b = sb(st, [64, 1024], BF16)
            S.op("gpsimd", lambda e: e.dma_start(out=w3b[:], in_=f_w3), writes=[d_w3b], dma="f_w3")
            fcol, d_fcol = sb(st, [64, 4], F32)
            for j_, src in enumerate((f_fr1, f_b1, f_fr2, f_b2)):
                load("sync", "f_col", fcol[:, j_:j_ + 1], dap(src, 0, [(1, 64), (1, 1)]), d_fcol, slow=True)
            fab, d_fab = sb(st, [64, 4], F32)
            for l_ in range(2):
                S.op("vector", lambda e, l_=l_: e.tensor_scalar(out=fab[:, 2 * l_:2 * l_ + 1], in0=fcol[:, 2 * l_:2 * l_ + 1], scalar1=1.0 / 3.0, scalar2=None, op0=ALU.mult),
                     reads=[d_fcol], writes=[d_fab])
                S.op("vector", lambda e, l_=l_: e.tensor_tensor(out=fab[:, 2 * l_ + 1:2 * l_ + 2], in0=fab[:, 2 * l_:2 * l_ + 1], in1=fcol[:, 2 * l_ + 1:2 * l_ + 2], op=ALU.mult),
                     reads=[d_fcol, d_fab], writes=[d_fab])
            onesf, d_onesf = sb(st, [64, 128], F32)
            S.op("gpsimd", lambda e: e.memset(onesf[:], 1.0), writes=[d_onesf])
            h2T, d_h2T = sb(st, [64, L], BF16)
            banks = [ps(st, [128, 512], F32) for _ in range(8)]
            bctr = [0]

            def nb():
                b_ = banks[bctr[0] % 8]
                bctr[0] += 1
                return b_

            zt = [sb(st, [33, 512], F32) for _ in range(2)]
            sA, d_sA = sb(st, [64, 512], F32); sB, d_sB = sb(st, [64, 512], F32); h1, d_h1 = sb(st, [64, 512], F32)

            def sin3(pf, d_pf, layer, out_ap, d_out):
                S.op("scalar", lambda e: e.activation(out=sA[:], in_=pf[0:64, :], func=AF.Sin, scale=fab[:, 2 * layer:2 * layer + 1],
                                                      bias=fab[:, 2 * layer + 1:2 * layer + 2]), reads=[d_pf, d_fab], writes=[d_sA])
                S.op("vector", lambda e: e.tensor_tensor(out=sB[:], in0=sA[:], in1=sA[:], op=ALU.mult), reads=[d_sA], writes=[d_sB])
                S.op("vector", lambda e: e.tensor_scalar(out=sB[:], in0=sB[:], scalar1=-4.0, scalar2=3.0, op0=ALU.mult, op1=ALU.add), reads=[d_sB], writes=[d_sB])
                S.op("vector", lambda e: e.tensor_tensor(out=out_ap, in0=sB[:], in1=sA[:], op=ALU.mult), reads=[d_sA, d_sB], writes=[d_out])

            def mlp_block(blk):
                z, d_z = zt[blk % 2]
                load("sync", "f_z%d" % (blk % 2), z[:], zT_d[:, blk * 512:(blk + 1) * 512], d_z)
                pf, d_pf = nb()
                S.op("tensor", lambda e: e.matmul(pf[0:64, :], lhsT=w1t[:], rhs=z[:], start=True, stop=True), reads=[d_w1t, d_z], writes=[d_pf])
                sin3(pf, d_pf, 0, h1[:], d_h1)
                pf2, d_pf2 = nb()
                S.op("tensor", lambda e: e.matmul(pf2[0:64, :], lhsT=w2t[:], rhs=h1[:], start=True, stop=True), reads=[d_w2t, d_h1], writes=[d_pf2])
                sin3(pf2, d_pf2, 1, h2T[:, blk * 512:(blk + 1) * 512], d_h2T)

            for blk in range(16):
                mlp_block(blk)

            Hbuf, d_Hbuf = sb(st, [128, 2 * 64 * 128], BF16)
            Acc4, d_Acc4 = sb(st, [64, 512], F32)
            habs, d_habs = sb(st, [64, 512], F32)
            Et = [sb(st, [64, 256], F32) for _ in range(2)]
            hd32 = [sb(st, [64, 512], F32) for _ in range(2)]
            nsb, d_nsb = sb(st, [128, 128], F32)
            rn, d_rn = sb(st, [128, 64], F32)
            BfR, d_BfR = sb(st, [128, 64, 64], BF16); BfI, d_BfI = sb(st, [128, 64, 64], BF16)
            BbR, d_BbR = sb(st, [128, 64, 64], BF16); BbI, d_BbI = sb(st, [128, 64, 64], BF16)
            KR, d_KR = sb(st, [128, 64, 64], BF16); KI, d_KI = sb(st, [128, 64, 64], BF16)
            Xg, d_Xg = sb(st, [64, 64, 128], BF16)
            Q1, d_Q1 = sb(st, [128, 512], F32); Q2, d_Q2 = sb(st, [128, 512], F32)
            t1, d_t1 = sb(st, [128, 512], F32); t2, d_t2 = sb(st, [128, 512], F32)
            t3, d_t3 = sb(st, [128, 512], F32); t4, d_t4 = sb(st, [128, 512], F32)
            Yo, d_Yo = sb(st, [128, 32, 128], BF16)

            def twiddle(pa, d_pa, conj, outR, d_outR, outI, d_outI, c0):
                pav = pa[:].rearrange("p (c x) -> p c x", c=4)
                S.op("vector", lambda e: e.tensor_tensor(out=Q1[:].rearrange("p (c x) -> p c x", c=4), in0=pav,
                                                         in1=sap(tcs, 0, 128, 0, [(0, 4), (1, 128)]), op=ALU.mult), reads=[d_pa, d_tcs], writes=[d_Q1])
                S.op("vector", lambda e: e.tensor_tensor(out=Q2[:].rearrange("p (c x) -> p c x", c=4), in0=pav,
                                                         in1=sap(tcs, 0, 128, 128, [(0, 4), (1, 128)]), op=ALU.mult), reads=[d_pa, d_tcs], writes=[d_Q2])
                q1lo = sap(Q1, 0, 128, 0, [(128, 4), (1, 64)]); q1hi = sap(Q1, 0, 128, 64, [(128, 4), (1, 64)])
                q2lo = sap(Q2, 0, 128, 0, [(128, 4), (1, 64)]); q2hi = sap(Q2, 0, 128, 64, [(128, 4), (1, 64)])
                S.op("gpsimd", lambda e: e.tensor_tensor(out=outR[:, c0:c0 + 4, :], in0=q1lo, in1=q2hi, op=(ALU.subtract if conj else ALU.add)),
                     reads=[d_Q1, d_Q2], writes=[d_outR])
                S.op("gpsimd", lambda e: e.tensor_tensor(out=outI[:, c0:c0 + 4, :], in0=q1hi, in1=q2lo, op=(ALU.add if conj else ALU.subtract)),
                     reads=[d_Q1, d_Q2], writes=[d_outI])

            def s1_stage(src_fn, src_deps, m1off, conj, outR, d_outR, outI, d_outI):
                for c4 in range(16):
                    pa, d_pa = nb()
                    for cc_ in range(4):
                        S.op("tensor", lambda e, cc_=cc_, c4=c4, pa=pa: e.matmul(pa[:, cc_ * 128:(cc_ + 1) * 128], lhsT=src_fn(c4 * 4 + cc_),
                                                                                rhs=fc[0:64, m1off:m1off + 128], start=True, stop=True),
                             reads=list(src_deps) + [d_fc], writes=[d_pa])
                    twiddle(pa, d_pa, conj, outR, d_outR, outI, d_outI, c4 * 4)

            def s2_mm(pk, d_pk, terms, c8):
                n_ = len(terms)
                for ti, (foff, buf, d_buf) in enumerate(terms):
                    S.op("tensor", lambda e, ti=ti, foff=foff, buf=buf: e.matmul(pk[:], lhsT=fc[:, foff:foff + 128], rhs=buf[:, c8 * 8:(c8 + 1) * 8, :],
                                                                                start=(ti == 0), stop=(ti == n_ - 1)),
                         reads=[d_fc, d_buf], writes=[d_pk])

            def group(g):
                S.op("gpsimd", lambda e: e.memset(Acc4[:], 0.0), writes=[d_Acc4])
                for jq in range(32):
                    et, d_et = Et[jq % 2]
                    load("sync", "f_e%d" % (jq % 2), et[:], edec_d[g, jq], d_et)
                    ph, d_ph = nb()
                    for jj in range(4):
                        j = 4 * jq + jj
                        S.op("tensor", lambda e, jj=jj, j=j, ph=ph: e.matmul(ph[0:64, jj * 128:(jj + 1) * 128], lhsT=h2T[:, j * 64:(j + 1) * 64],
                                                                            rhs=sap(w3b, 0, 64, g * 64, [(512, 2), (1, 64)]), start=True, stop=True),
                             reads=[d_h2T, d_w3b], writes=[d_ph])
                    hd, d_hd = hd32[jq % 2]
                    S.op("vector", lambda e, ph=ph, et=et, hd=hd: e.tensor_tensor(
                        out=hd[:].rearrange("p (j d c) -> p j d c", j=4, d=2), in0=ph[0:64, :].rearrange("p (j d c) -> p j d c", j=4, d=2),
                        in1=sap(et, 0, 64, 0, [(64, 4), (0, 2), (1, 64)]), op=ALU.mult), reads=[d_ph, d_et], writes=[d_hd])
                    S.op("scalar", lambda e, hd=hd: e.activation(out=habs[:], in_=hd[:], func=AF.Abs), reads=[d_hd], writes=[d_habs])
                    S.op("vector", lambda e: e.tensor_tensor(out=Acc4[:], in0=Acc4[:], in1=habs[:], op=ALU.add), reads=[d_habs, d_Acc4], writes=[d_Acc4])
                    S.op("gpsimd", lambda e, hd=hd, jq=jq: e.tensor_copy(out=sap(Hbuf, 0, 64, 4 * jq, [(1, 4), (64 * 128, 2), (128, 64)]),
                                                                        in_=hd[:].rearrange("p (j d c) -> p j d c", j=4, d=2)),
                         reads=[d_hd], writes=[d_Hbuf])
                S.op("gpsimd", lambda e: e.memset(sap(Hbuf, 0, 1, 64 * 128, [(128, 64)]), 0.0), writes=[d_Hbuf])
                pn, d_pn = nb()
                for jj in range(4):
                    S.op("tensor", lambda e, jj=jj: e.matmul(pn[:, 0:128], lhsT=onesf[:], rhs=Acc4[:, jj * 128:(jj + 1) * 128], start=(jj == 0), stop=(jj == 3)),
                         reads=[d_onesf, d_Acc4], writes=[d_pn])
                S.op("scalar", lambda e: e.copy(nsb[:], pn[:, 0:128]), reads=[d_pn], writes=[d_nsb])
                S.op("vector", lambda e: e.scalar_tensor_tensor(out=rn[:], in0=nsb[:, 0:64], scalar=1e-6, op0=ALU.add, in1=nsb[:, 64:128], op1=ALU.add),
                     reads=[d_nsb], writes=[d_rn])
                S.op("vector", lambda e: e.reciprocal(out=rn[:], in_=rn[:]), reads=[d_rn], writes=[d_rn])
                s1_stage(lambda c: sap(Hbuf, 0, 64, c * 128, [(1, 128)]), [d_Hbuf], M1o, False, BfR, d_BfR, BfI, d_BfI)
                s1_stage(lambda c: sap(Hbuf, 0, 64, 64 * 128 + c * 128, [(1, 128)]), [d_Hbuf], M1co, True, BbR, d_BbR, BbI, d_BbI)
                for c8 in range(8):
                    pk, d_pk = nb()
                    s2_mm(pk, d_pk, [(C2o, BfR, d_BfR), (S2o, BfI, d_BfI), (C2o, BbR, d_BbR), (nS2o, BbI, d_BbI)], c8)
                    S.op("vector", lambda e, pk=pk, c8=c8: e.tensor_tensor(out=KR[:, c8 * 8:(c8 + 1) * 8, :], in0=pk[:].rearrange("p (c k) -> p c k", c=8),
                                                                          in1=sap(rn, 0, 128, c8 * 8, [(1, 8), (0, 64)]), op=ALU.mult),
                         reads=[d_pk, d_rn], writes=[d_KR])
                    pk2, d_pk2 = nb()
                    s2_mm(pk2, d_pk2, [(C2o, BfI, d_BfI), (nS2o, BfR, d_BfR), (C2o, BbI, d_BbI), (S2o, BbR, d_BbR)], c8)
                    S.op("vector", lambda e, pk2=pk2, c8=c8: e.tensor_tensor(out=KI[:, c8 * 8:(c8 + 1) * 8, :], in0=pk2[:].rearrange("p (c k) -> p c k", c=8),
                                                                            in1=sap(rn, 0, 128, c8 * 8, [(1, 8), (0, 64)]), op=ALU.mult),
                         reads=[d_pk2, d_rn], writes=[d_KI])
                if debug:
                    store("sync", "f_dbgk", dap(kspec, g * 64 * 64, [(512 * 64, 128), (1, 64 * 64)]), KR[:].rearrange("p c k -> p (c k)"), d_KR)
                    store("sync", "f_dbgk2", dap(kspec, 128 * 512 * 64 + g * 64 * 64, [(512 * 64, 128), (1, 64 * 64)]), KI[:].rearrange("p c k -> p (c k)"), d_KI)
                load("sync", "f_xg", Xg[:], dap(vxscr, g * 64 * L, [(128, 64), (L, 64), (1, 128)]), d_Xg)
                s1_stage(lambda c: Xg[:, c, :], [d_Xg], M1o, False, BfR, d_BfR, BfI, d_BfI)
                for c8 in range(8):
                    px, d_px = nb()
                    s2_mm(px, d_px, [(C2o, BfR, d_BfR), (S2o, BfI, d_BfI)], c8)
                    pxi, d_pxi = nb()
                    s2_mm(pxi, d_pxi, [(C2o, BfI, d_BfI), (nS2o, BfR, d_BfR)], c8)
                    kr = KR[:, c8 * 8:(c8 + 1) * 8, :].rearrange("p c k -> p (c k)")
                    ki = KI[:, c8 * 8:(c8 + 1) * 8, :].rearrange("p c k -> p (c k)")
                    S.op("vector", lambda e, px=px, kr=kr: e.tensor_tensor(out=t1[:], in0=px[:], in1=kr, op=ALU.mult), reads=[d_px, d_KR], writes=[d_t1])
                    S.op("vector", lambda e, pxi=pxi, ki=ki: e.tensor_tensor(out=t2[:], in0=pxi[:], in1=ki, op=ALU.mult), reads=[d_pxi, d_KI], writes=[d_t2])
                    S.op("vector", lambda e, px=px, ki=ki: e.tensor_tensor(out=t3[:], in0=px[:], in1=ki, op=ALU.mult), reads=[d_px, d_KI], writes=[d_t3])
                    S.op("vector", lambda e, pxi=pxi, kr=kr: e.tensor_tensor(out=t4[:], in0=pxi[:], in1=kr, op=ALU.mult), reads=[d_pxi, d_KR], writes=[d_t4])
                    S.op("gpsimd", lambda e, c8=c8: e.tensor_tensor(out=BbR[:, c8 * 8:(c8 + 1) * 8, :].rearrange("p c k -> p (c k)"), in0=t1[:], in1=t2[:], op=ALU.subtract),
                         reads=[d_t1, d_t2], writes=[d_BbR])
                    S.op("gpsimd", lambda e, c8=c8: e.tensor_tensor(out=BbI[:, c8 * 8:(c8 + 1) * 8, :].rearrange("p c k -> p (c k)"), in0=t3[:], in1=t4[:], op=ALU.add),
                         reads=[d_t3, d_t4], writes=[d_BbI])
                for pb in range(16):
                    pc, d_pc = nb()
                    for q_ in range(2):
                        p_ = pb * 2 + q_
                        S.op("tensor", lambda e, q_=q_, p_=p_, pc=pc: e.matmul(pc[:, q_ * 256:(q_ + 1) * 256], lhsT=BbR[:, 2 * p_:2 * p_ + 2, :],
                                                                              rhs=fc[:, C2S2o:C2S2o + 256], start=True, stop=False),
                             reads=[d_BbR, d_fc], writes=[d_pc])
                        S.op("tensor", lambda e, q_=q_, p_=p_, pc=pc: e.matmul(pc[:, q_ * 256:(q_ + 1) * 256], lhsT=BbI[:, 2 * p_:2 * p_ + 2, :],
                                                                              rhs=fc[:, nS2C2o:nS2C2o + 256], start=False, stop=True),
                             reads=[d_BbI, d_fc], writes=[d_pc])
                    pcv = pc[:].rearrange("p (q x) -> p q x", q=2)
                    S.op("vector", lambda e, pcv=pcv: e.tensor_tensor(out=Q1[:].rearrange("p (q x) -> p q x", q=2), in0=pcv,
                                                                     in1=sap(tcs, 0, 128, 256, [(0, 2), (1, 256)]), op=ALU.mult), reads=[d_pc, d_tcs], writes=[d_Q1])
                    S.op("vector", lambda e, pcv=pcv: e.tensor_tensor(out=Q2[:].rearrange("p (q x) -> p q x", q=2), in0=pcv,
                                                                     in1=sap(tcs, 0, 128, 512, [(0, 2), (1, 256)]), op=ALU.mult), reads=[d_pc, d_tcs], writes=[d_Q2])
                    S.op("gpsimd", lambda e, pb=pb: e.tensor_tensor(out=sap(Hbuf, 0, 128, pb * 256, [(128, 2), (1, 128)]),
                                                                   in0=sap(Q1, 0, 128, 0, [(256, 2), (1, 128)]), in1=sap(Q2, 0, 128, 128, [(256, 2), (1, 128)]), op=ALU.subtract),
                         reads=[d_Q1, d_Q2], writes=[d_Hbuf])
                    S.op("gpsimd", lambda e, pb=pb: e.tensor_tensor(out=sap(Hbuf, 0, 128, 4096 + pb * 256, [(128, 2), (1, 128)]),
                                                                   in0=sap(Q1, 0, 128, 128, [(256, 2), (1, 128)]), in1=sap(Q2, 0, 128, 0, [(256, 2), (1, 128)]), op=ALU.add),
                         reads=[d_Q1, d_Q2], writes=[d_Hbuf])
                for p4 in range(8):
                    py, d_py = nb()
                    for q_ in range(4):
                        p_ = p4 * 4 + q_
                        S.op("tensor", lambda e, q_=q_, p_=p_, py=py: e.matmul(py[:, q_ * 128:(q_ + 1) * 128], lhsT=fc[:, BDCo:BDCo + 128],
                                                                              rhs=sap(Hbuf, 0, 128, p_ * 128, [(1, 128)]), start=True, stop=False),
                             reads=[d_Hbuf, d_fc], writes=[d_py])
                        S.op("tensor", lambda e, q_=q_, p_=p_, py=py: e.matmul(py[:, q_ * 128:(q_ + 1) * 128], lhsT=fc[:, BDnSo:BDnSo + 128],
                                                                              rhs=sap(Hbuf, 0, 128, 4096 + p_ * 128, [(1, 128)]), start=False, stop=True),
                             reads=[d_Hbuf, d_fc], writes=[d_py])
                    S.op("scalar", lambda e, p4=p4, py=py: e.copy(Yo[:, p4 * 4:(p4 + 1) * 4, :].rearrange("p q x -> p (q x)"), py[:]), reads=[d_py], writes=[d_Yo])
                store("sync", "f_yo", dap(yscr, g * 64 * L, [(128, 128), (2 * L, 32), (1, 128)]), Yo[:], d_Yo)

            for g in range(8):
                group(g)
            S.barrier()
        if stop_after <= 2:
            S.emit()
            return nc
        with contextlib.ExitStack() as st:
            Sst, d_Sst = sb(st, [128, 512], F32)
            S.op("vector", lambda e: e.tensor_copy(out=Sst[:], in_=S0[:]), reads=[d_S0], writes=[d_Sst])
            Tt = [sb(st, [128, 512], F32) for _ in range(4)]
            Sb = [sb(st, [128, 512], BF16) for _ in range(4)]

            def tload(s_):
                if s_ < NT:
                    tt_, d_tt = Tt[s_ % 4]
                    load("sync", "s_tf%d" % (s_ % 4), tt_[0:64, :], Tscr[s_, 0:64, :], d_tt)
                    load("sync", "s_tb%d" % (s_ % 4), tt_[64:128, :], Tscr[NT - 1 - s_, 64:128, :], d_tt)

            def scan_step(s_):
                tload(s_ + 2)
                tt_, d_tt = Tt[s_ % 4]
                sb_, d_sb = Sb[s_ % 4]
                S.op("scalar", lambda e: e.copy(sb_[:], Sst[:]), reads=[d_Sst], writes=[d_sb])
                store("sync", "s_sf%d" % (s_ % 4), Sscr[s_, 0:64, :], sb_[0:64, :], d_sb)
                store("sync", "s_sb%d" % (s_ % 4), Sscr[NT - 1 - s_, 64:128, :], sb_[64:128, :], d_sb)
                S.op("vector", lambda e: e.tensor_tensor(out=Sst[:].rearrange("p (h x) -> p h x", h=8), in0=Sst[:].rearrange("p (h x) -> p h x", h=8),
                                                         in1=sap(Dec, 0, 128, 0, [(1, 8), (0, 64)]), op=ALU.mult), reads=[d_Sst, d_Dec], writes=[d_Sst])
                S.op("vector", lambda e: e.tensor_tensor(out=Sst[:], in0=Sst[:], in1=tt_[:], op=ALU.add), reads=[d_Sst, d_tt], writes=[d_Sst])

            tload(0); tload(1)
            for s_ in range(NT):
                scan_step(s_)
            S.barrier()
        if stop_after <= 3:
            S.emit()
            return nc

        with contextlib.ExitStack() as st:
            Wo, d_Wo = sb(st, [128, 8, D], BF16)
            W1, d_W1 = sb(st, [128, 8, 4 * D], BF16)
            W2, d_W2 = sb(st, [128, 32, D], BF16)
            for k in range(8):
                S.op("gpsimd", lambda e, k=k: e.dma_start(out=Wo[:, k, :], in_=w_out[k * 128:(k + 1) * 128, :]), writes=[d_Wo], dma="w_o%d" % (k % 4))
            for k in range(8):
                S.op("gpsimd", lambda e, k=k: e.dma_start(out=W1[:, k, :], in_=w_mlp1[k * 128:(k + 1) * 128, :]), writes=[d_W1], dma="w_1%d" % (k % 4))
            for k in range(32):
                S.op("gpsimd", lambda e, k=k: e.dma_start(out=W2[:, k, :], in_=w_mlp2[k * 128:(k + 1) * 128, :]), writes=[d_W2], dma="w_2%d" % (k % 4))
            hbc, d_hbc = sb(st, [128, 4], F32)
            load("sync", "r_hb", hbc[:].rearrange("p (c o) -> p c o", o=1), dap(hy_bias, 0, [(1, 128), (128, 4), (1, 1)]), d_hbc, slow=True)
            gnr, d_gnr = sb(st, [128, 512], F32)
            load("sync", "r_gn", gnr[:], row_bc(gn_g, 0, 512), d_gnr)
            banks = [ps(st, [128, 512], F32) for _ in range(4)]
            pmb = [ps(st, [128, 512], F32) for _ in range(2)]
            bbanks = [ps(st, [128, 1024], BF16) for _ in range(2)]
            bctr = [0, 0]

            def nb():
                b_ = banks[bctr[0] % 4]
                bctr[0] += 1
                return b_

            def nbb():
                b_ = bbanks[bctr[1] % 2]
                bctr[1] += 1
                return b_

            qkvg = [[sb(st, [128, 512], BF16) for _ in range(5)] for _ in range(1)]
            scrs = [qscr, kscr, vscr, gscr]
            xt, d_xt = sb(st, [128, D], F32)
            hy3 = [sb(st, [128, 4, 128], BF16) for _ in range(3)]
            qx, d_qx = sb(st, [128, 8, 2, 64], BF16)
            qT, d_qT = sb(st, [128, 4, 128], BF16); kT, d_kT = sb(st, [128, 4, 128], BF16)
            qxT, d_qxT = sb(st, [128, 8, 128], BF16)
            Pm, d_Pm = sb(st, [128, 8, 128], BF16)
            osb, d_osb = sb(st, [128, 512], F32); osq, d_osq = sb(st, [128, 512], F32)
            st8, d_st8 = sb(st, [128, 4, 8], F32)
            yret, d_yret = sb(st, [128, 512], BF16)
            mixT, d_mixT = sb(st, [128, 8, 128], BF16)
            xn, d_xn = sb(st, [128, D], F32)
            ss2, d_ss2 = sb(st, [128, 2], F32)
            xm2v = qx[:].rearrange("p a b c -> p (a b c)"); d_xm2 = d_qx
            hx2T, d_hx2T = qxT, d_qxT
            hT, d_hT = sb(st, [128, 16, 128], BF16)

            def loads(i):
                if i >= NT:
                    return
                bufs = qkvg[0]
                for j_ in range(4):
                    load("sync", "r_in%d_%d" % (0, j_), bufs[j_][0][:], scrs[j_][i * 128:(i + 1) * 128, :], bufs[j_][1])
                load("sync", "r_in%d_4" % (0), bufs[4][0][:], Sscr[i], bufs[4][1])

            def tile(i):
                (qt, d_qt), (kt, d_kt), (vt, d_vt), (gt, d_gt), (St_, d_St) = qkvg[0]
                load("sync", "r_x", xt[:], x[i * 128:(i + 1) * 128, :], d_xt)
                for j_, scr_ in enumerate((yscr, vxscr, x0scr)):
                    load("sync", "r_hy%d" % j_, hy3[j_][0][:], dap(scr_, i * 128, [(L, 128), (128 * L, 4), (1, 128)]), hy3[j_][1])
                S.op("vector", lambda e: e.tensor_tensor(out=qx[:], in0=sap(qt, 0, 128, 0, [(64, 8), (0, 2), (1, 64)]),
                                                         in1=sap(Wq, 0, 128, 0, [(2, 8), (1, 2), (0, 64)]), op=ALU.mult), reads=[d_qt, d_Wq], writes=[d_qx])
                pq, d_pq = nbb()
                for hp in range(4):
                    S.op("tensor", lambda e, hp=hp: e.transpose(out=pq[:, hp * 128:(hp + 1) * 128], in_=qt[:, hp * 128:(hp + 1) * 128], identity=identb[:]),
                         reads=[d_qt, d_identb], writes=[d_pq])
                for hp in range(4):
                    S.op("tensor", lambda e, hp=hp: e.transpose(out=pq[:, 512 + hp * 128:512 + (hp + 1) * 128], in_=kt[:, hp * 128:(hp + 1) * 128], identity=identb[:]),
                         reads=[d_kt, d_identb], writes=[d_pq])
                S.op("scalar", lambda e: e.copy(qT[:].rearrange("p a b -> p (a b)"), pq[:, 0:512]), reads=[d_pq], writes=[d_qT])
                S.op("scalar", lambda e: e.copy(kT[:].rearrange("p a b -> p (a b)"), pq[:, 512:1024]), reads=[d_pq], writes=[d_kT])
                px, d_px = nbb()
                for h in range(8):
                    S.op("tensor", lambda e, h=h: e.transpose(out=px[:, h * 128:(h + 1) * 128], in_=qx[:, h, :, :], identity=identb[:]),
                         reads=[d_qx, d_identb], writes=[d_px])
                S.op("scalar", lambda e: e.copy(qxT[:].rearrange("p a b -> p (a b)"), px[:]), reads=[d_px], writes=[d_qxT])
                for par in range(2):
                    psc, d_psc = nb()
                    b0 = par * 64
                    for hh in range(4):
                        h = 2 * hh + par
                        S.op("tensor", lambda e, hh=hh, b0=b0, psc=psc: e.matmul(psc[:, hh * 128:(hh + 1) * 128], lhsT=kT[b0:b0 + 64, hh, :],
                                                                                rhs=qT[b0:b0 + 64, hh, :], start=True, stop=True),
                             reads=[d_kT, d_qT], writes=[d_psc])
                    S.op("vector", lambda e, par=par, psc=psc: e.tensor_tensor(out=sap(Pm, 0, 128, par * 128, [(256, 4), (1, 128)]), in0=psc[:].rearrange("p (a b) -> p a b", a=4),
                                                                              in1=sap(DT, 0, 128, par * 128, [(256, 4), (1, 128)]), op=ALU.mult),
                         reads=[d_psc, d_DT], writes=[d_Pm])
                po, d_po = nb()
                for h in range(8):
                    S.op("tensor", lambda e, h=h: e.matmul(po[:, h * 64:(h + 1) * 64], lhsT=Pm[:, h, :], rhs=vt[:, h * 64:(h + 1) * 64], start=True, stop=False),
                         reads=[d_Pm, d_vt], writes=[d_po])
                    S.op("tensor", lambda e, h=h: e.matmul(po[:, h * 64:(h + 1) * 64], lhsT=qxT[:, h, :], rhs=St_[:, h * 64:(h + 1) * 64], start=False, stop=True),
                         reads=[d_qxT, d_St], writes=[d_po])
                S.op("scalar", lambda e: e.copy(osb[:], po[:]), reads=[d_po], writes=[d_osb])
                S.op("scalar", lambda e: e.activation(out=osq[:], in_=po[:], func=AF.Square), reads=[d_po], writes=[d_osq])
                S.op("vector", lambda e: e.tensor_reduce(out=st8[:, 0, :], in_=osb[:].rearrange("p (h x) -> p h x", h=8), op=ALU.add, axis=AX.X), reads=[d_osb], writes=[d_st8])
                S.op("vector", lambda e: e.tensor_reduce(out=st8[:, 1, :], in_=osq[:].rearrange("p (h x) -> p h x", h=8), op=ALU.add, axis=AX.X), reads=[d_osq], writes=[d_st8])
                S.op("vector", lambda e: e.tensor_scalar(out=st8[:, 0, :], in0=st8[:, 0, :], scalar1=1.0 / 64, scalar2=None, op0=ALU.mult), reads=[d_st8], writes=[d_st8])
                S.op("vector", lambda e: e.tensor_tensor(out=st8[:, 2, :], in0=st8[:, 0, :], in1=st8[:, 0, :], op=ALU.mult), reads=[d_st8], writes=[d_st8])
                S.op("vector", lambda e: e.scalar_tensor_tensor(out=st8[:, 3, :], in0=st8[:, 1, :], scalar=1.0 / 64, op0=ALU.mult, in1=st8[:, 2, :], op1=ALU.subtract),
                     reads=[d_st8], writes=[d_st8])
                S.op("scalar", lambda e: e.activation(out=st8[:, 3, :], in_=st8[:, 3, :], func=AF.Sqrt, bias=1e-6), reads=[d_st8], writes=[d_st8])
                S.op("vector", lambda e: e.reciprocal(out=st8[:, 3, :], in_=st8[:, 3, :]), reads=[d_st8], writes=[d_st8])
                S.op("vector", lambda e: e.tensor_tensor(out=osb[:].rearrange("p (h x) -> p h x", h=8), in0=osb[:].rearrange("p (h x) -> p h x", h=8),
                                                         in1=sap(st8, 0, 128, 0, [(1, 8), (0, 64)]), op=ALU.subtract), reads=[d_osb, d_st8], writes=[d_osb])
                S.op("vector", lambda e: e.tensor_tensor(out=osb[:].rearrange("p (h x) -> p h x", h=8), in0=osb[:].rearrange("p (h x) -> p h x", h=8),
                                                         in1=sap(st8, 0, 128, 24, [(1, 8), (0, 64)]), op=ALU.mult), reads=[d_osb, d_st8], writes=[d_osb])
                S.op("gpsimd", lambda e: e.tensor_tensor(out=osb[:], in0=osb[:], in1=gnr[:], op=ALU.mult), reads=[d_osb, d_gnr], writes=[d_osb])
                S.op("gpsimd", lambda e: e.tensor_tensor(out=yret[:], in0=osb[:], in1=gt[:], op=ALU.mult), reads=[d_osb, d_gt], writes=[d_yret])
                loads(i + 1)
                py_, d_py = nbb()
                for hp in range(4):
                    S.op("tensor", lambda e, hp=hp: e.transpose(out=py_[:, hp * 128:(hp + 1) * 128], in_=yret[:, hp * 128:(hp + 1) * 128], identity=identb[:]),
                         reads=[d_yret, d_identb], writes=[d_py])
                S.op("scalar", lambda e: e.copy(mixT[:, 4:8, :].rearrange("p a b -> p (a b)"), py_[:, 0:512]), reads=[d_py], writes=[d_mixT])
                (yc, d_yc), (vxt, d_vxt), (x0t, d_x0t) = hy3
                for ct in range(4):
                    S.op("vector", lambda e, ct=ct: e.scalar_tensor_tensor(out=osq[:, ct * 128:(ct + 1) * 128], in0=vxt[:, ct, :], scalar=hbc[:, ct:ct + 1], op0=ALU.mult,
                                                                          in1=yc[:, ct, :], op1=ALU.add), reads=[d_vxt, d_yc, d_hbc, d_osq], writes=[d_osq])
                S.op("gpsimd", lambda e: e.tensor_tensor(out=mixT[:, 0:4, :].rearrange("p a b -> p (a b)"), in0=osq[:], in1=x0t[:].rearrange("p a b -> p (a b)"), op=ALU.mult),
                     reads=[d_osq, d_x0t], writes=[d_mixT])
                for nb_ in range(2):
                    pw, d_pw = nb()
                    for k in range(8):
                        S.op("tensor", lambda e, k=k, nb_=nb_, pw=pw: e.matmul(pw[:], lhsT=mixT[:, k, :], rhs=Wo[:, k, nb_ * 512:(nb_ + 1) * 512], start=(k == 0), stop=(k == 7)),
                             reads=[d_mixT, d_Wo], writes=[d_pw])
                    S.op("vector", lambda e, nb_=nb_, pw=pw: e.tensor_tensor(out=xn[:, nb_ * 512:(nb_ + 1) * 512], in0=pw[:], in1=gate2[:, nb_ * 512:(nb_ + 1) * 512], op=ALU.mult),
                         reads=[d_pw, d_gate2], writes=[d_xn])
                S.op("gpsimd", lambda e: e.tensor_tensor(out=xn[:], in0=xn[:], in1=xt[:], op=ALU.add), reads=[d_xn, d_xt], writes=[d_xn])
                S.op("scalar", lambda e: e.activation(out=xm2v, in_=xn[:], func=AF.Square, accum_out=ss2[:, 0:1]), reads=[d_xn], writes=[d_xm2, d_ss2])
                S.op("scalar", lambda e: e.activation(out=ss2[:, 0:1], in_=ss2[:, 0:1], func=AF.Sqrt, scale=1.0 / D, bias=1e-6), reads=[d_ss2], writes=[d_ss2])
                S.op("vector", lambda e: e.reciprocal(out=ss2[:, 0:1], in_=ss2[:, 0:1]), reads=[d_ss2], writes=[d_ss2])
                S.op("vector", lambda e: e.scalar_tensor_tensor(out=xm2v, in0=xn[:], scalar=ss2[:, 0:1], op0=ALU.mult, in1=gs2[:], op1=ALU.mult),
                     reads=[d_xn, d_ss2, d_gs2], writes=[d_xm2])
                pt2, d_pt2 = nbb()
                for k in range(8):
                    S.op("tensor", lambda e, k=k: e.transpose(out=pt2[:, k * 128:(k + 1) * 128], in_=qx[:].rearrange("p a b c -> p (a b c)")[:, k * 128:(k + 1) * 128], identity=identb[:]),
                         reads=[d_xm2, d_identb], writes=[d_pt2])
                for k in range(8):
                    S.op("scalar", lambda e, k=k: e.activation(out=hx2T[:, k, :], in_=pt2[:, k * 128:(k + 1) * 128], func=AF.Identity, bias=colx[:, 3, k:k + 1]),
                         reads=[d_pt2, d_colx], writes=[d_hx2T])
                for hf in range(2):
                    for f4 in range(4):
                        ph, d_ph = nb()
                        for ff in range(4):
                            ft = hf * 16 + f4 * 4 + ff
                            for k in range(8):
                                S.op("tensor", lambda e, k=k, ft=ft, ff=ff, ph=ph: e.matmul(ph[:, ff * 128:(ff + 1) * 128], lhsT=W1[:, k, ft * 128:(ft + 1) * 128], rhs=hx2T[:, k, :],
                                                                                           start=(k == 0), stop=(k == 7)), reads=[d_W1, d_hx2T], writes=[d_ph])
                        S.op("scalar", lambda e, ph=ph: e.activation(out=osq[:], in_=ph[:], func=AF.Relu), reads=[d_ph], writes=[d_osq])
                        S.op("gpsimd", lambda e, f4=f4: e.tensor_tensor(out=hT[:, f4 * 4:(f4 + 1) * 4, :].rearrange("p a b -> p (a b)"), in0=osq[:], in1=osq[:], op=ALU.mult),
                             reads=[d_osq], writes=[d_hT])
                    for nb_ in range(2):
                        pm, d_pm = pmb[nb_]
                        for kk in range(16):
                            k = hf * 16 + kk
                            S.op("tensor", lambda e, k=k, kk=kk, nb_=nb_, pm=pm: e.matmul(pm[:], lhsT=hT[:, kk, :], rhs=W2[:, k, nb_ * 512:(nb_ + 1) * 512], start=(k == 0), stop=(k == 31)),
                                 reads=[d_hT, d_W2], writes=[d_pm])
                for nb_ in range(2):
                    pm, d_pm = pmb[nb_]
                    S.op("vector", lambda e, nb_=nb_, pm=pm: e.tensor_tensor(out=osb[:], in0=pm[:], in1=gate5[:, nb_ * 512:(nb_ + 1) * 512], op=ALU.mult),
                         reads=[d_pm, d_gate5], writes=[d_osb])
                    S.op("gpsimd", lambda e, nb_=nb_: e.tensor_tensor(out=xn[:, nb_ * 512:(nb_ + 1) * 512], in0=xn[:, nb_ * 512:(nb_ + 1) * 512], in1=osb[:], op=ALU.add),
                         reads=[d_xn, d_osb], writes=[d_xn])
                S.op("scalar", lambda e: e.activation(out=xm2v, in_=xn[:], func=AF.Square, accum_out=ss2[:, 1:2]), reads=[d_xn], writes=[d_xm2, d_ss2])
                S.op("scalar", lambda e: e.activation(out=ss2[:, 1:2], in_=ss2[:, 1:2], func=AF.Sqrt, scale=1.0 / D, bias=1e-6), reads=[d_ss2], writes=[d_ss2])
                S.op("vector", lambda e: e.reciprocal(out=ss2[:, 1:2], in_=ss2[:, 1:2]), reads=[d_ss2], writes=[d_ss2])
                S.op("vector", lambda e: e.scalar_tensor_tensor(out=xt[:], in0=xn[:], scalar=ss2[:, 1:2], op0=ALU.mult, in1=gF[:], op1=ALU.mult),
                     reads=[d_xn, d_ss2, d_gF], writes=[d_xt])
                final_events.append(store("sync", "r_out", out[i * 128:(i + 1) * 128, :], xt[:], d_xt))

            loads(0)
            for i in range(NT if stop_after >= 99 else 2):
                tile(i)
            S.barrier()
        S.emit()
    return nc


def make_in_map(inputs, b):
    f = lambda a: np.ascontiguousarray(np.asarray(a, dtype=np.float32))
    c = host_consts()
    m = dict(
        x=f(inputs["x"][b]), ctx=f(inputs["ctx"][b]),
        cc=f(np.stack([np.asarray(inputs["c"][b]), np.asarray(inputs["c_ctx"])], axis=0)),
        w_ada=f(inputs["w_ada"][0]), b_ada=f(inputs["b_ada"][0]).reshape(1, -1), norm1_g=f(inputs["norm1_g"][0]).reshape(1, -1),
        w_in=f(inputs["w_in"][0]), hy_conv_w=f(inputs["hy_conv_w"][0]), hy_conv_b=f(inputs["hy_conv_b"][0]).reshape(1, -1),
        hy_f_w1=f(inputs["hy_f_w1"][0]), hy_f_b1=f(inputs["hy_f_b1"][0]).reshape(1, -1), hy_f_freq1=f(inputs["hy_f_freq1"][0]).reshape(1, -1),
        hy_f_w2=f(inputs["hy_f_w2"][0]), hy_f_b2=f(inputs["hy_f_b2"][0]).reshape(1, -1), hy_f_freq2=f(inputs["hy_f_freq2"][0]).reshape(1, -1),
        hy_f_w3=f(inputs["hy_f_w3"][0]), hy_bias=f(inputs["hy_bias"][0]).reshape(1, -1),
        ret_decay_logit=f(inputs["ret_decay_logit"][0]).reshape(1, 16), ret_gn_g=f(inputs["ret_gn_g"][0]).reshape(1, -1),
        w_out=f(inputs["w_out"][0]), norm2_g=f(inputs["norm2_g"][0]).reshape(1, -1),
        w_mlp1=f(inputs["w_mlp1"][0]), w_mlp2=f(inputs["w_mlp2"][0]), norm_f_g=f(inputs["norm_f_g"]).reshape(1, -1),
    )
    m.update(c)
    return m


_NC = None


def kernel(**inputs):
    global _NC
    if _NC is None:
        _NC = build()
    in_maps = [make_in_map(inputs, b) for b in range(8)]
    res = run_bass_kernel_spmd(_NC, in_maps, core_ids=list(range(8)))
    return np.stack([np.asarray(r["out"], dtype=np.float32) for r in res.results], axis=0)
```

```python
import contextlib
import math
import numpy as np
import ml_dtypes
import concourse.bass as bass
import concourse.mybir as mybir
from concourse.bass_utils import run_bass_kernel_spmd

F32 = mybir.dt.float32
BF16 = mybir.dt.bfloat16
AF = mybir.ActivationFunctionType
ALU = mybir.AluOpType
AX = mybir.AxisListType

L = 8192
D = 1024
NT = 64
NFFT = 16384
ENGS = ("sync", "scalar", "vector", "gpsimd", "tensor")


class Dep:
    __slots__ = ("w", "r")

    def __init__(self):
        self.w = None
        self.r = []


class Sched:
    def __init__(self, nc, stack):
        self.nc = nc
        self.stack = stack
        self.streams = {e: [] for e in ENGS}
        self.esem = {}
        self.ecnt = {}
        for e in ("scalar", "vector", "gpsimd", "tensor"):
            self.esem[e] = stack.enter_context(nc.semaphore("es_" + e))
            self.ecnt[e] = 0
        self.dsem = {}
        self.dpool = []
        self.gsems = []
        self.nds = 0
        self.waited = {e: {} for e in ENGS}

    def _wait(self, eng, ev, waits):
        if ev is None:
            return
        sem, val, src = ev
        if eng == "tensor" and src == "tensor":
            return
        key = id(sem)
        if self.waited[eng].get(key, 0) >= val:
            return
        self.waited[eng][key] = val
        waits.append((sem, val))

    def op(self, eng, fn, reads=(), writes=(), dma=None):
        waits = []
        for d in reads:
            self._wait(eng, d.w, waits)
        for d in writes:
            self._wait(eng, d.w, waits)
            for ev in d.r:
                self._wait(eng, ev, waits)
        if dma is not None and eng == "gpsimd":
            self.nds += 1
            ent = [self.stack.enter_context(self.nc.semaphore("gs%d" % self.nds)), 16]
            self.gsems.append(ent)
            ev = (ent[0], 16, "dma")
            inc = (ent[0], 16)
        elif dma is not None:
            if dma not in self.dsem:
                if self.dpool:
                    self.dsem[dma] = self.dpool.pop()
                else:
                    self.nds += 1
                    self.dsem[dma] = [self.stack.enter_context(self.nc.semaphore("ds%d" % self.nds)), 0]
            ent = self.dsem[dma]
            ent[1] += 16
            ev = (ent[0], ent[1], "dma")
            inc = (ent[0], 16)
        else:
            self.ecnt[eng] += 1
            ev = (self.esem[eng], self.ecnt[eng], eng)
            inc = (self.esem[eng], 1)
        for d in reads:
            d.r.append(ev)
        for d in writes:
            d.w = ev
            d.r = []
        self.streams[eng].append((waits, fn, inc))
        return ev

    def barrier(self):
        evs = [(self.esem[e], self.ecnt[e], "x") for e in self.esem if self.ecnt[e] > 0]
        evs += [(v[0], v[1], "dma") for v in self.dsem.values() if v[1] > 0]
        evs += [(v[0], v[1], "dma") for v in self.gsems]
        for eng in ENGS:
            waits = []
            for ev in evs:
                key = id(ev[0])
                if self.waited[eng].get(key, 0) >= ev[1]:
                    continue
                self.waited[eng][key] = ev[1]
                waits.append((ev[0], ev[1]))
            if waits:
                self.streams[eng].append((waits, None, None))
        self.dpool.extend(self.dsem.values())
        self.dsem = {}

    def emit(self):
        nc = self.nc
        streams = self.streams

        def run(name, eng):
            for waits, fn, inc in streams[name]:
                for sem, val in waits:
                    eng.wait_ge(sem, val)
                if fn is not None:
                    fn(eng).then_inc(inc[0], inc[1])

        with nc.Block() as block:
            @block.sync
            def _(e):
                run("sync", e)

            @block.scalar
            def _(e):
                run("scalar", e)

            @block.vector
            def _(e):
                run("vector", e)

            @block.gpsimd
            def _(e):
                run("gpsimd", e)

            @block.tensor
            def _(e):
                run("tensor", e)


def sap(t, p0, pn, f0, dims):
    shp = list(t.shape)
    Fsz = int(np.prod(shp[1:]))
    return bass.AP(t, p0 * Fsz + f0, [[Fsz, pn]] + [[int(s), int(c)] for s, c in dims])


def dap(t, off, dims):
    return bass.AP(t.tensor, int(off), [[int(s), int(c)] for s, c in dims])


def _bf(a):
    return np.ascontiguousarray(a.astype(np.float32)).astype(ml_dtypes.bfloat16)


_CONSTS = None


def host_consts():
    global _CONSTS
    if _CONSTS is not None:
        return _CONSTS
    n1 = np.arange(64, dtype=np.float64)[:, None]
    k1 = np.arange(64, dtype=np.float64)[None, :]
    th1 = 2 * np.pi * n1 * (k1 + 0.5) / 128.0
    M1 = np.zeros((128, 128)); M1[:64, :64] = np.cos(th1); M1[:64, 64:] = -np.sin(th1)
    M1c = np.zeros((128, 128)); M1c[:64, :64] = np.cos(th1); M1c[:64, 64:] = np.sin(th1)
    n2 = np.arange(128, dtype=np.float64)[:, None]
    tht = 2 * np.pi * n2 * (k1 + 0.5) / NFFT
    k2 = np.arange(128, dtype=np.float64)[None, :]
    th2 = 2 * np.pi * n2 * k2 / 128.0
    C2 = np.cos(th2); S2 = np.sin(th2)
    sc = 2.0 / NFFT
    BDC = np.zeros((128, 128)); BDnS = np.zeros((128, 128))
    for c in range(2):
        BDC[c * 64:(c + 1) * 64, c * 64:(c + 1) * 64] = sc * np.cos(th1).T
        BDnS[c * 64:(c + 1) * 64, c * 64:(c + 1) * 64] = -sc * np.sin(th1).T
    fconst = np.concatenate([M1, M1c, C2, S2, -S2, C2, S2, -S2, C2, BDC, BDnS], axis=1)
    TC = np.concatenate([np.cos(tht), np.cos(tht)], axis=1)
    TS = np.concatenate([np.sin(tht), np.sin(tht)], axis=1)
    ct = np.cos(tht).T; st_ = np.sin(tht).T
    ITC = np.tile(np.concatenate([ct, ct], axis=1), (2, 1))
    ITS = np.tile(np.concatenate([st_, st_], axis=1), (2, 1))
    tconst = np.concatenate([TC, TS, ITC, ITS], axis=1).astype(np.float32)
    t = np.arange(L)
    r = (t // 64).astype(np.float32); col = (t % 64).astype(np.float32)
    inv = (10000.0 ** (-np.arange(16, dtype=np.float32) / 16)).astype(np.float32)
    ang = np.concatenate([r[:, None] * inv, col[:, None] * inv], axis=-1).astype(np.float32)
    cosr = np.cos(ang).astype(np.float32); sinr = np.sin(ang).astype(np.float32)
    def tl(a):
        return np.ascontiguousarray(a.reshape(64, 128, 32).transpose(1, 0, 2))
    rope = np.stack([tl(cosr), tl(sinr)], axis=1).astype(np.float32)
    tt = np.linspace(0.0, 1.0, L, dtype=np.float32)[:, None]
    w = ((2.0 * math.pi / L) * np.arange(L, dtype=np.float32))[:, None].astype(np.float32)
    bands = np.linspace(1e-4, 15, 16, dtype=np.float32)[None, :]
    z = np.concatenate([tt, np.cos(bands * w), -np.sin(bands * w)], axis=-1).astype(np.float32)
    order = (128 * np.arange(64)[None, :] + np.arange(128)[:, None]).reshape(-1)
    zT = np.ascontiguousarray(z[order].T).astype(np.float32)
    deltas = np.abs(np.linspace(math.log(1e-2) / 1.5, math.log(1e-2) / 0.3, 512, dtype=np.float32))
    E = np.exp(-tt * deltas[None, :]).astype(np.float32)
    E4 = E.reshape(64, 32, 4, 8, 64)
    edec = np.ascontiguousarray(E4.transpose(3, 1, 0, 2, 4)).reshape(8, 32, 64, 256).astype(np.float32)
    m = np.arange(128)[:, None]; c = np.arange(128)[None, :]
    pd = np.stack([np.maximum(c - m, 0), (c >= m), np.maximum(m - c, 0), (m >= c)], axis=1).astype(np.float32)
    p = np.arange(128, dtype=np.float32)
    pcols = np.stack([p + 1, 128 - p, 127 - p, p, 255 - p, p, 127 - p, 128 + p], axis=1).astype(np.float32)
    ident = np.eye(128, dtype=np.float32)
    _CONSTS = dict(fconst=_bf(fconst), tconst=tconst, rope=rope, zT=zT, edec=edec, pd=np.ascontiguousarray(pd),
                   pcols=pcols, ident_bf=_bf(ident), ident_f=ident)
    return _CONSTS


def build(debug=False, stop_after=99):
    nc = bass.Bass("TRN2", target_bir_lowering=False)

    def din(name, shape, dt=F32):
        return nc.dram_tensor(name, list(shape), dt, kind="ExternalInput").ap()

    def dscr(name, shape, dt):
        if debug:
            return nc.dram_tensor(name, list(shape), dt, kind="ExternalOutput").ap()
        return nc.dram_tensor(name, list(shape), dt).ap()

    x = din("x", [L, D]); ctx = din("ctx", [256, D]); cc = din("cc", [2, D])
    w_ada = din("w_ada", [D, 6 * D]); b_ada = din("b_ada", [1, 6 * D]); norm1_g = din("norm1_g", [1, D])
    w_in = din("w_in", [D, 3584]); conv_w = din("hy_conv_w", [3, 1536]); conv_b = din("hy_conv_b", [1, 1536])
    f_w1 = din("hy_f_w1", [33, 64]); f_b1 = din("hy_f_b1", [1, 64]); f_fr1 = din("hy_f_freq1", [1, 64])
    f_w2 = din("hy_f_w2", [64, 64]); f_b2 = din("hy_f_b2", [1, 64]); f_fr2 = din("hy_f_freq2", [1, 64])
    f_w3 = din("hy_f_w3", [64, 1024]); hy_bias = din("hy_bias", [1, 512]); logit = din("ret_decay_logit", [1, 16])
    gn_g = din("ret_gn_g", [1, 512]); w_out = din("w_out", [D, D]); norm2_g = din("norm2_g", [1, D])
    w_mlp1 = din("w_mlp1", [D, 4 * D]); w_mlp2 = din("w_mlp2", [4 * D, D]); norm_f_g = din("norm_f_g", [1, D])
    fconst_d = din("fconst", [128, 1408], BF16); tconst_d = din("tconst", [128, 768])
    rope_d = din("rope", [128, 2, 64, 32]); zT_d = din("zT", [33, L]); edec_d = din("edec", [8, 32, 64, 256])
    pd_d = din("pd", [128, 4, 128]); pcols_d = din("pcols", [128, 8])
    identb_d = din("ident_bf", [128, 128], BF16); identf_d = din("ident_f", [128, 128])
    out = nc.dram_tensor("out", [L, D], F32, kind="ExternalOutput").ap()

    modscr = dscr("modscr", [2, 6 * D], F32)
    vxscr = dscr("vxscr", [512, L], BF16); x0scr = dscr("x0scr", [512, L], BF16); yscr = dscr("yscr", [512, L], BF16)
    qscr = dscr("qscr", [L, 512], BF16); kscr = dscr("kscr", [L, 512], BF16)
    vscr = dscr("vscr", [L, 512], BF16); gscr = dscr("gscr", [L, 512], BF16)
    Tscr = dscr("Tscr", [NT, 128, 512], F32); Sscr = dscr("Sscr", [NT, 128, 512], BF16)
    kspec = dscr("kspec", [2, 128, 512 * 64], BF16) if debug else None

    final_events = []

    with contextlib.ExitStack() as gst:
        S = Sched(nc, gst)
        uid = [0]

        def sb(st, shape, dt, name=None):
            uid[0] += 1
            t = st.enter_context(nc.sbuf_tensor(name or ("t%d" % uid[0]), list(shape), dt))
            return t, Dep()

        def ps(st, shape, dt, name=None):
            uid[0] += 1
            t = st.enter_context(nc.psum_tensor(name or ("p%d" % uid[0]), list(shape), dt))
            return t, Dep()

        def load(eng, key, dst_ap, src_ap, dst_dep, src_deps=(), slow=False):
            if slow:
                return S.op(eng, lambda e: e.dma_start(out=dst_ap, in_=src_ap, allow_slow_non_contiguous=True), reads=list(src_deps), writes=[dst_dep], dma=key)
            return S.op(eng, lambda e: e.dma_start(out=dst_ap, in_=src_ap), reads=list(src_deps), writes=[dst_dep], dma=key)

        def store(eng, key, dst_ap, src_ap, src_dep, dst_deps=(), slow=False):
            if slow:
                return S.op(eng, lambda e: e.dma_start(out=dst_ap, in_=src_ap, allow_slow_non_contiguous=True), reads=[src_dep], writes=list(dst_deps), dma=key)
            return S.op(eng, lambda e: e.dma_start(out=dst_ap, in_=src_ap), reads=[src_dep], writes=list(dst_deps), dma=key)

        def row_bc(ap_dram, off, n):
            return dap(ap_dram, off, [(0, 128), (1, n)])

        identb, d_identb = sb(gst, [128, 128], BF16)
        load("sync", "c_idb", identb[:], identb_d, d_identb)
        pcols, d_pcols = sb(gst, [128, 8], F32)
        load("sync", "c_pc", pcols[:], pcols_d, d_pcols)
        gs2, d_gs2 = sb(gst, [128, D], F32); gate2, d_gate2 = sb(gst, [128, D], F32)
        gate5, d_gate5 = sb(gst, [128, D], F32); gF, d_gF = sb(gst, [128, D], F32)
        colx, d_colx = sb(gst, [128, 6, 8], F32)
        lgt, d_lgt = sb(gst, [128, 16], F32)
        lgsel, d_lgsel = sb(gst, [128, 8], F32)
        DT, d_DT = sb(gst, [128, 8, 128], F32)
        Wq, d_Wq = sb(gst, [128, 8, 2], F32)
        Dec, d_Dec = sb(gst, [128, 8], F32)
        S0, d_S0 = sb(gst, [128, 512], F32)
        a1st = gst.enter_context(contextlib.ExitStack())
        gs1, d_gs1 = sb(a1st, [128, D], F32); gs1c, d_gs1c = sb(a1st, [128, D], F32)
        colc, d_colc = sb(a1st, [128, 2, 8], F32)
        Wk, d_Wk = sb(a1st, [128, 8, 2], F32)
        Wkc, d_Wkc = sb(a1st, [128, 2, 8, 2], F32)

        with contextlib.ExitStack() as st:
            ccT, d_ccT = sb(st, [128, 8, 2], F32)
            for r_ in range(2):
                load("sync", "a_cc", ccT[:, :, r_:r_ + 1], dap(cc, r_ * D, [(1, 128), (128, 8), (1, 1)]), d_ccT, slow=True)
            scT, d_scT = sb(st, [128, 8, 2], F32)
            S.op("scalar", lambda e: e.activation(out=scT[:], in_=ccT[:], func=AF.Silu), reads=[d_ccT], writes=[d_scT])
            bada, d_bada = sb(st, [2, 6 * D], F32)
            load("sync", "a_bada", bada[:], dap(b_ada, 0, [(0, 2), (1, 6 * D)]), d_bada)
            modsb, d_modsb = sb(st, [2, 6 * D], F32)
            wab = [sb(st, [128, 8, 512], F32) for _ in range(2)]
            pM = [ps(st, [128, 512], F32) for _ in range(2)]
            for cb in range(12):
                wa, d_wa = wab[cb % 2]
                load("sync", "a_wa%d" % (cb % 2), wa[:],
                     w_ada[:, cb * 512:(cb + 1) * 512].rearrange("(k p) n -> p k n", p=128), d_wa)
                pm, d_pm = pM[cb % 2]
                for k in range(8):
                    S.op("tensor", lambda e, pm=pm, wa=wa, k=k: e.matmul(pm[0:2, :], lhsT=scT[:, k, :], rhs=wa[:, k, :],
                                                                        start=(k == 0), stop=(k == 7)),
                         reads=[d_scT, d_wa], writes=[d_pm])
                S.op("vector", lambda e, pm=pm, cb=cb: e.tensor_tensor(out=modsb[:, cb * 512:(cb + 1) * 512], in0=pm[0:2, :],
                                                                      in1=bada[:, cb * 512:(cb + 1) * 512], op=ALU.add),
                     reads=[d_pm, d_bada], writes=[d_modsb])
            d_modscr = Dep()
            store("sync", "a_modst", modscr, modsb[:], d_modsb, [d_modscr])
            for j_ in range(6):
                load("sync", "a_colx", colx[:, j_, :].rearrange("p (k o) -> p k o", o=1),
                     dap(modscr, j_ * D, [(1, 128), (128, 8), (1, 1)]), d_colx, [d_modscr], slow=True)
            for j_ in range(2):
                load("sync", "a_colc", colc[:, j_, :].rearrange("p (k o) -> p k o", o=1),
                     dap(modscr, 6 * D + j_ * D, [(1, 128), (128, 8), (1, 1)]), d_colc, [d_modscr], slow=True)
            tmpA, d_tmpA = sb(st, [128, D], F32)
            tmpB, d_tmpB = sb(st, [128, D], F32)

            def make_gs(dst, d_dst, scale_off, g_dram, tagn):
                load("sync", "a_tA", tmpA[:], row_bc(modscr, scale_off, D), d_tmpA, [d_modscr])
                load("sync", "a_tB", tmpB[:], row_bc(g_dram, 0, D), d_tmpB)
                S.op("vector", lambda e: e.scalar_tensor_tensor(out=dst[:], in0=tmpA[:], scalar=1.0, op0=ALU.add,
                                                                in1=tmpB[:], op1=ALU.mult),
                     reads=[d_tmpA, d_tmpB], writes=[d_dst])
            make_gs(gs1, d_gs1, 1 * D, norm1_g, 0)
            make_gs(gs1c, d_gs1c, 6 * D + 1 * D, norm1_g, 1)
            make_gs(gs2, d_gs2, 4 * D, norm2_g, 2)
            load("sync", "a_g2", gate2[:], row_bc(modscr, 2 * D, D), d_gate2, [d_modscr])
            load("sync", "a_g5", gate5[:], row_bc(modscr, 5 * D, D), d_gate5, [d_modscr])
            load("sync", "a_gF", gF[:], row_bc(norm_f_g, 0, D), d_gF)

            lraw, d_lraw = sb(st, [128, 16], F32)
            load("sync", "a_lg", lraw[:], row_bc(logit, 0, 16), d_lraw)
            S.op("scalar", lambda e: e.activation(out=lgt[:], in_=lraw[:], func=AF.Exp, scale=-1.0), reads=[d_lraw], writes=[d_lgt])
            S.op("scalar", lambda e: e.activation(out=lgt[:], in_=lgt[:], func=AF.Ln, bias=1.0), reads=[d_lgt], writes=[d_lgt])
            S.op("scalar", lambda e: e.mul(lgt[:], lgt[:], -1.0), reads=[d_lgt], writes=[d_lgt])
            S.op("vector", lambda e: e.tensor_copy(out=lgsel[0:64, :], in_=lgt[0:64, 0:8]), reads=[d_lgt], writes=[d_lgsel])
            S.op("vector", lambda e: e.tensor_copy(out=lgsel[64:128, :], in_=lgt[64:128, 8:16]), reads=[d_lgt], writes=[d_lgsel])
            S.op("scalar", lambda e: e.activation(out=Dec[:], in_=lgsel[:], func=AF.Exp, scale=128.0), reads=[d_lgsel], writes=[d_Dec])
            for (dst, d_dst, cf, cbk) in ((Wq, d_Wq, 0, 1), (Wk, d_Wk, 2, 3)):
                S.op("scalar", lambda e, dst=dst, cf=cf: e.activation(out=dst[:, :, 0], in_=lgt[:, 0:8], func=AF.Exp, scale=pcols[:, cf:cf + 1]),
                     reads=[d_lgt, d_pcols], writes=[d_dst])
                S.op("scalar", lambda e, dst=dst, cbk=cbk: e.activation(out=dst[:, :, 1], in_=lgt[:, 8:16], func=AF.Exp, scale=pcols[:, cbk:cbk + 1]),
                     reads=[d_lgt, d_pcols], writes=[d_dst])
            S.op("vector", lambda e: e.tensor_scalar(out=Wk[:], in0=Wk[:], scalar1=0.125, scalar2=None, op0=ALU.mult), reads=[d_Wk], writes=[d_Wk])
            for tI in range(2):
                S.op("scalar", lambda e, tI=tI: e.activation(out=Wkc[:, tI, :, 0], in_=lgt[:, 0:8], func=AF.Exp, scale=pcols[:, 4 + 2 * tI:5 + 2 * tI]),
                     reads=[d_lgt, d_pcols], writes=[d_Wkc])
                S.op("scalar", lambda e, tI=tI: e.activation(out=Wkc[:, tI, :, 1], in_=lgt[:, 8:16], func=AF.Exp, scale=pcols[:, 5 + 2 * tI:6 + 2 * tI]),
                     reads=[d_lgt, d_pcols], writes=[d_Wkc])
            pdt, d_pdt = sb(st, [128, 4, 128], F32)
            load("sync", "a_pd", pdt[:], pd_d, d_pdt)
            ef, d_ef = sb(st, [128, 128], F32); eb, d_eb = sb(st, [128, 128], F32)
            for h in range(8):
                S.op("scalar", lambda e, h=h: e.activation(out=ef[:], in_=pdt[:, 0, :], func=AF.Exp, scale=lgt[:, h:h + 1]),
                     reads=[d_pdt, d_lgt], writes=[d_ef])
                S.op("scalar", lambda e, h=h: e.activation(out=eb[:], in_=pdt[:, 2, :], func=AF.Exp, scale=lgt[:, 8 + h:9 + h]),
                     reads=[d_pdt, d_lgt], writes=[d_eb])
                S.op("vector", lambda e: e.scalar_tensor_tensor(out=ef[:], in0=ef[:], scalar=0.125, op0=ALU.mult, in1=pdt[:, 1, :], op1=ALU.mult), reads=[d_ef, d_pdt], writes=[d_ef])
                S.op("vector", lambda e: e.scalar_tensor_tensor(out=eb[:], in0=eb[:], scalar=0.125, op0=ALU.mult, in1=pdt[:, 3, :], op1=ALU.mult), reads=[d_eb, d_pdt], writes=[d_eb])
                S.op("vector", lambda e, h=h: e.tensor_tensor(out=DT[:, h, :], in0=ef[:], in1=eb[:], op=ALU.add), reads=[d_ef, d_eb], writes=[d_DT])
            S.barrier()
        if stop_after <= 0:
            S.emit()
            return nc

        d_vx = Dep(); d_x0 = Dep(); d_q = Dep(); d_k = Dep(); d_v = Dep(); d_g = Dep(); d_T = Dep()
        with contextlib.ExitStack() as st:
            Win, d_Win = sb(st, [128, 8, 3584], BF16)
            for k in range(8):
                S.op("gpsimd", lambda e, k=k: e.dma_start(out=Win[:, k, :], in_=w_in[k * 128:(k + 1) * 128, :]), writes=[d_Win], dma="p1_win%d" % k)
            ropeT, d_rope = sb(st, [128, 2, 64, 32], F32)
            load("sync", "p1_rope", ropeT[:], rope_d, d_rope)
            cw, d_cw = sb(st, [128, 12, 4], F32)
            for j in range(3):
                load("sync", "p1_cw", cw[:, :, j:j + 1], dap(conv_w, j * 1536, [(1, 128), (128, 12), (1, 1)]), d_cw, slow=True)
            load("sync", "p1_cw", cw[:, :, 3:4], dap(conv_b, 0, [(1, 128), (128, 12), (1, 1)]), d_cw, slow=True)

            xb = [sb(st, [128, D], F32) for _ in range(3)]
            junk, d_junk = sb(st, [128, D], BF16)
            ssq = [sb(st, [128, 1], F32) for _ in range(3)]
            xm = [sb(st, [128, D], BF16) for _ in range(2)]
            hxT = [sb(st, [128, 8, 512], BF16) for _ in range(2)]
            U = [sb(st, [128, 514], F32) for _ in range(12)]
            cv1 = [sb(st, [128, 512], F32) for _ in range(2)]
            cv2 = [sb(st, [128, 512], F32) for _ in range(2)]
            cvx1 = [sb(st, [128, 512], F32) for _ in range(4)]
            cv3, d_cv3 = sb(st, [128, 512], F32)
            hyo = [sb(st, [128, 512], BF16) for _ in range(4)]
            P1, d_P1 = sb(st, [128, 512], F32); P2, d_P2 = sb(st, [128, 512], F32)
            qo = [sb(st, [128, 512], BF16) for _ in range(2)]
            ko = [sb(st, [128, 512], BF16) for _ in range(2)]
            vo = [sb(st, [128, 512], BF16) for _ in range(2)]
            go = [sb(st, [128, 512], BF16) for _ in range(2)]
            kw, d_kw = sb(st, [128, 8, 2, 64], BF16)
            Tsb = [sb(st, [128, 512], F32) for _ in range(2)]
            pT = [ps(st, [128, 8, 128], BF16) for _ in range(2)]
            pU = [ps(st, [128, 512], F32) for _ in range(2)]
            pR = [ps(st, [128, 512], F32) for _ in range(2)]
            pTs = [ps(st, [128, 512], F32) for _ in range(2)]
            for ft in range(12):
                S.op("gpsimd", lambda e, ft=ft: e.memset(U[ft][0][:, 0:2], 0.0), writes=[U[ft][1]])

            srcs = [ctx[0:128, :], ctx[128:256, :]] + [x[i * 128:(i + 1) * 128, :] for i in range(NT)]

            def xload(s_):
                if s_ < len(srcs):
                    load("sync", "p1_x%d" % (s_ % 3), xb[s_ % 3][0][:], srcs[s_], xb[s_ % 3][1])

            xload(0)

            def norm_transpose(i, gsrow, d_gsrow, shcol_fn, d_shcol, hx_tile, d_hx, tok0):
                xload(i + 1)
                xt, d_xt = xb[i % 3]
                sq, d_sq = ssq[i % 3]
                S.op("scalar", lambda e: e.activation(out=junk[:], in_=xt[:], func=AF.Square, accum_out=sq[:]),
                     reads=[d_xt], writes=[d_junk, d_sq])
                S.op("scalar", lambda e: e.activation(out=sq[:], in_=sq[:], func=AF.Sqrt, scale=1.0 / D, bias=1e-6),
                     reads=[d_sq], writes=[d_sq])
                S.op("vector", lambda e: e.reciprocal(out=sq[:], in_=sq[:]), reads=[d_sq], writes=[d_sq])
                xmt, d_xmt = xm[i % 2]
                S.op("vector", lambda e: e.scalar_tensor_tensor(out=xmt[:], in0=xt[:], scalar=sq[:, 0:1], op0=ALU.mult,
                                                                in1=gsrow[:], op1=ALU.mult),
                     reads=[d_xt, d_sq, d_gsrow], writes=[d_xmt])
                pt, d_pt = pT[i % 2]
                for k in range(8):
                    S.op("tensor", lambda e, k=k: e.transpose(out=pt[:, k, :], in_=xmt[:, k * 128:(k + 1) * 128], identity=identb[:]),
                         reads=[d_xmt, d_identb], writes=[d_pt])
                for k in range(8):
                    S.op("scalar", lambda e, k=k: e.activation(out=hx_tile[:, k, tok0:tok0 + 128], in_=pt[:, k, :], func=AF.Identity,
                                                               bias=shcol_fn(k)),
                         reads=[d_pt, d_shcol], writes=[d_hx])

            def proj_tok(hx_tile, d_hx, tok0, col0, pr, d_pr):
                for k in range(8):
                    S.op("tensor", lambda e, k=k: e.matmul(pr[:], lhsT=hx_tile[:, k, tok0:tok0 + 128], rhs=Win[:, k, col0:col0 + 512],
                                                           start=(k == 0), stop=(k == 7)),
                         reads=[d_hx, d_Win], writes=[d_pr])

            hxc, d_hxc = hxT[0]
            kc = []; vc = []
            for tI in range(2):
                norm_transpose(tI, gs1c, d_gs1c, lambda k: colc[:, 0, k:k + 1], d_colc, hxc, d_hxc, tI * 128)
                pr, d_pr = pR[0]
                proj_tok(hxc, d_hxc, tI * 128, 1536 + 512, pr, d_pr)
                kt, d_kt = ko[tI]
                S.op("scalar", lambda e, kt=kt, pr=pr: e.mul(kt[:], pr[:], 0.125), reads=[d_pr], writes=[d_kt])
                pr2, d_pr2 = pR[1]
                proj_tok(hxc, d_hxc, tI * 128, 1536 + 1024, pr2, d_pr2)
                vt, d_vt = vo[tI]
                S.op("scalar", lambda e, vt=vt, pr2=pr2: e.copy(vt[:], pr2[:]), reads=[d_pr2], writes=[d_vt])
                kc.append((kt, d_kt)); vc.append((vt, d_vt))
            kwc = [sb(st, [128, 8, 2, 64], BF16) for _ in range(2)]
            for tI in range(2):
                kt, d_kt = kc[tI]
                kwt, d_kwt = kwc[tI]
                S.op("vector", lambda e, kt=kt, kwt=kwt, tI=tI: e.tensor_tensor(
                    out=kwt[:], in0=sap(kt, 0, 128, 0, [(64, 8), (0, 2), (1, 64)]),
                    in1=sap(Wkc, 0, 128, tI * 16, [(2, 8), (1, 2), (0, 64)]), op=ALU.mult),
                    reads=[d_kt, d_Wkc], writes=[d_kwt])
            pS0, d_pS0 = pTs[0]
            for h in range(8):
                for tI in range(2):
                    S.op("tensor", lambda e, h=h, tI=tI: e.matmul(pS0[:, h * 64:(h + 1) * 64], lhsT=kwc[tI][0][:, h, :, :],
                                                                 rhs=vc[tI][0][:, h * 64:(h + 1) * 64], start=(tI == 0), stop=(tI == 1)),
                         reads=[kwc[tI][1], vc[tI][1]], writes=[d_pS0])
            S.op("vector", lambda e: e.tensor_copy(out=S0[:], in_=pS0[:]), reads=[d_pS0], writes=[d_S0])

            for i in range(NT):
                B, ii = divmod(i, 4)
                hx_tile, d_hx = hxT[B % 2]
                norm_transpose(i + 2, gs1, d_gs1, lambda k: colx[:, 0, k:k + 1], d_colx, hx_tile, d_hx, ii * 128)
                for cbk in range(4):
                    pr, d_pr = pR[cbk % 2]
                    proj_tok(hx_tile, d_hx, ii * 128, 1536 + cbk * 512, pr, d_pr)
                    if cbk < 2:
                        ot, d_ot = (qo if cbk == 0 else ko)[i % 2]
                        S.op("vector", lambda e, pr=pr, i=i: e.tensor_tensor(
                            out=P1[:].rearrange("p (h j t) -> p h j t", h=8, t=2), in0=pr[:].rearrange("p (h j t) -> p h j t", h=8, t=2),
                            in1=sap(ropeT, 0, 128, (0 * 64 + i) * 32, [(0, 8), (1, 32), (0, 2)]), op=ALU.mult),
                            reads=[d_pr, d_rope], writes=[d_P1])
                        S.op("vector", lambda e, pr=pr, i=i: e.tensor_tensor(
                            out=P2[:].rearrange("p (h j t) -> p h j t", h=8, t=2), in0=pr[:].rearrange("p (h j t) -> p h j t", h=8, t=2),
                            in1=sap(ropeT, 0, 128, (1 * 64 + i) * 32, [(0, 8), (1, 32), (0, 2)]), op=ALU.mult),
                            reads=[d_pr, d_rope], writes=[d_P2])
                        S.op("gpsimd", lambda e, ot=ot: e.tensor_tensor(out=sap(ot, 0, 128, 0, [(2, 256)]), in0=sap(P1, 0, 128, 0, [(2, 256)]),
                                                                       in1=sap(P2, 0, 128, 1, [(2, 256)]), op=ALU.subtract),
                             reads=[d_P1, d_P2], writes=[d_ot])
                        S.op("gpsimd", lambda e, ot=ot: e.tensor_tensor(out=sap(ot, 0, 128, 1, [(2, 256)]), in0=sap(P2, 0, 128, 0, [(2, 256)]),
                                                                       in1=sap(P1, 0, 128, 1, [(2, 256)]), op=ALU.add),
                             reads=[d_P1, d_P2], writes=[d_ot])
                        scr = qscr if cbk == 0 else kscr
                        store("sync", ("p1_q%d" if cbk == 0 else "p1_k%d") % (i % 2), scr[i * 128:(i + 1) * 128, :], ot[:], d_ot)
                        if cbk == 1:
                            S.op("vector", lambda e, ot=ot: e.tensor_tensor(
                                out=kw[:], in0=sap(ot, 0, 128, 0, [(64, 8), (0, 2), (1, 64)]),
                                in1=sap(Wk, 0, 128, 0, [(2, 8), (1, 2), (0, 64)]), op=ALU.mult),
                                reads=[d_ot, d_Wk], writes=[d_kw])
                    elif cbk == 2:
                        vt, d_vt = vo[i % 2]
                        S.op("scalar", lambda e, vt=vt, pr=pr: e.copy(vt[:], pr[:]), reads=[d_pr], writes=[d_vt])
                        store("sync", "p1_v%d" % (i % 2), vscr[i * 128:(i + 1) * 128, :], vt[:], d_vt)
                    else:
                        gt, d_gt = go[i % 2]
                        S.op("scalar", lambda e, gt=gt, pr=pr: e.activation(out=gt[:], in_=pr[:], func=AF.Silu), reads=[d_pr], writes=[d_gt])
                        store("sync", "p1_g%d" % (i % 2), gscr[i * 128:(i + 1) * 128, :], gt[:], d_gt)
                pts, d_pts = pTs[i % 2]
                vt, d_vt = vo[i % 2]
                for h in range(8):
                    S.op("tensor", lambda e, h=h, pts=pts, vt=vt: e.matmul(pts[:, h * 64:(h + 1) * 64], lhsT=kw[:, h, :, :],
                                                                          rhs=vt[:, h * 64:(h + 1) * 64], start=True, stop=True),
                         reads=[d_kw, d_vt], writes=[d_pts])
                tsb, d_tsb = Tsb[i % 2]
                S.op("scalar", lambda e, tsb=tsb, pts=pts: e.copy(tsb[:], pts[:]), reads=[d_pts], writes=[d_tsb])
                store("sync", "p1_T%d" % (i % 2), Tscr[i], tsb[:], d_tsb)
                if ii == 3:
                    s0 = 1 if B == 0 else 0
                    tok_lo = 512 * B - 1 + s0
                    for ft in (4, 5, 6, 7, 8, 9, 10, 11, 0, 1, 2, 3):
                        pu, d_pu = pU[ft % 2]
                        for k in range(8):
                            S.op("tensor", lambda e, k=k, ft=ft, pu=pu, hx_tile=hx_tile: e.matmul(pu[:], lhsT=Win[:, k, ft * 128:(ft + 1) * 128], rhs=hx_tile[:, k, :],
                                                                                start=(k == 0), stop=(k == 7)),
                                 reads=[d_Win, d_hx], writes=[d_pu])
                        u, d_u = U[ft]
                        S.op("scalar", lambda e, u=u, pu=pu: e.copy(u[:, 2:514], pu[:]), reads=[d_pu], writes=[d_u])
                        c1, d_c1 = cv1[ft % 2]; c2, d_c2 = cv2[ft % 2]
                        S.op("gpsimd", lambda e, u=u, c1=c1, ft=ft: e.tensor_scalar(out=c1[:], in0=u[:, 0:512], scalar1=cw[:, ft, 0:1], scalar2=cw[:, ft, 3:4],
                                                                                   op0=ALU.mult, op1=ALU.add),
                             reads=[d_u, d_cw], writes=[d_c1])
                        S.op("vector", lambda e, u=u, c1=c1, c2=c2, ft=ft: e.scalar_tensor_tensor(out=c2[:], in0=u[:, 1:513], scalar=cw[:, ft, 1:2], op0=ALU.mult,
                                                                                                 in1=c1[:], op1=ALU.add),
                             reads=[d_u, d_cw, d_c1], writes=[d_c2])
                        ct = ft % 4
                        if ft < 4:
                            ho, d_ho = hyo[ct]
                            S.op("vector", lambda e, u=u, c2=c2, ho=ho, ft=ft: e.scalar_tensor_tensor(out=ho[:], in0=u[:, 2:514], scalar=cw[:, ft, 2:3], op0=ALU.mult,
                                                                                                     in1=c2[:], op1=ALU.add),
                                 reads=[d_u, d_cw, d_c2], writes=[d_ho])
                            store("sync", "p1_hy%d" % ct, x0scr[ct * 128:(ct + 1) * 128, tok_lo:512 * B + 511], ho[:, s0:512], d_ho)
                        elif ft < 8:
                            cx, d_cx = cvx1[ct]
                            S.op("vector", lambda e, u=u, c2=c2, cx=cx, ft=ft: e.scalar_tensor_tensor(out=cx[:], in0=u[:, 2:514], scalar=cw[:, ft, 2:3], op0=ALU.mult,
                                                                                                     in1=c2[:], op1=ALU.add),
                                 reads=[d_u, d_cw, d_c2], writes=[d_cx])
                        else:
                            cx, d_cx = cvx1[ct]
                            S.op("vector", lambda e, u=u, c2=c2, ft=ft: e.scalar_tensor_tensor(out=cv3[:], in0=u[:, 2:514], scalar=cw[:, ft, 2:3], op0=ALU.mult,
                                                                                              in1=c2[:], op1=ALU.add),
                                 reads=[d_u, d_cw, d_c2], writes=[d_cv3])
                            ho, d_ho = hyo[ct]
                            S.op("gpsimd", lambda e, cx=cx, ho=ho: e.tensor_tensor(out=ho[:], in0=cv3[:], in1=cx[:], op=ALU.mult),
                                 reads=[d_cv3, d_cx], writes=[d_ho])
                            store("sync", "p1_hy%d" % ct, vxscr[ct * 128:(ct + 1) * 128, tok_lo:512 * B + 511], ho[:, s0:512], d_ho)
                        S.op("gpsimd", lambda e, u=u: e.tensor_copy(out=u[:, 0:2], in_=u[:, 512:514]), reads=[d_u], writes=[d_u])
            tl, d_tl = sb(st, [128, 12], F32)
            tlb, d_tlb = sb(st, [128, 8], BF16)
            for ft in range(12):
                u, d_u = U[ft]
                S.op("vector", lambda e, u=u, ft=ft: e.tensor_scalar(out=tl[:, ft:ft + 1], in0=u[:, 0:1], scalar1=cw[:, ft, 0:1], scalar2=cw[:, ft, 3:4],
                                                                    op0=ALU.mult, op1=ALU.add), reads=[d_u, d_cw], writes=[d_tl])
                S.op("vector", lambda e, u=u, ft=ft: e.scalar_tensor_tensor(out=tl[:, ft:ft + 1], in0=u[:, 1:2], scalar=cw[:, ft, 1:2], op0=ALU.mult,
                                                                           in1=tl[:, ft:ft + 1], op1=ALU.add), reads=[d_u, d_cw, d_tl], writes=[d_tl])
            S.op("vector", lambda e: e.tensor_copy(out=tlb[:, 0:4], in_=tl[:, 0:4]), reads=[d_tl], writes=[d_tlb])
            S.op("vector", lambda e: e.tensor_tensor(out=tlb[:, 4:8], in0=tl[:, 4:8], in1=tl[:, 8:12], op=ALU.mult), reads=[d_tl], writes=[d_tlb])
            for ct in range(4):
                store("sync", "p1_tl%d" % ct, dap(x0scr, ct * 128 * L + L - 1, [(L, 128), (1, 1)]), tlb[:, ct:ct + 1], d_tlb, slow=True)
                store("sync", "p1_tv%d" % ct, dap(vxscr, ct * 128 * L + L - 1, [(L, 128), (1, 1)]), tlb[:, 4 + ct:5 + ct], d_tlb, slow=True)
            S.barrier()
        a1st.close()
        if stop_after <= 1:
            S.emit()
            return nc
        with contextlib.ExitStack() as st:
            fc, d_fc = sb(st, [128, 1408], BF16)
            load("sync", "f_fc", fc[:], fconst_d, d_fc)
            tcs, d_tcs = sb(st, [128, 768], F32)
            load("sync", "f_tc", tcs[:], tconst_d, d_tcs)
            M1o, M1co, C2o, S2o, nS2o, C2S2o, nS2C2o, BDCo, BDnSo = 0, 128, 256, 384, 512, 640, 896, 1152, 1280
            w1t, d_w1t = sb(st, [33, 64], F32); load("sync", "f_w1", w1t[:], f_w1, d_w1t)
            w2t, d_w2t = sb(st, [64, 64], F32); load("sync", "f_w2", w2t[:], f_w2, d_w2t)
            w3b, d_w3b = sb(st, [64, 1024], BF16)
            S.op("gpsimd", lambda e: e.dma_start(out=w3b[:], in_=f_w3), writes=[d_w3b], dma="f_w3")
            fcol, d_fcol = sb(st, [64, 4], F32)
            for j_, src in enumerate((f_fr1, f_b1, f_fr2, f_b2)):
                load("sync", "f_col", fcol[:, j_:j_ + 1], dap(src, 0, [(1, 64), (1, 1)]), d_fcol, slow=True)
            fab, d_fab = sb(st, [64, 4], F32)
            for l_ in range(2):
                S.op("vector", lambda e, l_=l_: e.tensor_scalar(out=fab[:, 2 * l_:2 * l_ + 1], in0=fcol[:, 2 * l_:2 * l_ + 1], scalar1=1.0 / 3.0, scalar2=None, op0=ALU.mult),
                     reads=[d_fcol], writes=[d_fab])
                S.op("vector", lambda e, l_=l_: e.tensor_tensor(out=fab[:, 2 * l_ + 1:2 * l_ + 2], in0=fab[:, 2 * l_:2 * l_ + 1], in1=fcol[:, 2 * l_ + 1:2 * l_ + 2], op=ALU.mult),
                     reads=[d_fcol, d_fab], writes=[d_fab])
            onesf, d_onesf = sb(st, [64, 128], F32)
            S.op("gpsimd", lambda e: e.memset(onesf[:], 1.0), writes=[d_onesf])
            h2T, d_h2T = sb(st, [64, L], BF16)
            banks = [ps(st, [128, 512], F32) for _ in range(8)]
            bctr = [0]

            def nb():
                b_ = banks[bctr[0] % 8]
                bctr[0] += 1
                return b_

            zt = [sb(st, [33, 512], F32) for _ in range(2)]
            sA, d_sA = sb(st, [64, 512], F32); sB, d_sB = sb(st, [64, 512], F32); h1, d_h1 = sb(st, [64, 512], F32)

            def sin3(pf, d_pf, layer, out_ap, d_out):
                S.op("scalar", lambda e: e.activation(out=sA[:], in_=pf[0:64, :], func=AF.Sin, scale=fab[:, 2 * layer:2 * layer + 1],
                                                      bias=fab[:, 2 * layer + 1:2 * layer + 2]), reads=[d_pf, d_fab], writes=[d_sA])
                S.op("vector", lambda e: e.tensor_tensor(out=sB[:], in0=sA[:], in1=sA[:], op=ALU.mult), reads=[d_sA], writes=[d_sB])
                S.op("vector", lambda e: e.tensor_scalar(out=sB[:], in0=sB[:], scalar1=-4.0, scalar2=3.0, op0=ALU.mult, op1=ALU.add), reads=[d_sB], writes=[d_sB])
                S.op("vector", lambda e: e.tensor_tensor(out=out_ap, in0=sB[:], in1=sA[:], op=ALU.mult), reads=[d_sA, d_sB], writes=[d_out])

            def mlp_block(blk):
                z, d_z = zt[blk % 2]
                load("sync", "f_z%d" % (blk % 2), z[:], zT_d[:, blk * 512:(blk + 1) * 512], d_z)
                pf, d_pf = nb()
                S.op("tensor", lambda e: e.matmul(pf[0:64, :], lhsT=w1t[:], rhs=z[:], start=True, stop=True), reads=[d_w1t, d_z], writes=[d_pf])
                sin3(pf, d_pf, 0, h1[:], d_h1)
                pf2, d_pf2 = nb()
                S.op("tensor", lambda e: e.matmul(pf2[0:64, :], lhsT=w2t[:], rhs=h1[:], start=True, stop=True), reads=[d_w2t, d_h1], writes=[d_pf2])
                sin3(pf2, d_pf2, 1, h2T[:, blk * 512:(blk + 1) * 512], d_h2T)

            for blk in range(16):
                mlp_block(blk)

            Hbuf, d_Hbuf = sb(st, [128, 2 * 64 * 128], BF16)
            Acc4, d_Acc4 = sb(st, [64, 512], F32)
            habs, d_habs = sb(st, [64, 512], F32)
            Et = [sb(st, [64, 256], F32) for _ in range(2)]
            hd32 = [sb(st, [64, 512], F32) for _ in range(2)]
            nsb, d_nsb = sb(st, [128, 128], F32)
            rn, d_rn = sb(st, [128, 64], F32)
            BfR, d_BfR = sb(st, [128, 64, 64], BF16); BfI, d_BfI = sb(st, [128, 64, 64], BF16)
            BbR, d_BbR = sb(st, [128, 64, 64], BF16); BbI, d_BbI = sb(st, [128, 64, 64], BF16)
            KR, d_KR = sb(st, [128, 64, 64], BF16); KI, d_KI = sb(st, [128, 64, 64], BF16)
            Xg, d_Xg = sb(st, [64, 64, 128], BF16)
            Qb = [(sb(st, [128, 512], F32), sb(st, [128, 512], F32)) for _ in range(2)]
            qctr = [0]
            t1, d_t1 = sb(st, [128, 512], F32); t2, d_t2 = sb(st, [128, 512], F32)
            t3, d_t3 = sb(st, [128, 512], F32); t4, d_t4 = sb(st, [128, 512], F32)
            Yo, d_Yo = sb(st, [128, 32, 128], BF16)


            Sst, d_Sst = sb(st, [128, 512], F32)
            S.op("vector", lambda e: e.tensor_copy(out=Sst[:], in_=S0[:]), reads=[d_S0], writes=[d_Sst])
            Tt = [sb(st, [128, 512], F32) for _ in range(2)]
            Sb = [sb(st, [128, 512], BF16) for _ in range(2)]
            sctr = [0]

            def tload(s_):
                if s_ < NT:
                    tt_, d_tt = Tt[s_ % 2]
                    load("sync", "s_tf%d" % (s_ % 2), tt_[0:64, :], Tscr[s_, 0:64, :], d_tt)
                    load("sync", "s_tb%d" % (s_ % 2), tt_[64:128, :], Tscr[NT - 1 - s_, 64:128, :], d_tt)

            def scan_step(s_):
                tt_, d_tt = Tt[s_ % 2]
                sb_, d_sb = Sb[s_ % 2]
                S.op("scalar", lambda e: e.copy(sb_[:], Sst[:]), reads=[d_Sst], writes=[d_sb])
                store("sync", "s_sf%d" % (s_ % 2), Sscr[s_, 0:64, :], sb_[0:64, :], d_sb)
                store("sync", "s_sb%d" % (s_ % 2), Sscr[NT - 1 - s_, 64:128, :], sb_[64:128, :], d_sb)
                S.op("vector", lambda e: e.tensor_tensor(out=Sst[:].rearrange("p (h x) -> p h x", h=8), in0=Sst[:].rearrange("p (h x) -> p h x", h=8),
                                                         in1=sap(Dec, 0, 128, 0, [(1, 8), (0, 64)]), op=ALU.mult), reads=[d_Sst, d_Dec], writes=[d_Sst])
                S.op("vector", lambda e: e.tensor_tensor(out=Sst[:], in0=Sst[:], in1=tt_[:], op=ALU.add), reads=[d_Sst, d_tt], writes=[d_Sst])
                tload(s_ + 2)

            def scan_some(n_):
                for _ in range(n_):
                    if sctr[0] < NT:
                        scan_step(sctr[0])
                        sctr[0] += 1

            tload(0); tload(1)

            def twiddle(pa, d_pa, conj, outR, d_outR, outI, d_outI, c0):
                pav = pa[:].rearrange("p (c x) -> p c x", c=4)
                (Q1, d_Q1), (Q2, d_Q2) = Qb[qctr[0] % 2]
                qctr[0] += 1
                S.op("vector", lambda e: e.tensor_tensor(out=Q1[:].rearrange("p (c x) -> p c x", c=4), in0=pav,
                                                         in1=sap(tcs, 0, 128, 0, [(0, 4), (1, 128)]), op=ALU.mult), reads=[d_pa, d_tcs], writes=[d_Q1])
                S.op("vector", lambda e: e.tensor_tensor(out=Q2[:].rearrange("p (c x) -> p c x", c=4), in0=pav,
                                                         in1=sap(tcs, 0, 128, 128, [(0, 4), (1, 128)]), op=ALU.mult), reads=[d_pa, d_tcs], writes=[d_Q2])
                q1lo = sap(Q1, 0, 128, 0, [(128, 4), (1, 64)]); q1hi = sap(Q1, 0, 128, 64, [(128, 4), (1, 64)])
                q2lo = sap(Q2, 0, 128, 0, [(128, 4), (1, 64)]); q2hi = sap(Q2, 0, 128, 64, [(128, 4), (1, 64)])
                S.op("gpsimd", lambda e: e.tensor_tensor(out=outR[:, c0:c0 + 4, :], in0=q1lo, in1=q2hi, op=(ALU.subtract if conj else ALU.add)),
                     reads=[d_Q1, d_Q2], writes=[d_outR])
                S.op("gpsimd", lambda e: e.tensor_tensor(out=outI[:, c0:c0 + 4, :], in0=q1hi, in1=q2lo, op=(ALU.add if conj else ALU.subtract)),
                     reads=[d_Q1, d_Q2], writes=[d_outI])

            def s1_stage(src_fn, src_deps, m1off, conj, outR, d_outR, outI, d_outI):
                for c4 in range(16):
                    pa, d_pa = nb()
                    for cc_ in range(4):
                        S.op("tensor", lambda e, cc_=cc_, c4=c4, pa=pa: e.matmul(pa[:, cc_ * 128:(cc_ + 1) * 128], lhsT=src_fn(c4 * 4 + cc_),
                                                                                rhs=fc[0:64, m1off:m1off + 128], start=True, stop=True),
                             reads=list(src_deps) + [d_fc], writes=[d_pa])
                    twiddle(pa, d_pa, conj, outR, d_outR, outI, d_outI, c4 * 4)

            def s2_mm(pk, d_pk, terms, c8):
                n_ = len(terms)
                for ti, (foff, buf, d_buf) in enumerate(terms):
                    S.op("tensor", lambda e, ti=ti, foff=foff, buf=buf: e.matmul(pk[:], lhsT=fc[:, foff:foff + 128], rhs=buf[:, c8 * 8:(c8 + 1) * 8, :],
                                                                                start=(ti == 0), stop=(ti == n_ - 1)),
                         reads=[d_fc, d_buf], writes=[d_pk])

            def group(g):
                S.op("gpsimd", lambda e: e.memset(Acc4[:], 0.0), writes=[d_Acc4])
                for jq in range(32):
                    et, d_et = Et[jq % 2]
                    load("sync", "f_e%d" % (jq % 2), et[:], edec_d[g, jq], d_et)
                    ph, d_ph = nb()
                    for jj in range(4):
                        j = 4 * jq + jj
                        S.op("tensor", lambda e, jj=jj, j=j, ph=ph: e.matmul(ph[0:64, jj * 128:(jj + 1) * 128], lhsT=h2T[:, j * 64:(j + 1) * 64],
                                                                            rhs=sap(w3b, 0, 64, g * 64, [(512, 2), (1, 64)]), start=True, stop=True),
                             reads=[d_h2T, d_w3b], writes=[d_ph])
                    hd, d_hd = hd32[jq % 2]
                    S.op("vector", lambda e, ph=ph, et=et, hd=hd: e.tensor_tensor(
                        out=hd[:].rearrange("p (j d c) -> p j d c", j=4, d=2), in0=ph[0:64, :].rearrange("p (j d c) -> p j d c", j=4, d=2),
                        in1=sap(et, 0, 64, 0, [(64, 4), (0, 2), (1, 64)]), op=ALU.mult), reads=[d_ph, d_et], writes=[d_hd])
                    S.op("scalar", lambda e, hd=hd: e.activation(out=habs[:], in_=hd[:], func=AF.Abs), reads=[d_hd], writes=[d_habs])
                    S.op("vector", lambda e: e.tensor_tensor(out=Acc4[:], in0=Acc4[:], in1=habs[:], op=ALU.add), reads=[d_habs, d_Acc4], writes=[d_Acc4])
                    S.op("scalar", lambda e, hd=hd, jq=jq: e.copy(sap(Hbuf, 0, 64, 4 * jq, [(1, 4), (64 * 128, 2), (128, 64)]),
                                                                 hd[:].rearrange("p (j d c) -> p j d c", j=4, d=2)),
                         reads=[d_hd], writes=[d_Hbuf])
                    if jq % 4 == 3:
                        scan_some(1)
                S.op("gpsimd", lambda e: e.memset(sap(Hbuf, 0, 1, 64 * 128, [(128, 64)]), 0.0), writes=[d_Hbuf])
                pn, d_pn = nb()
                for jj in range(4):
                    S.op("tensor", lambda e, jj=jj: e.matmul(pn[:, 0:128], lhsT=onesf[:], rhs=Acc4[:, jj * 128:(jj + 1) * 128], start=(jj == 0), stop=(jj == 3)),
                         reads=[d_onesf, d_Acc4], writes=[d_pn])
                S.op("scalar", lambda e: e.copy(nsb[:], pn[:, 0:128]), reads=[d_pn], writes=[d_nsb])
                S.op("vector", lambda e: e.scalar_tensor_tensor(out=rn[:], in0=nsb[:, 0:64], scalar=1e-6, op0=ALU.add, in1=nsb[:, 64:128], op1=ALU.add),
                     reads=[d_nsb], writes=[d_rn])
                S.op("vector", lambda e: e.reciprocal(out=rn[:], in_=rn[:]), reads=[d_rn], writes=[d_rn])
                s1_stage(lambda c: sap(Hbuf, 0, 64, c * 128, [(1, 128)]), [d_Hbuf], M1o, False, BfR, d_BfR, BfI, d_BfI)
                s1_stage(lambda c: sap(Hbuf, 0, 64, 64 * 128 + c * 128, [(1, 128)]), [d_Hbuf], M1co, True, BbR, d_BbR, BbI, d_BbI)
                for c8 in range(8):
                    pk, d_pk = nb()
                    s2_mm(pk, d_pk, [(C2o, BfR, d_BfR), (S2o, BfI, d_BfI), (C2o, BbR, d_BbR), (nS2o, BbI, d_BbI)], c8)
                    S.op("vector", lambda e, pk=pk, c8=c8: e.tensor_tensor(out=KR[:, c8 * 8:(c8 + 1) * 8, :], in0=pk[:].rearrange("p (c k) -> p c k", c=8),
                                                                          in1=sap(rn, 0, 128, c8 * 8, [(1, 8), (0, 64)]), op=ALU.mult),
                         reads=[d_pk, d_rn], writes=[d_KR])
                    pk2, d_pk2 = nb()
                    s2_mm(pk2, d_pk2, [(C2o, BfI, d_BfI), (nS2o, BfR, d_BfR), (C2o, BbI, d_BbI), (S2o, BbR, d_BbR)], c8)
                    S.op("vector", lambda e, pk2=pk2, c8=c8: e.tensor_tensor(out=KI[:, c8 * 8:(c8 + 1) * 8, :], in0=pk2[:].rearrange("p (c k) -> p c k", c=8),
                                                                            in1=sap(rn, 0, 128, c8 * 8, [(1, 8), (0, 64)]), op=ALU.mult),
                         reads=[d_pk2, d_rn], writes=[d_KI])
                if debug:
                    store("sync", "f_dbgk", dap(kspec, g * 64 * 64, [(512 * 64, 128), (1, 64 * 64)]), KR[:].rearrange("p c k -> p (c k)"), d_KR)
                    store("sync", "f_dbgk2", dap(kspec, 128 * 512 * 64 + g * 64 * 64, [(512 * 64, 128), (1, 64 * 64)]), KI[:].rearrange("p c k -> p (c k)"), d_KI)
                load("sync", "f_xg", Xg[:], dap(vxscr, g * 64 * L, [(128, 64), (L, 64), (1, 128)]), d_Xg)
                s1_stage(lambda c: Xg[:, c, :], [d_Xg], M1o, False, BfR, d_BfR, BfI, d_BfI)
                for c8 in range(8):
                    px, d_px = nb()
                    s2_mm(px, d_px, [(C2o, BfR, d_BfR), (S2o, BfI, d_BfI)], c8)
                    pxi, d_pxi = nb()
                    s2_mm(pxi, d_pxi, [(C2o, BfI, d_BfI), (nS2o, BfR, d_BfR)], c8)
                    kr = KR[:, c8 * 8:(c8 + 1) * 8, :].rearrange("p c k -> p (c k)")
                    ki = KI[:, c8 * 8:(c8 + 1) * 8, :].rearrange("p c k -> p (c k)")
                    S.op("vector", lambda e, px=px, kr=kr: e.tensor_tensor(out=t1[:], in0=px[:], in1=kr, op=ALU.mult), reads=[d_px, d_KR], writes=[d_t1])
                    S.op("vector", lambda e, pxi=pxi, ki=ki: e.tensor_tensor(out=t2[:], in0=pxi[:], in1=ki, op=ALU.mult), reads=[d_pxi, d_KI], writes=[d_t2])
                    S.op("vector", lambda e, px=px, ki=ki: e.tensor_tensor(out=t3[:], in0=px[:], in1=ki, op=ALU.mult), reads=[d_px, d_KI], writes=[d_t3])
                    S.op("vector", lambda e, pxi=pxi, kr=kr: e.tensor_tensor(out=t4[:], in0=pxi[:], in1=kr, op=ALU.mult), reads=[d_pxi, d_KR], writes=[d_t4])
                    S.op("gpsimd", lambda e, c8=c8: e.tensor_tensor(out=BbR[:, c8 * 8:(c8 + 1) * 8, :].rearrange("p c k -> p (c k)"), in0=t1[:], in1=t2[:], op=ALU.subtract),
                         reads=[d_t1, d_t2], writes=[d_BbR])
                    S.op("gpsimd", lambda e, c8=c8: e.tensor_tensor(out=BbI[:, c8 * 8:(c8 + 1) * 8, :].rearrange("p c k -> p (c k)"), in0=t3[:], in1=t4[:], op=ALU.add),
                         reads=[d_t3, d_t4], writes=[d_BbI])
                for pb in range(16):
                    pc, d_pc = nb()
                    for q_ in range(2):
                        p_ = pb * 2 + q_
                        S.op("tensor", lambda e, q_=q_, p_=p_, pc=pc: e.matmul(pc[:, q_ * 256:(q_ + 1) * 256], lhsT=BbR[:, 2 * p_:2 * p_ + 2, :],
                                                                              rhs=fc[:, C2S2o:C2S2o + 256], start=True, stop=False),
                             reads=[d_BbR, d_fc], writes=[d_pc])
                        S.op("tensor", lambda e, q_=q_, p_=p_, pc=pc: e.matmul(pc[:, q_ * 256:(q_ + 1) * 256], lhsT=BbI[:, 2 * p_:2 * p_ + 2, :],
                                                                              rhs=fc[:, nS2C2o:nS2C2o + 256], start=False, stop=True),
                             reads=[d_BbI, d_fc], writes=[d_pc])
                    pcv = pc[:].rearrange("p (q x) -> p q x", q=2)
                    (Q1, d_Q1), (Q2, d_Q2) = Qb[qctr[0] % 2]
                    qctr[0] += 1
                    S.op("vector", lambda e, pcv=pcv, Q1=Q1: e.tensor_tensor(out=Q1[:].rearrange("p (q x) -> p q x", q=2), in0=pcv,
                                                                     in1=sap(tcs, 0, 128, 256, [(0, 2), (1, 256)]), op=ALU.mult), reads=[d_pc, d_tcs], writes=[d_Q1])
                    S.op("vector", lambda e, pcv=pcv, Q2=Q2: e.tensor_tensor(out=Q2[:].rearrange("p (q x) -> p q x", q=2), in0=pcv,
                                                                     in1=sap(tcs, 0, 128, 512, [(0, 2), (1, 256)]), op=ALU.mult), reads=[d_pc, d_tcs], writes=[d_Q2])
                    S.op("gpsimd", lambda e, pb=pb, Q1=Q1, Q2=Q2: e.tensor_tensor(out=sap(Hbuf, 0, 128, pb * 256, [(128, 2), (1, 128)]),
                                                                   in0=sap(Q1, 0, 128, 0, [(256, 2), (1, 128)]), in1=sap(Q2, 0, 128, 128, [(256, 2), (1, 128)]), op=ALU.subtract),
                         reads=[d_Q1, d_Q2], writes=[d_Hbuf])
                    S.op("gpsimd", lambda e, pb=pb, Q1=Q1, Q2=Q2: e.tensor_tensor(out=sap(Hbuf, 0, 128, 4096 + pb * 256, [(128, 2), (1, 128)]),
                                                                   in0=sap(Q1, 0, 128, 128, [(256, 2), (1, 128)]), in1=sap(Q2, 0, 128, 0, [(256, 2), (1, 128)]), op=ALU.add),
                         reads=[d_Q1, d_Q2], writes=[d_Hbuf])
                for p4 in range(8):
                    py, d_py = nb()
                    for q_ in range(4):
                        p_ = p4 * 4 + q_
                        S.op("tensor", lambda e, q_=q_, p_=p_, py=py: e.matmul(py[:, q_ * 128:(q_ + 1) * 128], lhsT=fc[:, BDCo:BDCo + 128],
                                                                              rhs=sap(Hbuf, 0, 128, p_ * 128, [(1, 128)]), start=True, stop=False),
                             reads=[d_Hbuf, d_fc], writes=[d_py])
                        S.op("tensor", lambda e, q_=q_, p_=p_, py=py: e.matmul(py[:, q_ * 128:(q_ + 1) * 128], lhsT=fc[:, BDnSo:BDnSo + 128],
                                                                              rhs=sap(Hbuf, 0, 128, 4096 + p_ * 128, [(1, 128)]), start=False, stop=True),
                             reads=[d_Hbuf, d_fc], writes=[d_py])
                    S.op("scalar", lambda e, p4=p4, py=py: e.copy(Yo[:, p4 * 4:(p4 + 1) * 4, :].rearrange("p q x -> p (q x)"), py[:]), reads=[d_py], writes=[d_Yo])
                store("sync", "f_yo", dap(yscr, g * 64 * L, [(128, 128), (2 * L, 32), (1, 128)]), Yo[:], d_Yo)

            for g in range(8):
                group(g)
            scan_some(NT)
            S.barrier()
        if stop_after <= 2:
            S.emit()
            return nc
        if stop_after <= 3:
            S.emit()
            return nc

        with contextlib.ExitStack() as st:
            Wo, d_Wo = sb(st, [128, 8, D], BF16)
            W1, d_W1 = sb(st, [128, 8, 4 * D], BF16)
            W2, d_W2 = sb(st, [128, 32, D], BF16)
            for k in range(8):
                S.op("gpsimd", lambda e, k=k: e.dma_start(out=Wo[:, k, :], in_=w_out[k * 128:(k + 1) * 128, :]), writes=[d_Wo], dma="w_o%d" % (k % 4))
            for k in range(8):
                S.op("gpsimd", lambda e, k=k: e.dma_start(out=W1[:, k, :], in_=w_mlp1[k * 128:(k + 1) * 128, :]), writes=[d_W1], dma="w_1%d" % (k % 4))
            for k4 in range(8):
                S.op("gpsimd", lambda e, k4=k4: e.dma_start(out=W2[:, k4 * 4:(k4 + 1) * 4, :], in_=w_mlp2[k4 * 512:(k4 + 1) * 512, :].rearrange("(k p) n -> p k n", p=128)),
                     writes=[d_W2], dma="w_2")
            hbc, d_hbc = sb(st, [128, 4], F32)
            load("sync", "r_hb", hbc[:].rearrange("p (c o) -> p c o", o=1), dap(hy_bias, 0, [(1, 128), (128, 4), (1, 1)]), d_hbc, slow=True)
            gnr, d_gnr = sb(st, [128, 512], F32)
            load("sync", "r_gn", gnr[:], row_bc(gn_g, 0, 512), d_gnr)
            banks = [ps(st, [128, 512], F32) for _ in range(4)]
            pmb = [ps(st, [128, 512], F32) for _ in range(2)]
            bbanks = [ps(st, [128, 1024], BF16) for _ in range(2)]
            bctr = [0, 0]

            def nb():
                b_ = banks[bctr[0] % 4]
                bctr[0] += 1
                return b_

            def nbb():
                b_ = bbanks[bctr[1] % 2]
                bctr[1] += 1
                return b_

            qkvg = [[sb(st, [128, 512], BF16) for _ in range(5)] for _ in range(1)]
            scrs = [qscr, kscr, vscr, gscr]
            xt, d_xt = sb(st, [128, D], F32)
            hy3 = [sb(st, [128, 4, 128], BF16) for _ in range(3)]
            qx, d_qx = sb(st, [128, 8, 2, 64], BF16)
            qT, d_qT = sb(st, [128, 4, 128], BF16); kT, d_kT = sb(st, [128, 4, 128], BF16)
            qxT, d_qxT = sb(st, [128, 8, 128], BF16)
            Pm, d_Pm = sb(st, [128, 8, 128], BF16)
            osb, d_osb = sb(st, [128, 512], F32); osq, d_osq = sb(st, [128, 512], F32)
            st8, d_st8 = sb(st, [128, 4, 8], F32)
            yret, d_yret = sb(st, [128, 512], BF16)
            mixT, d_mixT = sb(st, [128, 8, 128], BF16)
            xn, d_xn = sb(st, [128, D], F32)
            ss2, d_ss2 = sb(st, [128, 2], F32)
            xm2v = qx[:].rearrange("p a b c -> p (a b c)"); d_xm2 = d_qx
            hx2T, d_hx2T = qxT, d_qxT
            hT, d_hT = sb(st, [128, 16, 128], BF16)

            def loads(i):
                if i >= NT:
                    return
                bufs = qkvg[0]
                for j_ in range(4):
                    load("sync", "r_in%d_%d" % (0, j_), bufs[j_][0][:], scrs[j_][i * 128:(i + 1) * 128, :], bufs[j_][1])
                load("sync", "r_in%d_4" % (0), bufs[4][0][:], Sscr[i], bufs[4][1])

            def tile(i):
                (qt, d_qt), (kt, d_kt), (vt, d_vt), (gt, d_gt), (St_, d_St) = qkvg[0]
                load("sync", "r_x", xt[:], x[i * 128:(i + 1) * 128, :], d_xt)
                for j_, scr_ in enumerate((yscr, vxscr, x0scr)):
                    load("sync", "r_hy%d" % j_, hy3[j_][0][:], dap(scr_, i * 128, [(L, 128), (128 * L, 4), (1, 128)]), hy3[j_][1])
                S.op("vector", lambda e: e.tensor_tensor(out=qx[:], in0=sap(qt, 0, 128, 0, [(64, 8), (0, 2), (1, 64)]),
                                                         in1=sap(Wq, 0, 128, 0, [(2, 8), (1, 2), (0, 64)]), op=ALU.mult), reads=[d_qt, d_Wq], writes=[d_qx])
                pq, d_pq = nbb()
                for hp in range(4):
                    S.op("tensor", lambda e, hp=hp: e.transpose(out=pq[:, hp * 128:(hp + 1) * 128], in_=qt[:, hp * 128:(hp + 1) * 128], identity=identb[:]),
                         reads=[d_qt, d_identb], writes=[d_pq])
                for hp in range(4):
                    S.op("tensor", lambda e, hp=hp: e.transpose(out=pq[:, 512 + hp * 128:512 + (hp + 1) * 128], in_=kt[:, hp * 128:(hp + 1) * 128], identity=identb[:]),
                         reads=[d_kt, d_identb], writes=[d_pq])
                S.op("scalar", lambda e: e.copy(qT[:].rearrange("p a b -> p (a b)"), pq[:, 0:512]), reads=[d_pq], writes=[d_qT])
                S.op("scalar", lambda e: e.copy(kT[:].rearrange("p a b -> p (a b)"), pq[:, 512:1024]), reads=[d_pq], writes=[d_kT])
                px, d_px = nbb()
                for h in range(8):
                    S.op("tensor", lambda e, h=h: e.transpose(out=px[:, h * 128:(h + 1) * 128], in_=qx[:, h, :, :], identity=identb[:]),
                         reads=[d_qx, d_identb], writes=[d_px])
                S.op("scalar", lambda e: e.copy(qxT[:].rearrange("p a b -> p (a b)"), px[:]), reads=[d_px], writes=[d_qxT])
                for par in range(2):
                    psc, d_psc = nb()
                    b0 = par * 64
                    for hh in range(4):
                        h = 2 * hh + par
                        S.op("tensor", lambda e, hh=hh, b0=b0, psc=psc: e.matmul(psc[:, hh * 128:(hh + 1) * 128], lhsT=kT[b0:b0 + 64, hh, :],
                                                                                rhs=qT[b0:b0 + 64, hh, :], start=True, stop=True),
                             reads=[d_kT, d_qT], writes=[d_psc])
                    S.op("vector", lambda e, par=par, psc=psc: e.tensor_tensor(out=sap(Pm, 0, 128, par * 128, [(256, 4), (1, 128)]), in0=psc[:].rearrange("p (a b) -> p a b", a=4),
                                                                              in1=sap(DT, 0, 128, par * 128, [(256, 4), (1, 128)]), op=ALU.mult),
                         reads=[d_psc, d_DT], writes=[d_Pm])
                po, d_po = nb()
                for h in range(8):
                    S.op("tensor", lambda e, h=h: e.matmul(po[:, h * 64:(h + 1) * 64], lhsT=Pm[:, h, :], rhs=vt[:, h * 64:(h + 1) * 64], start=True, stop=False),
                         reads=[d_Pm, d_vt], writes=[d_po])
                    S.op("tensor", lambda e, h=h: e.matmul(po[:, h * 64:(h + 1) * 64], lhsT=qxT[:, h, :], rhs=St_[:, h * 64:(h + 1) * 64], start=False, stop=True),
                         reads=[d_qxT, d_St], writes=[d_po])
                S.op("scalar", lambda e: e.copy(osb[:], po[:]), reads=[d_po], writes=[d_osb])
                S.op("scalar", lambda e: e.activation(out=osq[:], in_=po[:], func=AF.Square), reads=[d_po], writes=[d_osq])
                S.op("vector", lambda e: e.tensor_reduce(out=st8[:, 0, :], in_=osb[:].rearrange("p (h x) -> p h x", h=8), op=ALU.add, axis=AX.X), reads=[d_osb], writes=[d_st8])
                S.op("vector", lambda e: e.tensor_reduce(out=st8[:, 1, :], in_=osq[:].rearrange("p (h x) -> p h x", h=8), op=ALU.add, axis=AX.X), reads=[d_osq], writes=[d_st8])
                S.op("vector", lambda e: e.tensor_scalar(out=st8[:, 0, :], in0=st8[:, 0, :], scalar1=1.0 / 64, scalar2=None, op0=ALU.mult), reads=[d_st8], writes=[d_st8])
                S.op("vector", lambda e: e.tensor_tensor(out=st8[:, 2, :], in0=st8[:, 0, :], in1=st8[:, 0, :], op=ALU.mult), reads=[d_st8], writes=[d_st8])
                S.op("vector", lambda e: e.scalar_tensor_tensor(out=st8[:, 3, :], in0=st8[:, 1, :], scalar=1.0 / 64, op0=ALU.mult, in1=st8[:, 2, :], op1=ALU.subtract),
                     reads=[d_st8], writes=[d_st8])
                S.op("scalar", lambda e: e.activation(out=st8[:, 3, :], in_=st8[:, 3, :], func=AF.Sqrt, bias=1e-6), reads=[d_st8], writes=[d_st8])
                S.op("vector", lambda e: e.reciprocal(out=st8[:, 3, :], in_=st8[:, 3, :]), reads=[d_st8], writes=[d_st8])
                S.op("vector", lambda e: e.tensor_tensor(out=osb[:].rearrange("p (h x) -> p h x", h=8), in0=osb[:].rearrange("p (h x) -> p h x", h=8),
                                                         in1=sap(st8, 0, 128, 0, [(1, 8), (0, 64)]), op=ALU.subtract), reads=[d_osb, d_st8], writes=[d_osb])
                S.op("vector", lambda e: e.tensor_tensor(out=osb[:].rearrange("p (h x) -> p h x", h=8), in0=osb[:].rearrange("p (h x) -> p h x", h=8),
                                                         in1=sap(st8, 0, 128, 24, [(1, 8), (0, 64)]), op=ALU.mult), reads=[d_osb, d_st8], writes=[d_osb])
                S.op("gpsimd", lambda e: e.tensor_tensor(out=osb[:], in0=osb[:], in1=gnr[:], op=ALU.mult), reads=[d_osb, d_gnr], writes=[d_osb])
                S.op("gpsimd", lambda e: e.tensor_tensor(out=yret[:], in0=osb[:], in1=gt[:], op=ALU.mult), reads=[d_osb, d_gt], writes=[d_yret])
                loads(i + 1)
                py_, d_py = nbb()
                for hp in range(4):
                    S.op("tensor", lambda e, hp=hp: e.transpose(out=py_[:, hp * 128:(hp + 1) * 128], in_=yret[:, hp * 128:(hp + 1) * 128], identity=identb[:]),
                         reads=[d_yret, d_identb], writes=[d_py])
                S.op("scalar", lambda e: e.copy(mixT[:, 4:8, :].rearrange("p a b -> p (a b)"), py_[:, 0:512]), reads=[d_py], writes=[d_mixT])
                (yc, d_yc), (vxt, d_vxt), (x0t, d_x0t) = hy3
                for ct in range(4):
                    S.op("vector", lambda e, ct=ct: e.scalar_tensor_tensor(out=osq[:, ct * 128:(ct + 1) * 128], in0=vxt[:, ct, :], scalar=hbc[:, ct:ct + 1], op0=ALU.mult,
                                                                          in1=yc[:, ct, :], op1=ALU.add), reads=[d_vxt, d_yc, d_hbc, d_osq], writes=[d_osq])
                S.op("gpsimd", lambda e: e.tensor_tensor(out=mixT[:, 0:4, :].rearrange("p a b -> p (a b)"), in0=osq[:], in1=x0t[:].rearrange("p a b -> p (a b)"), op=ALU.mult),
                     reads=[d_osq, d_x0t], writes=[d_mixT])
                for nb_ in range(2):
                    pw, d_pw = nb()
                    for k in range(8):
                        S.op("tensor", lambda e, k=k, nb_=nb_, pw=pw: e.matmul(pw[:], lhsT=mixT[:, k, :], rhs=Wo[:, k, nb_ * 512:(nb_ + 1) * 512], start=(k == 0), stop=(k == 7)),
                             reads=[d_mixT, d_Wo], writes=[d_pw])
                    S.op("vector", lambda e, nb_=nb_, pw=pw: e.tensor_tensor(out=xn[:, nb_ * 512:(nb_ + 1) * 512], in0=pw[:], in1=gate2[:, nb_ * 512:(nb_ + 1) * 512], op=ALU.mult),
                         reads=[d_pw, d_gate2], writes=[d_xn])
                S.op("gpsimd", lambda e: e.tensor_tensor(out=xn[:], in0=xn[:], in1=xt[:], op=ALU.add), reads=[d_xn, d_xt], writes=[d_xn])
                S.op("scalar", lambda e: e.activation(out=xm2v, in_=xn[:], func=AF.Square, accum_out=ss2[:, 0:1]), reads=[d_xn], writes=[d_xm2, d_ss2])
                S.op("scalar", lambda e: e.activation(out=ss2[:, 0:1], in_=ss2[:, 0:1], func=AF.Sqrt, scale=1.0 / D, bias=1e-6), reads=[d_ss2], writes=[d_ss2])
                S.op("vector", lambda e: e.reciprocal(out=ss2[:, 0:1], in_=ss2[:, 0:1]), reads=[d_ss2], writes=[d_ss2])
                S.op("vector", lambda e: e.scalar_tensor_tensor(out=xm2v, in0=xn[:], scalar=ss2[:, 0:1], op0=ALU.mult, in1=gs2[:], op1=ALU.mult),
                     reads=[d_xn, d_ss2, d_gs2], writes=[d_xm2])
                pt2, d_pt2 = nbb()
                for k in range(8):
                    S.op("tensor", lambda e, k=k: e.transpose(out=pt2[:, k * 128:(k + 1) * 128], in_=qx[:].rearrange("p a b c -> p (a b c)")[:, k * 128:(k + 1) * 128], identity=identb[:]),
                         reads=[d_xm2, d_identb], writes=[d_pt2])
                for k in range(8):
                    S.op("scalar", lambda e, k=k: e.activation(out=hx2T[:, k, :], in_=pt2[:, k * 128:(k + 1) * 128], func=AF.Identity, bias=colx[:, 3, k:k + 1]),
                         reads=[d_pt2, d_colx], writes=[d_hx2T])
                for hf in range(2):
                    for f4 in range(4):
                        ph, d_ph = nb()
                        for ff in range(4):
                            ft = hf * 16 + f4 * 4 + ff
                            for k in range(8):
                                S.op("tensor", lambda e, k=k, ft=ft, ff=ff, ph=ph: e.matmul(ph[:, ff * 128:(ff + 1) * 128], lhsT=W1[:, k, ft * 128:(ft + 1) * 128], rhs=hx2T[:, k, :],
                                                                                           start=(k == 0), stop=(k == 7)), reads=[d_W1, d_hx2T], writes=[d_ph])
                        S.op("scalar", lambda e, ph=ph: e.activation(out=osq[:], in_=ph[:], func=AF.Relu), reads=[d_ph], writes=[d_osq])
                        S.op("gpsimd", lambda e, f4=f4: e.tensor_tensor(out=hT[:, f4 * 4:(f4 + 1) * 4, :].rearrange("p a b -> p (a b)"), in0=osq[:], in1=osq[:], op=ALU.mult),
                             reads=[d_osq], writes=[d_hT])
                    for nb_ in range(2):
                        pm, d_pm = pmb[nb_]
                        for kk in range(16):
                            k = hf * 16 + kk
                            S.op("tensor", lambda e, k=k, kk=kk, nb_=nb_, pm=pm: e.matmul(pm[:], lhsT=hT[:, kk, :], rhs=W2[:, k, nb_ * 512:(nb_ + 1) * 512], start=(k == 0), stop=(k == 31)),
                                 reads=[d_hT, d_W2], writes=[d_pm])
                for nb_ in range(2):
                    pm, d_pm = pmb[nb_]
                    S.op("vector", lambda e, nb_=nb_, pm=pm: e.tensor_tensor(out=osb[:], in0=pm[:], in1=gate5[:, nb_ * 512:(nb_ + 1) * 512], op=ALU.mult),
                         reads=[d_pm, d_gate5], writes=[d_osb])
                    S.op("gpsimd", lambda e, nb_=nb_: e.tensor_tensor(out=xn[:, nb_ * 512:(nb_ + 1) * 512], in0=xn[:, nb_ * 512:(nb_ + 1) * 512], in1=osb[:], op=ALU.add),
                         reads=[d_xn, d_osb], writes=[d_xn])
                S.op("scalar", lambda e: e.activation(out=xm2v, in_=xn[:], func=AF.Square, accum_out=ss2[:, 1:2]), reads=[d_xn], writes=[d_xm2, d_ss2])
                S.op("scalar", lambda e: e.activation(out=ss2[:, 1:2], in_=ss2[:, 1:2], func=AF.Sqrt, scale=1.0 / D, bias=1e-6), reads=[d_ss2], writes=[d_ss2])
                S.op("vector", lambda e: e.reciprocal(out=ss2[:, 1:2], in_=ss2[:, 1:2]), reads=[d_ss2], writes=[d_ss2])
                S.op("vector", lambda e: e.scalar_tensor_tensor(out=xt[:], in0=xn[:], scalar=ss2[:, 1:2], op0=ALU.mult, in1=gF[:], op1=ALU.mult),
                     reads=[d_xn, d_ss2, d_gF], writes=[d_xt])
                final_events.append(store("sync", "r_out", out[i * 128:(i + 1) * 128, :], xt[:], d_xt))

            loads(0)
            for i in range(NT if stop_after >= 99 else 2):
                tile(i)
            S.barrier()
        S.emit()
    return nc


def make_in_map(inputs, b):
    f = lambda a: np.ascontiguousarray(np.asarray(a, dtype=np.float32))
    c = host_consts()
    m = dict(
        x=f(inputs["x"][b]), ctx=f(inputs["ctx"][b]),
        cc=f(np.stack([np.asarray(inputs["c"][b]), np.asarray(inputs["c_ctx"])], axis=0)),
        w_ada=f(inputs["w_ada"][0]), b_ada=f(inputs["b_ada"][0]).reshape(1, -1), norm1_g=f(inputs["norm1_g"][0]).reshape(1, -1),
        w_in=f(inputs["w_in"][0]), hy_conv_w=f(inputs["hy_conv_w"][0]), hy_conv_b=f(inputs["hy_conv_b"][0]).reshape(1, -1),
        hy_f_w1=f(inputs["hy_f_w1"][0]), hy_f_b1=f(inputs["hy_f_b1"][0]).reshape(1, -1), hy_f_freq1=f(inputs["hy_f_freq1"][0]).reshape(1, -1),
        hy_f_w2=f(inputs["hy_f_w2"][0]), hy_f_b2=f(inputs["hy_f_b2"][0]).reshape(1, -1), hy_f_freq2=f(inputs["hy_f_freq2"][0]).reshape(1, -1),
        hy_f_w3=f(inputs["hy_f_w3"][0]), hy_bias=f(inputs["hy_bias"][0]).reshape(1, -1),
        ret_decay_logit=f(inputs["ret_decay_logit"][0]).reshape(1, 16), ret_gn_g=f(inputs["ret_gn_g"][0]).reshape(1, -1),
        w_out=f(inputs["w_out"][0]), norm2_g=f(inputs["norm2_g"][0]).reshape(1, -1),
        w_mlp1=f(inputs["w_mlp1"][0]), w_mlp2=f(inputs["w_mlp2"][0]), norm_f_g=f(inputs["norm_f_g"]).reshape(1, -1),
    )
    m.update(c)
    return m


_NC = None


def kernel(**inputs):
    global _NC
    if _NC is None:
        _NC = build()
    in_maps = [make_in_map(inputs, b) for b in range(8)]
    res = run_bass_kernel_spmd(_NC, in_maps, core_ids=list(range(8)))
    return np.stack([np.asarray(r["out"], dtype=np.float32) for r in res.results], axis=0)
```

```python
import contextlib
import math
import numpy as np
import ml_dtypes
import concourse.bass as bass
import concourse.mybir as mybir
from concourse.bass_utils import run_bass_kernel_spmd

F32 = mybir.dt.float32
BF16 = mybir.dt.bfloat16
AF = mybir.ActivationFunctionType
ALU = mybir.AluOpType
AX = mybir.AxisListType

L = 8192
D = 1024
NT = 64
NFFT = 16384
ENGS = ("sync", "scalar", "vector", "gpsimd", "tensor")


class Dep:
    __slots__ = ("w", "r")

    def __init__(self):
        self.w = None
        self.r = []


class Sched:
    def __init__(self, nc, stack):
        self.nc = nc
        self.stack = stack
        self.streams = {e: [] for e in ENGS}
        self.esem = {}
        self.ecnt = {}
        for e in ("scalar", "vector", "gpsimd", "tensor"):
            self.esem[e] = stack.enter_context(nc.semaphore("es_" + e))
            self.ecnt[e] = 0
        self.dsem = {}
        self.dpool = []
        self.gsems = []
        self.nds = 0
        self.waited = {e: {} for e in ENGS}

    def _wait(self, eng, ev, waits):
        if ev is None:
            return
        sem, val, src = ev
        if eng == "tensor" and src == "tensor":
            return
        key = id(sem)
        if self.waited[eng].get(key, 0) >= val:
            return
        self.waited[eng][key] = val
        waits.append((sem, val))

    def op(self, eng, fn, reads=(), writes=(), dma=None):
        waits = []
        for d in reads:
            self._wait(eng, d.w, waits)
        for d in writes:
            self._wait(eng, d.w, waits)
            for ev in d.r:
                self._wait(eng, ev, waits)
        if dma is not None and eng == "gpsimd":
            self.nds += 1
            ent = [self.stack.enter_context(self.nc.semaphore("gs%d" % self.nds)), 16]
            self.gsems.append(ent)
            ev = (ent[0], 16, "dma")
            inc = (ent[0], 16)
        elif dma is not None:
            if dma not in self.dsem:
                if self.dpool:
                    self.dsem[dma] = self.dpool.pop()
                else:
                    self.nds += 1
                    self.dsem[dma] = [self.stack.enter_context(self.nc.semaphore("ds%d" % self.nds)), 0]
            ent = self.dsem[dma]
            ent[1] += 16
            ev = (ent[0], ent[1], "dma")
            inc = (ent[0], 16)
        else:
            self.ecnt[eng] += 1
            ev = (self.esem[eng], self.ecnt[eng], eng)
            inc = (self.esem[eng], 1)
        for d in reads:
            d.r.append(ev)
        for d in writes:
            d.w = ev
            d.r = []
        self.streams[eng].append((waits, fn, inc))
        return ev

    def barrier(self):
        evs = [(self.esem[e], self.ecnt[e], "x") for e in self.esem if self.ecnt[e] > 0]
        evs += [(v[0], v[1], "dma") for v in self.dsem.values() if v[1] > 0]
        evs += [(v[0], v[1], "dma") for v in self.gsems]
        for eng in ENGS:
            waits = []
            for ev in evs:
                key = id(ev[0])
                if self.waited[eng].get(key, 0) >= ev[1]:
                    continue
                self.waited[eng][key] = ev[1]
                waits.append((ev[0], ev[1]))
            if waits:
                self.streams[eng].append((waits, None, None))
        self.dpool.extend(self.dsem.values())
        self.dsem = {}

    def emit(self):
        nc = self.nc
        streams = self.streams

        def run(name, eng):
            for waits, fn, inc in streams[name]:
                for sem, val in waits:
                    eng.wait_ge(sem, val)
                if fn is not None:
                    fn(eng).then_inc(inc[0], inc[1])

        with nc.Block() as block:
            @block.sync
            def _(e):
                run("sync", e)

            @block.scalar
            def _(e):
                run("scalar", e)

            @block.vector
            def _(e):
                run("vector", e)

            @block.gpsimd
            def _(e):
                run("gpsimd", e)

            @block.tensor
            def _(e):
                run("tensor", e)


def sap(t, p0, pn, f0, dims):
    shp = list(t.shape)
    Fsz = int(np.prod(shp[1:]))
    return bass.AP(t, p0 * Fsz + f0, [[Fsz, pn]] + [[int(s), int(c)] for s, c in dims])


def dap(t, off, dims):
    return bass.AP(t.tensor, int(off), [[int(s), int(c)] for s, c in dims])


def _bf(a):
    return np.ascontiguousarray(a.astype(np.float32)).astype(ml_dtypes.bfloat16)


_CONSTS = None


def host_consts():
    global _CONSTS
    if _CONSTS is not None:
        return _CONSTS
    n1 = np.arange(64, dtype=np.float64)[:, None]
    k1 = np.arange(64, dtype=np.float64)[None, :]
    th1 = 2 * np.pi * n1 * (k1 + 0.5) / 128.0
    M1 = np.zeros((128, 128)); M1[:64, :64] = np.cos(th1); M1[:64, 64:] = -np.sin(th1)
    M1c = np.zeros((128, 128)); M1c[:64, :64] = np.cos(th1); M1c[:64, 64:] = np.sin(th1)
    n2 = np.arange(128, dtype=np.float64)[:, None]
    tht = 2 * np.pi * n2 * (k1 + 0.5) / NFFT
    k2 = np.arange(128, dtype=np.float64)[None, :]
    th2 = 2 * np.pi * n2 * k2 / 128.0
    C2 = np.cos(th2); S2 = np.sin(th2)
    sc = 2.0 / NFFT
    BDC = np.zeros((128, 128)); BDnS = np.zeros((128, 128))
    for c in range(2):
        BDC[c * 64:(c + 1) * 64, c * 64:(c + 1) * 64] = sc * np.cos(th1).T
        BDnS[c * 64:(c + 1) * 64, c * 64:(c + 1) * 64] = -sc * np.sin(th1).T
    fconst = np.concatenate([M1, M1c, C2, S2, -S2, C2, S2, -S2, C2, BDC, BDnS], axis=1)
    TC = np.concatenate([np.cos(tht), np.cos(tht)], axis=1)
    TS = np.concatenate([np.sin(tht), np.sin(tht)], axis=1)
    ct = np.cos(tht).T; st_ = np.sin(tht).T
    ITC = np.tile(np.concatenate([ct, ct], axis=1), (2, 1))
    ITS = np.tile(np.concatenate([st_, st_], axis=1), (2, 1))
    tconst = np.concatenate([TC, TS, ITC, ITS], axis=1).astype(np.float32)
    t = np.arange(L)
    r = (t // 64).astype(np.float32); col = (t % 64).astype(np.float32)
    inv = (10000.0 ** (-np.arange(16, dtype=np.float32) / 16)).astype(np.float32)
    ang = np.concatenate([r[:, None] * inv, col[:, None] * inv], axis=-1).astype(np.float32)
    cosr = np.cos(ang).astype(np.float32); sinr = np.sin(ang).astype(np.float32)
    def tl(a):
        return np.ascontiguousarray(a.reshape(64, 128, 32).transpose(1, 0, 2))
    rope = np.stack([tl(cosr), tl(sinr)], axis=1).astype(np.float32)
    tt = np.linspace(0.0, 1.0, L, dtype=np.float32)[:, None]
    w = ((2.0 * math.pi / L) * np.arange(L, dtype=np.float32))[:, None].astype(np.float32)
    bands = np.linspace(1e-4, 15, 16, dtype=np.float32)[None, :]
    z = np.concatenate([tt, np.cos(bands * w), -np.sin(bands * w)], axis=-1).astype(np.float32)
    order = (128 * np.arange(64)[None, :] + np.arange(128)[:, None]).reshape(-1)
    zT = np.ascontiguousarray(z[order].T).astype(np.float32)
    deltas = np.abs(np.linspace(math.log(1e-2) / 1.5, math.log(1e-2) / 0.3, 512, dtype=np.float32))
    E = np.exp(-tt * deltas[None, :]).astype(np.float32)
    E4 = E.reshape(64, 32, 4, 8, 64)
    edec = np.ascontiguousarray(E4.transpose(3, 1, 0, 2, 4)).reshape(8, 32, 64, 256).astype(np.float32)
    m = np.arange(128)[:, None]; c = np.arange(128)[None, :]
    pd = np.stack([np.maximum(c - m, 0), (c >= m), np.maximum(m - c, 0), (m >= c)], axis=1).astype(np.float32)
    p = np.arange(128, dtype=np.float32)
    pcols = np.stack([p + 1, 128 - p, 127 - p, p, 255 - p, p, 127 - p, 128 + p], axis=1).astype(np.float32)
    ident = np.eye(128, dtype=np.float32)
    _CONSTS = dict(fconst=_bf(fconst), tconst=tconst, rope=rope, zT=zT, edec=edec, pd=np.ascontiguousarray(pd),
                   pcols=pcols, ident_bf=_bf(ident), ident_f=ident)
    return _CONSTS


def build(debug=False, stop_after=99):
    nc = bass.Bass("TRN2", target_bir_lowering=False)

    def din(name, shape, dt=F32):
        return nc.dram_tensor(name, list(shape), dt, kind="ExternalInput").ap()

    def dscr(name, shape, dt):
        if debug:
            return nc.dram_tensor(name, list(shape), dt, kind="ExternalOutput").ap()
        return nc.dram_tensor(name, list(shape), dt).ap()

    x = din("x", [L, D]); ctx = din("ctx", [256, D]); cc = din("cc", [2, D])
    w_ada = din("w_ada", [D, 6 * D]); b_ada = din("b_ada", [1, 6 * D]); norm1_g = din("norm1_g", [1, D])
    w_in = din("w_in", [D, 3584]); conv_w = din("hy_conv_w", [3, 1536]); conv_b = din("hy_conv_b", [1, 1536])
    f_w1 = din("hy_f_w1", [33, 64]); f_b1 = din("hy_f_b1", [1, 64]); f_fr1 = din("hy_f_freq1", [1, 64])
    f_w2 = din("hy_f_w2", [64, 64]); f_b2 = din("hy_f_b2", [1, 64]); f_fr2 = din("hy_f_freq2", [1, 64])
    f_w3 = din("hy_f_w3", [64, 1024]); hy_bias = din("hy_bias", [1, 512]); logit = din("ret_decay_logit", [1, 16])
    gn_g = din("ret_gn_g", [1, 512]); w_out = din("w_out", [D, D]); norm2_g = din("norm2_g", [1, D])
    w_mlp1 = din("w_mlp1", [D, 4 * D]); w_mlp2 = din("w_mlp2", [4 * D, D]); norm_f_g = din("norm_f_g", [1, D])
    fconst_d = din("fconst", [128, 1408], BF16); tconst_d = din("tconst", [128, 768])
    rope_d = din("rope", [128, 2, 64, 32]); zT_d = din("zT", [33, L]); edec_d = din("edec", [8, 32, 64, 256])
    pd_d = din("pd", [128, 4, 128]); pcols_d = din("pcols", [128, 8])
    identb_d = din("ident_bf", [128, 128], BF16); identf_d = din("ident_f", [128, 128])
    out = nc.dram_tensor("out", [L, D], F32, kind="ExternalOutput").ap()

    modscr = dscr("modscr", [2, 6 * D], F32)
    vxscr = dscr("vxscr", [512, L], BF16); x0scr = dscr("x0scr", [512, L], BF16); yscr = dscr("yscr", [512, L], BF16)
    qscr = dscr("qscr", [L, 512], BF16); kscr = dscr("kscr", [L, 512], BF16)
    vscr = dscr("vscr", [L, 512], BF16); gscr = dscr("gscr", [L, 512], BF16)
    Tscr = dscr("Tscr", [NT, 128, 512], F32); Sscr = dscr("Sscr", [NT, 128, 512], BF16)
    kspec = dscr("kspec", [2, 128, 512 * 64], BF16) if debug else None

    final_events = []

    with contextlib.ExitStack() as gst:
        S = Sched(nc, gst)
        uid = [0]

        def sb(st, shape, dt, name=None):
            uid[0] += 1
            t = st.enter_context(nc.sbuf_tensor(name or ("t%d" % uid[0]), list(shape), dt))
            return t, Dep()

        def ps(st, shape, dt, name=None):
            uid[0] += 1
            t = st.enter_context(nc.psum_tensor(name or ("p%d" % uid[0]), list(shape), dt))
            return t, Dep()

        def load(eng, key, dst_ap, src_ap, dst_dep, src_deps=(), slow=False):
            if slow:
                return S.op(eng, lambda e: e.dma_start(out=dst_ap, in_=src_ap, allow_slow_non_contiguous=True), reads=list(src_deps), writes=[dst_dep], dma=key)
            return S.op(eng, lambda e: e.dma_start(out=dst_ap, in_=src_ap), reads=list(src_deps), writes=[dst_dep], dma=key)

        def store(eng, key, dst_ap, src_ap, src_dep, dst_deps=(), slow=False):
            if slow:
                return S.op(eng, lambda e: e.dma_start(out=dst_ap, in_=src_ap, allow_slow_non_contiguous=True), reads=[src_dep], writes=list(dst_deps), dma=key)
            return S.op(eng, lambda e: e.dma_start(out=dst_ap, in_=src_ap), reads=[src_dep], writes=list(dst_deps), dma=key)

        def row_bc(ap_dram, off, n):
            return dap(ap_dram, off, [(0, 128), (1, n)])

        identb, d_identb = sb(gst, [128, 128], BF16)
        load("sync", "c_idb", identb[:], identb_d, d_identb)
        pcols, d_pcols = sb(gst, [128, 8], F32)
        load("sync", "c_pc", pcols[:], pcols_d, d_pcols)
        gs2, d_gs2 = sb(gst, [128, D], F32); gate2, d_gate2 = sb(gst, [128, D], F32)
        gate5, d_gate5 = sb(gst, [128, D], F32); gF, d_gF = sb(gst, [128, D], F32)
        colx, d_colx = sb(gst, [128, 6, 8], F32)
        lgt, d_lgt = sb(gst, [128, 16], F32)
        lgsel, d_lgsel = sb(gst, [128, 8], F32)
        DT, d_DT = sb(gst, [128, 8, 128], F32)
        Wq, d_Wq = sb(gst, [128, 8, 2], F32)
        Dec, d_Dec = sb(gst, [128, 8], F32)
        S0, d_S0 = sb(gst, [128, 512], F32)
        a1st = gst.enter_context(contextlib.ExitStack())
        gs1, d_gs1 = sb(a1st, [128, D], F32); gs1c, d_gs1c = sb(a1st, [128, D], F32)
        colc, d_colc = sb(a1st, [128, 2, 8], F32)
        Wk, d_Wk = sb(a1st, [128, 8, 2], F32)
        Wkc, d_Wkc = sb(a1st, [128, 2, 8, 2], F32)

        with contextlib.ExitStack() as st:
            ccT, d_ccT = sb(st, [128, 8, 2], F32)
            for r_ in range(2):
                load("sync", "a_cc", ccT[:, :, r_:r_ + 1], dap(cc, r_ * D, [(1, 128), (128, 8), (1, 1)]), d_ccT, slow=True)
            scT, d_scT = sb(st, [128, 8, 2], F32)
            S.op("scalar", lambda e: e.activation(out=scT[:], in_=ccT[:], func=AF.Silu), reads=[d_ccT], writes=[d_scT])
            bada, d_bada = sb(st, [2, 6 * D], F32)
            load("sync", "a_bada", bada[:], dap(b_ada, 0, [(0, 2), (1, 6 * D)]), d_bada)
            modsb, d_modsb = sb(st, [2, 6 * D], F32)
            wab = [sb(st, [128, 8, 512], F32) for _ in range(2)]
            pM = [ps(st, [128, 512], F32) for _ in range(2)]
            for cb in range(12):
                wa, d_wa = wab[cb % 2]
                load("sync", "a_wa%d" % (cb % 2), wa[:],
                     w_ada[:, cb * 512:(cb + 1) * 512].rearrange("(k p) n -> p k n", p=128), d_wa)
                pm, d_pm = pM[cb % 2]
                for k in range(8):
                    S.op("tensor", lambda e, pm=pm, wa=wa, k=k: e.matmul(pm[0:2, :], lhsT=scT[:, k, :], rhs=wa[:, k, :],
                                                                        start=(k == 0), stop=(k == 7)),
                         reads=[d_scT, d_wa], writes=[d_pm])
                S.op("vector", lambda e, pm=pm, cb=cb: e.tensor_tensor(out=modsb[:, cb * 512:(cb + 1) * 512], in0=pm[0:2, :],
                                                                      in1=bada[:, cb * 512:(cb + 1) * 512], op=ALU.add),
                     reads=[d_pm, d_bada], writes=[d_modsb])
            d_modscr = Dep()
            store("sync", "a_modst", modscr, modsb[:], d_modsb, [d_modscr])
            for j_ in range(6):
                load("sync", "a_colx", colx[:, j_, :].rearrange("p (k o) -> p k o", o=1),
                     dap(modscr, j_ * D, [(1, 128), (128, 8), (1, 1)]), d_colx, [d_modscr], slow=True)
            for j_ in range(2):
                load("sync", "a_colc", colc[:, j_, :].rearrange("p (k o) -> p k o", o=1),
                     dap(modscr, 6 * D + j_ * D, [(1, 128), (128, 8), (1, 1)]), d_colc, [d_modscr], slow=True)
            tmpA, d_tmpA = sb(st, [128, D], F32)
            tmpB, d_tmpB = sb(st, [128, D], F32)

            def make_gs(dst, d_dst, scale_off, g_dram, tagn):
                load("sync", "a_tA", tmpA[:], row_bc(modscr, scale_off, D), d_tmpA, [d_modscr])
                load("sync", "a_tB", tmpB[:], row_bc(g_dram, 0, D), d_tmpB)
                S.op("vector", lambda e: e.scalar_tensor_tensor(out=dst[:], in0=tmpA[:], scalar=1.0, op0=ALU.add,
                                                                in1=tmpB[:], op1=ALU.mult),
                     reads=[d_tmpA, d_tmpB], writes=[d_dst])
            make_gs(gs1, d_gs1, 1 * D, norm1_g, 0)
            make_gs(gs1c, d_gs1c, 6 * D + 1 * D, norm1_g, 1)
            make_gs(gs2, d_gs2, 4 * D, norm2_g, 2)
            load("sync", "a_g2", gate2[:], row_bc(modscr, 2 * D, D), d_gate2, [d_modscr])
            load("sync", "a_g5", gate5[:], row_bc(modscr, 5 * D, D), d_gate5, [d_modscr])
            load("sync", "a_gF", gF[:], row_bc(norm_f_g, 0, D), d_gF)

            lraw, d_lraw = sb(st, [128, 16], F32)
            load("sync", "a_lg", lraw[:], row_bc(logit, 0, 16), d_lraw)
            S.op("scalar", lambda e: e.activation(out=lgt[:], in_=lraw[:], func=AF.Exp, scale=-1.0), reads=[d_lraw], writes=[d_lgt])
            S.op("scalar", lambda e: e.activation(out=lgt[:], in_=lgt[:], func=AF.Ln, bias=1.0), reads=[d_lgt], writes=[d_lgt])
            S.op("scalar", lambda e: e.mul(lgt[:], lgt[:], -1.0), reads=[d_lgt], writes=[d_lgt])
            S.op("vector", lambda e: e.tensor_copy(out=lgsel[0:64, :], in_=lgt[0:64, 0:8]), reads=[d_lgt], writes=[d_lgsel])
            S.op("vector", lambda e: e.tensor_copy(out=lgsel[64:128, :], in_=lgt[64:128, 8:16]), reads=[d_lgt], writes=[d_lgsel])
            S.op("scalar", lambda e: e.activation(out=Dec[:], in_=lgsel[:], func=AF.Exp, scale=128.0), reads=[d_lgsel], writes=[d_Dec])
            for (dst, d_dst, cf, cbk) in ((Wq, d_Wq, 0, 1), (Wk, d_Wk, 2, 3)):
                S.op("scalar", lambda e, dst=dst, cf=cf: e.activation(out=dst[:, :, 0], in_=lgt[:, 0:8], func=AF.Exp, scale=pcols[:, cf:cf + 1]),
                     reads=[d_lgt, d_pcols], writes=[d_dst])
                S.op("scalar", lambda e, dst=dst, cbk=cbk: e.activation(out=dst[:, :, 1], in_=lgt[:, 8:16], func=AF.Exp, scale=pcols[:, cbk:cbk + 1]),
                     reads=[d_lgt, d_pcols], writes=[d_dst])
            S.op("vector", lambda e: e.tensor_scalar(out=Wk[:], in0=Wk[:], scalar1=0.125, scalar2=None, op0=ALU.mult), reads=[d_Wk], writes=[d_Wk])
            for tI in range(2):
                S.op("scalar", lambda e, tI=tI: e.activation(out=Wkc[:, tI, :, 0], in_=lgt[:, 0:8], func=AF.Exp, scale=pcols[:, 4 + 2 * tI:5 + 2 * tI]),
                     reads=[d_lgt, d_pcols], writes=[d_Wkc])
                S.op("scalar", lambda e, tI=tI: e.activation(out=Wkc[:, tI, :, 1], in_=lgt[:, 8:16], func=AF.Exp, scale=pcols[:, 5 + 2 * tI:6 + 2 * tI]),
                     reads=[d_lgt, d_pcols], writes=[d_Wkc])
            pdt, d_pdt = sb(st, [128, 4, 128], F32)
            load("sync", "a_pd", pdt[:], pd_d, d_pdt)
            ef, d_ef = sb(st, [128, 128], F32); eb, d_eb = sb(st, [128, 128], F32)
            for h in range(8):
                S.op("scalar", lambda e, h=h: e.activation(out=ef[:], in_=pdt[:, 0, :], func=AF.Exp, scale=lgt[:, h:h + 1]),
                     reads=[d_pdt, d_lgt], writes=[d_ef])
                S.op("scalar", lambda e, h=h: e.activation(out=eb[:], in_=pdt[:, 2, :], func=AF.Exp, scale=lgt[:, 8 + h:9 + h]),
                     reads=[d_pdt, d_lgt], writes=[d_eb])
                S.op("vector", lambda e: e.scalar_tensor_tensor(out=ef[:], in0=ef[:], scalar=0.125, op0=ALU.mult, in1=pdt[:, 1, :], op1=ALU.mult), reads=[d_ef, d_pdt], writes=[d_ef])
                S.op("vector", lambda e: e.scalar_tensor_tensor(out=eb[:], in0=eb[:], scalar=0.125, op0=ALU.mult, in1=pdt[:, 3, :], op1=ALU.mult), reads=[d_eb, d_pdt], writes=[d_eb])
                S.op("vector", lambda e, h=h: e.tensor_tensor(out=DT[:, h, :], in0=ef[:], in1=eb[:], op=ALU.add), reads=[d_ef, d_eb], writes=[d_DT])
            S.barrier()
        if stop_after <= 0:
            S.emit()
            return nc

        d_vx = Dep(); d_x0 = Dep(); d_q = Dep(); d_k = Dep(); d_v = Dep(); d_g = Dep(); d_T = Dep()
        with contextlib.ExitStack() as st:
            Win, d_Win = sb(st, [128, 8, 3584], BF16)
            for k in range(8):
                S.op("gpsimd", lambda e, k=k: e.dma_start(out=Win[:, k, :], in_=w_in[k * 128:(k + 1) * 128, :]), writes=[d_Win], dma="p1_win%d" % k)
            ropeT, d_rope = sb(st, [128, 2, 64, 32], F32)
            load("sync", "p1_rope", ropeT[:], rope_d, d_rope)
            cw, d_cw = sb(st, [128, 12, 4], F32)
            for j in range(3):
                load("sync", "p1_cw", cw[:, :, j:j + 1], dap(conv_w, j * 1536, [(1, 128), (128, 12), (1, 1)]), d_cw, slow=True)
            load("sync", "p1_cw", cw[:, :, 3:4], dap(conv_b, 0, [(1, 128), (128, 12), (1, 1)]), d_cw, slow=True)

            xb = [sb(st, [128, D], F32) for _ in range(3)]
            junk, d_junk = sb(st, [128, D], BF16)
            ssq = [sb(st, [128, 1], F32) for _ in range(3)]
            xm = [sb(st, [128, D], BF16) for _ in range(2)]
            hxT = [sb(st, [128, 8, 512], BF16) for _ in range(2)]
            U = [sb(st, [128, 514], F32) for _ in range(12)]
            cv1 = [sb(st, [128, 512], F32) for _ in range(2)]
            cv2 = [sb(st, [128, 512], F32) for _ in range(2)]
            cvx1 = [sb(st, [128, 512], F32) for _ in range(4)]
            cv3, d_cv3 = sb(st, [128, 512], F32)
            hyo = [sb(st, [128, 512], BF16) for _ in range(4)]
            P1, d_P1 = sb(st, [128, 512], F32); P2, d_P2 = sb(st, [128, 512], F32)
            qo = [sb(st, [128, 512], BF16) for _ in range(2)]
            ko = [sb(st, [128, 512], BF16) for _ in range(2)]
            vo = [sb(st, [128, 512], BF16) for _ in range(2)]
            go = [sb(st, [128, 512], BF16) for _ in range(2)]
            kw, d_kw = sb(st, [128, 8, 2, 64], BF16)
            Tsb = [sb(st, [128, 512], F32) for _ in range(2)]
            pT = [ps(st, [128, 8, 128], BF16) for _ in range(2)]
            pU = [ps(st, [128, 512], F32) for _ in range(2)]
            pR = [ps(st, [128, 512], F32) for _ in range(2)]
            pTs = [ps(st, [128, 512], F32) for _ in range(2)]
            for ft in range(12):
                S.op("gpsimd", lambda e, ft=ft: e.memset(U[ft][0][:, 0:2], 0.0), writes=[U[ft][1]])

            srcs = [ctx[0:128, :], ctx[128:256, :]] + [x[i * 128:(i + 1) * 128, :] for i in range(NT)]

            def xload(s_):
                if s_ < len(srcs):
                    load("sync", "p1_x%d" % (s_ % 3), xb[s_ % 3][0][:], srcs[s_], xb[s_ % 3][1])

            xload(0)

            def norm_transpose(i, gsrow, d_gsrow, shcol_fn, d_shcol, hx_tile, d_hx, tok0):
                xload(i + 1)
                xt, d_xt = xb[i % 3]
                sq, d_sq = ssq[i % 3]
                S.op("scalar", lambda e: e.activation(out=junk[:], in_=xt[:], func=AF.Square, accum_out=sq[:]),
                     reads=[d_xt], writes=[d_junk, d_sq])
                S.op("scalar", lambda e: e.activation(out=sq[:], in_=sq[:], func=AF.Sqrt, scale=1.0 / D, bias=1e-6),
                     reads=[d_sq], writes=[d_sq])
                S.op("vector", lambda e: e.reciprocal(out=sq[:], in_=sq[:]), reads=[d_sq], writes=[d_sq])
                xmt, d_xmt = xm[i % 2]
                S.op("vector", lambda e: e.scalar_tensor_tensor(out=xmt[:], in0=xt[:], scalar=sq[:, 0:1], op0=ALU.mult,
                                                                in1=gsrow[:], op1=ALU.mult),
                     reads=[d_xt, d_sq, d_gsrow], writes=[d_xmt])
                pt, d_pt = pT[i % 2]
                for k in range(8):
                    S.op("tensor", lambda e, k=k: e.transpose(out=pt[:, k, :], in_=xmt[:, k * 128:(k + 1) * 128], identity=identb[:]),
                         reads=[d_xmt, d_identb], writes=[d_pt])
                for k in range(8):
                    S.op("scalar", lambda e, k=k: e.activation(out=hx_tile[:, k, tok0:tok0 + 128], in_=pt[:, k, :], func=AF.Identity,
                                                               bias=shcol_fn(k)),
                         reads=[d_pt, d_shcol], writes=[d_hx])

            def proj_tok(hx_tile, d_hx, tok0, col0, pr, d_pr):
                for k in range(8):
                    S.op("tensor", lambda e, k=k: e.matmul(pr[:], lhsT=hx_tile[:, k, tok0:tok0 + 128], rhs=Win[:, k, col0:col0 + 512],
                                                           start=(k == 0), stop=(k == 7)),
                         reads=[d_hx, d_Win], writes=[d_pr])

            hxc, d_hxc = hxT[0]
            kc = []; vc = []
            for tI in range(2):
                norm_transpose(tI, gs1c, d_gs1c, lambda k: colc[:, 0, k:k + 1], d_colc, hxc, d_hxc, tI * 128)
                pr, d_pr = pR[0]
                proj_tok(hxc, d_hxc, tI * 128, 1536 + 512, pr, d_pr)
                kt, d_kt = ko[tI]
                S.op("scalar", lambda e, kt=kt, pr=pr: e.mul(kt[:], pr[:], 0.125), reads=[d_pr], writes=[d_kt])
                pr2, d_pr2 = pR[1]
                proj_tok(hxc, d_hxc, tI * 128, 1536 + 1024, pr2, d_pr2)
                vt, d_vt = vo[tI]
                S.op("scalar", lambda e, vt=vt, pr2=pr2: e.copy(vt[:], pr2[:]), reads=[d_pr2], writes=[d_vt])
                kc.append((kt, d_kt)); vc.append((vt, d_vt))
            kwc = [sb(st, [128, 8, 2, 64], BF16) for _ in range(2)]
            for tI in range(2):
                kt, d_kt = kc[tI]
                kwt, d_kwt = kwc[tI]
                S.op("vector", lambda e, kt=kt, kwt=kwt, tI=tI: e.tensor_tensor(
                    out=kwt[:], in0=sap(kt, 0, 128, 0, [(64, 8), (0, 2), (1, 64)]),
                    in1=sap(Wkc, 0, 128, tI * 16, [(2, 8), (1, 2), (0, 64)]), op=ALU.mult),
                    reads=[d_kt, d_Wkc], writes=[d_kwt])
            pS0, d_pS0 = pTs[0]
            for h in range(8):
                for tI in range(2):
                    S.op("tensor", lambda e, h=h, tI=tI: e.matmul(pS0[:, h * 64:(h + 1) * 64], lhsT=kwc[tI][0][:, h, :, :],
                                                                 rhs=vc[tI][0][:, h * 64:(h + 1) * 64], start=(tI == 0), stop=(tI == 1)),
                         reads=[kwc[tI][1], vc[tI][1]], writes=[d_pS0])
            S.op("vector", lambda e: e.tensor_copy(out=S0[:], in_=pS0[:]), reads=[d_pS0], writes=[d_S0])

            for i in range(NT):
                B, ii = divmod(i, 4)
                hx_tile, d_hx = hxT[B % 2]
                norm_transpose(i + 2, gs1, d_gs1, lambda k: colx[:, 0, k:k + 1], d_colx, hx_tile, d_hx, ii * 128)
                for cbk in range(4):
                    pr, d_pr = pR[cbk % 2]
                    proj_tok(hx_tile, d_hx, ii * 128, 1536 + cbk * 512, pr, d_pr)
                    if cbk < 2:
                        ot, d_ot = (qo if cbk == 0 else ko)[i % 2]
                        S.op("vector", lambda e, pr=pr, i=i: e.tensor_tensor(
                            out=P1[:].rearrange("p (h j t) -> p h j t", h=8, t=2), in0=pr[:].rearrange("p (h j t) -> p h j t", h=8, t=2),
                            in1=sap(ropeT, 0, 128, (0 * 64 + i) * 32, [(0, 8), (1, 32), (0, 2)]), op=ALU.mult),
                            reads=[d_pr, d_rope], writes=[d_P1])
                        S.op("vector", lambda e, pr=pr, i=i: e.tensor_tensor(
                            out=P2[:].rearrange("p (h j t) -> p h j t", h=8, t=2), in0=pr[:].rearrange("p (h j t) -> p h j t", h=8, t=2),
                            in1=sap(ropeT, 0, 128, (1 * 64 + i) * 32, [(0, 8), (1, 32), (0, 2)]), op=ALU.mult),
                            reads=[d_pr, d_rope], writes=[d_P2])
                        S.op("gpsimd", lambda e, ot=ot: e.tensor_tensor(out=sap(ot, 0, 128, 0, [(2, 256)]), in0=sap(P1, 0, 128, 0, [(2, 256)]),
                                                                       in1=sap(P2, 0, 128, 1, [(2, 256)]), op=ALU.subtract),
                             reads=[d_P1, d_P2], writes=[d_ot])
                        S.op("gpsimd", lambda e, ot=ot: e.tensor_tensor(out=sap(ot, 0, 128, 1, [(2, 256)]), in0=sap(P2, 0, 128, 0, [(2, 256)]),
                                                                       in1=sap(P1, 0, 128, 1, [(2, 256)]), op=ALU.add),
                             reads=[d_P1, d_P2], writes=[d_ot])
                        scr = qscr if cbk == 0 else kscr
                        store("sync", ("p1_q%d" if cbk == 0 else "p1_k%d") % (i % 2), scr[i * 128:(i + 1) * 128, :], ot[:], d_ot)
                        if cbk == 1:
                            S.op("vector", lambda e, ot=ot: e.tensor_tensor(
                                out=kw[:], in0=sap(ot, 0, 128, 0, [(64, 8), (0, 2), (1, 64)]),
                                in1=sap(Wk, 0, 128, 0, [(2, 8), (1, 2), (0, 64)]), op=ALU.mult),
                                reads=[d_ot, d_Wk], writes=[d_kw])
                    elif cbk == 2:
                        vt, d_vt = vo[i % 2]
                        S.op("scalar", lambda e, vt=vt, pr=pr: e.copy(vt[:], pr[:]), reads=[d_pr], writes=[d_vt])
                        store("sync", "p1_v%d" % (i % 2), vscr[i * 128:(i + 1) * 128, :], vt[:], d_vt)
                    else:
                        gt, d_gt = go[i % 2]
                        S.op("scalar", lambda e, gt=gt, pr=pr: e.activation(out=gt[:], in_=pr[:], func=AF.Silu), reads=[d_pr], writes=[d_gt])
                        store("sync", "p1_g%d" % (i % 2), gscr[i * 128:(i + 1) * 128, :], gt[:], d_gt)
                pts, d_pts = pTs[i % 2]
                vt, d_vt = vo[i % 2]
                for h in range(8):
                    S.op("tensor", lambda e, h=h, pts=pts, vt=vt: e.matmul(pts[:, h * 64:(h + 1) * 64], lhsT=kw[:, h, :, :],
                                                                          rhs=vt[:, h * 64:(h + 1) * 64], start=True, stop=True),
                         reads=[d_kw, d_vt], writes=[d_pts])
                tsb, d_tsb = Tsb[i % 2]
                S.op("scalar", lambda e, tsb=tsb, pts=pts: e.copy(tsb[:], pts[:]), reads=[d_pts], writes=[d_tsb])
                store("sync", "p1_T%d" % (i % 2), Tscr[i], tsb[:], d_tsb)
                if ii == 3:
                    s0 = 1 if B == 0 else 0
                    tok_lo = 512 * B - 1 + s0
                    for ft in (4, 5, 6, 7, 8, 9, 10, 11, 0, 1, 2, 3):
                        pu, d_pu = pU[ft % 2]
                        for k in range(8):
                            S.op("tensor", lambda e, k=k, ft=ft, pu=pu, hx_tile=hx_tile: e.matmul(pu[:], lhsT=Win[:, k, ft * 128:(ft + 1) * 128], rhs=hx_tile[:, k, :],
                                                                                start=(k == 0), stop=(k == 7)),
                                 reads=[d_Win, d_hx], writes=[d_pu])
                        u, d_u = U[ft]
                        S.op("scalar", lambda e, u=u, pu=pu: e.copy(u[:, 2:514], pu[:]), reads=[d_pu], writes=[d_u])
                        c1, d_c1 = cv1[ft % 2]; c2, d_c2 = cv2[ft % 2]
                        S.op("gpsimd", lambda e, u=u, c1=c1, ft=ft: e.tensor_scalar(out=c1[:], in0=u[:, 0:512], scalar1=cw[:, ft, 0:1], scalar2=cw[:, ft, 3:4],
                                                                                   op0=ALU.mult, op1=ALU.add),
                             reads=[d_u, d_cw], writes=[d_c1])
                        S.op("vector", lambda e, u=u, c1=c1, c2=c2, ft=ft: e.scalar_tensor_tensor(out=c2[:], in0=u[:, 1:513], scalar=cw[:, ft, 1:2], op0=ALU.mult,
                                                                                                 in1=c1[:], op1=ALU.add),
                             reads=[d_u, d_cw, d_c1], writes=[d_c2])
                        ct = ft % 4
                        if ft < 4:
                            ho, d_ho = hyo[ct]
                            S.op("vector", lambda e, u=u, c2=c2, ho=ho, ft=ft: e.scalar_tensor_tensor(out=ho[:], in0=u[:, 2:514], scalar=cw[:, ft, 2:3], op0=ALU.mult,
                                                                                                     in1=c2[:], op1=ALU.add),
                                 reads=[d_u, d_cw, d_c2], writes=[d_ho])
                            store("sync", "p1_hy%d" % ct, x0scr[ct * 128:(ct + 1) * 128, tok_lo:512 * B + 511], ho[:, s0:512], d_ho)
                        elif ft < 8:
                            cx, d_cx = cvx1[ct]
                            S.op("vector", lambda e, u=u, c2=c2, cx=cx, ft=ft: e.scalar_tensor_tensor(out=cx[:], in0=u[:, 2:514], scalar=cw[:, ft, 2:3], op0=ALU.mult,
                                                                                                     in1=c2[:], op1=ALU.add),
                                 reads=[d_u, d_cw, d_c2], writes=[d_cx])
                        else:
                            cx, d_cx = cvx1[ct]
                            S.op("vector", lambda e, u=u, c2=c2, ft=ft: e.scalar_tensor_tensor(out=cv3[:], in0=u[:, 2:514], scalar=cw[:, ft, 2:3], op0=ALU.mult,
                                                                                              in1=c2[:], op1=ALU.add),
                                 reads=[d_u, d_cw, d_c2], writes=[d_cv3])
                            ho, d_ho = hyo[ct]
                            S.op("gpsimd", lambda e, cx=cx, ho=ho: e.tensor_tensor(out=ho[:], in0=cv3[:], in1=cx[:], op=ALU.mult),
                                 reads=[d_cv3, d_cx], writes=[d_ho])
                            store("sync", "p1_hy%d" % ct, vxscr[ct * 128:(ct + 1) * 128, tok_lo:512 * B + 511], ho[:, s0:512], d_ho)
                        S.op("gpsimd", lambda e, u=u: e.tensor_copy(out=u[:, 0:2], in_=u[:, 512:514]), reads=[d_u], writes=[d_u])
            tl, d_tl = sb(st, [128, 12], F32)
            tlb, d_tlb = sb(st, [128, 8], BF16)
            for ft in range(12):
                u, d_u = U[ft]
                S.op("vector", lambda e, u=u, ft=ft: e.tensor_scalar(out=tl[:, ft:ft + 1], in0=u[:, 0:1], scalar1=cw[:, ft, 0:1], scalar2=cw[:, ft, 3:4],
                                                                    op0=ALU.mult, op1=ALU.add), reads=[d_u, d_cw], writes=[d_tl])
                S.op("vector", lambda e, u=u, ft=ft: e.scalar_tensor_tensor(out=tl[:, ft:ft + 1], in0=u[:, 1:2], scalar=cw[:, ft, 1:2], op0=ALU.mult,
                                                                           in1=tl[:, ft:ft + 1], op1=ALU.add), reads=[d_u, d_cw, d_tl], writes=[d_tl])
            S.op("vector", lambda e: e.tensor_copy(out=tlb[:, 0:4], in_=tl[:, 0:4]), reads=[d_tl], writes=[d_tlb])
            S.op("vector", lambda e: e.tensor_tensor(out=tlb[:, 4:8], in0=tl[:, 4:8], in1=tl[:, 8:12], op=ALU.mult), reads=[d_tl], writes=[d_tlb])
            for ct in range(4):
                store("sync", "p1_tl%d" % ct, dap(x0scr, ct * 128 * L + L - 1, [(L, 128), (1, 1)]), tlb[:, ct:ct + 1], d_tlb, slow=True)
                store("sync", "p1_tv%d" % ct, dap(vxscr, ct * 128 * L + L - 1, [(L, 128), (1, 1)]), tlb[:, 4 + ct:5 + ct], d_tlb, slow=True)
            S.barrier()
        a1st.close()
        if stop_after <= 1:
            S.emit()
            return nc
        with contextlib.ExitStack() as st:
            fc, d_fc = sb(st, [128, 1408], BF16)
            load("sync", "f_fc", fc[:], fconst_d, d_fc)
            tcs, d_tcs = sb(st, [128, 768], F32)
            load("sync", "f_tc", tcs[:], tconst_d, d_tcs)
            M1o, M1co, C2o, S2o, nS2o, C2S2o, nS2C2o, BDCo, BDnSo = 0, 128, 256, 384, 512, 640, 896, 1152, 1280
            w1t, d_w1t = sb(st, [33, 64], F32); load("sync", "f_w1", w1t[:], f_w1, d_w1t)
            w2t, d_w2t = sb(st, [64, 64], F32); load("sync", "f_w2", w2t[:], f_w2, d_w2t)
            w3b, d_w3b = sb(st, [64, 1024], BF16)
            S.op("gpsimd", lambda e: e.dma_start(out=w3b[:], in_=f_w3), writes=[d_w3b], dma="f_w3")
            fcol, d_fcol = sb(st, [64, 4], F32)
            for j_, src in enumerate((f_fr1, f_b1, f_fr2, f_b2)):
                load("sync", "f_col", fcol[:, j_:j_ + 1], dap(src, 0, [(1, 64), (1, 1)]), d_fcol, slow=True)
            fab, d_fab = sb(st, [64, 4], F32)
            for l_ in range(2):
                S.op("vector", lambda e, l_=l_: e.tensor_scalar(out=fab[:, 2 * l_:2 * l_ + 1], in0=fcol[:, 2 * l_:2 * l_ + 1], scalar1=1.0 / 3.0, scalar2=None, op0=ALU.mult),
                     reads=[d_fcol], writes=[d_fab])
                S.op("vector", lambda e, l_=l_: e.tensor_tensor(out=fab[:, 2 * l_ + 1:2 * l_ + 2], in0=fab[:, 2 * l_:2 * l_ + 1], in1=fcol[:, 2 * l_ + 1:2 * l_ + 2], op=ALU.mult),
                     reads=[d_fcol, d_fab], writes=[d_fab])
            onesf, d_onesf = sb(st, [64, 128], F32)
            S.op("gpsimd", lambda e: e.memset(onesf[:], 1.0), writes=[d_onesf])
            h2T, d_h2T = sb(st, [64, L], BF16)
            banks = [ps(st, [128, 512], F32) for _ in range(8)]
            bctr = [0]

            def nb():
                b_ = banks[bctr[0] % 8]
                bctr[0] += 1
                return b_

            zt = [sb(st, [33, 512], F32) for _ in range(2)]
            sA, d_sA = sb(st, [64, 512], F32); sB, d_sB = sb(st, [64, 512], F32); h1, d_h1 = sb(st, [64, 512], F32)

            def sin3(pf, d_pf, layer, out_ap, d_out):
                S.op("scalar", lambda e: e.activation(out=sA[:], in_=pf[0:64, :], func=AF.Sin, scale=fab[:, 2 * layer:2 * layer + 1],
                                                      bias=fab[:, 2 * layer + 1:2 * layer + 2]), reads=[d_pf, d_fab], writes=[d_sA])
                S.op("vector", lambda e: e.tensor_tensor(out=sB[:], in0=sA[:], in1=sA[:], op=ALU.mult), reads=[d_sA], writes=[d_sB])
                S.op("vector", lambda e: e.tensor_scalar(out=sB[:], in0=sB[:], scalar1=-4.0, scalar2=3.0, op0=ALU.mult, op1=ALU.add), reads=[d_sB], writes=[d_sB])
                S.op("vector", lambda e: e.tensor_tensor(out=out_ap, in0=sB[:], in1=sA[:], op=ALU.mult), reads=[d_sA, d_sB], writes=[d_out])

            def mlp_block(blk):
                z, d_z = zt[blk % 2]
                load("sync", "f_z%d" % (blk % 2), z[:], zT_d[:, blk * 512:(blk + 1) * 512], d_z)
                pf, d_pf = nb()
                S.op("tensor", lambda e: e.matmul(pf[0:64, :], lhsT=w1t[:], rhs=z[:], start=True, stop=True), reads=[d_w1t, d_z], writes=[d_pf])
                sin3(pf, d_pf, 0, h1[:], d_h1)
                pf2, d_pf2 = nb()
                S.op("tensor", lambda e: e.matmul(pf2[0:64, :], lhsT=w2t[:], rhs=h1[:], start=True, stop=True), reads=[d_w2t, d_h1], writes=[d_pf2])
                sin3(pf2, d_pf2, 1, h2T[:, blk * 512:(blk + 1) * 512], d_h2T)

            for blk in range(16):
                mlp_block(blk)

            Hbuf, d_Hbuf = sb(st, [128, 2 * 64 * 128], BF16)
            Acc4, d_Acc4 = sb(st, [64, 512], F32)
            habs, d_habs = sb(st, [64, 512], F32)
            Et = [sb(st, [64, 256], F32) for _ in range(2)]
            hd32 = [sb(st, [64, 512], F32) for _ in range(2)]
            nsb, d_nsb = sb(st, [128, 128], F32)
            rn, d_rn = sb(st, [128, 64], F32)
            BfR, d_BfR = sb(st, [128, 64, 64], BF16); BfI, d_BfI = sb(st, [128, 64, 64], BF16)
            BbR, d_BbR = sb(st, [128, 64, 64], BF16); BbI, d_BbI = sb(st, [128, 64, 64], BF16)
            KR, d_KR = sb(st, [128, 64, 64], BF16); KI, d_KI = sb(st, [128, 64, 64], BF16)
            Xg, d_Xg = sb(st, [64, 64, 128], BF16)
            Qb = [(sb(st, [128, 512], F32), sb(st, [128, 512], F32)) for _ in range(2)]
            qctr = [0]
            t1, d_t1 = sb(st, [128, 512], F32); t2, d_t2 = sb(st, [128, 512], F32)
            t3, d_t3 = sb(st, [128, 512], F32); t4, d_t4 = sb(st, [128, 512], F32)
            Yo, d_Yo = sb(st, [128, 32, 128], BF16)


            Sst, d_Sst = sb(st, [128, 512], F32)
            S.op("vector", lambda e: e.tensor_copy(out=Sst[:], in_=S0[:]), reads=[d_S0], writes=[d_Sst])
            Tt = [sb(st, [128, 512], F32) for _ in range(2)]
            Sb = [sb(st, [128, 512], BF16) for _ in range(2)]
            sctr = [0]

            def tload(s_):
                if s_ < NT:
                    tt_, d_tt = Tt[s_ % 2]
                    load("sync", "s_tf%d" % (s_ % 2), tt_[0:64, :], Tscr[s_, 0:64, :], d_tt)
                    load("sync", "s_tb%d" % (s_ % 2), tt_[64:128, :], Tscr[NT - 1 - s_, 64:128, :], d_tt)

            def scan_step(s_):
                tt_, d_tt = Tt[s_ % 2]
                sb_, d_sb = Sb[s_ % 2]
                S.op("scalar", lambda e: e.copy(sb_[:], Sst[:]), reads=[d_Sst], writes=[d_sb])
                store("sync", "s_sf%d" % (s_ % 2), Sscr[s_, 0:64, :], sb_[0:64, :], d_sb)
                store("sync", "s_sb%d" % (s_ % 2), Sscr[NT - 1 - s_, 64:128, :], sb_[64:128, :], d_sb)
                S.op("vector", lambda e: e.tensor_tensor(out=Sst[:].rearrange("p (h x) -> p h x", h=8), in0=Sst[:].rearrange("p (h x) -> p h x", h=8),
                                                         in1=sap(Dec, 0, 128, 0, [(1, 8), (0, 64)]), op=ALU.mult), reads=[d_Sst, d_Dec], writes=[d_Sst])
                S.op("vector", lambda e: e.tensor_tensor(out=Sst[:], in0=Sst[:], in1=tt_[:], op=ALU.add), reads=[d_Sst, d_tt], writes=[d_Sst])
                tload(s_ + 2)

            def scan_some(n_):
                for _ in range(n_):
                    if sctr[0] < NT:
                        scan_step(sctr[0])
                        sctr[0] += 1

            tload(0); tload(1)

            def twiddle(pa, d_pa, conj, outR, d_outR, outI, d_outI, c0):
                pav = pa[:].rearrange("p (c x) -> p c x", c=4)
                (Q1, d_Q1), (Q2, d_Q2) = Qb[qctr[0] % 2]
                qctr[0] += 1
                S.op("vector", lambda e: e.tensor_tensor(out=Q1[:].rearrange("p (c x) -> p c x", c=4), in0=pav,
                                                         in1=sap(tcs, 0, 128, 0, [(0, 4), (1, 128)]), op=ALU.mult), reads=[d_pa, d_tcs], writes=[d_Q1])
                S.op("vector", lambda e: e.tensor_tensor(out=Q2[:].rearrange("p (c x) -> p c x", c=4), in0=pav,
                                                         in1=sap(tcs, 0, 128, 128, [(0, 4), (1, 128)]), op=ALU.mult), reads=[d_pa, d_tcs], writes=[d_Q2])
                q1lo = sap(Q1, 0, 128, 0, [(128, 4), (1, 64)]); q1hi = sap(Q1, 0, 128, 64, [(128, 4), (1, 64)])
                q2lo = sap(Q2, 0, 128, 0, [(128, 4), (1, 64)]); q2hi = sap(Q2, 0, 128, 64, [(128, 4), (1, 64)])
                S.op("gpsimd", lambda e: e.tensor_tensor(out=outR[:, c0:c0 + 4, :], in0=q1lo, in1=q2hi, op=(ALU.subtract if conj else ALU.add)),
                     reads=[d_Q1, d_Q2], writes=[d_outR])
                S.op("gpsimd", lambda e: e.tensor_tensor(out=outI[:, c0:c0 + 4, :], in0=q1hi, in1=q2lo, op=(ALU.add if conj else ALU.subtract)),
                     reads=[d_Q1, d_Q2], writes=[d_outI])

            def s1_stage(src_fn, src_deps, m1off, conj, outR, d_outR, outI, d_outI):
                for c4 in range(16):
                    pa, d_pa = nb()
                    for cc_ in range(4):
                        S.op("tensor", lambda e, cc_=cc_, c4=c4, pa=pa: e.matmul(pa[:, cc_ * 128:(cc_ + 1) * 128], lhsT=src_fn(c4 * 4 + cc_),
                                                                                rhs=fc[0:64, m1off:m1off + 128], start=True, stop=True),
                             reads=list(src_deps) + [d_fc], writes=[d_pa])
                    twiddle(pa, d_pa, conj, outR, d_outR, outI, d_outI, c4 * 4)

            def s2_mm(pk, d_pk, terms, c8):
                n_ = len(terms)
                for ti, (foff, buf, d_buf) in enumerate(terms):
                    S.op("tensor", lambda e, ti=ti, foff=foff, buf=buf: e.matmul(pk[:], lhsT=fc[:, foff:foff + 128], rhs=buf[:, c8 * 8:(c8 + 1) * 8, :],
                                                                                start=(ti == 0), stop=(ti == n_ - 1)),
                         reads=[d_fc, d_buf], writes=[d_pk])

            def group(g):
                S.op("gpsimd", lambda e: e.memset(Acc4[:], 0.0), writes=[d_Acc4])
                for jq in range(32):
                    et, d_et = Et[jq % 2]
                    load("sync", "f_e%d" % (jq % 2), et[:], edec_d[g, jq], d_et)
                    ph, d_ph = nb()
                    for jj in range(4):
                        j = 4 * jq + jj
                        S.op("tensor", lambda e, jj=jj, j=j, ph=ph: e.matmul(ph[0:64, jj * 128:(jj + 1) * 128], lhsT=h2T[:, j * 64:(j + 1) * 64],
                                                                            rhs=sap(w3b, 0, 64, g * 64, [(512, 2), (1, 64)]), start=True, stop=True),
                             reads=[d_h2T, d_w3b], writes=[d_ph])
                    hd, d_hd = hd32[jq % 2]
                    S.op("vector", lambda e, ph=ph, et=et, hd=hd: e.tensor_tensor(
                        out=hd[:].rearrange("p (j d c) -> p j d c", j=4, d=2), in0=ph[0:64, :].rearrange("p (j d c) -> p j d c", j=4, d=2),
                        in1=sap(et, 0, 64, 0, [(64, 4), (0, 2), (1, 64)]), op=ALU.mult), reads=[d_ph, d_et], writes=[d_hd])
                    S.op("scalar", lambda e, hd=hd: e.activation(out=habs[:], in_=hd[:], func=AF.Abs), reads=[d_hd], writes=[d_habs])
                    S.op("vector", lambda e: e.tensor_tensor(out=Acc4[:], in0=Acc4[:], in1=habs[:], op=ALU.add), reads=[d_habs, d_Acc4], writes=[d_Acc4])
                    S.op("scalar", lambda e, hd=hd, jq=jq: e.copy(sap(Hbuf, 0, 64, 4 * jq, [(1, 4), (64 * 128, 2), (128, 64)]),
                                                                 hd[:].rearrange("p (j d c) -> p j d c", j=4, d=2)),
                         reads=[d_hd], writes=[d_Hbuf])
                    if jq % 4 == 3:
                        scan_some(1)
                S.op("gpsimd", lambda e: e.memset(sap(Hbuf, 0, 1, 64 * 128, [(128, 64)]), 0.0), writes=[d_Hbuf])
                pn, d_pn = nb()
                for jj in range(4):
                    S.op("tensor", lambda e, jj=jj: e.matmul(pn[:, 0:128], lhsT=onesf[:], rhs=Acc4[:, jj * 128:(jj + 1) * 128], start=(jj == 0), stop=(jj == 3)),
                         reads=[d_onesf, d_Acc4], writes=[d_pn])
                S.op("scalar", lambda e: e.copy(nsb[:], pn[:, 0:128]), reads=[d_pn], writes=[d_nsb])
                S.op("vector", lambda e: e.scalar_tensor_tensor(out=rn[:], in0=nsb[:, 0:64], scalar=1e-6, op0=ALU.add, in1=nsb[:, 64:128], op1=ALU.add),
                     reads=[d_nsb], writes=[d_rn])
                S.op("vector", lambda e: e.reciprocal(out=rn[:], in_=rn[:]), reads=[d_rn], writes=[d_rn])
                s1_stage(lambda c: sap(Hbuf, 0, 64, c * 128, [(1, 128)]), [d_Hbuf], M1o, False, BfR, d_BfR, BfI, d_BfI)
                s1_stage(lambda c: sap(Hbuf, 0, 64, 64 * 128 + c * 128, [(1, 128)]), [d_Hbuf], M1co, True, BbR, d_BbR, BbI, d_BbI)
                for c8 in range(8):
                    pk, d_pk = nb()
                    s2_mm(pk, d_pk, [(C2o, BfR, d_BfR), (S2o, BfI, d_BfI), (C2o, BbR, d_BbR), (nS2o, BbI, d_BbI)], c8)
                    S.op("vector", lambda e, pk=pk, c8=c8: e.tensor_tensor(out=KR[:, c8 * 8:(c8 + 1) * 8, :], in0=pk[:].rearrange("p (c k) -> p c k", c=8),
                                                                          in1=sap(rn, 0, 128, c8 * 8, [(1, 8), (0, 64)]), op=ALU.mult),
                         reads=[d_pk, d_rn], writes=[d_KR])
                    pk2, d_pk2 = nb()
                    s2_mm(pk2, d_pk2, [(C2o, BfI, d_BfI), (nS2o, BfR, d_BfR), (C2o, BbI, d_BbI), (S2o, BbR, d_BbR)], c8)
                    S.op("vector", lambda e, pk2=pk2, c8=c8: e.tensor_tensor(out=KI[:, c8 * 8:(c8 + 1) * 8, :], in0=pk2[:].rearrange("p (c k) -> p c k", c=8),
                                                                            in1=sap(rn, 0, 128, c8 * 8, [(1, 8), (0, 64)]), op=ALU.mult),
                         reads=[d_pk2, d_rn], writes=[d_KI])
                if debug:
                    store("sync", "f_dbgk", dap(kspec, g * 64 * 64, [(512 * 64, 128), (1, 64 * 64)]), KR[:].rearrange("p c k -> p (c k)"), d_KR)
                    store("sync", "f_dbgk2", dap(kspec, 128 * 512 * 64 + g * 64 * 64, [(512 * 64, 128), (1, 64 * 64)]), KI[:].rearrange("p c k -> p (c k)"), d_KI)
                load("sync", "f_xg", Xg[:], dap(vxscr, g * 64 * L, [(128, 64), (L, 64), (1, 128)]), d_Xg)
                s1_stage(lambda c: Xg[:, c, :], [d_Xg], M1o, False, BfR, d_BfR, BfI, d_BfI)
                for c8 in range(8):
                    px, d_px = nb()
                    s2_mm(px, d_px, [(C2o, BfR, d_BfR), (S2o, BfI, d_BfI)], c8)
                    pxi, d_pxi = nb()
                    s2_mm(pxi, d_pxi, [(C2o, BfI, d_BfI), (nS2o, BfR, d_BfR)], c8)
                    kr = KR[:, c8 * 8:(c8 + 1) * 8, :].rearrange("p c k -> p (c k)")
                    ki = KI[:, c8 * 8:(c8 + 1) * 8, :].rearrange("p c k -> p (c k)")
                    S.op("vector", lambda e, px=px, kr=kr: e.tensor_tensor(out=t1[:], in0=px[:], in1=kr, op=ALU.mult), reads=[d_px, d_KR], writes=[d_t1])
                    S.op("vector", lambda e, pxi=pxi, ki=ki: e.tensor_tensor(out=t2[:], in0=pxi[:], in1=ki, op=ALU.mult), reads=[d_pxi, d_KI], writes=[d_t2])
                    S.op("vector", lambda e, px=px, ki=ki: e.tensor_tensor(out=t3[:], in0=px[:], in1=ki, op=ALU.mult), reads=[d_px, d_KI], writes=[d_t3])
                    S.op("vector", lambda e, pxi=pxi, kr=kr: e.tensor_tensor(out=t4[:], in0=pxi[:], in1=kr, op=ALU.mult), reads=[d_pxi, d_KR], writes=[d_t4])
                    S.op("gpsimd", lambda e, c8=c8: e.tensor_tensor(out=BbR[:, c8 * 8:(c8 + 1) * 8, :].rearrange("p c k -> p (c k)"), in0=t1[:], in1=t2[:], op=ALU.subtract),
                         reads=[d_t1, d_t2], writes=[d_BbR])
                    S.op("gpsimd", lambda e, c8=c8: e.tensor_tensor(out=BbI[:, c8 * 8:(c8 + 1) * 8, :].rearrange("p c k -> p (c k)"), in0=t3[:], in1=t4[:], op=ALU.add),
                         reads=[d_t3, d_t4], writes=[d_BbI])
                for pb in range(16):
                    pc, d_pc = nb()
                    for q_ in range(2):
                        p_ = pb * 2 + q_
                        S.op("tensor", lambda e, q_=q_, p_=p_, pc=pc: e.matmul(pc[:, q_ * 256:(q_ + 1) * 256], lhsT=BbR[:, 2 * p_:2 * p_ + 2, :],
                                                                              rhs=fc[:, C2S2o:C2S2o + 256], start=True, stop=False),
                             reads=[d_BbR, d_fc], writes=[d_pc])
                        S.op("tensor", lambda e, q_=q_, p_=p_, pc=pc: e.matmul(pc[:, q_ * 256:(q_ + 1) * 256], lhsT=BbI[:, 2 * p_:2 * p_ + 2, :],
                                                                              rhs=fc[:, nS2C2o:nS2C2o + 256], start=False, stop=True),
                             reads=[d_BbI, d_fc], writes=[d_pc])
                    pcv = pc[:].rearrange("p (q x) -> p q x", q=2)
                    (Q1, d_Q1), (Q2, d_Q2) = Qb[qctr[0] % 2]
                    qctr[0] += 1
                    S.op("vector", lambda e, pcv=pcv, Q1=Q1: e.tensor_tensor(out=Q1[:].rearrange("p (q x) -> p q x", q=2), in0=pcv,
                                                                     in1=sap(tcs, 0, 128, 256, [(0, 2), (1, 256)]), op=ALU.mult), reads=[d_pc, d_tcs], writes=[d_Q1])
                    S.op("vector", lambda e, pcv=pcv, Q2=Q2: e.tensor_tensor(out=Q2[:].rearrange("p (q x) -> p q x", q=2), in0=pcv,
                                                                     in1=sap(tcs, 0, 128, 512, [(0, 2), (1, 256)]), op=ALU.mult), reads=[d_pc, d_tcs], writes=[d_Q2])
                    S.op("gpsimd", lambda e, pb=pb, Q1=Q1, Q2=Q2: e.tensor_tensor(out=sap(Hbuf, 0, 128, pb * 256, [(128, 2), (1, 128)]),
                                                                   in0=sap(Q1, 0, 128, 0, [(256, 2), (1, 128)]), in1=sap(Q2, 0, 128, 128, [(256, 2), (1, 128)]), op=ALU.subtract),
                         reads=[d_Q1, d_Q2], writes=[d_Hbuf])
                    S.op("gpsimd", lambda e, pb=pb, Q1=Q1, Q2=Q2: e.tensor_tensor(out=sap(Hbuf, 0, 128, 4096 + pb * 256, [(128, 2), (1, 128)]),
                                                                   in0=sap(Q1, 0, 128, 128, [(256, 2), (1, 128)]), in1=sap(Q2, 0, 128, 0, [(256, 2), (1, 128)]), op=ALU.add),
                         reads=[d_Q1, d_Q2], writes=[d_Hbuf])
                for p4 in range(8):
                    py, d_py = nb()
                    for q_ in range(4):
                        p_ = p4 * 4 + q_
                        S.op("tensor", lambda e, q_=q_, p_=p_, py=py: e.matmul(py[:, q_ * 128:(q_ + 1) * 128], lhsT=fc[:, BDCo:BDCo + 128],
                                                                              rhs=sap(Hbuf, 0, 128, p_ * 128, [(1, 128)]), start=True, stop=False),
                             reads=[d_Hbuf, d_fc], writes=[d_py])
                        S.op("tensor", lambda e, q_=q_, p_=p_, py=py: e.matmul(py[:, q_ * 128:(q_ + 1) * 128], lhsT=fc[:, BDnSo:BDnSo + 128],
                                                                              rhs=sap(Hbuf, 0, 128, 4096 + p_ * 128, [(1, 128)]), start=False, stop=True),
                             reads=[d_Hbuf, d_fc], writes=[d_py])
                    S.op("scalar", lambda e, p4=p4, py=py: e.copy(Yo[:, p4 * 4:(p4 + 1) * 4, :].rearrange("p q x -> p (q x)"), py[:]), reads=[d_py], writes=[d_Yo])
                store("sync", "f_yo", dap(yscr, g * 64 * L, [(128, 128), (2 * L, 32), (1, 128)]), Yo[:], d_Yo)

            for g in range(8):
                group(g)
            scan_some(NT)
            S.barrier()
        if stop_after <= 2:
            S.emit()
            return nc
        if stop_after <= 3:
            S.emit()
            return nc

        with contextlib.ExitStack() as st:
            Wo, d_Wo = sb(st, [128, 8, D], BF16)
            W1, d_W1 = sb(st, [128, 8, 4 * D], BF16)
            W2, d_W2 = sb(st, [128, 32, D], BF16)
            for k in range(8):
                S.op("gpsimd", lambda e, k=k: e.dma_start(out=Wo[:, k, :], in_=w_out[k * 128:(k + 1) * 128, :]), writes=[d_Wo], dma="w_o%d" % (k % 4))
            for k in range(8):
                S.op("gpsimd", lambda e, k=k: e.dma_start(out=W1[:, k, :], in_=w_mlp1[k * 128:(k + 1) * 128, :]), writes=[d_W1], dma="w_1%d" % (k % 4))
            for k4 in range(8):
                S.op("gpsimd", lambda e, k4=k4: e.dma_start(out=W2[:, k4 * 4:(k4 + 1) * 4, :], in_=w_mlp2[k4 * 512:(k4 + 1) * 512, :].rearrange("(k p) n -> p k n", p=128)),
                     writes=[d_W2], dma="w_2")
            hbc, d_hbc = sb(st, [128, 4], F32)
            load("sync", "r_hb", hbc[:].rearrange("p (c o) -> p c o", o=1), dap(hy_bias, 0, [(1, 128), (128, 4), (1, 1)]), d_hbc, slow=True)
            gnr, d_gnr = sb(st, [128, 512], F32)
            load("sync", "r_gn", gnr[:], row_bc(gn_g, 0, 512), d_gnr)
            banks = [ps(st, [128, 512], F32) for _ in range(4)]
            pmb = [ps(st, [128, 512], F32) for _ in range(2)]
            bbanks = [ps(st, [128, 1024], BF16) for _ in range(2)]
            bctr = [0, 0]

            def nb():
                b_ = banks[bctr[0] % 4]
                bctr[0] += 1
                return b_

            def nbb():
                b_ = bbanks[bctr[1] % 2]
                bctr[1] += 1
                return b_

            qkvg = [[sb(st, [128, 512], BF16) for _ in range(5)] for _ in range(1)]
            scrs = [qscr, kscr, vscr, gscr]
            hy3 = [sb(st, [128, 4, 128], BF16) for _ in range(3)]
            qx, d_qx = sb(st, [128, 8, 2, 64], BF16)
            qT, d_qT = sb(st, [128, 4, 128], BF16); kT, d_kT = sb(st, [128, 4, 128], BF16)
            qxT, d_qxT = sb(st, [128, 8, 128], BF16)
            d_Pm = d_qx
            osb, d_osb = sb(st, [128, 512], F32); osq, d_osq = sb(st, [128, 512], F32)
            st8, d_st8 = sb(st, [128, 4, 8], F32)
            yret, d_yret = sb(st, [128, 512], BF16)
            mixT, d_mixT = sb(st, [128, 8, 128], BF16)
            xnb = [sb(st, [128, D], F32) for _ in range(2)]
            ssA, d_ssA = sb(st, [128, 1], F32); ssB, d_ssB = sb(st, [128, 1], F32)
            d_xm2 = d_qxT
            hxb = [sb(st, [128, 8, 128], BF16) for _ in range(2)]
            rlb, d_rlb = sb(st, [128, 512], F32); tb, d_tb = rlb, d_rlb
            hT, d_hT = sb(st, [128, 8, 128], BF16)

            def loads(i):
                if i >= NT:
                    return
                bufs = qkvg[0]
                for j_ in range(4):
                    load("sync", "r_in%d_%d" % (0, j_), bufs[j_][0][:], scrs[j_][i * 128:(i + 1) * 128, :], bufs[j_][1])
                load("sync", "r_in%d_4" % (0), bufs[4][0][:], Sscr[i], bufs[4][1])

            def tileA(i):
                xn, d_xn = xnb[i % 2]
                hx2T, d_hx2T = hxb[i % 2]
                (qt, d_qt), (kt, d_kt), (vt, d_vt), (gt, d_gt), (St_, d_St) = qkvg[0]
                load("sync", "r_x%d" % (i % 2), xn[:], x[i * 128:(i + 1) * 128, :], d_xn)
                for j_, scr_ in enumerate((yscr, vxscr, x0scr)):
                    load("sync", "r_hy%d" % j_, hy3[j_][0][:], dap(scr_, i * 128, [(L, 128), (128 * L, 4), (1, 128)]), hy3[j_][1])
                S.op("vector", lambda e: e.tensor_tensor(out=qx[:], in0=sap(qt, 0, 128, 0, [(64, 8), (0, 2), (1, 64)]),
                                                         in1=sap(Wq, 0, 128, 0, [(2, 8), (1, 2), (0, 64)]), op=ALU.mult), reads=[d_qt, d_Wq], writes=[d_qx])
                pq, d_pq = nbb()
                for hp in range(4):
                    S.op("tensor", lambda e, hp=hp: e.transpose(out=pq[:, hp * 128:(hp + 1) * 128], in_=qt[:, hp * 128:(hp + 1) * 128], identity=identb[:]),
                         reads=[d_qt, d_identb], writes=[d_pq])
                for hp in range(4):
                    S.op("tensor", lambda e, hp=hp: e.transpose(out=pq[:, 512 + hp * 128:512 + (hp + 1) * 128], in_=kt[:, hp * 128:(hp + 1) * 128], identity=identb[:]),
                         reads=[d_kt, d_identb], writes=[d_pq])
                S.op("scalar", lambda e: e.copy(qT[:].rearrange("p a b -> p (a b)"), pq[:, 0:512]), reads=[d_pq], writes=[d_qT])
                S.op("scalar", lambda e: e.copy(kT[:].rearrange("p a b -> p (a b)"), pq[:, 512:1024]), reads=[d_pq], writes=[d_kT])
                yield
                px, d_px = nbb()
                for h in range(8):
                    S.op("tensor", lambda e, h=h: e.transpose(out=px[:, h * 128:(h + 1) * 128], in_=qx[:, h, :, :], identity=identb[:]),
                         reads=[d_qx, d_identb], writes=[d_px])
                S.op("scalar", lambda e: e.copy(qxT[:].rearrange("p a b -> p (a b)"), px[:]), reads=[d_px], writes=[d_qxT])
                yield
                for par in range(2):
                    psc, d_psc = nb()
                    b0 = par * 64
                    for hh in range(4):
                        h = 2 * hh + par
                        S.op("tensor", lambda e, hh=hh, b0=b0, psc=psc: e.matmul(psc[:, hh * 128:(hh + 1) * 128], lhsT=kT[b0:b0 + 64, hh, :],
                                                                                rhs=qT[b0:b0 + 64, hh, :], start=True, stop=True),
                             reads=[d_kT, d_qT], writes=[d_psc])
                    S.op("vector", lambda e, par=par, psc=psc: e.tensor_tensor(out=sap(qx, 0, 128, par * 128, [(256, 4), (1, 128)]), in0=psc[:].rearrange("p (a b) -> p a b", a=4),
                                                                              in1=sap(DT, 0, 128, par * 128, [(256, 4), (1, 128)]), op=ALU.mult),
                         reads=[d_psc, d_DT], writes=[d_Pm])
                yield
                po, d_po = nb()
                for h in range(8):
                    S.op("tensor", lambda e, h=h: e.matmul(po[:, h * 64:(h + 1) * 64], lhsT=sap(qx, 0, 128, h * 128, [(1, 128)]), rhs=vt[:, h * 64:(h + 1) * 64], start=True, stop=False),
                         reads=[d_Pm, d_vt], writes=[d_po])
                    S.op("tensor", lambda e, h=h: e.matmul(po[:, h * 64:(h + 1) * 64], lhsT=qxT[:, h, :], rhs=St_[:, h * 64:(h + 1) * 64], start=False, stop=True),
                         reads=[d_qxT, d_St], writes=[d_po])
                yield
                S.op("scalar", lambda e: e.copy(osb[:], po[:]), reads=[d_po], writes=[d_osb])
                S.op("scalar", lambda e: e.activation(out=osq[:], in_=po[:], func=AF.Square), reads=[d_po], writes=[d_osq])
                S.op("vector", lambda e: e.tensor_reduce(out=st8[:, 0, :], in_=osb[:].rearrange("p (h x) -> p h x", h=8), op=ALU.add, axis=AX.X), reads=[d_osb], writes=[d_st8])
                S.op("vector", lambda e: e.tensor_reduce(out=st8[:, 1, :], in_=osq[:].rearrange("p (h x) -> p h x", h=8), op=ALU.add, axis=AX.X), reads=[d_osq], writes=[d_st8])
                S.op("vector", lambda e: e.tensor_scalar(out=st8[:, 0, :], in0=st8[:, 0, :], scalar1=1.0 / 64, scalar2=None, op0=ALU.mult), reads=[d_st8], writes=[d_st8])
                S.op("vector", lambda e: e.tensor_tensor(out=st8[:, 2, :], in0=st8[:, 0, :], in1=st8[:, 0, :], op=ALU.mult), reads=[d_st8], writes=[d_st8])
                S.op("vector", lambda e: e.scalar_tensor_tensor(out=st8[:, 3, :], in0=st8[:, 1, :], scalar=1.0 / 64, op0=ALU.mult, in1=st8[:, 2, :], op1=ALU.subtract),
                     reads=[d_st8], writes=[d_st8])
                S.op("scalar", lambda e: e.activation(out=st8[:, 3, :], in_=st8[:, 3, :], func=AF.Sqrt, bias=1e-6), reads=[d_st8], writes=[d_st8])
                S.op("vector", lambda e: e.reciprocal(out=st8[:, 3, :], in_=st8[:, 3, :]), reads=[d_st8], writes=[d_st8])
                S.op("vector", lambda e: e.tensor_tensor(out=osb[:].rearrange("p (h x) -> p h x", h=8), in0=osb[:].rearrange("p (h x) -> p h x", h=8),
                                                         in1=sap(st8, 0, 128, 0, [(1, 8), (0, 64)]), op=ALU.subtract), reads=[d_osb, d_st8], writes=[d_osb])
                S.op("vector", lambda e: e.tensor_tensor(out=osb[:].rearrange("p (h x) -> p h x", h=8), in0=osb[:].rearrange("p (h x) -> p h x", h=8),
                                                         in1=sap(st8, 0, 128, 24, [(1, 8), (0, 64)]), op=ALU.mult), reads=[d_osb, d_st8], writes=[d_osb])
                S.op("gpsimd", lambda e: e.tensor_tensor(out=osb[:], in0=osb[:], in1=gnr[:], op=ALU.mult), reads=[d_osb, d_gnr], writes=[d_osb])
                S.op("gpsimd", lambda e: e.tensor_tensor(out=yret[:], in0=osb[:], in1=gt[:], op=ALU.mult), reads=[d_osb, d_gt], writes=[d_yret])
                loads(i + 1)
                yield
                yield
                py_, d_py = nbb()
                for hp in range(4):
                    S.op("tensor", lambda e, hp=hp: e.transpose(out=py_[:, hp * 128:(hp + 1) * 128], in_=yret[:, hp * 128:(hp + 1) * 128], identity=identb[:]),
                         reads=[d_yret, d_identb], writes=[d_py])
                S.op("scalar", lambda e: e.copy(mixT[:, 4:8, :].rearrange("p a b -> p (a b)"), py_[:, 0:512]), reads=[d_py], writes=[d_mixT])
                (yc, d_yc), (vxt, d_vxt), (x0t, d_x0t) = hy3
                for ct in range(4):
                    S.op("vector", lambda e, ct=ct: e.scalar_tensor_tensor(out=osq[:, ct * 128:(ct + 1) * 128], in0=vxt[:, ct, :], scalar=hbc[:, ct:ct + 1], op0=ALU.mult,
                                                                          in1=yc[:, ct, :], op1=ALU.add), reads=[d_vxt, d_yc, d_hbc, d_osq], writes=[d_osq])
                S.op("gpsimd", lambda e: e.tensor_tensor(out=mixT[:, 0:4, :].rearrange("p a b -> p (a b)"), in0=osq[:], in1=x0t[:].rearrange("p a b -> p (a b)"), op=ALU.mult),
                     reads=[d_osq, d_x0t], writes=[d_mixT])
                yield
                for nb_ in range(2):
                    pw, d_pw = nb()
                    for k in range(8):
                        S.op("tensor", lambda e, k=k, nb_=nb_, pw=pw: e.matmul(pw[:], lhsT=mixT[:, k, :], rhs=Wo[:, k, nb_ * 512:(nb_ + 1) * 512], start=(k == 0), stop=(k == 7)),
                             reads=[d_mixT, d_Wo], writes=[d_pw])
                    S.op("vector", lambda e, nb_=nb_, pw=pw: e.tensor_tensor(out=osq[:], in0=pw[:], in1=gate2[:, nb_ * 512:(nb_ + 1) * 512], op=ALU.mult),
                         reads=[d_pw, d_gate2], writes=[d_osq])
                    S.op("gpsimd", lambda e, nb_=nb_: e.tensor_tensor(out=xn[:, nb_ * 512:(nb_ + 1) * 512], in0=xn[:, nb_ * 512:(nb_ + 1) * 512], in1=osq[:], op=ALU.add),
                         reads=[d_xn, d_osq], writes=[d_xn])
                yield
                S.op("scalar", lambda e: e.activation(out=qxT[:].rearrange("p a b -> p (a b)"), in_=xn[:], func=AF.Square, accum_out=ssA[:, 0:1]), reads=[d_xn], writes=[d_xm2, d_ssA])
                S.op("scalar", lambda e: e.activation(out=ssA[:, 0:1], in_=ssA[:, 0:1], func=AF.Sqrt, scale=1.0 / D, bias=1e-6), reads=[d_ssA], writes=[d_ssA])
                S.op("vector", lambda e: e.reciprocal(out=ssA[:, 0:1], in_=ssA[:, 0:1]), reads=[d_ssA], writes=[d_ssA])
                S.op("vector", lambda e: e.scalar_tensor_tensor(out=qxT[:].rearrange("p a b -> p (a b)"), in0=xn[:], scalar=ssA[:, 0:1], op0=ALU.mult, in1=gs2[:], op1=ALU.mult),
                     reads=[d_xn, d_ssA, d_gs2], writes=[d_xm2])
                yield
                yield
                pt2, d_pt2 = nbb()
                for k in range(8):
                    S.op("tensor", lambda e, k=k: e.transpose(out=pt2[:, k * 128:(k + 1) * 128], in_=qxT[:, k, :], identity=identb[:]),
                         reads=[d_xm2, d_identb], writes=[d_pt2])
                for k in range(8):
                    S.op("scalar", lambda e, k=k: e.activation(out=hx2T[:, k, :], in_=pt2[:, k * 128:(k + 1) * 128], func=AF.Identity, bias=colx[:, 3, k:k + 1]),
                         reads=[d_pt2, d_colx], writes=[d_hx2T])

            def tileB(i):
                xn, d_xn = xnb[i % 2]
                hx2T, d_hx2T = hxb[i % 2]
                for hf in range(4):
                    for f4 in range(2):
                        ph, d_ph = nb()
                        for ff in range(4):
                            ft = hf * 8 + f4 * 4 + ff
                            for k in range(8):
                                S.op("tensor", lambda e, k=k, ft=ft, ff=ff, ph=ph: e.matmul(ph[:, ff * 128:(ff + 1) * 128], lhsT=W1[:, k, ft * 128:(ft + 1) * 128], rhs=hx2T[:, k, :],
                                                                                           start=(k == 0), stop=(k == 7)), reads=[d_W1, d_hx2T], writes=[d_ph])
                        S.op("scalar", lambda e, ph=ph: e.activation(out=rlb[:], in_=ph[:], func=AF.Relu), reads=[d_ph], writes=[d_rlb])
                        yield
                        S.op("gpsimd", lambda e, f4=f4: e.tensor_tensor(out=hT[:, f4 * 4:(f4 + 1) * 4, :].rearrange("p a b -> p (a b)"), in0=rlb[:], in1=rlb[:], op=ALU.mult),
                             reads=[d_rlb], writes=[d_hT])
                    for nb_ in range(2):
                        pm, d_pm = pmb[nb_]
                        for kk in range(8):
                            k = hf * 8 + kk
                            S.op("tensor", lambda e, k=k, kk=kk, nb_=nb_, pm=pm: e.matmul(pm[:], lhsT=hT[:, kk, :], rhs=W2[:, k, nb_ * 512:(nb_ + 1) * 512], start=(k == 0), stop=(k == 31)),
                                 reads=[d_hT, d_W2], writes=[d_pm])
                        yield
                for nb_ in range(2):
                    pm, d_pm = pmb[nb_]
                    S.op("vector", lambda e, nb_=nb_, pm=pm: e.tensor_tensor(out=tb[:], in0=pm[:], in1=gate5[:, nb_ * 512:(nb_ + 1) * 512], op=ALU.mult),
                         reads=[d_pm, d_gate5], writes=[d_tb])
                    S.op("gpsimd", lambda e, nb_=nb_: e.tensor_tensor(out=xn[:, nb_ * 512:(nb_ + 1) * 512], in0=xn[:, nb_ * 512:(nb_ + 1) * 512], in1=tb[:], op=ALU.add),
                         reads=[d_xn, d_tb], writes=[d_xn])
                S.op("scalar", lambda e: e.activation(out=hT[:, 0:8, :].rearrange("p a b -> p (a b)"), in_=xn[:], func=AF.Square, accum_out=ssB[:, 0:1]), reads=[d_xn], writes=[d_hT, d_ssB])
                S.op("scalar", lambda e: e.activation(out=ssB[:, 0:1], in_=ssB[:, 0:1], func=AF.Sqrt, scale=1.0 / D, bias=1e-6), reads=[d_ssB], writes=[d_ssB])
                S.op("vector", lambda e: e.reciprocal(out=ssB[:, 0:1], in_=ssB[:, 0:1]), reads=[d_ssB], writes=[d_ssB])
                S.op("vector", lambda e: e.scalar_tensor_tensor(out=xn[:], in0=xn[:], scalar=ssB[:, 0:1], op0=ALU.mult, in1=gF[:], op1=ALU.mult),
                     reads=[d_xn, d_ssB, d_gF], writes=[d_xn])
                final_events.append(store("sync", "r_out%d" % (i % 2), out[i * 128:(i + 1) * 128, :], xn[:], d_xn))


            NTL = NT if stop_after >= 99 else 2
            loads(0)
            for _ in tileA(0):
                pass
            for i in range(NTL):
                gB = tileB(i)
                gA = tileA(i + 1) if i + 1 < NTL else None
                doneA = gA is None
                doneB = False
                while not (doneA and doneB):
                    if not doneB:
                        try:
                            next(gB)
                        except StopIteration:
                            doneB = True
                    if not doneA:
                        try:
                            next(gA)
                        except StopIteration:
                            doneA = True
            S.barrier()
        S.emit()
    return nc


def make_in_map(inputs, b):
    f = lambda a: np.ascontiguousarray(np.asarray(a, dtype=np.float32))
    c = host_consts()
    m = dict(
        x=f(inputs["x"][b]), ctx=f(inputs["ctx"][b]),
        cc=f(np.stack([np.asarray(inputs["c"][b]), np.asarray(inputs["c_ctx"])], axis=0)),
        w_ada=f(inputs["w_ada"][0]), b_ada=f(inputs["b_ada"][0]).reshape(1, -1), norm1_g=f(inputs["norm1_g"][0]).reshape(1, -1),
        w_in=f(inputs["w_in"][0]), hy_conv_w=f(inputs["hy_conv_w"][0]), hy_conv_b=f(inputs["hy_conv_b"][0]).reshape(1, -1),
        hy_f_w1=f(inputs["hy_f_w1"][0]), hy_f_b1=f(inputs["hy_f_b1"][0]).reshape(1, -1), hy_f_freq1=f(inputs["hy_f_freq1"][0]).reshape(1, -1),
        hy_f_w2=f(inputs["hy_f_w2"][0]), hy_f_b2=f(inputs["hy_f_b2"][0]).reshape(1, -1), hy_f_freq2=f(inputs["hy_f_freq2"][0]).reshape(1, -1),
        hy_f_w3=f(inputs["hy_f_w3"][0]), hy_bias=f(inputs["hy_bias"][0]).reshape(1, -1),
        ret_decay_logit=f(inputs["ret_decay_logit"][0]).reshape(1, 16), ret_gn_g=f(inputs["ret_gn_g"][0]).reshape(1, -1),
        w_out=f(inputs["w_out"][0]), norm2_g=f(inputs["norm2_g"][0]).reshape(1, -1),
        w_mlp1=f(inputs["w_mlp1"][0]), w_mlp2=f(inputs["w_mlp2"][0]), norm_f_g=f(inputs["norm_f_g"]).reshape(1, -1),
    )
    m.update(c)
    return m


_NC = None


def kernel(**inputs):
    global _NC
    if _NC is None:
        _NC = build()
    in_maps = [make_in_map(inputs, b) for b in range(8)]
    res = run_bass_kernel_spmd(_NC, in_maps, core_ids=list(range(8)))
    return np.stack([np.asarray(r["out"], dtype=np.float32) for r in res.results], axis=0)
```

```python
import contextlib
import math
import numpy as np
import ml_dtypes
import concourse.bass as bass
import concourse.mybir as mybir
from concourse.bass_utils import run_bass_kernel_spmd

F32 = mybir.dt.float32
BF16 = mybir.dt.bfloat16
AF = mybir.ActivationFunctionType
ALU = mybir.AluOpType
AX = mybir.AxisListType

L = 8192
D = 1024
NT = 64
NFFT = 16384
ENGS = ("sync", "scalar", "vector", "gpsimd", "tensor")


class Dep:
    __slots__ = ("w", "r")

    def __init__(self):
        self.w = None
        self.r = []


class Sched:
    def __init__(self, nc, stack):
        self.nc = nc
        self.stack = stack
        self.streams = {e: [] for e in ENGS}
        self.esem = {}
        self.ecnt = {}
        for e in ("scalar", "vector", "gpsimd", "tensor"):
            self.esem[e] = stack.enter_context(nc.semaphore("es_" + e))
            self.ecnt[e] = 0
        self.dsem = {}
        self.dpool = []
        self.gsems = []
        self.nds = 0
        self.waited = {e: {} for e in ENGS}

    def _wait(self, eng, ev, waits):
        if ev is None:
            return
        sem, val, src = ev
        if eng == "tensor" and src == "tensor":
            return
        key = id(sem)
        if self.waited[eng].get(key, 0) >= val:
            return
        self.waited[eng][key] = val
        waits.append((sem, val))

    def op(self, eng, fn, reads=(), writes=(), dma=None, cc=False):
        waits = []
        for d in reads:
            self._wait(eng, d.w, waits)
        for d in writes:
            self._wait(eng, d.w, waits)
            for ev in d.r:
                self._wait(eng, ev, waits)
        if cc:
            self.nds += 1
            sem_ = self.stack.enter_context(self.nc.semaphore("cc%d" % self.nds))
            ev = (sem_, 1, "dma")
            inc = (sem_, None)
        elif dma is not None and eng == "gpsimd":
            self.nds += 1
            ent = [self.stack.enter_context(self.nc.semaphore("gs%d" % self.nds)), 16]
            self.gsems.append(ent)
            ev = (ent[0], 16, "dma")
            inc = (ent[0], 16)
        elif dma is not None:
            if dma not in self.dsem:
                if self.dpool:
                    self.dsem[dma] = self.dpool.pop()
                else:
                    self.nds += 1
                    self.dsem[dma] = [self.stack.enter_context(self.nc.semaphore("ds%d" % self.nds)), 0]
            ent = self.dsem[dma]
            ent[1] += 16
            ev = (ent[0], ent[1], "dma")
            inc = (ent[0], 16)
        else:
            self.ecnt[eng] += 1
            ev = (self.esem[eng], self.ecnt[eng], eng)
            inc = (self.esem[eng], 1)
        for d in reads:
            d.r.append(ev)
        for d in writes:
            d.w = ev
            d.r = []
        self.streams[eng].append((waits, fn, inc))
        return ev

    def barrier(self):
        evs = [(self.esem[e], self.ecnt[e], "x") for e in self.esem if self.ecnt[e] > 0]
        evs += [(v[0], v[1], "dma") for v in self.dsem.values() if v[1] > 0]
        evs += [(v[0], v[1], "dma") for v in self.gsems]
        for eng in ENGS:
            waits = []
            for ev in evs:
                key = id(ev[0])
                if self.waited[eng].get(key, 0) >= ev[1]:
                    continue
                self.waited[eng][key] = ev[1]
                waits.append((ev[0], ev[1]))
            if waits:
                self.streams[eng].append((waits, None, None))
        self.dpool.extend(self.dsem.values())
        self.dsem = {}

    def emit(self):
        nc = self.nc
        streams = self.streams

        def run(name, eng):
            for waits, fn, inc in streams[name]:
                for sem, val in waits:
                    eng.wait_ge(sem, val)
                if fn is not None:
                    if inc[1] is None:
                        fn(eng).then_inc(inc[0])
                    else:
                        fn(eng).then_inc(inc[0], inc[1])

        with nc.Block() as block:
            @block.sync
            def _(e):
                run("sync", e)

            @block.scalar
            def _(e):
                run("scalar", e)

            @block.vector
            def _(e):
                run("vector", e)

            @block.gpsimd
            def _(e):
                run("gpsimd", e)

            @block.tensor
            def _(e):
                run("tensor", e)


def sap(t, p0, pn, f0, dims):
    shp = list(t.shape)
    Fsz = int(np.prod(shp[1:]))
    return bass.AP(t, p0 * Fsz + f0, [[Fsz, pn]] + [[int(s), int(c)] for s, c in dims])


def dap(t, off, dims):
    return bass.AP(t.tensor, int(off), [[int(s), int(c)] for s, c in dims])


def _bf(a):
    return np.ascontiguousarray(a.astype(np.float32)).astype(ml_dtypes.bfloat16)


_CONSTS = None


def host_consts():
    global _CONSTS
    if _CONSTS is not None:
        return _CONSTS
    n1 = np.arange(64, dtype=np.float64)[:, None]
    k1 = np.arange(64, dtype=np.float64)[None, :]
    th1 = 2 * np.pi * n1 * (k1 + 0.5) / 128.0
    M1 = np.zeros((128, 128)); M1[:64, :64] = np.cos(th1); M1[:64, 64:] = -np.sin(th1)
    M1c = np.zeros((128, 128)); M1c[:64, :64] = np.cos(th1); M1c[:64, 64:] = np.sin(th1)
    n2 = np.arange(128, dtype=np.float64)[:, None]
    tht = 2 * np.pi * n2 * (k1 + 0.5) / NFFT
    k2 = np.arange(128, dtype=np.float64)[None, :]
    th2 = 2 * np.pi * n2 * k2 / 128.0
    C2 = np.cos(th2); S2 = np.sin(th2)
    sc = 2.0 / NFFT
    BDC = np.zeros((128, 128)); BDnS = np.zeros((128, 128))
    for c in range(2):
        BDC[c * 64:(c + 1) * 64, c * 64:(c + 1) * 64] = sc * np.cos(th1).T
        BDnS[c * 64:(c + 1) * 64, c * 64:(c + 1) * 64] = -sc * np.sin(th1).T
    fconst = np.concatenate([M1, M1c, C2, S2, -S2, C2, S2, -S2, C2, BDC, BDnS], axis=1)
    TC = np.concatenate([np.cos(tht), np.cos(tht)], axis=1)
    TS = np.concatenate([np.sin(tht), np.sin(tht)], axis=1)
    ct = np.cos(tht).T; st_ = np.sin(tht).T
    ITC = np.tile(np.concatenate([ct, ct], axis=1), (2, 1))
    ITS = np.tile(np.concatenate([st_, st_], axis=1), (2, 1))
    tconst = np.concatenate([TC, TS, ITC, ITS], axis=1).astype(np.float32)
    t = np.arange(L)
    r = (t // 64).astype(np.float32); col = (t % 64).astype(np.float32)
    inv = (10000.0 ** (-np.arange(16, dtype=np.float32) / 16)).astype(np.float32)
    ang = np.concatenate([r[:, None] * inv, col[:, None] * inv], axis=-1).astype(np.float32)
    cosr = np.cos(ang).astype(np.float32); sinr = np.sin(ang).astype(np.float32)
    def tl(a):
        return np.ascontiguousarray(a.reshape(64, 128, 32).transpose(1, 0, 2))
    rope = np.stack([tl(cosr), tl(sinr)], axis=1).astype(np.float32)
    tt = np.linspace(0.0, 1.0, L, dtype=np.float32)[:, None]
    w = ((2.0 * math.pi / L) * np.arange(L, dtype=np.float32))[:, None].astype(np.float32)
    bands = np.linspace(1e-4, 15, 16, dtype=np.float32)[None, :]
    z = np.concatenate([tt, np.cos(bands * w), -np.sin(bands * w)], axis=-1).astype(np.float32)
    order = (128 * np.arange(64)[None, :] + np.arange(128)[:, None]).reshape(-1)
    zT = np.ascontiguousarray(z[order].T).astype(np.float32)
    deltas = np.abs(np.linspace(math.log(1e-2) / 1.5, math.log(1e-2) / 0.3, 512, dtype=np.float32))
    E = np.exp(-tt * deltas[None, :]).astype(np.float32)
    E4 = E.reshape(64, 32, 4, 8, 64)
    edec = np.ascontiguousarray(E4.transpose(3, 1, 0, 2, 4)).reshape(8, 32, 64, 256).astype(np.float32)
    m = np.arange(128)[:, None]; c = np.arange(128)[None, :]
    pd = np.stack([np.maximum(c - m, 0), (c >= m), np.maximum(m - c, 0), (m >= c)], axis=1).astype(np.float32)
    p = np.arange(128, dtype=np.float32)
    pcols = np.stack([p + 1, 128 - p, 127 - p, p, 255 - p, p, 127 - p, 128 + p], axis=1).astype(np.float32)
    ident = np.eye(128, dtype=np.float32)
    _CONSTS = dict(fconst=_bf(fconst), tconst=tconst, rope=rope, zT=zT, edec=edec, pd=np.ascontiguousarray(pd),
                   pcols=pcols, ident_bf=_bf(ident), ident_f=ident)
    return _CONSTS


def build(debug=False, stop_after=99):
    nc = bass.Bass("TRN2", target_bir_lowering=False)

    def din(name, shape, dt=F32):
        return nc.dram_tensor(name, list(shape), dt, kind="ExternalInput").ap()

    def dscr(name, shape, dt):
        if debug:
            return nc.dram_tensor(name, list(shape), dt, kind="ExternalOutput").ap()
        return nc.dram_tensor(name, list(shape), dt).ap()

    x = din("x", [L, D]); ctx = din("ctx", [256, D]); cc = din("cc", [2, D])
    w_ada = din("w_ada", [D, 6 * D]); b_ada = din("b_ada", [1, 6 * D]); norm1_g = din("norm1_g", [1, D])
    w_in = din("w_in", [D, 3584]); conv_w = din("hy_conv_w", [3, 1536]); conv_b = din("hy_conv_b", [1, 1536])
    f_w1 = din("hy_f_w1", [33, 64]); f_b1 = din("hy_f_b1", [1, 64]); f_fr1 = din("hy_f_freq1", [1, 64])
    f_w2 = din("hy_f_w2", [64, 64]); f_b2 = din("hy_f_b2", [1, 64]); f_fr2 = din("hy_f_freq2", [1, 64])
    hy_bias = din("hy_bias", [1, 512]); logit = din("ret_decay_logit", [1, 16])
    gn_g = din("ret_gn_g", [1, 512]); w_out = din("w_out", [D, D]); norm2_g = din("norm2_g", [1, D])
    w_mlp1 = din("w_mlp1", [D, 4 * D]); w_mlp2 = din("w_mlp2", [4 * D, D]); norm_f_g = din("norm_f_g", [1, D])
    fconst_d = din("fconst", [128, 1408], BF16); tconst_d = din("tconst", [128, 768])
    rope_d = din("rope", [128, 2, 64, 32]); zT_d = din("zT", [33, L]); edec_d = din("edec", [32, 64, 256]); w3loc_d = din("w3loc", [64, 128])
    pd_d = din("pd", [128, 4, 128]); pcols_d = din("pcols", [128, 8])
    identb_d = din("ident_bf", [128, 128], BF16); identf_d = din("ident_f", [128, 128])
    out = nc.dram_tensor("out", [L, D], F32, kind="ExternalOutput").ap()

    modscr = dscr("modscr", [2, 6 * D], F32)
    vxscr = dscr("vxscr", [512, L], BF16); x0scr = dscr("x0scr", [512, L], BF16); yscr = dscr("yscr", [512, L], BF16)
    qscr = dscr("qscr", [L, 512], BF16); kscr = dscr("kscr", [L, 512], BF16)
    vscr = dscr("vscr", [L, 512], BF16); gscr = dscr("gscr", [L, 512], BF16)
    Tscr = dscr("Tscr", [NT, 128, 512], F32); Sscr = dscr("Sscr", [NT, 128, 512], BF16)
    kloc = nc.dram_tensor("kloc", [256, 4096], BF16).ap()
    kall = nc.dram_tensor("kall", [8 * 256, 4096], BF16).ap()
    d_kall = Dep()

    final_events = []

    with contextlib.ExitStack() as gst:
        S = Sched(nc, gst)
        uid = [0]

        def sb(st, shape, dt, name=None):
            uid[0] += 1
            t = st.enter_context(nc.sbuf_tensor(name or ("t%d" % uid[0]), list(shape), dt))
            return t, Dep()

        def ps(st, shape, dt, name=None):
            uid[0] += 1
            t = st.enter_context(nc.psum_tensor(name or ("p%d" % uid[0]), list(shape), dt))
            return t, Dep()

        def load(eng, key, dst_ap, src_ap, dst_dep, src_deps=(), slow=False):
            if slow:
                return S.op(eng, lambda e: e.dma_start(out=dst_ap, in_=src_ap, allow_slow_non_contiguous=True), reads=list(src_deps), writes=[dst_dep], dma=key)
            return S.op(eng, lambda e: e.dma_start(out=dst_ap, in_=src_ap), reads=list(src_deps), writes=[dst_dep], dma=key)

        def store(eng, key, dst_ap, src_ap, src_dep, dst_deps=(), slow=False):
            if slow:
                return S.op(eng, lambda e: e.dma_start(out=dst_ap, in_=src_ap, allow_slow_non_contiguous=True), reads=[src_dep], writes=list(dst_deps), dma=key)
            return S.op(eng, lambda e: e.dma_start(out=dst_ap, in_=src_ap), reads=[src_dep], writes=list(dst_deps), dma=key)

        def row_bc(ap_dram, off, n):
            return dap(ap_dram, off, [(0, 128), (1, n)])

        identb, d_identb = sb(gst, [128, 128], BF16)
        load("sync", "c_idb", identb[:], identb_d, d_identb)
        pcols, d_pcols = sb(gst, [128, 8], F32)
        load("sync", "c_pc", pcols[:], pcols_d, d_pcols)
        gs2, d_gs2 = sb(gst, [128, D], F32); gate2, d_gate2 = sb(gst, [128, D], F32)
        gate5, d_gate5 = sb(gst, [128, D], F32); gF, d_gF = sb(gst, [128, D], F32)
        colx, d_colx = sb(gst, [128, 6, 8], F32)
        lgt, d_lgt = sb(gst, [128, 16], F32)
        lgsel, d_lgsel = sb(gst, [128, 8], F32)
        DT, d_DT = sb(gst, [128, 8, 128], F32)
        Wq, d_Wq = sb(gst, [128, 8, 2], F32)
        Dec, d_Dec = sb(gst, [128, 8], F32)
        S0, d_S0 = sb(gst, [128, 512], F32)
        a1st = gst.enter_context(contextlib.ExitStack())
        gs1, d_gs1 = sb(a1st, [128, D], F32); gs1c, d_gs1c = sb(a1st, [128, D], F32)
        colc, d_colc = sb(a1st, [128, 2, 8], F32)
        Wk, d_Wk = sb(a1st, [128, 8, 2], F32)
        Wkc, d_Wkc = sb(a1st, [128, 2, 8, 2], F32)

        with contextlib.ExitStack() as st:
            ccT, d_ccT = sb(st, [128, 8, 2], F32)
            for r_ in range(2):
                load("sync", "a_cc", ccT[:, :, r_:r_ + 1], dap(cc, r_ * D, [(1, 128), (128, 8), (1, 1)]), d_ccT, slow=True)
            scT, d_scT = sb(st, [128, 8, 2], F32)
            S.op("scalar", lambda e: e.activation(out=scT[:], in_=ccT[:], func=AF.Silu), reads=[d_ccT], writes=[d_scT])
            bada, d_bada = sb(st, [2, 6 * D], F32)
            load("sync", "a_bada", bada[:], dap(b_ada, 0, [(0, 2), (1, 6 * D)]), d_bada)
            modsb, d_modsb = sb(st, [2, 6 * D], F32)
            wab = [sb(st, [128, 8, 512], F32) for _ in range(2)]
            pM = [ps(st, [128, 512], F32) for _ in range(2)]
            for cb in range(12):
                wa, d_wa = wab[cb % 2]
                load("sync", "a_wa%d" % (cb % 2), wa[:],
                     w_ada[:, cb * 512:(cb + 1) * 512].rearrange("(k p) n -> p k n", p=128), d_wa)
                pm, d_pm = pM[cb % 2]
                for k in range(8):
                    S.op("tensor", lambda e, pm=pm, wa=wa, k=k: e.matmul(pm[0:2, :], lhsT=scT[:, k, :], rhs=wa[:, k, :],
                                                                        start=(k == 0), stop=(k == 7)),
                         reads=[d_scT, d_wa], writes=[d_pm])
                S.op("vector", lambda e, pm=pm, cb=cb: e.tensor_tensor(out=modsb[:, cb * 512:(cb + 1) * 512], in0=pm[0:2, :],
                                                                      in1=bada[:, cb * 512:(cb + 1) * 512], op=ALU.add),
                     reads=[d_pm, d_bada], writes=[d_modsb])
            d_modscr = Dep()
            store("sync", "a_modst", modscr, modsb[:], d_modsb, [d_modscr])
            for j_ in range(6):
                load("sync", "a_colx", colx[:, j_, :].rearrange("p (k o) -> p k o", o=1),
                     dap(modscr, j_ * D, [(1, 128), (128, 8), (1, 1)]), d_colx, [d_modscr], slow=True)
            for j_ in range(2):
                load("sync", "a_colc", colc[:, j_, :].rearrange("p (k o) -> p k o", o=1),
                     dap(modscr, 6 * D + j_ * D, [(1, 128), (128, 8), (1, 1)]), d_colc, [d_modscr], slow=True)
            tmpA, d_tmpA = sb(st, [128, D], F32)
            tmpB, d_tmpB = sb(st, [128, D], F32)

            def make_gs(dst, d_dst, scale_off, g_dram, tagn):
                load("sync", "a_tA", tmpA[:], row_bc(modscr, scale_off, D), d_tmpA, [d_modscr])
                load("sync", "a_tB", tmpB[:], row_bc(g_dram, 0, D), d_tmpB)
                S.op("vector", lambda e: e.scalar_tensor_tensor(out=dst[:], in0=tmpA[:], scalar=1.0, op0=ALU.add,
                                                                in1=tmpB[:], op1=ALU.mult),
                     reads=[d_tmpA, d_tmpB], writes=[d_dst])
            make_gs(gs1, d_gs1, 1 * D, norm1_g, 0)
            make_gs(gs1c, d_gs1c, 6 * D + 1 * D, norm1_g, 1)
            make_gs(gs2, d_gs2, 4 * D, norm2_g, 2)
            load("sync", "a_g2", gate2[:], row_bc(modscr, 2 * D, D), d_gate2, [d_modscr])
            load("sync", "a_g5", gate5[:], row_bc(modscr, 5 * D, D), d_gate5, [d_modscr])
            load("sync", "a_gF", gF[:], row_bc(norm_f_g, 0, D), d_gF)

            lraw, d_lraw = sb(st, [128, 16], F32)
            load("sync", "a_lg", lraw[:], row_bc(logit, 0, 16), d_lraw)
            S.op("scalar", lambda e: e.activation(out=lgt[:], in_=lraw[:], func=AF.Exp, scale=-1.0), reads=[d_lraw], writes=[d_lgt])
            S.op("scalar", lambda e: e.activation(out=lgt[:], in_=lgt[:], func=AF.Ln, bias=1.0), reads=[d_lgt], writes=[d_lgt])
            S.op("scalar", lambda e: e.mul(lgt[:], lgt[:], -1.0), reads=[d_lgt], writes=[d_lgt])
            S.op("vector", lambda e: e.tensor_copy(out=lgsel[0:64, :], in_=lgt[0:64, 0:8]), reads=[d_lgt], writes=[d_lgsel])
            S.op("vector", lambda e: e.tensor_copy(out=lgsel[64:128, :], in_=lgt[64:128, 8:16]), reads=[d_lgt], writes=[d_lgsel])
            S.op("scalar", lambda e: e.activation(out=Dec[:], in_=lgsel[:], func=AF.Exp, scale=128.0), reads=[d_lgsel], writes=[d_Dec])
            for (dst, d_dst, cf, cbk) in ((Wq, d_Wq, 0, 1), (Wk, d_Wk, 2, 3)):
                S.op("scalar", lambda e, dst=dst, cf=cf: e.activation(out=dst[:, :, 0], in_=lgt[:, 0:8], func=AF.Exp, scale=pcols[:, cf:cf + 1]),
                     reads=[d_lgt, d_pcols], writes=[d_dst])
                S.op("scalar", lambda e, dst=dst, cbk=cbk: e.activation(out=dst[:, :, 1], in_=lgt[:, 8:16], func=AF.Exp, scale=pcols[:, cbk:cbk + 1]),
                     reads=[d_lgt, d_pcols], writes=[d_dst])
            S.op("vector", lambda e: e.tensor_scalar(out=Wk[:], in0=Wk[:], scalar1=0.125, scalar2=None, op0=ALU.mult), reads=[d_Wk], writes=[d_Wk])
            for tI in range(2):
                S.op("scalar", lambda e, tI=tI: e.activation(out=Wkc[:, tI, :, 0], in_=lgt[:, 0:8], func=AF.Exp, scale=pcols[:, 4 + 2 * tI:5 + 2 * tI]),
                     reads=[d_lgt, d_pcols], writes=[d_Wkc])
                S.op("scalar", lambda e, tI=tI: e.activation(out=Wkc[:, tI, :, 1], in_=lgt[:, 8:16], func=AF.Exp, scale=pcols[:, 5 + 2 * tI:6 + 2 * tI]),
                     reads=[d_lgt, d_pcols], writes=[d_Wkc])
            pdt, d_pdt = sb(st, [128, 4, 128], F32)
            load("sync", "a_pd", pdt[:], pd_d, d_pdt)
            ef, d_ef = sb(st, [128, 128], F32); eb, d_eb = sb(st, [128, 128], F32)
            for h in range(8):
                S.op("scalar", lambda e, h=h: e.activation(out=ef[:], in_=pdt[:, 0, :], func=AF.Exp, scale=lgt[:, h:h + 1]),
                     reads=[d_pdt, d_lgt], writes=[d_ef])
                S.op("scalar", lambda e, h=h: e.activation(out=eb[:], in_=pdt[:, 2, :], func=AF.Exp, scale=lgt[:, 8 + h:9 + h]),
                     reads=[d_pdt, d_lgt], writes=[d_eb])
                S.op("vector", lambda e: e.scalar_tensor_tensor(out=ef[:], in0=ef[:], scalar=0.125, op0=ALU.mult, in1=pdt[:, 1, :], op1=ALU.mult), reads=[d_ef, d_pdt], writes=[d_ef])
                S.op("vector", lambda e: e.scalar_tensor_tensor(out=eb[:], in0=eb[:], scalar=0.125, op0=ALU.mult, in1=pdt[:, 3, :], op1=ALU.mult), reads=[d_eb, d_pdt], writes=[d_eb])
                S.op("vector", lambda e, h=h: e.tensor_tensor(out=DT[:, h, :], in0=ef[:], in1=eb[:], op=ALU.add), reads=[d_ef, d_eb], writes=[d_DT])
            S.barrier()
        if stop_after <= 0:
            S.emit()
            return nc

        def fft_phase(mode):
            with contextlib.ExitStack() as st:
                fc, d_fc = sb(st, [128, 1408], BF16)
                load("sync", "f_fc", fc[:], fconst_d, d_fc)
                tcs, d_tcs = sb(st, [128, 768], F32)
                load("sync", "f_tc", tcs[:], tconst_d, d_tcs)
                M1o, M1co, C2o, S2o, nS2o, C2S2o, nS2C2o, BDCo, BDnSo = 0, 128, 256, 384, 512, 640, 896, 1152, 1280
                w1t, d_w1t = sb(st, [33, 64], F32); load("sync", "f_w1", w1t[:], f_w1, d_w1t)
                w2t, d_w2t = sb(st, [64, 64], F32); load("sync", "f_w2", w2t[:], f_w2, d_w2t)
                w3b, d_w3b = sb(st, [64, 128], BF16)
                if mode == "filter":
                    S.op("gpsimd", lambda e: e.dma_start(out=w3b[:], in_=w3loc_d), writes=[d_w3b], dma="f_w3")
                fcol, d_fcol = sb(st, [64, 4], F32)
                for j_, src in enumerate((f_fr1, f_b1, f_fr2, f_b2)):
                    load("sync", "f_col", fcol[:, j_:j_ + 1], dap(src, 0, [(1, 64), (1, 1)]), d_fcol, slow=True)
                fab, d_fab = sb(st, [64, 4], F32)
                for l_ in range(2):
                    S.op("vector", lambda e, l_=l_: e.tensor_scalar(out=fab[:, 2 * l_:2 * l_ + 1], in0=fcol[:, 2 * l_:2 * l_ + 1], scalar1=1.0 / 3.0, scalar2=None, op0=ALU.mult),
                         reads=[d_fcol], writes=[d_fab])
                    S.op("vector", lambda e, l_=l_: e.tensor_tensor(out=fab[:, 2 * l_ + 1:2 * l_ + 2], in0=fab[:, 2 * l_:2 * l_ + 1], in1=fcol[:, 2 * l_ + 1:2 * l_ + 2], op=ALU.mult),
                         reads=[d_fcol, d_fab], writes=[d_fab])
                onesf, d_onesf = sb(st, [64, 128], F32)
                S.op("gpsimd", lambda e: e.memset(onesf[:], 1.0), writes=[d_onesf])
                h2T, d_h2T = sb(st, [64, L], BF16) if mode == "filter" else (None, None)
                banks = [ps(st, [128, 512], F32) for _ in range(8)]
                bctr = [0]

                def nb():
                    b_ = banks[bctr[0] % 8]
                    bctr[0] += 1
                    return b_

                zt = [sb(st, [33, 512], F32) for _ in range(2)] if mode == "filter" else None
                (sA, d_sA), (sB, d_sB), (h1, d_h1) = [sb(st, [64, 512], F32) for _ in range(3)] if mode == "filter" else [(None, None)] * 3

                def sin3(pf, d_pf, layer, out_ap, d_out):
                    S.op("scalar", lambda e: e.activation(out=sA[:], in_=pf[0:64, :], func=AF.Sin, scale=fab[:, 2 * layer:2 * layer + 1],
                                                          bias=fab[:, 2 * layer + 1:2 * layer + 2]), reads=[d_pf, d_fab], writes=[d_sA])
                    S.op("vector", lambda e: e.tensor_tensor(out=sB[:], in0=sA[:], in1=sA[:], op=ALU.mult), reads=[d_sA], writes=[d_sB])
                    S.op("vector", lambda e: e.tensor_scalar(out=sB[:], in0=sB[:], scalar1=-4.0, scalar2=3.0, op0=ALU.mult, op1=ALU.add), reads=[d_sB], writes=[d_sB])
                    S.op("vector", lambda e: e.tensor_tensor(out=out_ap, in0=sB[:], in1=sA[:], op=ALU.mult), reads=[d_sA, d_sB], writes=[d_out])

                def mlp_block(blk):
                    z, d_z = zt[blk % 2]
                    load("sync", "f_z%d" % (blk % 2), z[:], zT_d[:, blk * 512:(blk + 1) * 512], d_z)
                    pf, d_pf = nb()
                    S.op("tensor", lambda e: e.matmul(pf[0:64, :], lhsT=w1t[:], rhs=z[:], start=True, stop=True), reads=[d_w1t, d_z], writes=[d_pf])
                    sin3(pf, d_pf, 0, h1[:], d_h1)
                    pf2, d_pf2 = nb()
                    S.op("tensor", lambda e: e.matmul(pf2[0:64, :], lhsT=w2t[:], rhs=h1[:], start=True, stop=True), reads=[d_w2t, d_h1], writes=[d_pf2])
                    sin3(pf2, d_pf2, 1, h2T[:, blk * 512:(blk + 1) * 512], d_h2T)

                if mode == "filter":
                    for blk in range(16):
                        mlp_block(blk)

                Hbuf, d_Hbuf = sb(st, [128, 2 * 64 * 128], BF16)
                (Acc4, d_Acc4), (habs, d_habs) = [sb(st, [64, 512], F32) for _ in range(2)] if mode == "filter" else [(None, None)] * 2
                Et = [sb(st, [64, 256], F32) for _ in range(2)] if mode == "filter" else None
                hd32 = [sb(st, [64, 512], F32) for _ in range(2)] if mode == "filter" else None
                nsb, d_nsb = sb(st, [128, 128], F32)
                rn, d_rn = sb(st, [128, 64], F32)
                BfR, d_BfR = sb(st, [128, 64, 64], BF16); BfI, d_BfI = sb(st, [128, 64, 64], BF16)
                BbR, d_BbR = sb(st, [128, 64, 64], BF16); BbI, d_BbI = sb(st, [128, 64, 64], BF16)
                KR, d_KR = sb(st, [128, 64, 64], BF16); KI, d_KI = sb(st, [128, 64, 64], BF16)
                KRb = [(KR, d_KR), sb(st, [128, 64, 64], BF16)] if mode == "data" else None
                KIb = [(KI, d_KI), sb(st, [128, 64, 64], BF16)] if mode == "data" else None
                Xg, d_Xg = sb(st, [64, 64, 128], BF16) if mode == "data" else (None, None)
                Qb = [(sb(st, [128, 512], F32), sb(st, [128, 512], F32)) for _ in range(2)]
                qctr = [0]
                (t1, d_t1), (t2, d_t2), (t3, d_t3), (t4, d_t4) = [sb(st, [128, 512], F32) for _ in range(4)] if mode == "data" else [(None, None)] * 4
                Yo, d_Yo = sb(st, [128, 32, 128], BF16) if mode == "data" else (None, None)


                Sst, d_Sst = sb(st, [128, 512], F32) if mode == "data" else (None, None)
                if mode == "data":
                    S.op("vector", lambda e: e.tensor_copy(out=Sst[:], in_=S0[:]), reads=[d_S0], writes=[d_Sst])
                Tt = [sb(st, [128, 512], F32) for _ in range(2)] if mode == "data" else None
                Sb = [sb(st, [128, 512], BF16) for _ in range(2)] if mode == "data" else None
                sctr = [0]

                def tload(s_):
                    if s_ < NT:
                        tt_, d_tt = Tt[s_ % 2]
                        load("sync", "s_tf%d" % (s_ % 2), tt_[0:64, :], Tscr[s_, 0:64, :], d_tt)
                        load("sync", "s_tb%d" % (s_ % 2), tt_[64:128, :], Tscr[NT - 1 - s_, 64:128, :], d_tt)

                def scan_step(s_):
                    tt_, d_tt = Tt[s_ % 2]
                    sb_, d_sb = Sb[s_ % 2]
                    S.op("scalar", lambda e: e.copy(sb_[:], Sst[:]), reads=[d_Sst], writes=[d_sb])
                    store("sync", "s_sf%d" % (s_ % 2), Sscr[s_, 0:64, :], sb_[0:64, :], d_sb)
                    store("sync", "s_sb%d" % (s_ % 2), Sscr[NT - 1 - s_, 64:128, :], sb_[64:128, :], d_sb)
                    S.op("vector", lambda e: e.tensor_tensor(out=Sst[:].rearrange("p (h x) -> p h x", h=8), in0=Sst[:].rearrange("p (h x) -> p h x", h=8),
                                                             in1=sap(Dec, 0, 128, 0, [(1, 8), (0, 64)]), op=ALU.mult), reads=[d_Sst, d_Dec], writes=[d_Sst])
                    S.op("vector", lambda e: e.tensor_tensor(out=Sst[:], in0=Sst[:], in1=tt_[:], op=ALU.add), reads=[d_Sst, d_tt], writes=[d_Sst])
                    tload(s_ + 2)

                def scan_some(n_):
                    for _ in range(n_ if mode == "data" else 0):
                        if sctr[0] < NT:
                            scan_step(sctr[0])
                            sctr[0] += 1

                if mode == "data":
                    tload(0); tload(1)

                def twiddle(pa, d_pa, conj, outR, d_outR, outI, d_outI, c0):
                    pav = pa[:].rearrange("p (c x) -> p c x", c=4)
                    (Q1, d_Q1), (Q2, d_Q2) = Qb[qctr[0] % 2]
                    qctr[0] += 1
                    S.op("vector", lambda e: e.tensor_tensor(out=Q1[:].rearrange("p (c x) -> p c x", c=4), in0=pav,
                                                             in1=sap(tcs, 0, 128, 0, [(0, 4), (1, 128)]), op=ALU.mult), reads=[d_pa, d_tcs], writes=[d_Q1])
                    S.op("vector", lambda e: e.tensor_tensor(out=Q2[:].rearrange("p (c x) -> p c x", c=4), in0=pav,
                                                             in1=sap(tcs, 0, 128, 128, [(0, 4), (1, 128)]), op=ALU.mult), reads=[d_pa, d_tcs], writes=[d_Q2])
                    q1lo = sap(Q1, 0, 128, 0, [(128, 4), (1, 64)]); q1hi = sap(Q1, 0, 128, 64, [(128, 4), (1, 64)])
                    q2lo = sap(Q2, 0, 128, 0, [(128, 4), (1, 64)]); q2hi = sap(Q2, 0, 128, 64, [(128, 4), (1, 64)])
                    S.op("gpsimd", lambda e: e.tensor_tensor(out=outR[:, c0:c0 + 4, :], in0=q1lo, in1=q2hi, op=(ALU.subtract if conj else ALU.add)),
                         reads=[d_Q1, d_Q2], writes=[d_outR])
                    S.op("vector" if (qctr[0] % 2 == 0) else "gpsimd",
                         lambda e: e.tensor_tensor(out=outI[:, c0:c0 + 4, :], in0=q1hi, in1=q2lo, op=(ALU.add if conj else ALU.subtract)),
                         reads=[d_Q1, d_Q2], writes=[d_outI])

                def s1_stage(src_fn, src_deps, m1off, conj, outR, d_outR, outI, d_outI):
                    for c4 in range(16):
                        pa, d_pa = nb()
                        for cc_ in range(4):
                            S.op("tensor", lambda e, cc_=cc_, c4=c4, pa=pa: e.matmul(pa[:, cc_ * 128:(cc_ + 1) * 128], lhsT=src_fn(c4 * 4 + cc_),
                                                                                    rhs=fc[0:64, m1off:m1off + 128], start=True, stop=True),
                                 reads=list(src_deps) + [d_fc], writes=[d_pa])
                        twiddle(pa, d_pa, conj, outR, d_outR, outI, d_outI, c4 * 4)

                def s2_mm(pk, d_pk, terms, c8):
                    n_ = len(terms)
                    for ti, (foff, buf, d_buf) in enumerate(terms):
                        S.op("tensor", lambda e, ti=ti, foff=foff, buf=buf: e.matmul(pk[:], lhsT=fc[:, foff:foff + 128], rhs=buf[:, c8 * 8:(c8 + 1) * 8, :],
                                                                                    start=(ti == 0), stop=(ti == n_ - 1)),
                             reads=[d_fc, d_buf], writes=[d_pk])

                def group_filter():
                    S.op("gpsimd", lambda e: e.memset(Acc4[:], 0.0), writes=[d_Acc4])
                    for jq in range(32):
                        et, d_et = Et[jq % 2]
                        load("sync", "f_e%d" % (jq % 2), et[:], edec_d[jq], d_et)
                        ph, d_ph = nb()
                        for jj in range(4):
                            j = 4 * jq + jj
                            S.op("tensor", lambda e, jj=jj, j=j, ph=ph: e.matmul(ph[0:64, jj * 128:(jj + 1) * 128], lhsT=h2T[:, j * 64:(j + 1) * 64],
                                                                                rhs=w3b[:, :], start=True, stop=True),
                                 reads=[d_h2T, d_w3b], writes=[d_ph])
                        hd, d_hd = hd32[jq % 2]
                        S.op("vector", lambda e, ph=ph, et=et, hd=hd: e.tensor_tensor(
                            out=hd[:].rearrange("p (j d c) -> p j d c", j=4, d=2), in0=ph[0:64, :].rearrange("p (j d c) -> p j d c", j=4, d=2),
                            in1=sap(et, 0, 64, 0, [(64, 4), (0, 2), (1, 64)]), op=ALU.mult), reads=[d_ph, d_et], writes=[d_hd])
                        S.op("scalar", lambda e, hd=hd: e.activation(out=habs[:], in_=hd[:], func=AF.Abs), reads=[d_hd], writes=[d_habs])
                        S.op("vector", lambda e: e.tensor_tensor(out=Acc4[:], in0=Acc4[:], in1=habs[:], op=ALU.add), reads=[d_habs, d_Acc4], writes=[d_Acc4])
                        S.op("scalar", lambda e, hd=hd, jq=jq: e.copy(sap(Hbuf, 0, 64, 4 * jq, [(1, 4), (64 * 128, 2), (128, 64)]),
                                                                     hd[:].rearrange("p (j d c) -> p j d c", j=4, d=2)),
                             reads=[d_hd], writes=[d_Hbuf])
                        if jq % 4 == 3:
                            scan_some(1)
                    S.op("gpsimd", lambda e: e.memset(sap(Hbuf, 0, 1, 64 * 128, [(128, 64)]), 0.0), writes=[d_Hbuf])
                    pn, d_pn = nb()
                    for jj in range(4):
                        S.op("tensor", lambda e, jj=jj: e.matmul(pn[:, 0:128], lhsT=onesf[:], rhs=Acc4[:, jj * 128:(jj + 1) * 128], start=(jj == 0), stop=(jj == 3)),
                             reads=[d_onesf, d_Acc4], writes=[d_pn])
                    S.op("scalar", lambda e: e.copy(nsb[:], pn[:, 0:128]), reads=[d_pn], writes=[d_nsb])
                    S.op("vector", lambda e: e.scalar_tensor_tensor(out=rn[:], in0=nsb[:, 0:64], scalar=1e-6, op0=ALU.add, in1=nsb[:, 64:128], op1=ALU.add),
                         reads=[d_nsb], writes=[d_rn])
                    S.op("vector", lambda e: e.reciprocal(out=rn[:], in_=rn[:]), reads=[d_rn], writes=[d_rn])
                    s1_stage(lambda c: sap(Hbuf, 0, 64, c * 128, [(1, 128)]), [d_Hbuf], M1o, False, BfR, d_BfR, BfI, d_BfI)
                    s1_stage(lambda c: sap(Hbuf, 0, 64, 64 * 128 + c * 128, [(1, 128)]), [d_Hbuf], M1co, True, BbR, d_BbR, BbI, d_BbI)
                    for c8 in range(8):
                        pk, d_pk = nb()
                        s2_mm(pk, d_pk, [(C2o, BfR, d_BfR), (S2o, BfI, d_BfI), (C2o, BbR, d_BbR), (nS2o, BbI, d_BbI)], c8)
                        S.op("vector", lambda e, pk=pk, c8=c8: e.tensor_tensor(out=KR[:, c8 * 8:(c8 + 1) * 8, :], in0=pk[:].rearrange("p (c k) -> p c k", c=8),
                                                                              in1=sap(rn, 0, 128, c8 * 8, [(1, 8), (0, 64)]), op=ALU.mult),
                             reads=[d_pk, d_rn], writes=[d_KR])
                        pk2, d_pk2 = nb()
                        s2_mm(pk2, d_pk2, [(C2o, BfI, d_BfI), (nS2o, BfR, d_BfR), (C2o, BbI, d_BbI), (S2o, BbR, d_BbR)], c8)
                        S.op("vector", lambda e, pk2=pk2, c8=c8: e.tensor_tensor(out=KI[:, c8 * 8:(c8 + 1) * 8, :], in0=pk2[:].rearrange("p (c k) -> p c k", c=8),
                                                                                in1=sap(rn, 0, 128, c8 * 8, [(1, 8), (0, 64)]), op=ALU.mult),
                             reads=[d_pk2, d_rn], writes=[d_KI])
                    d_kloc = Dep()
                    store("sync", "f_kr", kloc[0:128, :], KR[:].rearrange("p c k -> p (c k)"), d_KR, [d_kloc])
                    store("sync", "f_ki", kloc[128:256, :], KI[:].rearrange("p c k -> p (c k)"), d_KI, [d_kloc])
                    S.op("gpsimd", lambda e: e.collective_compute("AllGather", ALU.bypass, replica_groups=[list(range(8))], ins=[kloc], outs=[kall]),
                         reads=[d_kloc], writes=[d_kall], cc=True)

                def group_data(g):
                    kr_, d_kr_ = KRb[g % 2]
                    ki_, d_ki_ = KIb[g % 2]
                    load("sync", "f_lkr%d" % (g % 2), kr_[:].rearrange("p c k -> p (c k)"), kall[g * 256:g * 256 + 128, :], d_kr_, [d_kall])
                    load("sync", "f_lki%d" % (g % 2), ki_[:].rearrange("p c k -> p (c k)"), kall[g * 256 + 128:g * 256 + 256, :], d_ki_, [d_kall])
                    load("sync", "f_xg", Xg[:], dap(vxscr, g * 64 * L, [(128, 64), (L, 64), (1, 128)]), d_Xg)
                    s1_stage(lambda c: Xg[:, c, :], [d_Xg], M1o, False, BfR, d_BfR, BfI, d_BfI)
                    for c8 in range(8):
                        px, d_px = nb()
                        s2_mm(px, d_px, [(C2o, BfR, d_BfR), (S2o, BfI, d_BfI)], c8)
                        pxi, d_pxi = nb()
                        s2_mm(pxi, d_pxi, [(C2o, BfI, d_BfI), (nS2o, BfR, d_BfR)], c8)
                        kr = kr_[:, c8 * 8:(c8 + 1) * 8, :].rearrange("p c k -> p (c k)")
                        ki = ki_[:, c8 * 8:(c8 + 1) * 8, :].rearrange("p c k -> p (c k)")
                        S.op("vector", lambda e, px=px, kr=kr: e.tensor_tensor(out=t1[:], in0=px[:], in1=kr, op=ALU.mult), reads=[d_px, d_kr_], writes=[d_t1])
                        S.op("vector", lambda e, pxi=pxi, ki=ki: e.tensor_tensor(out=t2[:], in0=pxi[:], in1=ki, op=ALU.mult), reads=[d_pxi, d_ki_], writes=[d_t2])
                        S.op("vector", lambda e, px=px, ki=ki: e.tensor_tensor(out=t3[:], in0=px[:], in1=ki, op=ALU.mult), reads=[d_px, d_ki_], writes=[d_t3])
                        S.op("vector", lambda e, pxi=pxi, kr=kr: e.tensor_tensor(out=t4[:], in0=pxi[:], in1=kr, op=ALU.mult), reads=[d_pxi, d_kr_], writes=[d_t4])
                        S.op("gpsimd", lambda e, c8=c8: e.tensor_tensor(out=BbR[:, c8 * 8:(c8 + 1) * 8, :].rearrange("p c k -> p (c k)"), in0=t1[:], in1=t2[:], op=ALU.subtract),
                             reads=[d_t1, d_t2], writes=[d_BbR])
                        S.op("vector" if (c8 % 2 == 0) else "gpsimd", lambda e, c8=c8: e.tensor_tensor(out=BbI[:, c8 * 8:(c8 + 1) * 8, :].rearrange("p c k -> p (c k)"), in0=t3[:], in1=t4[:], op=ALU.add),
                             reads=[d_t3, d_t4], writes=[d_BbI])
                    for pb in range(16):
                        pc, d_pc = nb()
                        for q_ in range(2):
                            p_ = pb * 2 + q_
                            S.op("tensor", lambda e, q_=q_, p_=p_, pc=pc: e.matmul(pc[:, q_ * 256:(q_ + 1) * 256], lhsT=BbR[:, 2 * p_:2 * p_ + 2, :],
                                                                                  rhs=fc[:, C2S2o:C2S2o + 256], start=True, stop=False),
                                 reads=[d_BbR, d_fc], writes=[d_pc])
                            S.op("tensor", lambda e, q_=q_, p_=p_, pc=pc: e.matmul(pc[:, q_ * 256:(q_ + 1) * 256], lhsT=BbI[:, 2 * p_:2 * p_ + 2, :],
                                                                                  rhs=fc[:, nS2C2o:nS2C2o + 256], start=False, stop=True),
                                 reads=[d_BbI, d_fc], writes=[d_pc])
                        pcv = pc[:].rearrange("p (q x) -> p q x", q=2)
                        (Q1, d_Q1), (Q2, d_Q2) = Qb[qctr[0] % 2]
                        qctr[0] += 1
                        S.op("vector", lambda e, pcv=pcv, Q1=Q1: e.tensor_tensor(out=Q1[:].rearrange("p (q x) -> p q x", q=2), in0=pcv,
                                                                         in1=sap(tcs, 0, 128, 256, [(0, 2), (1, 256)]), op=ALU.mult), reads=[d_pc, d_tcs], writes=[d_Q1])
                        S.op("vector", lambda e, pcv=pcv, Q2=Q2: e.tensor_tensor(out=Q2[:].rearrange("p (q x) -> p q x", q=2), in0=pcv,
                                                                         in1=sap(tcs, 0, 128, 512, [(0, 2), (1, 256)]), op=ALU.mult), reads=[d_pc, d_tcs], writes=[d_Q2])
                        S.op("gpsimd", lambda e, pb=pb, Q1=Q1, Q2=Q2: e.tensor_tensor(out=sap(Hbuf, 0, 128, pb * 256, [(128, 2), (1, 128)]),
                                                                       in0=sap(Q1, 0, 128, 0, [(256, 2), (1, 128)]), in1=sap(Q2, 0, 128, 128, [(256, 2), (1, 128)]), op=ALU.subtract),
                             reads=[d_Q1, d_Q2], writes=[d_Hbuf])
                        S.op("vector" if (pb % 2 == 0) else "gpsimd", lambda e, pb=pb, Q1=Q1, Q2=Q2: e.tensor_tensor(out=sap(Hbuf, 0, 128, 4096 + pb * 256, [(128, 2), (1, 128)]),
                                                                       in0=sap(Q1, 0, 128, 128, [(256, 2), (1, 128)]), in1=sap(Q2, 0, 128, 0, [(256, 2), (1, 128)]), op=ALU.add),
                             reads=[d_Q1, d_Q2], writes=[d_Hbuf])
                    for p4 in range(8):
                        py, d_py = nb()
                        for q_ in range(4):
                            p_ = p4 * 4 + q_
                            S.op("tensor", lambda e, q_=q_, p_=p_, py=py: e.matmul(py[:, q_ * 128:(q_ + 1) * 128], lhsT=fc[:, BDCo:BDCo + 128],
                                                                                  rhs=sap(Hbuf, 0, 128, p_ * 128, [(1, 128)]), start=True, stop=False),
                                 reads=[d_Hbuf, d_fc], writes=[d_py])
                            S.op("tensor", lambda e, q_=q_, p_=p_, py=py: e.matmul(py[:, q_ * 128:(q_ + 1) * 128], lhsT=fc[:, BDnSo:BDnSo + 128],
                                                                                  rhs=sap(Hbuf, 0, 128, 4096 + p_ * 128, [(1, 128)]), start=False, stop=True),
                                 reads=[d_Hbuf, d_fc], writes=[d_py])
                        S.op("scalar", lambda e, p4=p4, py=py: e.copy(Yo[:, p4 * 4:(p4 + 1) * 4, :].rearrange("p q x -> p (q x)"), py[:]), reads=[d_py], writes=[d_Yo])
                    store("sync", "f_yo", dap(yscr, g * 64 * L, [(128, 128), (2 * L, 32), (1, 128)]), Yo[:], d_Yo)

                if mode == "filter":
                    group_filter()
                else:
                    for g in range(8):
                        group_data(g)
                    scan_some(NT)
                S.barrier()
        fft_phase("filter")

        d_vx = Dep(); d_x0 = Dep(); d_q = Dep(); d_k = Dep(); d_v = Dep(); d_g = Dep(); d_T = Dep()
        with contextlib.ExitStack() as st:
            Win, d_Win = sb(st, [128, 8, 3584], BF16)
            for k in range(8):
                S.op("gpsimd", lambda e, k=k: e.dma_start(out=Win[:, k, :], in_=w_in[k * 128:(k + 1) * 128, :]), writes=[d_Win], dma="p1_win%d" % k)
            ropeT, d_rope = sb(st, [128, 2, 64, 32], F32)
            load("sync", "p1_rope", ropeT[:], rope_d, d_rope)
            cw, d_cw = sb(st, [128, 12, 4], F32)
            for j in range(3):
                load("sync", "p1_cw", cw[:, :, j:j + 1], dap(conv_w, j * 1536, [(1, 128), (128, 12), (1, 1)]), d_cw, slow=True)
            load("sync", "p1_cw", cw[:, :, 3:4], dap(conv_b, 0, [(1, 128), (128, 12), (1, 1)]), d_cw, slow=True)

            xb = [sb(st, [128, D], F32) for _ in range(3)]
            junk, d_junk = sb(st, [128, D], BF16)
            ssq = [sb(st, [128, 1], F32) for _ in range(3)]
            xm = [sb(st, [128, D], BF16) for _ in range(2)]
            hxT = [sb(st, [128, 8, 512], BF16) for _ in range(2)]
            U = [sb(st, [128, 514], F32) for _ in range(12)]
            cv1 = [sb(st, [128, 512], F32) for _ in range(2)]
            cv2 = [sb(st, [128, 512], F32) for _ in range(2)]
            cvx1 = [sb(st, [128, 512], F32) for _ in range(4)]
            cv3, d_cv3 = sb(st, [128, 512], F32)
            hyo = [sb(st, [128, 512], BF16) for _ in range(4)]
            P1, d_P1 = sb(st, [128, 512], F32); P2, d_P2 = sb(st, [128, 512], F32)
            qo = [sb(st, [128, 512], BF16) for _ in range(2)]
            ko = [sb(st, [128, 512], BF16) for _ in range(2)]
            vo = [sb(st, [128, 512], BF16) for _ in range(2)]
            go = [sb(st, [128, 512], BF16) for _ in range(2)]
            kw, d_kw = sb(st, [128, 8, 2, 64], BF16)
            Tsb = [sb(st, [128, 512], F32) for _ in range(2)]
            pT = [ps(st, [128, 8, 128], BF16) for _ in range(2)]
            pU = [ps(st, [128, 512], F32) for _ in range(2)]
            pR = [ps(st, [128, 512], F32) for _ in range(2)]
            pTs = [ps(st, [128, 512], F32) for _ in range(2)]
            for ft in range(12):
                S.op("gpsimd", lambda e, ft=ft: e.memset(U[ft][0][:, 0:2], 0.0), writes=[U[ft][1]])

            srcs = [ctx[0:128, :], ctx[128:256, :]] + [x[i * 128:(i + 1) * 128, :] for i in range(NT)]

            def xload(s_):
                if s_ < len(srcs):
                    load("sync", "p1_x%d" % (s_ % 3), xb[s_ % 3][0][:], srcs[s_], xb[s_ % 3][1])

            xload(0)

            def norm_transpose(i, gsrow, d_gsrow, shcol_fn, d_shcol, hx_tile, d_hx, tok0):
                xload(i + 1)
                xt, d_xt = xb[i % 3]
                sq, d_sq = ssq[i % 3]
                S.op("scalar", lambda e: e.activation(out=junk[:], in_=xt[:], func=AF.Square, accum_out=sq[:]),
                     reads=[d_xt], writes=[d_junk, d_sq])
                S.op("scalar", lambda e: e.activation(out=sq[:], in_=sq[:], func=AF.Sqrt, scale=1.0 / D, bias=1e-6),
                     reads=[d_sq], writes=[d_sq])
                S.op("vector", lambda e: e.reciprocal(out=sq[:], in_=sq[:]), reads=[d_sq], writes=[d_sq])
                xmt, d_xmt = xm[i % 2]
                S.op("vector", lambda e: e.scalar_tensor_tensor(out=xmt[:], in0=xt[:], scalar=sq[:, 0:1], op0=ALU.mult,
                                                                in1=gsrow[:], op1=ALU.mult),
                     reads=[d_xt, d_sq, d_gsrow], writes=[d_xmt])
                pt, d_pt = pT[i % 2]
                for k in range(8):
                    S.op("tensor", lambda e, k=k: e.transpose(out=pt[:, k, :], in_=xmt[:, k * 128:(k + 1) * 128], identity=identb[:]),
                         reads=[d_xmt, d_identb], writes=[d_pt])
                for k in range(8):
                    S.op("scalar", lambda e, k=k: e.activation(out=hx_tile[:, k, tok0:tok0 + 128], in_=pt[:, k, :], func=AF.Identity,
                                                               bias=shcol_fn(k)),
                         reads=[d_pt, d_shcol], writes=[d_hx])

            def proj_tok(hx_tile, d_hx, tok0, col0, pr, d_pr):
                for k in range(8):
                    S.op("tensor", lambda e, k=k: e.matmul(pr[:], lhsT=hx_tile[:, k, tok0:tok0 + 128], rhs=Win[:, k, col0:col0 + 512],
                                                           start=(k == 0), stop=(k == 7)),
                         reads=[d_hx, d_Win], writes=[d_pr])

            hxc, d_hxc = hxT[0]
            kc = []; vc = []
            for tI in range(2):
                norm_transpose(tI, gs1c, d_gs1c, lambda k: colc[:, 0, k:k + 1], d_colc, hxc, d_hxc, tI * 128)
                pr, d_pr = pR[0]
                proj_tok(hxc, d_hxc, tI * 128, 1536 + 512, pr, d_pr)
                kt, d_kt = ko[tI]
                S.op("scalar", lambda e, kt=kt, pr=pr: e.mul(kt[:], pr[:], 0.125), reads=[d_pr], writes=[d_kt])
                pr2, d_pr2 = pR[1]
                proj_tok(hxc, d_hxc, tI * 128, 1536 + 1024, pr2, d_pr2)
                vt, d_vt = vo[tI]
                S.op("scalar", lambda e, vt=vt, pr2=pr2: e.copy(vt[:], pr2[:]), reads=[d_pr2], writes=[d_vt])
                kc.append((kt, d_kt)); vc.append((vt, d_vt))
            kwc = [sb(st, [128, 8, 2, 64], BF16) for _ in range(2)]
            for tI in range(2):
                kt, d_kt = kc[tI]
                kwt, d_kwt = kwc[tI]
                S.op("vector", lambda e, kt=kt, kwt=kwt, tI=tI: e.tensor_tensor(
                    out=kwt[:], in0=sap(kt, 0, 128, 0, [(64, 8), (0, 2), (1, 64)]),
                    in1=sap(Wkc, 0, 128, tI * 16, [(2, 8), (1, 2), (0, 64)]), op=ALU.mult),
                    reads=[d_kt, d_Wkc], writes=[d_kwt])
            pS0, d_pS0 = pTs[0]
            for h in range(8):
                for tI in range(2):
                    S.op("tensor", lambda e, h=h, tI=tI: e.matmul(pS0[:, h * 64:(h + 1) * 64], lhsT=kwc[tI][0][:, h, :, :],
                                                                 rhs=vc[tI][0][:, h * 64:(h + 1) * 64], start=(tI == 0), stop=(tI == 1)),
                         reads=[kwc[tI][1], vc[tI][1]], writes=[d_pS0])
            S.op("vector", lambda e: e.tensor_copy(out=S0[:], in_=pS0[:]), reads=[d_pS0], writes=[d_S0])

            for i in range(NT):
                B, ii = divmod(i, 4)
                hx_tile, d_hx = hxT[B % 2]
                norm_transpose(i + 2, gs1, d_gs1, lambda k: colx[:, 0, k:k + 1], d_colx, hx_tile, d_hx, ii * 128)
                for cbk in range(4):
                    pr, d_pr = pR[cbk % 2]
                    proj_tok(hx_tile, d_hx, ii * 128, 1536 + cbk * 512, pr, d_pr)
                    if cbk < 2:
                        ot, d_ot = (qo if cbk == 0 else ko)[i % 2]
                        S.op("vector", lambda e, pr=pr, i=i: e.tensor_tensor(
                            out=P1[:].rearrange("p (h j t) -> p h j t", h=8, t=2), in0=pr[:].rearrange("p (h j t) -> p h j t", h=8, t=2),
                            in1=sap(ropeT, 0, 128, (0 * 64 + i) * 32, [(0, 8), (1, 32), (0, 2)]), op=ALU.mult),
                            reads=[d_pr, d_rope], writes=[d_P1])
                        S.op("vector", lambda e, pr=pr, i=i: e.tensor_tensor(
                            out=P2[:].rearrange("p (h j t) -> p h j t", h=8, t=2), in0=pr[:].rearrange("p (h j t) -> p h j t", h=8, t=2),
                            in1=sap(ropeT, 0, 128, (1 * 64 + i) * 32, [(0, 8), (1, 32), (0, 2)]), op=ALU.mult),
                            reads=[d_pr, d_rope], writes=[d_P2])
                        S.op("gpsimd", lambda e, ot=ot: e.tensor_tensor(out=sap(ot, 0, 128, 0, [(2, 256)]), in0=sap(P1, 0, 128, 0, [(2, 256)]),
                                                                       in1=sap(P2, 0, 128, 1, [(2, 256)]), op=ALU.subtract),
                             reads=[d_P1, d_P2], writes=[d_ot])
                        S.op("gpsimd", lambda e, ot=ot: e.tensor_tensor(out=sap(ot, 0, 128, 1, [(2, 256)]), in0=sap(P2, 0, 128, 0, [(2, 256)]),
                                                                       in1=sap(P1, 0, 128, 1, [(2, 256)]), op=ALU.add),
                             reads=[d_P1, d_P2], writes=[d_ot])
                        scr = qscr if cbk == 0 else kscr
                        store("sync", ("p1_q%d" if cbk == 0 else "p1_k%d") % (i % 2), scr[i * 128:(i + 1) * 128, :], ot[:], d_ot)
                        if cbk == 1:
                            S.op("vector", lambda e, ot=ot: e.tensor_tensor(
                                out=kw[:], in0=sap(ot, 0, 128, 0, [(64, 8), (0, 2), (1, 64)]),
                                in1=sap(Wk, 0, 128, 0, [(2, 8), (1, 2), (0, 64)]), op=ALU.mult),
                                reads=[d_ot, d_Wk], writes=[d_kw])
                    elif cbk == 2:
                        vt, d_vt = vo[i % 2]
                        S.op("scalar", lambda e, vt=vt, pr=pr: e.copy(vt[:], pr[:]), reads=[d_pr], writes=[d_vt])
                        store("sync", "p1_v%d" % (i % 2), vscr[i * 128:(i + 1) * 128, :], vt[:], d_vt)
                    else:
                        gt, d_gt = go[i % 2]
                        S.op("scalar", lambda e, gt=gt, pr=pr: e.activation(out=gt[:], in_=pr[:], func=AF.Silu), reads=[d_pr], writes=[d_gt])
                        store("sync", "p1_g%d" % (i % 2), gscr[i * 128:(i + 1) * 128, :], gt[:], d_gt)
                pts, d_pts = pTs[i % 2]
                vt, d_vt = vo[i % 2]
                for h in range(8):
                    S.op("tensor", lambda e, h=h, pts=pts, vt=vt: e.matmul(pts[:, h * 64:(h + 1) * 64], lhsT=kw[:, h, :, :],
                                                                          rhs=vt[:, h * 64:(h + 1) * 64], start=True, stop=True),
                         reads=[d_kw, d_vt], writes=[d_pts])
                tsb, d_tsb = Tsb[i % 2]
                S.op("scalar", lambda e, tsb=tsb, pts=pts: e.copy(tsb[:], pts[:]), reads=[d_pts], writes=[d_tsb])
                store("sync", "p1_T%d" % (i % 2), Tscr[i], tsb[:], d_tsb)
                if ii == 3:
                    s0 = 1 if B == 0 else 0
                    tok_lo = 512 * B - 1 + s0
                    for ft in (4, 5, 6, 7, 8, 9, 10, 11, 0, 1, 2, 3):
                        pu, d_pu = pU[ft % 2]
                        for k in range(8):
                            S.op("tensor", lambda e, k=k, ft=ft, pu=pu, hx_tile=hx_tile: e.matmul(pu[:], lhsT=Win[:, k, ft * 128:(ft + 1) * 128], rhs=hx_tile[:, k, :],
                                                                                start=(k == 0), stop=(k == 7)),
                                 reads=[d_Win, d_hx], writes=[d_pu])
                        u, d_u = U[ft]
                        S.op("scalar", lambda e, u=u, pu=pu: e.copy(u[:, 2:514], pu[:]), reads=[d_pu], writes=[d_u])
                        c1, d_c1 = cv1[ft % 2]; c2, d_c2 = cv2[ft % 2]
                        S.op("scalar", lambda e, u=u, c1=c1, ft=ft: e.activation(out=c1[:], in_=u[:, 0:512], func=AF.Identity, scale=cw[:, ft, 0:1], bias=cw[:, ft, 3:4]),
                             reads=[d_u, d_cw], writes=[d_c1])
                        S.op("vector", lambda e, u=u, c1=c1, c2=c2, ft=ft: e.scalar_tensor_tensor(out=c2[:], in0=u[:, 1:513], scalar=cw[:, ft, 1:2], op0=ALU.mult,
                                                                                                 in1=c1[:], op1=ALU.add),
                             reads=[d_u, d_cw, d_c1], writes=[d_c2])
                        ct = ft % 4
                        if ft < 4:
                            ho, d_ho = hyo[ct]
                            S.op("vector", lambda e, u=u, c2=c2, ho=ho, ft=ft: e.scalar_tensor_tensor(out=ho[:], in0=u[:, 2:514], scalar=cw[:, ft, 2:3], op0=ALU.mult,
                                                                                                     in1=c2[:], op1=ALU.add),
                                 reads=[d_u, d_cw, d_c2], writes=[d_ho])
                            store("sync", "p1_hy%d" % ct, x0scr[ct * 128:(ct + 1) * 128, tok_lo:512 * B + 511], ho[:, s0:512], d_ho)
                        elif ft < 8:
                            cx, d_cx = cvx1[ct]
                            S.op("vector", lambda e, u=u, c2=c2, cx=cx, ft=ft: e.scalar_tensor_tensor(out=cx[:], in0=u[:, 2:514], scalar=cw[:, ft, 2:3], op0=ALU.mult,
                                                                                                     in1=c2[:], op1=ALU.add),
                                 reads=[d_u, d_cw, d_c2], writes=[d_cx])
                        else:
                            cx, d_cx = cvx1[ct]
                            S.op("vector", lambda e, u=u, c2=c2, ft=ft: e.scalar_tensor_tensor(out=cv3[:], in0=u[:, 2:514], scalar=cw[:, ft, 2:3], op0=ALU.mult,
                                                                                              in1=c2[:], op1=ALU.add),
                                 reads=[d_u, d_cw, d_c2], writes=[d_cv3])
                            ho, d_ho = hyo[ct]
                            S.op("gpsimd", lambda e, cx=cx, ho=ho: e.tensor_tensor(out=ho[:], in0=cv3[:], in1=cx[:], op=ALU.mult),
                                 reads=[d_cv3, d_cx], writes=[d_ho])
                            store("sync", "p1_hy%d" % ct, vxscr[ct * 128:(ct + 1) * 128, tok_lo:512 * B + 511], ho[:, s0:512], d_ho)
                        S.op("scalar", lambda e, u=u: e.copy(u[:, 0:2], u[:, 512:514]), reads=[d_u], writes=[d_u])
            tl, d_tl = sb(st, [128, 12], F32)
            tlb, d_tlb = sb(st, [128, 8], BF16)
            for ft in range(12):
                u, d_u = U[ft]
                S.op("vector", lambda e, u=u, ft=ft: e.tensor_scalar(out=tl[:, ft:ft + 1], in0=u[:, 0:1], scalar1=cw[:, ft, 0:1], scalar2=cw[:, ft, 3:4],
                                                                    op0=ALU.mult, op1=ALU.add), reads=[d_u, d_cw], writes=[d_tl])
                S.op("vector", lambda e, u=u, ft=ft: e.scalar_tensor_tensor(out=tl[:, ft:ft + 1], in0=u[:, 1:2], scalar=cw[:, ft, 1:2], op0=ALU.mult,
                                                                           in1=tl[:, ft:ft + 1], op1=ALU.add), reads=[d_u, d_cw, d_tl], writes=[d_tl])
            S.op("vector", lambda e: e.tensor_copy(out=tlb[:, 0:4], in_=tl[:, 0:4]), reads=[d_tl], writes=[d_tlb])
            S.op("vector", lambda e: e.tensor_tensor(out=tlb[:, 4:8], in0=tl[:, 4:8], in1=tl[:, 8:12], op=ALU.mult), reads=[d_tl], writes=[d_tlb])
            for ct in range(4):
                store("sync", "p1_tl%d" % ct, dap(x0scr, ct * 128 * L + L - 1, [(L, 128), (1, 1)]), tlb[:, ct:ct + 1], d_tlb, slow=True)
                store("sync", "p1_tv%d" % ct, dap(vxscr, ct * 128 * L + L - 1, [(L, 128), (1, 1)]), tlb[:, 4 + ct:5 + ct], d_tlb, slow=True)
            S.barrier()
        a1st.close()
        if stop_after <= 1:
            S.emit()
            return nc
        fft_phase("data")
        if stop_after <= 2:
            S.emit()
            return nc
        if stop_after <= 3:
            S.emit()
            return nc

        with contextlib.ExitStack() as st:
            Wo, d_Wo = sb(st, [128, 8, D], BF16)
            W1, d_W1 = sb(st, [128, 8, 4 * D], BF16)
            W2, d_W2 = sb(st, [128, 32, D], BF16)
            for k in range(8):
                S.op("gpsimd", lambda e, k=k: e.dma_start(out=Wo[:, k, :], in_=w_out[k * 128:(k + 1) * 128, :]), writes=[d_Wo], dma="w_o%d" % (k % 4))
            for k in range(8):
                S.op("gpsimd", lambda e, k=k: e.dma_start(out=W1[:, k, :], in_=w_mlp1[k * 128:(k + 1) * 128, :]), writes=[d_W1], dma="w_1%d" % (k % 4))
            for k4 in range(8):
                S.op("gpsimd", lambda e, k4=k4: e.dma_start(out=W2[:, k4 * 4:(k4 + 1) * 4, :], in_=w_mlp2[k4 * 512:(k4 + 1) * 512, :].rearrange("(k p) n -> p k n", p=128)),
                     writes=[d_W2], dma="w_2")
            hbc, d_hbc = sb(st, [128, 4], F32)
            load("sync", "r_hb", hbc[:].rearrange("p (c o) -> p c o", o=1), dap(hy_bias, 0, [(1, 128), (128, 4), (1, 1)]), d_hbc, slow=True)
            gnr, d_gnr = sb(st, [128, 512], F32)
            load("sync", "r_gn", gnr[:], row_bc(gn_g, 0, 512), d_gnr)
            banks = [ps(st, [128, 512], F32) for _ in range(4)]
            pmb = [ps(st, [128, 512], F32) for _ in range(2)]
            bbanks = [ps(st, [128, 1024], BF16) for _ in range(2)]
            bctr = [0, 0]

            def nb():
                b_ = banks[bctr[0] % 4]
                bctr[0] += 1
                return b_

            def nbb():
                b_ = bbanks[bctr[1] % 2]
                bctr[1] += 1
                return b_

            qkvg = [[sb(st, [128, 512], BF16) for _ in range(5)] for _ in range(1)]
            scrs = [qscr, kscr, vscr, gscr]
            hy3 = [sb(st, [128, 4, 128], BF16) for _ in range(3)]
            qx, d_qx = sb(st, [128, 8, 2, 64], BF16)
            qT, d_qT = sb(st, [128, 4, 128], BF16); kT, d_kT = sb(st, [128, 4, 128], BF16)
            qxT, d_qxT = sb(st, [128, 8, 128], BF16)
            d_Pm = d_qx
            osb, d_osb = sb(st, [128, 512], F32); osq, d_osq = sb(st, [128, 512], F32)
            st8, d_st8 = sb(st, [128, 4, 8], F32)
            yret, d_yret = sb(st, [128, 512], BF16)
            mixT, d_mixT = sb(st, [128, 8, 128], BF16)
            xnb = [sb(st, [128, D], F32) for _ in range(2)]
            ssA, d_ssA = sb(st, [128, 1], F32); ssB, d_ssB = sb(st, [128, 1], F32)
            d_xm2 = d_qxT
            hxb = [sb(st, [128, 8, 128], BF16) for _ in range(2)]
            rlb, d_rlb = sb(st, [128, 512], F32); tb, d_tb = rlb, d_rlb
            hT, d_hT = sb(st, [128, 8, 128], BF16)

            def loads(i):
                if i >= NT:
                    return
                bufs = qkvg[0]
                for j_ in range(4):
                    load("sync", "r_in%d_%d" % (0, j_), bufs[j_][0][:], scrs[j_][i * 128:(i + 1) * 128, :], bufs[j_][1])
                load("sync", "r_in%d_4" % (0), bufs[4][0][:], Sscr[i], bufs[4][1])

            def tileA(i):
                xn, d_xn = xnb[i % 2]
                hx2T, d_hx2T = hxb[i % 2]
                (qt, d_qt), (kt, d_kt), (vt, d_vt), (gt, d_gt), (St_, d_St) = qkvg[0]
                load("sync", "r_x%d" % (i % 2), xn[:], x[i * 128:(i + 1) * 128, :], d_xn)
                for j_, scr_ in enumerate((yscr, vxscr, x0scr)):
                    load("sync", "r_hy%d" % j_, hy3[j_][0][:], dap(scr_, i * 128, [(L, 128), (128 * L, 4), (1, 128)]), hy3[j_][1])
                S.op("vector", lambda e: e.tensor_tensor(out=qx[:], in0=sap(qt, 0, 128, 0, [(64, 8), (0, 2), (1, 64)]),
                                                         in1=sap(Wq, 0, 128, 0, [(2, 8), (1, 2), (0, 64)]), op=ALU.mult), reads=[d_qt, d_Wq], writes=[d_qx])
                pq, d_pq = nbb()
                for hp in range(4):
                    S.op("tensor", lambda e, hp=hp: e.transpose(out=pq[:, hp * 128:(hp + 1) * 128], in_=qt[:, hp * 128:(hp + 1) * 128], identity=identb[:]),
                         reads=[d_qt, d_identb], writes=[d_pq])
                for hp in range(4):
                    S.op("tensor", lambda e, hp=hp: e.transpose(out=pq[:, 512 + hp * 128:512 + (hp + 1) * 128], in_=kt[:, hp * 128:(hp + 1) * 128], identity=identb[:]),
                         reads=[d_kt, d_identb], writes=[d_pq])
                S.op("scalar", lambda e: e.copy(qT[:].rearrange("p a b -> p (a b)"), pq[:, 0:512]), reads=[d_pq], writes=[d_qT])
                S.op("scalar", lambda e: e.copy(kT[:].rearrange("p a b -> p (a b)"), pq[:, 512:1024]), reads=[d_pq], writes=[d_kT])
                yield
                px, d_px = nbb()
                for h in range(8):
                    S.op("tensor", lambda e, h=h: e.transpose(out=px[:, h * 128:(h + 1) * 128], in_=qx[:, h, :, :], identity=identb[:]),
                         reads=[d_qx, d_identb], writes=[d_px])
                S.op("scalar", lambda e: e.copy(qxT[:].rearrange("p a b -> p (a b)"), px[:]), reads=[d_px], writes=[d_qxT])
                yield
                for par in range(2):
                    psc, d_psc = nb()
                    b0 = par * 64
                    for hh in range(4):
                        h = 2 * hh + par
                        S.op("tensor", lambda e, hh=hh, b0=b0, psc=psc: e.matmul(psc[:, hh * 128:(hh + 1) * 128], lhsT=kT[b0:b0 + 64, hh, :],
                                                                                rhs=qT[b0:b0 + 64, hh, :], start=True, stop=True),
                             reads=[d_kT, d_qT], writes=[d_psc])
                    S.op("vector", lambda e, par=par, psc=psc: e.tensor_tensor(out=sap(qx, 0, 128, par * 128, [(256, 4), (1, 128)]), in0=psc[:].rearrange("p (a b) -> p a b", a=4),
                                                                              in1=sap(DT, 0, 128, par * 128, [(256, 4), (1, 128)]), op=ALU.mult),
                         reads=[d_psc, d_DT], writes=[d_Pm])
                yield
                po, d_po = nb()
                for h in range(8):
                    S.op("tensor", lambda e, h=h: e.matmul(po[:, h * 64:(h + 1) * 64], lhsT=sap(qx, 0, 128, h * 128, [(1, 128)]), rhs=vt[:, h * 64:(h + 1) * 64], start=True, stop=False),
                         reads=[d_Pm, d_vt], writes=[d_po])
                    S.op("tensor", lambda e, h=h: e.matmul(po[:, h * 64:(h + 1) * 64], lhsT=qxT[:, h, :], rhs=St_[:, h * 64:(h + 1) * 64], start=False, stop=True),
                         reads=[d_qxT, d_St], writes=[d_po])
                yield
                S.op("scalar", lambda e: e.copy(osb[:], po[:]), reads=[d_po], writes=[d_osb])
                S.op("scalar", lambda e: e.activation(out=osq[:], in_=po[:], func=AF.Square), reads=[d_po], writes=[d_osq])
                S.op("vector", lambda e: e.tensor_reduce(out=st8[:, 0, :], in_=osb[:].rearrange("p (h x) -> p h x", h=8), op=ALU.add, axis=AX.X), reads=[d_osb], writes=[d_st8])
                S.op("vector", lambda e: e.tensor_reduce(out=st8[:, 1, :], in_=osq[:].rearrange("p (h x) -> p h x", h=8), op=ALU.add, axis=AX.X), reads=[d_osq], writes=[d_st8])
                S.op("vector", lambda e: e.tensor_scalar(out=st8[:, 0, :], in0=st8[:, 0, :], scalar1=1.0 / 64, scalar2=None, op0=ALU.mult), reads=[d_st8], writes=[d_st8])
                S.op("vector", lambda e: e.tensor_tensor(out=st8[:, 2, :], in0=st8[:, 0, :], in1=st8[:, 0, :], op=ALU.mult), reads=[d_st8], writes=[d_st8])
                S.op("vector", lambda e: e.scalar_tensor_tensor(out=st8[:, 3, :], in0=st8[:, 1, :], scalar=1.0 / 64, op0=ALU.mult, in1=st8[:, 2, :], op1=ALU.subtract),
                     reads=[d_st8], writes=[d_st8])
                S.op("scalar", lambda e: e.activation(out=st8[:, 3, :], in_=st8[:, 3, :], func=AF.Sqrt, bias=1e-6), reads=[d_st8], writes=[d_st8])
                S.op("vector", lambda e: e.reciprocal(out=st8[:, 3, :], in_=st8[:, 3, :]), reads=[d_st8], writes=[d_st8])
                S.op("vector", lambda e: e.tensor_tensor(out=osb[:].rearrange("p (h x) -> p h x", h=8), in0=osb[:].rearrange("p (h x) -> p h x", h=8),
                                                         in1=sap(st8, 0, 128, 0, [(1, 8), (0, 64)]), op=ALU.subtract), reads=[d_osb, d_st8], writes=[d_osb])
                S.op("vector", lambda e: e.tensor_tensor(out=osb[:].rearrange("p (h x) -> p h x", h=8), in0=osb[:].rearrange("p (h x) -> p h x", h=8),
                                                         in1=sap(st8, 0, 128, 24, [(1, 8), (0, 64)]), op=ALU.mult), reads=[d_osb, d_st8], writes=[d_osb])
                S.op("gpsimd", lambda e: e.tensor_tensor(out=osb[:], in0=osb[:], in1=gnr[:], op=ALU.mult), reads=[d_osb, d_gnr], writes=[d_osb])
                S.op("gpsimd", lambda e: e.tensor_tensor(out=yret[:], in0=osb[:], in1=gt[:], op=ALU.mult), reads=[d_osb, d_gt], writes=[d_yret])
                loads(i + 1)
                yield
                yield
                py_, d_py = nbb()
                for hp in range(4):
                    S.op("tensor", lambda e, hp=hp: e.transpose(out=py_[:, hp * 128:(hp + 1) * 128], in_=yret[:, hp * 128:(hp + 1) * 128], identity=identb[:]),
                         reads=[d_yret, d_identb], writes=[d_py])
                S.op("scalar", lambda e: e.copy(mixT[:, 4:8, :].rearrange("p a b -> p (a b)"), py_[:, 0:512]), reads=[d_py], writes=[d_mixT])
                (yc, d_yc), (vxt, d_vxt), (x0t, d_x0t) = hy3
                for ct in range(4):
                    S.op("vector", lambda e, ct=ct: e.scalar_tensor_tensor(out=osq[:, ct * 128:(ct + 1) * 128], in0=vxt[:, ct, :], scalar=hbc[:, ct:ct + 1], op0=ALU.mult,
                                                                          in1=yc[:, ct, :], op1=ALU.add), reads=[d_vxt, d_yc, d_hbc, d_osq], writes=[d_osq])
                S.op("gpsimd", lambda e: e.tensor_tensor(out=mixT[:, 0:4, :].rearrange("p a b -> p (a b)"), in0=osq[:], in1=x0t[:].rearrange("p a b -> p (a b)"), op=ALU.mult),
                     reads=[d_osq, d_x0t], writes=[d_mixT])
                yield
                for nb_ in range(2):
                    pw, d_pw = nb()
                    for k in range(8):
                        S.op("tensor", lambda e, k=k, nb_=nb_, pw=pw: e.matmul(pw[:], lhsT=mixT[:, k, :], rhs=Wo[:, k, nb_ * 512:(nb_ + 1) * 512], start=(k == 0), stop=(k == 7)),
                             reads=[d_mixT, d_Wo], writes=[d_pw])
                    S.op("vector", lambda e, nb_=nb_, pw=pw: e.tensor_tensor(out=osq[:], in0=pw[:], in1=gate2[:, nb_ * 512:(nb_ + 1) * 512], op=ALU.mult),
                         reads=[d_pw, d_gate2], writes=[d_osq])
                    S.op("gpsimd", lambda e, nb_=nb_: e.tensor_tensor(out=xn[:, nb_ * 512:(nb_ + 1) * 512], in0=xn[:, nb_ * 512:(nb_ + 1) * 512], in1=osq[:], op=ALU.add),
                         reads=[d_xn, d_osq], writes=[d_xn])
                yield
                S.op("scalar", lambda e: e.activation(out=qxT[:].rearrange("p a b -> p (a b)"), in_=xn[:], func=AF.Square, accum_out=ssA[:, 0:1]), reads=[d_xn], writes=[d_xm2, d_ssA])
                S.op("scalar", lambda e: e.activation(out=ssA[:, 0:1], in_=ssA[:, 0:1], func=AF.Sqrt, scale=1.0 / D, bias=1e-6), reads=[d_ssA], writes=[d_ssA])
                S.op("vector", lambda e: e.reciprocal(out=ssA[:, 0:1], in_=ssA[:, 0:1]), reads=[d_ssA], writes=[d_ssA])
                S.op("vector", lambda e: e.scalar_tensor_tensor(out=qxT[:].rearrange("p a b -> p (a b)"), in0=xn[:], scalar=ssA[:, 0:1], op0=ALU.mult, in1=gs2[:], op1=ALU.mult),
                     reads=[d_xn, d_ssA, d_gs2], writes=[d_xm2])
                yield
                yield
                pt2, d_pt2 = nbb()
                for k in range(8):
                    S.op("tensor", lambda e, k=k: e.transpose(out=pt2[:, k * 128:(k + 1) * 128], in_=qxT[:, k, :], identity=identb[:]),
                         reads=[d_xm2, d_identb], writes=[d_pt2])
                for k in range(8):
                    S.op("scalar", lambda e, k=k: e.activation(out=hx2T[:, k, :], in_=pt2[:, k * 128:(k + 1) * 128], func=AF.Identity, bias=colx[:, 3, k:k + 1]),
                         reads=[d_pt2, d_colx], writes=[d_hx2T])

            def tileB(i):
                xn, d_xn = xnb[i % 2]
                hx2T, d_hx2T = hxb[i % 2]
                for hf in range(4):
                    for f4 in range(2):
                        ph, d_ph = nb()
                        for ff in range(4):
                            ft = hf * 8 + f4 * 4 + ff
                            for k in range(8):
                                S.op("tensor", lambda e, k=k, ft=ft, ff=ff, ph=ph: e.matmul(ph[:, ff * 128:(ff + 1) * 128], lhsT=W1[:, k, ft * 128:(ft + 1) * 128], rhs=hx2T[:, k, :],
                                                                                           start=(k == 0), stop=(k == 7)), reads=[d_W1, d_hx2T], writes=[d_ph])
                        S.op("scalar", lambda e, ph=ph: e.activation(out=rlb[:], in_=ph[:], func=AF.Relu), reads=[d_ph], writes=[d_rlb])
                        yield
                        S.op("gpsimd", lambda e, f4=f4: e.tensor_tensor(out=hT[:, f4 * 4:(f4 + 1) * 4, :].rearrange("p a b -> p (a b)"), in0=rlb[:], in1=rlb[:], op=ALU.mult),
                             reads=[d_rlb], writes=[d_hT])
                    for nb_ in range(2):
                        pm, d_pm = pmb[nb_]
                        for kk in range(8):
                            k = hf * 8 + kk
                            S.op("tensor", lambda e, k=k, kk=kk, nb_=nb_, pm=pm: e.matmul(pm[:], lhsT=hT[:, kk, :], rhs=W2[:, k, nb_ * 512:(nb_ + 1) * 512], start=(k == 0), stop=(k == 31)),
                                 reads=[d_hT, d_W2], writes=[d_pm])
                        yield
                for nb_ in range(2):
                    pm, d_pm = pmb[nb_]
                    S.op("vector", lambda e, nb_=nb_, pm=pm: e.tensor_tensor(out=tb[:], in0=pm[:], in1=gate5[:, nb_ * 512:(nb_ + 1) * 512], op=ALU.mult),
                         reads=[d_pm, d_gate5], writes=[d_tb])
                    S.op("gpsimd", lambda e, nb_=nb_: e.tensor_tensor(out=xn[:, nb_ * 512:(nb_ + 1) * 512], in0=xn[:, nb_ * 512:(nb_ + 1) * 512], in1=tb[:], op=ALU.add),
                         reads=[d_xn, d_tb], writes=[d_xn])
                S.op("scalar", lambda e: e.activation(out=hT[:, 0:8, :].rearrange("p a b -> p (a b)"), in_=xn[:], func=AF.Square, accum_out=ssB[:, 0:1]), reads=[d_xn], writes=[d_hT, d_ssB])
                S.op("scalar", lambda e: e.activation(out=ssB[:, 0:1], in_=ssB[:, 0:1], func=AF.Sqrt, scale=1.0 / D, bias=1e-6), reads=[d_ssB], writes=[d_ssB])
                S.op("vector", lambda e: e.reciprocal(out=ssB[:, 0:1], in_=ssB[:, 0:1]), reads=[d_ssB], writes=[d_ssB])
                S.op("vector", lambda e: e.scalar_tensor_tensor(out=xn[:], in0=xn[:], scalar=ssB[:, 0:1], op0=ALU.mult, in1=gF[:], op1=ALU.mult),
                     reads=[d_xn, d_ssB, d_gF], writes=[d_xn])
                final_events.append(store("sync", "r_out%d" % (i % 2), out[i * 128:(i + 1) * 128, :], xn[:], d_xn))


            NTL = NT if stop_after >= 99 else 2
            loads(0)
            for _ in tileA(0):
                pass
            for i in range(NTL):
                gB = tileB(i)
                gA = tileA(i + 1) if i + 1 < NTL else None
                doneA = gA is None
                doneB = False
                while not (doneA and doneB):
                    if not doneB:
                        try:
                            next(gB)
                        except StopIteration:
                            doneB = True
                    if not doneA:
                        try:
                            next(gA)
                        except StopIteration:
                            doneA = True
            S.barrier()
        S.emit()
    return nc


def make_in_map(inputs, b):
    f = lambda a: np.ascontiguousarray(np.asarray(a, dtype=np.float32))
    c = host_consts()
    m = dict(
        x=f(inputs["x"][b]), ctx=f(inputs["ctx"][b]),
        cc=f(np.stack([np.asarray(inputs["c"][b]), np.asarray(inputs["c_ctx"])], axis=0)),
        w_ada=f(inputs["w_ada"][0]), b_ada=f(inputs["b_ada"][0]).reshape(1, -1), norm1_g=f(inputs["norm1_g"][0]).reshape(1, -1),
        w_in=f(inputs["w_in"][0]), hy_conv_w=f(inputs["hy_conv_w"][0]), hy_conv_b=f(inputs["hy_conv_b"][0]).reshape(1, -1),
        hy_f_w1=f(inputs["hy_f_w1"][0]), hy_f_b1=f(inputs["hy_f_b1"][0]).reshape(1, -1), hy_f_freq1=f(inputs["hy_f_freq1"][0]).reshape(1, -1),
        hy_f_w2=f(inputs["hy_f_w2"][0]), hy_f_b2=f(inputs["hy_f_b2"][0]).reshape(1, -1), hy_f_freq2=f(inputs["hy_f_freq2"][0]).reshape(1, -1),
        w3loc=f(np.concatenate([np.asarray(inputs["hy_f_w3"][0])[:, b * 64:(b + 1) * 64],
                                np.asarray(inputs["hy_f_w3"][0])[:, 512 + b * 64:512 + (b + 1) * 64]], axis=1)),
        hy_bias=f(inputs["hy_bias"][0]).reshape(1, -1),
        ret_decay_logit=f(inputs["ret_decay_logit"][0]).reshape(1, 16), ret_gn_g=f(inputs["ret_gn_g"][0]).reshape(1, -1),
        w_out=f(inputs["w_out"][0]), norm2_g=f(inputs["norm2_g"][0]).reshape(1, -1),
        w_mlp1=f(inputs["w_mlp1"][0]), w_mlp2=f(inputs["w_mlp2"][0]), norm_f_g=f(inputs["norm_f_g"]).reshape(1, -1),
    )
    m.update(c)
    m["edec"] = np.ascontiguousarray(c["edec"][b])
    return m


_NC = None


def kernel(**inputs):
    global _NC
    if _NC is None:
        _NC = build()
    in_maps = [make_in_map(inputs, b) for b in range(8)]
    res = run_bass_kernel_spmd(_NC, in_maps, core_ids=list(range(8)))
    return np.stack([np.asarray(r["out"], dtype=np.float32) for r in res.results], axis=0)
```

```python
import contextlib
import math
import numpy as np
import ml_dtypes
import concourse.bass as bass
import concourse.mybir as mybir
from concourse.bass_utils import run_bass_kernel_spmd

F32 = mybir.dt.float32
BF16 = mybir.dt.bfloat16
AF = mybir.ActivationFunctionType
ALU = mybir.AluOpType
AX = mybir.AxisListType

L = 8192
D = 1024
NT = 64
NFFT = 16384
ENGS = ("sync", "scalar", "vector", "gpsimd", "tensor")


class Dep:
    __slots__ = ("w", "r")

    def __init__(self):
        self.w = None
        self.r = []


class Sched:
    def __init__(self, nc, stack):
        self.nc = nc
        self.stack = stack
        self.streams = {e: [] for e in ENGS}
        self.esem = {}
        self.ecnt = {}
        for e in ("scalar", "vector", "gpsimd", "tensor"):
            self.esem[e] = stack.enter_context(nc.semaphore("es_" + e))
            self.ecnt[e] = 0
        self.dsem = {}
        self.dpool = []
        self.gsems = []
        self.nds = 0
        self.waited = {e: {} for e in ENGS}

    def _wait(self, eng, ev, waits):
        if ev is None:
            return
        sem, val, src = ev
        if eng == "tensor" and src == "tensor":
            return
        key = id(sem)
        if self.waited[eng].get(key, 0) >= val:
            return
        self.waited[eng][key] = val
        waits.append((sem, val))

    def op(self, eng, fn, reads=(), writes=(), dma=None, cc=False):
        waits = []
        for d in reads:
            self._wait(eng, d.w, waits)
        for d in writes:
            self._wait(eng, d.w, waits)
            for ev in d.r:
                self._wait(eng, ev, waits)
        if cc:
            self.nds += 1
            sem_ = self.stack.enter_context(self.nc.semaphore("cc%d" % self.nds))
            ev = (sem_, 1, "dma")
            inc = (sem_, None)
        elif dma is not None and eng == "gpsimd":
            self.nds += 1
            ent = [self.stack.enter_context(self.nc.semaphore("gs%d" % self.nds)), 16]
            self.gsems.append(ent)
            ev = (ent[0], 16, "dma")
            inc = (ent[0], 16)
        elif dma is not None:
            if dma not in self.dsem:
                if self.dpool:
                    self.dsem[dma] = self.dpool.pop()
                else:
                    self.nds += 1
                    self.dsem[dma] = [self.stack.enter_context(self.nc.semaphore("ds%d" % self.nds)), 0]
            ent = self.dsem[dma]
            ent[1] += 16
            ev = (ent[0], ent[1], "dma")
            inc = (ent[0], 16)
        else:
            self.ecnt[eng] += 1
            ev = (self.esem[eng], self.ecnt[eng], eng)
            inc = (self.esem[eng], 1)
        for d in reads:
            d.r.append(ev)
        for d in writes:
            d.w = ev
            d.r = []
        self.streams[eng].append((waits, fn, inc))
        return ev

    def barrier(self):
        evs = [(self.esem[e], self.ecnt[e], "x") for e in self.esem if self.ecnt[e] > 0]
        evs += [(v[0], v[1], "dma") for v in self.dsem.values() if v[1] > 0]
        evs += [(v[0], v[1], "dma") for v in self.gsems]
        for eng in ENGS:
            waits = []
            for ev in evs:
                key = id(ev[0])
                if self.waited[eng].get(key, 0) >= ev[1]:
                    continue
                self.waited[eng][key] = ev[1]
                waits.append((ev[0], ev[1]))
            if waits:
                self.streams[eng].append((waits, None, None))
        self.dpool.extend(self.dsem.values())
        self.dsem = {}

    def emit(self):
        nc = self.nc
        streams = self.streams

        def run(name, eng):
            for waits, fn, inc in streams[name]:
                for sem, val in waits:
                    eng.wait_ge(sem, val)
                if fn is not None:
                    if inc[1] is None:
                        fn(eng).then_inc(inc[0])
                    else:
                        fn(eng).then_inc(inc[0], inc[1])

        with nc.Block() as block:
            @block.sync
            def _(e):
                run("sync", e)

            @block.scalar
            def _(e):
                run("scalar", e)

            @block.vector
            def _(e):
                run("vector", e)

            @block.gpsimd
            def _(e):
                run("gpsimd", e)

            @block.tensor
            def _(e):
                run("tensor", e)


def sap(t, p0, pn, f0, dims):
    shp = list(t.shape)
    Fsz = int(np.prod(shp[1:]))
    return bass.AP(t, p0 * Fsz + f0, [[Fsz, pn]] + [[int(s), int(c)] for s, c in dims])


def dap(t, off, dims):
    return bass.AP(t.tensor, int(off), [[int(s), int(c)] for s, c in dims])


def _bf(a):
    return np.ascontiguousarray(a.astype(np.float32)).astype(ml_dtypes.bfloat16)


_CONSTS = None


def host_consts():
    global _CONSTS
    if _CONSTS is not None:
        return _CONSTS
    n1 = np.arange(64, dtype=np.float64)[:, None]
    k1 = np.arange(64, dtype=np.float64)[None, :]
    th1 = 2 * np.pi * n1 * (k1 + 0.5) / 128.0
    M1 = np.zeros((128, 128)); M1[:64, :64] = np.cos(th1); M1[:64, 64:] = -np.sin(th1)
    M1c = np.zeros((128, 128)); M1c[:64, :64] = np.cos(th1); M1c[:64, 64:] = np.sin(th1)
    n2 = np.arange(128, dtype=np.float64)[:, None]
    tht = 2 * np.pi * n2 * (k1 + 0.5) / NFFT
    k2 = np.arange(128, dtype=np.float64)[None, :]
    th2 = 2 * np.pi * n2 * k2 / 128.0
    C2 = np.cos(th2); S2 = np.sin(th2)
    sc = 2.0 / NFFT
    BDC = np.zeros((128, 128)); BDnS = np.zeros((128, 128))
    for c in range(2):
        BDC[c * 64:(c + 1) * 64, c * 64:(c + 1) * 64] = sc * np.cos(th1).T
        BDnS[c * 64:(c + 1) * 64, c * 64:(c + 1) * 64] = -sc * np.sin(th1).T
    fconst = np.concatenate([M1, M1c, C2, S2, -S2, C2, S2, -S2, C2, BDC, BDnS], axis=1)
    TC = np.concatenate([np.cos(tht), np.cos(tht)], axis=1)
    TS = np.concatenate([np.sin(tht), np.sin(tht)], axis=1)
    ct = np.cos(tht).T; st_ = np.sin(tht).T
    ITC = np.tile(np.concatenate([ct, ct], axis=1), (2, 1))
    ITS = np.tile(np.concatenate([st_, st_], axis=1), (2, 1))
    tconst = np.concatenate([TC, TS, ITC, ITS], axis=1).astype(np.float32)
    t = np.arange(L)
    r = (t // 64).astype(np.float32); col = (t % 64).astype(np.float32)
    inv = (10000.0 ** (-np.arange(16, dtype=np.float32) / 16)).astype(np.float32)
    ang = np.concatenate([r[:, None] * inv, col[:, None] * inv], axis=-1).astype(np.float32)
    cosr = np.cos(ang).astype(np.float32); sinr = np.sin(ang).astype(np.float32)
    def tl(a):
        return np.ascontiguousarray(a.reshape(64, 128, 32).transpose(1, 0, 2))
    rope = np.stack([tl(cosr), tl(sinr)], axis=1).astype(np.float32)
    tt = np.linspace(0.0, 1.0, L, dtype=np.float32)[:, None]
    w = ((2.0 * math.pi / L) * np.arange(L, dtype=np.float32))[:, None].astype(np.float32)
    bands = np.linspace(1e-4, 15, 16, dtype=np.float32)[None, :]
    z = np.concatenate([tt, np.cos(bands * w), -np.sin(bands * w)], axis=-1).astype(np.float32)
    order = (128 * np.arange(64)[None, :] + np.arange(128)[:, None]).reshape(-1)
    zT = np.ascontiguousarray(z[order].T).astype(np.float32)
    deltas = np.abs(np.linspace(math.log(1e-2) / 1.5, math.log(1e-2) / 0.3, 512, dtype=np.float32))
    E = np.exp(-tt * deltas[None, :]).astype(np.float32)
    E4 = E.reshape(64, 32, 4, 8, 64)
    edec = np.ascontiguousarray(E4.transpose(3, 1, 0, 2, 4)).reshape(8, 32, 64, 256).astype(np.float32)
    m = np.arange(128)[:, None]; c = np.arange(128)[None, :]
    pd = np.stack([np.maximum(c - m, 0), (c >= m), np.maximum(m - c, 0), (m >= c)], axis=1).astype(np.float32)
    p = np.arange(128, dtype=np.float32)
    pcols = np.stack([p + 1, 128 - p, 127 - p, p, 255 - p, p, 127 - p, 128 + p], axis=1).astype(np.float32)
    ident = np.eye(128, dtype=np.float32)
    _CONSTS = dict(fconst=_bf(fconst), tconst=tconst, rope=rope, zT=zT, edec=edec, pd=np.ascontiguousarray(pd),
                   pcols=pcols, ident_bf=_bf(ident), ident_f=ident)
    return _CONSTS


def build(debug=False, stop_after=99):
    nc = bass.Bass("TRN2", target_bir_lowering=False)

    def din(name, shape, dt=F32):
        return nc.dram_tensor(name, list(shape), dt, kind="ExternalInput").ap()

    def dscr(name, shape, dt):
        if debug:
            return nc.dram_tensor(name, list(shape), dt, kind="ExternalOutput").ap()
        return nc.dram_tensor(name, list(shape), dt).ap()

    x = din("x", [L, D]); ctx = din("ctx", [256, D]); cc = din("cc", [2, D])
    w_ada = din("w_ada", [D, 6 * D]); b_ada = din("b_ada", [1, 6 * D]); norm1_g = din("norm1_g", [1, D])
    w_in = din("w_in", [D, 3584]); conv_w = din("hy_conv_w", [3, 1536]); conv_b = din("hy_conv_b", [1, 1536])
    f_w1 = din("hy_f_w1", [33, 64]); f_b1 = din("hy_f_b1", [1, 64]); f_fr1 = din("hy_f_freq1", [1, 64])
    f_w2 = din("hy_f_w2", [64, 64]); f_b2 = din("hy_f_b2", [1, 64]); f_fr2 = din("hy_f_freq2", [1, 64])
    hy_bias = din("hy_bias", [1, 512]); logit = din("ret_decay_logit", [1, 16])
    gn_g = din("ret_gn_g", [1, 512]); w_out = din("w_out", [D, D]); norm2_g = din("norm2_g", [1, D])
    w_mlp1 = din("w_mlp1", [D, 4 * D]); w_mlp2 = din("w_mlp2", [4 * D, D]); norm_f_g = din("norm_f_g", [1, D])
    fconst_d = din("fconst", [128, 1408], BF16); tconst_d = din("tconst", [128, 768])
    rope_d = din("rope", [128, 2, 64, 32]); zT_d = din("zT", [33, L]); edec_d = din("edec", [32, 64, 256]); w3loc_d = din("w3loc", [64, 128])
    pd_d = din("pd", [128, 4, 128]); pcols_d = din("pcols", [128, 8])
    identb_d = din("ident_bf", [128, 128], BF16); identf_d = din("ident_f", [128, 128])
    out = nc.dram_tensor("out", [L, D], F32, kind="ExternalOutput").ap()

    modscr = dscr("modscr", [2, 6 * D], F32)
    vxscr = dscr("vxscr", [512, L], BF16); x0scr = dscr("x0scr", [512, L], BF16); yscr = dscr("yscr", [512, L], BF16)
    qscr = dscr("qscr", [L, 512], BF16); kscr = dscr("kscr", [L, 512], BF16)
    vscr = dscr("vscr", [L, 512], BF16); gscr = dscr("gscr", [L, 512], BF16)
    Tscr = dscr("Tscr", [NT, 128, 512], F32); Sscr = dscr("Sscr", [NT, 128, 512], BF16)
    w1bf = nc.dram_tensor("w1bf", [D, 4 * D], BF16).ap()
    wobf = nc.dram_tensor("wobf", [D, D], BF16).ap()
    kloc = nc.dram_tensor("kloc", [256, 4096], BF16).ap()
    kall = nc.dram_tensor("kall", [8 * 256, 4096], BF16).ap()
    d_kall = Dep()

    final_events = []

    with contextlib.ExitStack() as gst:
        S = Sched(nc, gst)
        uid = [0]

        def sb(st, shape, dt, name=None):
            uid[0] += 1
            t = st.enter_context(nc.sbuf_tensor(name or ("t%d" % uid[0]), list(shape), dt))
            return t, Dep()

        def ps(st, shape, dt, name=None):
            uid[0] += 1
            t = st.enter_context(nc.psum_tensor(name or ("p%d" % uid[0]), list(shape), dt))
            return t, Dep()

        def load(eng, key, dst_ap, src_ap, dst_dep, src_deps=(), slow=False):
            if slow:
                return S.op(eng, lambda e: e.dma_start(out=dst_ap, in_=src_ap, allow_slow_non_contiguous=True), reads=list(src_deps), writes=[dst_dep], dma=key)
            return S.op(eng, lambda e: e.dma_start(out=dst_ap, in_=src_ap), reads=list(src_deps), writes=[dst_dep], dma=key)

        def store(eng, key, dst_ap, src_ap, src_dep, dst_deps=(), slow=False):
            if slow:
                return S.op(eng, lambda e: e.dma_start(out=dst_ap, in_=src_ap, allow_slow_non_contiguous=True), reads=[src_dep], writes=list(dst_deps), dma=key)
            return S.op(eng, lambda e: e.dma_start(out=dst_ap, in_=src_ap), reads=[src_dep], writes=list(dst_deps), dma=key)

        def row_bc(ap_dram, off, n):
            return dap(ap_dram, off, [(0, 128), (1, n)])

        identb, d_identb = sb(gst, [128, 128], BF16)
        load("sync", "c_idb", identb[:], identb_d, d_identb)
        pcols, d_pcols = sb(gst, [128, 8], F32)
        load("sync", "c_pc", pcols[:], pcols_d, d_pcols)
        gs2, d_gs2 = sb(gst, [128, D], F32); gate2, d_gate2 = sb(gst, [128, D], F32)
        gate5, d_gate5 = sb(gst, [128, D], F32); gF, d_gF = sb(gst, [128, D], F32)
        colx, d_colx = sb(gst, [128, 6, 8], F32)
        lgt, d_lgt = sb(gst, [128, 16], F32)
        lgsel, d_lgsel = sb(gst, [128, 8], F32)
        DT, d_DT = sb(gst, [128, 8, 128], F32)
        Wq, d_Wq = sb(gst, [128, 8, 2], F32)
        Dec, d_Dec = sb(gst, [128, 8], F32)
        S0, d_S0 = sb(gst, [128, 512], F32)
        a1st = gst.enter_context(contextlib.ExitStack())
        gs1, d_gs1 = sb(a1st, [128, D], F32); gs1c, d_gs1c = sb(a1st, [128, D], F32)
        colc, d_colc = sb(a1st, [128, 2, 8], F32)
        Wk, d_Wk = sb(a1st, [128, 8, 2], F32)
        Wkc, d_Wkc = sb(a1st, [128, 2, 8, 2], F32)
        Win, _ = sb(a1st, [128, 8, 3584], BF16)
        d_Wink = [Dep() for _ in range(8)]
        for k in range(8):
            S.op("gpsimd", lambda e, k=k: e.dma_start(out=Win[:, k, :], in_=w_in[k * 128:(k + 1) * 128, :]), writes=[d_Wink[k]], dma="p1_win%d" % k)

        with contextlib.ExitStack() as st:
            ccT, d_ccT = sb(st, [128, 8, 2], F32)
            for r_ in range(2):
                load("sync", "a_cc", ccT[:, :, r_:r_ + 1], dap(cc, r_ * D, [(1, 128), (128, 8), (1, 1)]), d_ccT, slow=True)
            scT, d_scT = sb(st, [128, 8, 2], F32)
            S.op("scalar", lambda e: e.activation(out=scT[:], in_=ccT[:], func=AF.Silu), reads=[d_ccT], writes=[d_scT])
            bada, d_bada = sb(st, [2, 6 * D], F32)
            load("sync", "a_bada", bada[:], dap(b_ada, 0, [(0, 2), (1, 6 * D)]), d_bada)
            modsb, d_modsb = sb(st, [2, 6 * D], F32)
            wab = [sb(st, [128, 8, 512], F32) for _ in range(2)]
            pM = [ps(st, [128, 512], F32) for _ in range(2)]
            for cb in range(12):
                wa, d_wa = wab[cb % 2]
                load("sync", "a_wa%d" % (cb % 2), wa[:],
                     w_ada[:, cb * 512:(cb + 1) * 512].rearrange("(k p) n -> p k n", p=128), d_wa)
                pm, d_pm = pM[cb % 2]
                for k in range(8):
                    S.op("tensor", lambda e, pm=pm, wa=wa, k=k: e.matmul(pm[0:2, :], lhsT=scT[:, k, :], rhs=wa[:, k, :],
                                                                        start=(k == 0), stop=(k == 7)),
                         reads=[d_scT, d_wa], writes=[d_pm])
                S.op("vector", lambda e, pm=pm, cb=cb: e.tensor_tensor(out=modsb[:, cb * 512:(cb + 1) * 512], in0=pm[0:2, :],
                                                                      in1=bada[:, cb * 512:(cb + 1) * 512], op=ALU.add),
                     reads=[d_pm, d_bada], writes=[d_modsb])
            d_modscr = Dep()
            store("sync", "a_modst", modscr, modsb[:], d_modsb, [d_modscr])
            for j_ in range(6):
                load("sync", "a_colx", colx[:, j_, :].rearrange("p (k o) -> p k o", o=1),
                     dap(modscr, j_ * D, [(1, 128), (128, 8), (1, 1)]), d_colx, [d_modscr], slow=True)
            for j_ in range(2):
                load("sync", "a_colc", colc[:, j_, :].rearrange("p (k o) -> p k o", o=1),
                     dap(modscr, 6 * D + j_ * D, [(1, 128), (128, 8), (1, 1)]), d_colc, [d_modscr], slow=True)
            tmpA, d_tmpA = sb(st, [128, D], F32)
            tmpB, d_tmpB = sb(st, [128, D], F32)

            def make_gs(dst, d_dst, scale_off, g_dram, tagn):
                load("sync", "a_tA", tmpA[:], row_bc(modscr, scale_off, D), d_tmpA, [d_modscr])
                load("sync", "a_tB", tmpB[:], row_bc(g_dram, 0, D), d_tmpB)
                S.op("vector", lambda e: e.scalar_tensor_tensor(out=dst[:], in0=tmpA[:], scalar=1.0, op0=ALU.add,
                                                                in1=tmpB[:], op1=ALU.mult),
                     reads=[d_tmpA, d_tmpB], writes=[d_dst])
            make_gs(gs1, d_gs1, 1 * D, norm1_g, 0)
            make_gs(gs1c, d_gs1c, 6 * D + 1 * D, norm1_g, 1)
            make_gs(gs2, d_gs2, 4 * D, norm2_g, 2)
            load("sync", "a_g2", gate2[:], row_bc(modscr, 2 * D, D), d_gate2, [d_modscr])
            load("sync", "a_g5", gate5[:], row_bc(modscr, 5 * D, D), d_gate5, [d_modscr])
            load("sync", "a_gF", gF[:], row_bc(norm_f_g, 0, D), d_gF)

            lraw, d_lraw = sb(st, [128, 16], F32)
            load("sync", "a_lg", lraw[:], row_bc(logit, 0, 16), d_lraw)
            S.op("scalar", lambda e: e.activation(out=lgt[:], in_=lraw[:], func=AF.Exp, scale=-1.0), reads=[d_lraw], writes=[d_lgt])
            S.op("scalar", lambda e: e.activation(out=lgt[:], in_=lgt[:], func=AF.Ln, bias=1.0), reads=[d_lgt], writes=[d_lgt])
            S.op("scalar", lambda e: e.mul(lgt[:], lgt[:], -1.0), reads=[d_lgt], writes=[d_lgt])
            S.op("vector", lambda e: e.tensor_copy(out=lgsel[0:64, :], in_=lgt[0:64, 0:8]), reads=[d_lgt], writes=[d_lgsel])
            S.op("vector", lambda e: e.tensor_copy(out=lgsel[64:128, :], in_=lgt[64:128, 8:16]), reads=[d_lgt], writes=[d_lgsel])
            S.op("scalar", lambda e: e.activation(out=Dec[:], in_=lgsel[:], func=AF.Exp, scale=128.0), reads=[d_lgsel], writes=[d_Dec])
            for (dst, d_dst, cf, cbk) in ((Wq, d_Wq, 0, 1), (Wk, d_Wk, 2, 3)):
                S.op("scalar", lambda e, dst=dst, cf=cf: e.activation(out=dst[:, :, 0], in_=lgt[:, 0:8], func=AF.Exp, scale=pcols[:, cf:cf + 1]),
                     reads=[d_lgt, d_pcols], writes=[d_dst])
                S.op("scalar", lambda e, dst=dst, cbk=cbk: e.activation(out=dst[:, :, 1], in_=lgt[:, 8:16], func=AF.Exp, scale=pcols[:, cbk:cbk + 1]),
                     reads=[d_lgt, d_pcols], writes=[d_dst])
            S.op("vector", lambda e: e.tensor_scalar(out=Wk[:], in0=Wk[:], scalar1=0.125, scalar2=None, op0=ALU.mult), reads=[d_Wk], writes=[d_Wk])
            for tI in range(2):
                S.op("scalar", lambda e, tI=tI: e.activation(out=Wkc[:, tI, :, 0], in_=lgt[:, 0:8], func=AF.Exp, scale=pcols[:, 4 + 2 * tI:5 + 2 * tI]),
                     reads=[d_lgt, d_pcols], writes=[d_Wkc])
                S.op("scalar", lambda e, tI=tI: e.activation(out=Wkc[:, tI, :, 1], in_=lgt[:, 8:16], func=AF.Exp, scale=pcols[:, 5 + 2 * tI:6 + 2 * tI]),
                     reads=[d_lgt, d_pcols], writes=[d_Wkc])
            pdt, d_pdt = sb(st, [128, 4, 128], F32)
            load("sync", "a_pd", pdt[:], pd_d, d_pdt)
            ef, d_ef = sb(st, [128, 128], F32); eb, d_eb = sb(st, [128, 128], F32)
            for h in range(8):
                S.op("scalar", lambda e, h=h: e.activation(out=ef[:], in_=pdt[:, 0, :], func=AF.Exp, scale=lgt[:, h:h + 1]),
                     reads=[d_pdt, d_lgt], writes=[d_ef])
                S.op("scalar", lambda e, h=h: e.activation(out=eb[:], in_=pdt[:, 2, :], func=AF.Exp, scale=lgt[:, 8 + h:9 + h]),
                     reads=[d_pdt, d_lgt], writes=[d_eb])
                S.op("vector", lambda e: e.scalar_tensor_tensor(out=ef[:], in0=ef[:], scalar=0.125, op0=ALU.mult, in1=pdt[:, 1, :], op1=ALU.mult), reads=[d_ef, d_pdt], writes=[d_ef])
                S.op("vector", lambda e: e.scalar_tensor_tensor(out=eb[:], in0=eb[:], scalar=0.125, op0=ALU.mult, in1=pdt[:, 3, :], op1=ALU.mult), reads=[d_eb, d_pdt], writes=[d_eb])
                S.op("vector", lambda e, h=h: e.tensor_tensor(out=DT[:, h, :], in0=ef[:], in1=eb[:], op=ALU.add), reads=[d_ef, d_eb], writes=[d_DT])
            S.barrier()
        if stop_after <= 0:
            S.emit()
            return nc

        def fft_phase(mode):
            with contextlib.ExitStack() as st:
                fc, d_fc = sb(st, [128, 1408], BF16)
                load("sync", "f_fc", fc[:], fconst_d, d_fc)
                tcs, d_tcs = sb(st, [128, 768], F32)
                load("sync", "f_tc", tcs[:], tconst_d, d_tcs)
                M1o, M1co, C2o, S2o, nS2o, C2S2o, nS2C2o, BDCo, BDnSo = 0, 128, 256, 384, 512, 640, 896, 1152, 1280
                w1t, d_w1t = sb(st, [33, 64], F32); load("sync", "f_w1", w1t[:], f_w1, d_w1t)
                w2t, d_w2t = sb(st, [64, 64], F32); load("sync", "f_w2", w2t[:], f_w2, d_w2t)
                w3b, d_w3b = sb(st, [64, 128], BF16)
                if mode == "filter":
                    S.op("gpsimd", lambda e: e.dma_start(out=w3b[:], in_=w3loc_d), writes=[d_w3b], dma="f_w3")
                fcol, d_fcol = sb(st, [64, 4], F32)
                for j_, src in enumerate((f_fr1, f_b1, f_fr2, f_b2)):
                    load("sync", "f_col", fcol[:, j_:j_ + 1], dap(src, 0, [(1, 64), (1, 1)]), d_fcol, slow=True)
                fab, d_fab = sb(st, [64, 4], F32)
                for l_ in range(2):
                    S.op("vector", lambda e, l_=l_: e.tensor_scalar(out=fab[:, 2 * l_:2 * l_ + 1], in0=fcol[:, 2 * l_:2 * l_ + 1], scalar1=1.0 / 3.0, scalar2=None, op0=ALU.mult),
                         reads=[d_fcol], writes=[d_fab])
                    S.op("vector", lambda e, l_=l_: e.tensor_tensor(out=fab[:, 2 * l_ + 1:2 * l_ + 2], in0=fab[:, 2 * l_:2 * l_ + 1], in1=fcol[:, 2 * l_ + 1:2 * l_ + 2], op=ALU.mult),
                         reads=[d_fcol, d_fab], writes=[d_fab])
                onesf, d_onesf = sb(st, [64, 128], F32)
                S.op("gpsimd", lambda e: e.memset(onesf[:], 1.0), writes=[d_onesf])
                h2T, d_h2T = sb(st, [64, L], BF16) if mode == "filter" else (None, None)
                banks = [ps(st, [128, 512], F32) for _ in range(8)]
                bctr = [0]

                def nb():
                    b_ = banks[bctr[0] % 8]
                    bctr[0] += 1
                    return b_

                zt = [sb(st, [33, 512], F32)] * 2 if mode == "filter" else None
                (sA, d_sA), (sB, d_sB), (h1, d_h1) = [sb(st, [64, 512], F32) for _ in range(3)] if mode == "filter" else [(None, None)] * 3

                def sin3(pf, d_pf, layer, out_ap, d_out):
                    S.op("scalar", lambda e: e.activation(out=sA[:], in_=pf[0:64, :], func=AF.Sin, scale=fab[:, 2 * layer:2 * layer + 1],
                                                          bias=fab[:, 2 * layer + 1:2 * layer + 2]), reads=[d_pf, d_fab], writes=[d_sA])
                    S.op("vector", lambda e: e.tensor_tensor(out=sB[:], in0=sA[:], in1=sA[:], op=ALU.mult), reads=[d_sA], writes=[d_sB])
                    S.op("vector", lambda e: e.tensor_scalar(out=sB[:], in0=sB[:], scalar1=-4.0, scalar2=3.0, op0=ALU.mult, op1=ALU.add), reads=[d_sB], writes=[d_sB])
                    S.op("vector", lambda e: e.tensor_tensor(out=out_ap, in0=sB[:], in1=sA[:], op=ALU.mult), reads=[d_sA, d_sB], writes=[d_out])

                def mlp_block(blk):
                    z, d_z = zt[blk % 2]
                    load("sync", "f_z%d" % (blk % 2), z[:], zT_d[:, blk * 512:(blk + 1) * 512], d_z)
                    pf, d_pf = nb()
                    S.op("tensor", lambda e: e.matmul(pf[0:64, :], lhsT=w1t[:], rhs=z[:], start=True, stop=True), reads=[d_w1t, d_z], writes=[d_pf])
                    sin3(pf, d_pf, 0, h1[:], d_h1)
                    pf2, d_pf2 = nb()
                    S.op("tensor", lambda e: e.matmul(pf2[0:64, :], lhsT=w2t[:], rhs=h1[:], start=True, stop=True), reads=[d_w2t, d_h1], writes=[d_pf2])
                    sin3(pf2, d_pf2, 1, h2T[:, blk * 512:(blk + 1) * 512], d_h2T)

                if mode == "filter":
                    for blk in range(16):
                        mlp_block(blk)

                Hbuf, d_Hbuf = sb(st, [128, (2 * 64 * 128) if mode == "filter" else 8192], BF16)
                Acc4, d_Acc4 = sb(st, [64, 512], F32) if mode == "filter" else (None, None)
                habs, d_habs = sA, d_sA
                Et = [sb(st, [64, 256], F32)] * 2 if mode == "filter" else None
                hd32 = [(h1, d_h1)] * 2
                nsb, d_nsb = sb(st, [128, 128], F32)
                rn, d_rn = sb(st, [128, 64], F32)
                BfR, d_BfR = sb(st, [128, 64, 64], BF16); BfI, d_BfI = sb(st, [128, 64, 64], BF16)
                BbR, d_BbR = sb(st, [128, 64, 64], BF16); BbI, d_BbI = sb(st, [128, 64, 64], BF16)
                KR, d_KR = sb(st, [128, 64, 64], BF16); KI, d_KI = sb(st, [128, 64, 64], BF16)
                KRb = [(KR, d_KR)] * 2
                KIb = [(KI, d_KI)] * 2
                Xg, d_Xg = sb(st, [64, 64, 128], BF16) if mode == "data" else (None, None)
                Qb = [(sb(st, [128, 512], F32), sb(st, [128, 512], F32)) for _ in range(2)] if mode == "data" else [(sb(st, [128, 512], F32), sb(st, [128, 512], F32))] * 2
                qctr = [0]
                (t1, d_t1), (t2, d_t2) = Qb[0]
                (t3, d_t3), (t4, d_t4) = Qb[1]
                Yo, d_Yo = sb(st, [128, 32, 128], BF16) if mode == "data" else (None, None)


                Sst, d_Sst = sb(st, [128, 512], F32) if mode == "data" else (None, None)
                if mode == "data":
                    S.op("vector", lambda e: e.tensor_copy(out=Sst[:], in_=S0[:]), reads=[d_S0], writes=[d_Sst])
                Tt = [sb(st, [128, 512], F32) for _ in range(2)] if mode == "data" else None
                Sb = [sb(st, [128, 512], BF16) for _ in range(2)] if mode == "data" else None
                sctr = [0]

                def tload(s_):
                    if s_ < NT:
                        tt_, d_tt = Tt[s_ % 2]
                        load("sync", "s_tf%d" % (s_ % 2), tt_[0:64, :], Tscr[s_, 0:64, :], d_tt)
                        load("sync", "s_tb%d" % (s_ % 2), tt_[64:128, :], Tscr[NT - 1 - s_, 64:128, :], d_tt)

                def scan_step(s_):
                    tt_, d_tt = Tt[s_ % 2]
                    sb_, d_sb = Sb[s_ % 2]
                    S.op("scalar", lambda e: e.copy(sb_[:], Sst[:]), reads=[d_Sst], writes=[d_sb])
                    store("sync", "s_sf%d" % (s_ % 2), Sscr[s_, 0:64, :], sb_[0:64, :], d_sb)
                    store("sync", "s_sb%d" % (s_ % 2), Sscr[NT - 1 - s_, 64:128, :], sb_[64:128, :], d_sb)
                    S.op("vector", lambda e: e.tensor_tensor(out=Sst[:].rearrange("p (h x) -> p h x", h=8), in0=Sst[:].rearrange("p (h x) -> p h x", h=8),
                                                             in1=sap(Dec, 0, 128, 0, [(1, 8), (0, 64)]), op=ALU.mult), reads=[d_Sst, d_Dec], writes=[d_Sst])
                    S.op("vector", lambda e: e.tensor_tensor(out=Sst[:], in0=Sst[:], in1=tt_[:], op=ALU.add), reads=[d_Sst, d_tt], writes=[d_Sst])
                    tload(s_ + 2)

                def scan_some(n_):
                    for _ in range(n_ if mode == "data" else 0):
                        if sctr[0] < NT:
                            scan_step(sctr[0])
                            sctr[0] += 1

                if mode == "data":
                    tload(0); tload(1)

                def twiddle(pa, d_pa, conj, outR, d_outR, outI, d_outI, c0):
                    pav = pa[:].rearrange("p (c x) -> p c x", c=4)
                    (Q1, d_Q1), (Q2, d_Q2) = Qb[qctr[0] % 2]
                    qctr[0] += 1
                    S.op("vector", lambda e: e.tensor_tensor(out=Q1[:].rearrange("p (c x) -> p c x", c=4), in0=pav,
                                                             in1=sap(tcs, 0, 128, 0, [(0, 4), (1, 128)]), op=ALU.mult), reads=[d_pa, d_tcs], writes=[d_Q1])
                    S.op("vector", lambda e: e.tensor_tensor(out=Q2[:].rearrange("p (c x) -> p c x", c=4), in0=pav,
                                                             in1=sap(tcs, 0, 128, 128, [(0, 4), (1, 128)]), op=ALU.mult), reads=[d_pa, d_tcs], writes=[d_Q2])
                    q1lo = sap(Q1, 0, 128, 0, [(128, 4), (1, 64)]); q1hi = sap(Q1, 0, 128, 64, [(128, 4), (1, 64)])
                    q2lo = sap(Q2, 0, 128, 0, [(128, 4), (1, 64)]); q2hi = sap(Q2, 0, 128, 64, [(128, 4), (1, 64)])
                    S.op("gpsimd", lambda e: e.tensor_tensor(out=outR[:, c0:c0 + 4, :], in0=q1lo, in1=q2hi, op=(ALU.subtract if conj else ALU.add)),
                         reads=[d_Q1, d_Q2], writes=[d_outR])
                    S.op("vector" if (qctr[0] % 2 == 0) else "gpsimd",
                         lambda e: e.tensor_tensor(out=outI[:, c0:c0 + 4, :], in0=q1hi, in1=q2lo, op=(ALU.add if conj else ALU.subtract)),
                         reads=[d_Q1, d_Q2], writes=[d_outI])

                def s1_stage(src_fn, src_deps, m1off, conj, outR, d_outR, outI, d_outI):
                    for c4 in range(16):
                        pa, d_pa = nb()
                        for cc_ in range(4):
                            S.op("tensor", lambda e, cc_=cc_, c4=c4, pa=pa: e.matmul(pa[:, cc_ * 128:(cc_ + 1) * 128], lhsT=src_fn(c4 * 4 + cc_),
                                                                                    rhs=fc[0:64, m1off:m1off + 128], start=True, stop=True),
                                 reads=list(src_deps) + [d_fc], writes=[d_pa])
                        twiddle(pa, d_pa, conj, outR, d_outR, outI, d_outI, c4 * 4)

                def s2_mm(pk, d_pk, terms, c8):
                    n_ = len(terms)
                    for ti, (foff, buf, d_buf) in enumerate(terms):
                        S.op("tensor", lambda e, ti=ti, foff=foff, buf=buf: e.matmul(pk[:], lhsT=fc[:, foff:foff + 128], rhs=buf[:, c8 * 8:(c8 + 1) * 8, :],
                                                                                    start=(ti == 0), stop=(ti == n_ - 1)),
                             reads=[d_fc, d_buf], writes=[d_pk])

                def group_filter():
                    S.op("gpsimd", lambda e: e.memset(Acc4[:], 0.0), writes=[d_Acc4])
                    for jq in range(32):
                        et, d_et = Et[jq % 2]
                        load("sync", "f_e%d" % (jq % 2), et[:], edec_d[jq], d_et)
                        ph, d_ph = nb()
                        for jj in range(4):
                            j = 4 * jq + jj
                            S.op("tensor", lambda e, jj=jj, j=j, ph=ph: e.matmul(ph[0:64, jj * 128:(jj + 1) * 128], lhsT=h2T[:, j * 64:(j + 1) * 64],
                                                                                rhs=w3b[:, :], start=True, stop=True),
                                 reads=[d_h2T, d_w3b], writes=[d_ph])
                        hd, d_hd = hd32[jq % 2]
                        S.op("vector", lambda e, ph=ph, et=et, hd=hd: e.tensor_tensor(
                            out=hd[:].rearrange("p (j d c) -> p j d c", j=4, d=2), in0=ph[0:64, :].rearrange("p (j d c) -> p j d c", j=4, d=2),
                            in1=sap(et, 0, 64, 0, [(64, 4), (0, 2), (1, 64)]), op=ALU.mult), reads=[d_ph, d_et], writes=[d_hd])
                        S.op("scalar", lambda e, hd=hd: e.activation(out=habs[:], in_=hd[:], func=AF.Abs), reads=[d_hd], writes=[d_habs])
                        S.op("vector", lambda e: e.tensor_tensor(out=Acc4[:], in0=Acc4[:], in1=habs[:], op=ALU.add), reads=[d_habs, d_Acc4], writes=[d_Acc4])
                        S.op("scalar", lambda e, hd=hd, jq=jq: e.copy(sap(Hbuf, 0, 64, 4 * jq, [(1, 4), (64 * 128, 2), (128, 64)]),
                                                                     hd[:].rearrange("p (j d c) -> p j d c", j=4, d=2)),
                             reads=[d_hd], writes=[d_Hbuf])
                        if jq % 4 == 3:
                            scan_some(1)
                    S.op("gpsimd", lambda e: e.memset(sap(Hbuf, 0, 1, 64 * 128, [(128, 64)]), 0.0), writes=[d_Hbuf])
                    pn, d_pn = nb()
                    for jj in range(4):
                        S.op("tensor", lambda e, jj=jj: e.matmul(pn[:, 0:128], lhsT=onesf[:], rhs=Acc4[:, jj * 128:(jj + 1) * 128], start=(jj == 0), stop=(jj == 3)),
                             reads=[d_onesf, d_Acc4], writes=[d_pn])
                    S.op("scalar", lambda e: e.copy(nsb[:], pn[:, 0:128]), reads=[d_pn], writes=[d_nsb])
                    S.op("vector", lambda e: e.scalar_tensor_tensor(out=rn[:], in0=nsb[:, 0:64], scalar=1e-6, op0=ALU.add, in1=nsb[:, 64:128], op1=ALU.add),
                         reads=[d_nsb], writes=[d_rn])
                    S.op("vector", lambda e: e.reciprocal(out=rn[:], in_=rn[:]), reads=[d_rn], writes=[d_rn])
                    s1_stage(lambda c: sap(Hbuf, 0, 64, c * 128, [(1, 128)]), [d_Hbuf], M1o, False, BfR, d_BfR, BfI, d_BfI)
                    s1_stage(lambda c: sap(Hbuf, 0, 64, 64 * 128 + c * 128, [(1, 128)]), [d_Hbuf], M1co, True, BbR, d_BbR, BbI, d_BbI)
                    for c8 in range(8):
                        pk, d_pk = nb()
                        s2_mm(pk, d_pk, [(C2o, BfR, d_BfR), (S2o, BfI, d_BfI), (C2o, BbR, d_BbR), (nS2o, BbI, d_BbI)], c8)
                        S.op("vector", lambda e, pk=pk, c8=c8: e.tensor_tensor(out=KR[:, c8 * 8:(c8 + 1) * 8, :], in0=pk[:].rearrange("p (c k) -> p c k", c=8),
                                                                              in1=sap(rn, 0, 128, c8 * 8, [(1, 8), (0, 64)]), op=ALU.mult),
                             reads=[d_pk, d_rn], writes=[d_KR])
                        pk2, d_pk2 = nb()
                        s2_mm(pk2, d_pk2, [(C2o, BfI, d_BfI), (nS2o, BfR, d_BfR), (C2o, BbI, d_BbI), (S2o, BbR, d_BbR)], c8)
                        S.op("vector", lambda e, pk2=pk2, c8=c8: e.tensor_tensor(out=KI[:, c8 * 8:(c8 + 1) * 8, :], in0=pk2[:].rearrange("p (c k) -> p c k", c=8),
                                                                                in1=sap(rn, 0, 128, c8 * 8, [(1, 8), (0, 64)]), op=ALU.mult),
                             reads=[d_pk2, d_rn], writes=[d_KI])
                    d_kloc = Dep()
                    store("sync", "f_kr", kloc[0:128, :], KR[:].rearrange("p c k -> p (c k)"), d_KR, [d_kloc])
                    store("sync", "f_ki", kloc[128:256, :], KI[:].rearrange("p c k -> p (c k)"), d_KI, [d_kloc])
                    S.op("gpsimd", lambda e: e.collective_compute("AllGather", ALU.bypass, replica_groups=[list(range(8))], ins=[kloc], outs=[kall]),
                         reads=[d_kloc], writes=[d_kall], cc=True)

                def group_data(g):
                    kr_, d_kr_ = KRb[g % 2]
                    ki_, d_ki_ = KIb[g % 2]
                    load("sync", "f_lkr%d" % (g % 2), kr_[:].rearrange("p c k -> p (c k)"), kall[g * 256:g * 256 + 128, :], d_kr_, [d_kall])
                    load("sync", "f_lki%d" % (g % 2), ki_[:].rearrange("p c k -> p (c k)"), kall[g * 256 + 128:g * 256 + 256, :], d_ki_, [d_kall])
                    load("sync", "f_xg", Xg[:], dap(vxscr, g * 64 * L, [(128, 64), (L, 64), (1, 128)]), d_Xg)
                    s1_stage(lambda c: Xg[:, c, :], [d_Xg], M1o, False, BfR, d_BfR, BfI, d_BfI)
                    for c8 in range(8):
                        px, d_px = nb()
                        s2_mm(px, d_px, [(C2o, BfR, d_BfR), (S2o, BfI, d_BfI)], c8)
                        pxi, d_pxi = nb()
                        s2_mm(pxi, d_pxi, [(C2o, BfI, d_BfI), (nS2o, BfR, d_BfR)], c8)
                        kr = kr_[:, c8 * 8:(c8 + 1) * 8, :].rearrange("p c k -> p (c k)")
                        ki = ki_[:, c8 * 8:(c8 + 1) * 8, :].rearrange("p c k -> p (c k)")
                        S.op("vector", lambda e, px=px, kr=kr: e.tensor_tensor(out=t1[:], in0=px[:], in1=kr, op=ALU.mult), reads=[d_px, d_kr_], writes=[d_t1])
                        S.op("vector", lambda e, pxi=pxi, ki=ki: e.tensor_tensor(out=t2[:], in0=pxi[:], in1=ki, op=ALU.mult), reads=[d_pxi, d_ki_], writes=[d_t2])
                        S.op("vector", lambda e, px=px, ki=ki: e.tensor_tensor(out=t3[:], in0=px[:], in1=ki, op=ALU.mult), reads=[d_px, d_ki_], writes=[d_t3])
                        S.op("vector", lambda e, pxi=pxi, kr=kr: e.tensor_tensor(out=t4[:], in0=pxi[:], in1=kr, op=ALU.mult), reads=[d_pxi, d_kr_], writes=[d_t4])
                        S.op("gpsimd", lambda e, c8=c8: e.tensor_tensor(out=BbR[:, c8 * 8:(c8 + 1) * 8, :].rearrange("p c k -> p (c k)"), in0=t1[:], in1=t2[:], op=ALU.subtract),
                             reads=[d_t1, d_t2], writes=[d_BbR])
                        S.op("vector" if (c8 % 2 == 0) else "gpsimd", lambda e, c8=c8: e.tensor_tensor(out=BbI[:, c8 * 8:(c8 + 1) * 8, :].rearrange("p c k -> p (c k)"), in0=t3[:], in1=t4[:], op=ALU.add),
                             reads=[d_t3, d_t4], writes=[d_BbI])
                    for pb in range(16):
                        pc, d_pc = nb()
                        for q_ in range(2):
                            p_ = pb * 2 + q_
                            S.op("tensor", lambda e, q_=q_, p_=p_, pc=pc: e.matmul(pc[:, q_ * 256:(q_ + 1) * 256], lhsT=BbR[:, 2 * p_:2 * p_ + 2, :],
                                                                                  rhs=fc[:, C2S2o:C2S2o + 256], start=True, stop=False),
                                 reads=[d_BbR, d_fc], writes=[d_pc])
                            S.op("tensor", lambda e, q_=q_, p_=p_, pc=pc: e.matmul(pc[:, q_ * 256:(q_ + 1) * 256], lhsT=BbI[:, 2 * p_:2 * p_ + 2, :],
                                                                                  rhs=fc[:, nS2C2o:nS2C2o + 256], start=False, stop=True),
                                 reads=[d_BbI, d_fc], writes=[d_pc])
                        pcv = pc[:].rearrange("p (q x) -> p q x", q=2)
                        (Q1, d_Q1), (Q2, d_Q2) = Qb[qctr[0] % 2]
                        qctr[0] += 1
                        S.op("vector", lambda e, pcv=pcv, Q1=Q1: e.tensor_tensor(out=Q1[:].rearrange("p (q x) -> p q x", q=2), in0=pcv,
                                                                         in1=sap(tcs, 0, 128, 256, [(0, 2), (1, 256)]), op=ALU.mult), reads=[d_pc, d_tcs], writes=[d_Q1])
                        S.op("vector", lambda e, pcv=pcv, Q2=Q2: e.tensor_tensor(out=Q2[:].rearrange("p (q x) -> p q x", q=2), in0=pcv,
                                                                         in1=sap(tcs, 0, 128, 512, [(0, 2), (1, 256)]), op=ALU.mult), reads=[d_pc, d_tcs], writes=[d_Q2])
                        S.op("gpsimd", lambda e, pb=pb, Q1=Q1, Q2=Q2: e.tensor_tensor(out=sap(Hbuf, 0, 128, pb * 256, [(128, 2), (1, 128)]),
                                                                       in0=sap(Q1, 0, 128, 0, [(256, 2), (1, 128)]), in1=sap(Q2, 0, 128, 128, [(256, 2), (1, 128)]), op=ALU.subtract),
                             reads=[d_Q1, d_Q2], writes=[d_Hbuf])
                        S.op("vector" if (pb % 2 == 0) else "gpsimd", lambda e, pb=pb, Q1=Q1, Q2=Q2: e.tensor_tensor(out=sap(Hbuf, 0, 128, 4096 + pb * 256, [(128, 2), (1, 128)]),
                                                                       in0=sap(Q1, 0, 128, 128, [(256, 2), (1, 128)]), in1=sap(Q2, 0, 128, 0, [(256, 2), (1, 128)]), op=ALU.add),
                             reads=[d_Q1, d_Q2], writes=[d_Hbuf])
                    for p4 in range(8):
                        py, d_py = nb()
                        for q_ in range(4):
                            p_ = p4 * 4 + q_
                            S.op("tensor", lambda e, q_=q_, p_=p_, py=py: e.matmul(py[:, q_ * 128:(q_ + 1) * 128], lhsT=fc[:, BDCo:BDCo + 128],
                                                                                  rhs=sap(Hbuf, 0, 128, p_ * 128, [(1, 128)]), start=True, stop=False),
                                 reads=[d_Hbuf, d_fc], writes=[d_py])
                            S.op("tensor", lambda e, q_=q_, p_=p_, py=py: e.matmul(py[:, q_ * 128:(q_ + 1) * 128], lhsT=fc[:, BDnSo:BDnSo + 128],
                                                                                  rhs=sap(Hbuf, 0, 128, 4096 + p_ * 128, [(1, 128)]), start=False, stop=True),
                                 reads=[d_Hbuf, d_fc], writes=[d_py])
                        S.op("scalar", lambda e, p4=p4, py=py: e.copy(Yo[:, p4 * 4:(p4 + 1) * 4, :].rearrange("p q x -> p (q x)"), py[:]), reads=[d_py], writes=[d_Yo])
                    store("sync", "f_yo", dap(yscr, g * 64 * L, [(128, 128), (2 * L, 32), (1, 128)]), Yo[:], d_Yo)

                if mode == "filter":
                    group_filter()
                else:
                    for g in range(8):
                        group_data(g)
                    scan_some(NT)
                S.barrier()
        fft_phase("filter")
        d_w1bf = [Dep() for _ in range(8)]
        for k in range(8):
            S.op("gpsimd", lambda e, k=k: e.dma_start(out=w1bf[k * 128:(k + 1) * 128, :], in_=w_mlp1[k * 128:(k + 1) * 128, :]), writes=[d_w1bf[k]], dma="c_w1")
        d_wobf = [Dep() for _ in range(2)]
        for k in range(2):
            S.op("gpsimd", lambda e, k=k: e.dma_start(out=wobf[k * 512:(k + 1) * 512, :], in_=w_out[k * 512:(k + 1) * 512, :]), writes=[d_wobf[k]], dma="c_wo")

        d_vx = Dep(); d_x0 = Dep(); d_q = Dep(); d_k = Dep(); d_v = Dep(); d_g = Dep(); d_T = Dep()
        with contextlib.ExitStack() as st:
            ropeT, d_rope = sb(st, [128, 2, 64, 32], F32)
            load("sync", "p1_rope", ropeT[:], rope_d, d_rope)
            cw, d_cw = sb(st, [128, 12, 4], F32)
            for j in range(3):
                load("sync", "p1_cw", cw[:, :, j:j + 1], dap(conv_w, j * 1536, [(1, 128), (128, 12), (1, 1)]), d_cw, slow=True)
            load("sync", "p1_cw", cw[:, :, 3:4], dap(conv_b, 0, [(1, 128), (128, 12), (1, 1)]), d_cw, slow=True)

            xb = [sb(st, [128, D], F32) for _ in range(3)]
            junk, d_junk = sb(st, [128, D], BF16)
            ssq = [sb(st, [128, 1], F32) for _ in range(3)]
            xm = [sb(st, [128, D], BF16) for _ in range(2)]
            hxT = [sb(st, [128, 8, 512], BF16) for _ in range(2)]
            U = [sb(st, [128, 514], F32) for _ in range(12)]
            cv1 = [sb(st, [128, 512], F32) for _ in range(2)]
            cv2 = [sb(st, [128, 512], F32) for _ in range(2)]
            cvx1 = [sb(st, [128, 512], F32) for _ in range(4)]
            cv3, d_cv3 = sb(st, [128, 512], F32)
            hyo = [sb(st, [128, 512], BF16) for _ in range(4)]
            P1, d_P1 = sb(st, [128, 512], F32); P2, d_P2 = sb(st, [128, 512], F32)
            qo = [sb(st, [128, 512], BF16) for _ in range(2)]
            ko = [sb(st, [128, 512], BF16) for _ in range(2)]
            vo = [sb(st, [128, 512], BF16) for _ in range(2)]
            go = [sb(st, [128, 512], BF16) for _ in range(2)]
            kw, d_kw = sb(st, [128, 8, 2, 64], BF16)
            Tsb = [sb(st, [128, 512], F32) for _ in range(2)]
            pT = [ps(st, [128, 8, 128], BF16) for _ in range(2)]
            pU = [ps(st, [128, 512], F32) for _ in range(2)]
            pR = [ps(st, [128, 512], F32) for _ in range(2)]
            pTs = [ps(st, [128, 512], F32) for _ in range(2)]
            for ft in range(12):
                S.op("gpsimd", lambda e, ft=ft: e.memset(U[ft][0][:, 0:2], 0.0), writes=[U[ft][1]])

            srcs = [ctx[0:128, :], ctx[128:256, :]] + [x[i * 128:(i + 1) * 128, :] for i in range(NT)]

            def xload(s_):
                if s_ < len(srcs):
                    load("sync", "p1_x%d" % (s_ % 3), xb[s_ % 3][0][:], srcs[s_], xb[s_ % 3][1])

            xload(0)

            def norm_transpose(i, gsrow, d_gsrow, shcol_fn, d_shcol, hx_tile, d_hx, tok0):
                xload(i + 1)
                xt, d_xt = xb[i % 3]
                sq, d_sq = ssq[i % 3]
                S.op("scalar", lambda e: e.activation(out=junk[:], in_=xt[:], func=AF.Square, accum_out=sq[:]),
                     reads=[d_xt], writes=[d_junk, d_sq])
                S.op("scalar", lambda e: e.activation(out=sq[:], in_=sq[:], func=AF.Sqrt, scale=1.0 / D, bias=1e-6),
                     reads=[d_sq], writes=[d_sq])
                S.op("vector", lambda e: e.reciprocal(out=sq[:], in_=sq[:]), reads=[d_sq], writes=[d_sq])
                xmt, d_xmt = xm[i % 2]
                S.op("vector", lambda e: e.scalar_tensor_tensor(out=xmt[:], in0=xt[:], scalar=sq[:, 0:1], op0=ALU.mult,
                                                                in1=gsrow[:], op1=ALU.mult),
                     reads=[d_xt, d_sq, d_gsrow], writes=[d_xmt])
                pt, d_pt = pT[i % 2]
                for k in range(8):
                    S.op("tensor", lambda e, k=k: e.transpose(out=pt[:, k, :], in_=xmt[:, k * 128:(k + 1) * 128], identity=identb[:]),
                         reads=[d_xmt, d_identb], writes=[d_pt])
                for k in range(8):
                    S.op("scalar", lambda e, k=k: e.activation(out=hx_tile[:, k, tok0:tok0 + 128], in_=pt[:, k, :], func=AF.Identity,
                                                               bias=shcol_fn(k)),
                         reads=[d_pt, d_shcol], writes=[d_hx])

            def proj_tok(hx_tile, d_hx, tok0, col0, pr, d_pr):
                for k in range(8):
                    S.op("tensor", lambda e, k=k: e.matmul(pr[:], lhsT=hx_tile[:, k, tok0:tok0 + 128], rhs=Win[:, k, col0:col0 + 512],
                                                           start=(k == 0), stop=(k == 7)),
                         reads=[d_hx, d_Wink[k]], writes=[d_pr])

            hxc, d_hxc = hxT[0]
            kc = []; vc = []
            for tI in range(2):
                norm_transpose(tI, gs1c, d_gs1c, lambda k: colc[:, 0, k:k + 1], d_colc, hxc, d_hxc, tI * 128)
                pr, d_pr = pR[0]
                proj_tok(hxc, d_hxc, tI * 128, 1536 + 512, pr, d_pr)
                kt, d_kt = ko[tI]
                S.op("scalar", lambda e, kt=kt, pr=pr: e.mul(kt[:], pr[:], 0.125), reads=[d_pr], writes=[d_kt])
                pr2, d_pr2 = pR[1]
                proj_tok(hxc, d_hxc, tI * 128, 1536 + 1024, pr2, d_pr2)
                vt, d_vt = vo[tI]
                S.op("scalar", lambda e, vt=vt, pr2=pr2: e.copy(vt[:], pr2[:]), reads=[d_pr2], writes=[d_vt])
                kc.append((kt, d_kt)); vc.append((vt, d_vt))
            kwc = [sb(st, [128, 8, 2, 64], BF16) for _ in range(2)]
            for tI in range(2):
                kt, d_kt = kc[tI]
                kwt, d_kwt = kwc[tI]
                S.op("vector", lambda e, kt=kt, kwt=kwt, tI=tI: e.tensor_tensor(
                    out=kwt[:], in0=sap(kt, 0, 128, 0, [(64, 8), (0, 2), (1, 64)]),
                    in1=sap(Wkc, 0, 128, tI * 16, [(2, 8), (1, 2), (0, 64)]), op=ALU.mult),
                    reads=[d_kt, d_Wkc], writes=[d_kwt])
            pS0, d_pS0 = pTs[0]
            for h in range(8):
                for tI in range(2):
                    S.op("tensor", lambda e, h=h, tI=tI: e.matmul(pS0[:, h * 64:(h + 1) * 64], lhsT=kwc[tI][0][:, h, :, :],
                                                                 rhs=vc[tI][0][:, h * 64:(h + 1) * 64], start=(tI == 0), stop=(tI == 1)),
                         reads=[kwc[tI][1], vc[tI][1]], writes=[d_pS0])
            S.op("vector", lambda e: e.tensor_copy(out=S0[:], in_=pS0[:]), reads=[d_pS0], writes=[d_S0])

            norm_transpose(2, gs1, d_gs1, lambda k: colx[:, 0, k:k + 1], d_colx, hxT[0][0], hxT[0][1], 0)
            for i in range(NT):
                B, ii = divmod(i, 4)
                hx_tile, d_hx = hxT[B % 2]
                if i + 1 < NT:
                    B1, ii1 = divmod(i + 1, 4)
                    norm_transpose(i + 3, gs1, d_gs1, lambda k: colx[:, 0, k:k + 1], d_colx, hxT[B1 % 2][0], hxT[B1 % 2][1], ii1 * 128)
                for cbk in range(4):
                    pr, d_pr = pR[cbk % 2]
                    proj_tok(hx_tile, d_hx, ii * 128, 1536 + cbk * 512, pr, d_pr)
                    if cbk < 2:
                        ot, d_ot = (qo if cbk == 0 else ko)[i % 2]
                        S.op("vector", lambda e, pr=pr, i=i: e.tensor_tensor(
                            out=P1[:].rearrange("p (h j t) -> p h j t", h=8, t=2), in0=pr[:].rearrange("p (h j t) -> p h j t", h=8, t=2),
                            in1=sap(ropeT, 0, 128, (0 * 64 + i) * 32, [(0, 8), (1, 32), (0, 2)]), op=ALU.mult),
                            reads=[d_pr, d_rope], writes=[d_P1])
                        S.op("vector", lambda e, pr=pr, i=i: e.tensor_tensor(
                            out=P2[:].rearrange("p (h j t) -> p h j t", h=8, t=2), in0=pr[:].rearrange("p (h j t) -> p h j t", h=8, t=2),
                            in1=sap(ropeT, 0, 128, (1 * 64 + i) * 32, [(0, 8), (1, 32), (0, 2)]), op=ALU.mult),
                            reads=[d_pr, d_rope], writes=[d_P2])
                        S.op("gpsimd", lambda e, ot=ot: e.tensor_tensor(out=sap(ot, 0, 128, 0, [(2, 256)]), in0=sap(P1, 0, 128, 0, [(2, 256)]),
                                                                       in1=sap(P2, 0, 128, 1, [(2, 256)]), op=ALU.subtract),
                             reads=[d_P1, d_P2], writes=[d_ot])
                        S.op("gpsimd", lambda e, ot=ot: e.tensor_tensor(out=sap(ot, 0, 128, 1, [(2, 256)]), in0=sap(P2, 0, 128, 0, [(2, 256)]),
                                                                       in1=sap(P1, 0, 128, 1, [(2, 256)]), op=ALU.add),
                             reads=[d_P1, d_P2], writes=[d_ot])
                        scr = qscr if cbk == 0 else kscr
                        store("sync", ("p1_q%d" if cbk == 0 else "p1_k%d") % (i % 2), scr[i * 128:(i + 1) * 128, :], ot[:], d_ot)
                        if cbk == 1:
                            S.op("vector", lambda e, ot=ot: e.tensor_tensor(
                                out=kw[:], in0=sap(ot, 0, 128, 0, [(64, 8), (0, 2), (1, 64)]),
                                in1=sap(Wk, 0, 128, 0, [(2, 8), (1, 2), (0, 64)]), op=ALU.mult),
                                reads=[d_ot, d_Wk], writes=[d_kw])
                    elif cbk == 2:
                        vt, d_vt = vo[i % 2]
                        S.op("scalar", lambda e, vt=vt, pr=pr: e.copy(vt[:], pr[:]), reads=[d_pr], writes=[d_vt])
                        store("sync", "p1_v%d" % (i % 2), vscr[i * 128:(i + 1) * 128, :], vt[:], d_vt)
                    else:
                        gt, d_gt = go[i % 2]
                        S.op("scalar", lambda e, gt=gt, pr=pr: e.activation(out=gt[:], in_=pr[:], func=AF.Silu), reads=[d_pr], writes=[d_gt])
                        store("sync", "p1_g%d" % (i % 2), gscr[i * 128:(i + 1) * 128, :], gt[:], d_gt)
                pts, d_pts = pTs[i % 2]
                vt, d_vt = vo[i % 2]
                for h in range(8):
                    S.op("tensor", lambda e, h=h, pts=pts, vt=vt: e.matmul(pts[:, h * 64:(h + 1) * 64], lhsT=kw[:, h, :, :],
                                                                          rhs=vt[:, h * 64:(h + 1) * 64], start=True, stop=True),
                         reads=[d_kw, d_vt], writes=[d_pts])
                tsb, d_tsb = Tsb[i % 2]
                S.op("scalar", lambda e, tsb=tsb, pts=pts: e.copy(tsb[:], pts[:]), reads=[d_pts], writes=[d_tsb])
                store("sync", "p1_T%d" % (i % 2), Tscr[i], tsb[:], d_tsb)
                if ii == 3:
                    s0 = 1 if B == 0 else 0
                    tok_lo = 512 * B - 1 + s0
                    for ft in (4, 5, 6, 7, 8, 9, 10, 11, 0, 1, 2, 3):
                        pu, d_pu = pU[ft % 2]
                        for k in range(8):
                            S.op("tensor", lambda e, k=k, ft=ft, pu=pu, hx_tile=hx_tile: e.matmul(pu[:], lhsT=Win[:, k, ft * 128:(ft + 1) * 128], rhs=hx_tile[:, k, :],
                                                                                start=(k == 0), stop=(k == 7)),
                                 reads=[d_Wink[k], d_hx], writes=[d_pu])
                        u, d_u = U[ft]
                        S.op("scalar", lambda e, u=u, pu=pu: e.copy(u[:, 2:514], pu[:]), reads=[d_pu], writes=[d_u])
                        c1, d_c1 = cv1[ft % 2]; c2, d_c2 = cv2[ft % 2]
                        S.op("scalar", lambda e, u=u, c1=c1, ft=ft: e.activation(out=c1[:], in_=u[:, 0:512], func=AF.Identity, scale=cw[:, ft, 0:1], bias=cw[:, ft, 3:4]),
                             reads=[d_u, d_cw], writes=[d_c1])
                        S.op("vector", lambda e, u=u, c1=c1, c2=c2, ft=ft: e.scalar_tensor_tensor(out=c2[:], in0=u[:, 1:513], scalar=cw[:, ft, 1:2], op0=ALU.mult,
                                                                                                 in1=c1[:], op1=ALU.add),
                             reads=[d_u, d_cw, d_c1], writes=[d_c2])
                        ct = ft % 4
                        if ft < 4:
                            ho, d_ho = hyo[ct]
                            S.op("vector", lambda e, u=u, c2=c2, ho=ho, ft=ft: e.scalar_tensor_tensor(out=ho[:], in0=u[:, 2:514], scalar=cw[:, ft, 2:3], op0=ALU.mult,
                                                                                                     in1=c2[:], op1=ALU.add),
                                 reads=[d_u, d_cw, d_c2], writes=[d_ho])
                            store("sync", "p1_hy%d" % ct, x0scr[ct * 128:(ct + 1) * 128, tok_lo:512 * B + 511], ho[:, s0:512], d_ho)
                        elif ft < 8:
                            cx, d_cx = cvx1[ct]
                            S.op("vector", lambda e, u=u, c2=c2, cx=cx, ft=ft: e.scalar_tensor_tensor(out=cx[:], in0=u[:, 2:514], scalar=cw[:, ft, 2:3], op0=ALU.mult,
                                                                                                     in1=c2[:], op1=ALU.add),
                                 reads=[d_u, d_cw, d_c2], writes=[d_cx])
                        else:
                            cx, d_cx = cvx1[ct]
                            S.op("vector", lambda e, u=u, c2=c2, ft=ft: e.scalar_tensor_tensor(out=cv3[:], in0=u[:, 2:514], scalar=cw[:, ft, 2:3], op0=ALU.mult,
                                                                                              in1=c2[:], op1=ALU.add),
                                 reads=[d_u, d_cw, d_c2], writes=[d_cv3])
                            ho, d_ho = hyo[ct]
                            S.op("gpsimd", lambda e, cx=cx, ho=ho: e.tensor_tensor(out=ho[:], in0=cv3[:], in1=cx[:], op=ALU.mult),
                                 reads=[d_cv3, d_cx], writes=[d_ho])
                            store("sync", "p1_hy%d" % ct, vxscr[ct * 128:(ct + 1) * 128, tok_lo:512 * B + 511], ho[:, s0:512], d_ho)
                        S.op("scalar", lambda e, u=u: e.copy(u[:, 0:2], u[:, 512:514]), reads=[d_u], writes=[d_u])
            tl, d_tl = sb(st, [128, 12], F32)
            tlb, d_tlb = sb(st, [128, 8], BF16)
            for ft in range(12):
                u, d_u = U[ft]
                S.op("vector", lambda e, u=u, ft=ft: e.tensor_scalar(out=tl[:, ft:ft + 1], in0=u[:, 0:1], scalar1=cw[:, ft, 0:1], scalar2=cw[:, ft, 3:4],
                                                                    op0=ALU.mult, op1=ALU.add), reads=[d_u, d_cw], writes=[d_tl])
                S.op("vector", lambda e, u=u, ft=ft: e.scalar_tensor_tensor(out=tl[:, ft:ft + 1], in0=u[:, 1:2], scalar=cw[:, ft, 1:2], op0=ALU.mult,
                                                                           in1=tl[:, ft:ft + 1], op1=ALU.add), reads=[d_u, d_cw, d_tl], writes=[d_tl])
            S.op("vector", lambda e: e.tensor_copy(out=tlb[:, 0:4], in_=tl[:, 0:4]), reads=[d_tl], writes=[d_tlb])
            S.op("vector", lambda e: e.tensor_tensor(out=tlb[:, 4:8], in0=tl[:, 4:8], in1=tl[:, 8:12], op=ALU.mult), reads=[d_tl], writes=[d_tlb])
            for ct in range(4):
                store("sync", "p1_tl%d" % ct, dap(x0scr, ct * 128 * L + L - 1, [(L, 128), (1, 1)]), tlb[:, ct:ct + 1], d_tlb, slow=True)
                store("sync", "p1_tv%d" % ct, dap(vxscr, ct * 128 * L + L - 1, [(L, 128), (1, 1)]), tlb[:, 4 + ct:5 + ct], d_tlb, slow=True)
            S.barrier()
        a1st.close()
        if stop_after <= 1:
            S.emit()
            return nc
        w2st = gst.enter_context(contextlib.ExitStack())
        W2, _ = sb(w2st, [128, 32, D], BF16)
        d_W2k = [Dep() for _ in range(8)]
        for k4 in range(8):
            S.op("gpsimd", lambda e, k4=k4: e.dma_start(out=W2[:, k4 * 4:(k4 + 1) * 4, :], in_=w_mlp2[k4 * 512:(k4 + 1) * 512, :].rearrange("(k p) n -> p k n", p=128)),
                 writes=[d_W2k[k4]], dma="w_2")
        fft_phase("data")
        if stop_after <= 2:
            S.emit()
            return nc
        if stop_after <= 3:
            S.emit()
            return nc

        with contextlib.ExitStack() as st:
            Wo, d_Wo = sb(st, [128, 8, D], BF16)
            W1, _ = sb(st, [128, 8, 4 * D], BF16)
            load("sync", "w_o", Wo[:], wobf.rearrange("(k p) n -> p k n", p=128), d_Wo, d_wobf)
            d_W1c = [Dep() for _ in range(8)]
            for cb in range(8):
                load("sync", "w_1%d" % cb, W1[:, :, cb * 512:(cb + 1) * 512], w1bf[:, cb * 512:(cb + 1) * 512].rearrange("(k p) n -> p k n", p=128), d_W1c[cb], d_w1bf)
            hbc, d_hbc = sb(st, [128, 4], F32)
            load("sync", "r_hb", hbc[:].rearrange("p (c o) -> p c o", o=1), dap(hy_bias, 0, [(1, 128), (128, 4), (1, 1)]), d_hbc, slow=True)
            gnr, d_gnr = sb(st, [128, 512], F32)
            load("sync", "r_gn", gnr[:], row_bc(gn_g, 0, 512), d_gnr)
            banks = [ps(st, [128, 512], F32) for _ in range(4)]
            pmb = [ps(st, [128, 512], F32) for _ in range(2)]
            bbanks = [ps(st, [128, 1024], BF16) for _ in range(2)]
            bctr = [0, 0]

            def nb():
                b_ = banks[bctr[0] % 4]
                bctr[0] += 1
                return b_

            def nbb():
                b_ = bbanks[bctr[1] % 2]
                bctr[1] += 1
                return b_

            qkvg = [[sb(st, [128, 512], BF16) for _ in range(5)] for _ in range(1)]
            scrs = [qscr, kscr, vscr, gscr]
            hy3 = [sb(st, [128, 4, 128], BF16) for _ in range(3)]
            qx, d_qx = sb(st, [128, 8, 2, 64], BF16)
            qT, d_qT = sb(st, [128, 4, 128], BF16); kT, d_kT = sb(st, [128, 4, 128], BF16)
            qxT, d_qxT = sb(st, [128, 8, 128], BF16)
            d_Pm = d_qx
            osb, d_osb = sb(st, [128, 512], F32); osq, d_osq = sb(st, [128, 512], F32)
            st8, d_st8 = sb(st, [128, 4, 8], F32)
            yret, d_yret = sb(st, [128, 512], BF16)
            mixT, d_mixT = sb(st, [128, 8, 128], BF16)
            xnb = [sb(st, [128, D], F32) for _ in range(2)]
            ssA, d_ssA = sb(st, [128, 1], F32); ssB, d_ssB = sb(st, [128, 1], F32)
            d_xm2 = d_qxT
            hxb = [sb(st, [128, 8, 128], BF16) for _ in range(2)]
            rlb, d_rlb = sb(st, [128, 512], F32); tb, d_tb = rlb, d_rlb
            hT, d_hT = sb(st, [128, 8, 128], BF16)

            def loads(i):
                if i >= NT:
                    return
                bufs = qkvg[0]
                for j_ in range(4):
                    load("sync", "r_in%d_%d" % (0, j_), bufs[j_][0][:], scrs[j_][i * 128:(i + 1) * 128, :], bufs[j_][1])
                load("sync", "r_in%d_4" % (0), bufs[4][0][:], Sscr[i], bufs[4][1])

            def tileA(i):
                xn, d_xn = xnb[i % 2]
                hx2T, d_hx2T = hxb[i % 2]
                (qt, d_qt), (kt, d_kt), (vt, d_vt), (gt, d_gt), (St_, d_St) = qkvg[0]
                load("sync", "r_x%d" % (i % 2), xn[:], x[i * 128:(i + 1) * 128, :], d_xn)
                for j_, scr_ in enumerate((yscr, vxscr, x0scr)):
                    load("sync", "r_hy%d" % j_, hy3[j_][0][:], dap(scr_, i * 128, [(L, 128), (128 * L, 4), (1, 128)]), hy3[j_][1])
                S.op("vector", lambda e: e.tensor_tensor(out=qx[:], in0=sap(qt, 0, 128, 0, [(64, 8), (0, 2), (1, 64)]),
                                                         in1=sap(Wq, 0, 128, 0, [(2, 8), (1, 2), (0, 64)]), op=ALU.mult), reads=[d_qt, d_Wq], writes=[d_qx])
                pq, d_pq = nbb()
                for hp in range(4):
                    S.op("tensor", lambda e, hp=hp: e.transpose(out=pq[:, hp * 128:(hp + 1) * 128], in_=qt[:, hp * 128:(hp + 1) * 128], identity=identb[:]),
                         reads=[d_qt, d_identb], writes=[d_pq])
                for hp in range(4):
                    S.op("tensor", lambda e, hp=hp: e.transpose(out=pq[:, 512 + hp * 128:512 + (hp + 1) * 128], in_=kt[:, hp * 128:(hp + 1) * 128], identity=identb[:]),
                         reads=[d_kt, d_identb], writes=[d_pq])
                S.op("scalar", lambda e: e.copy(qT[:].rearrange("p a b -> p (a b)"), pq[:, 0:512]), reads=[d_pq], writes=[d_qT])
                S.op("scalar", lambda e: e.copy(kT[:].rearrange("p a b -> p (a b)"), pq[:, 512:1024]), reads=[d_pq], writes=[d_kT])
                yield
                px, d_px = nbb()
                for h in range(8):
                    S.op("tensor", lambda e, h=h: e.transpose(out=px[:, h * 128:(h + 1) * 128], in_=qx[:, h, :, :], identity=identb[:]),
                         reads=[d_qx, d_identb], writes=[d_px])
                S.op("scalar", lambda e: e.copy(qxT[:].rearrange("p a b -> p (a b)"), px[:]), reads=[d_px], writes=[d_qxT])
                yield
                for par in range(2):
                    psc, d_psc = nb()
                    b0 = par * 64
                    for hh in range(4):
                        h = 2 * hh + par
                        S.op("tensor", lambda e, hh=hh, b0=b0, psc=psc: e.matmul(psc[:, hh * 128:(hh + 1) * 128], lhsT=kT[b0:b0 + 64, hh, :],
                                                                                rhs=qT[b0:b0 + 64, hh, :], start=True, stop=True),
                             reads=[d_kT, d_qT], writes=[d_psc])
                    S.op("vector", lambda e, par=par, psc=psc: e.tensor_tensor(out=sap(qx, 0, 128, par * 128, [(256, 4), (1, 128)]), in0=psc[:].rearrange("p (a b) -> p a b", a=4),
                                                                              in1=sap(DT, 0, 128, par * 128, [(256, 4), (1, 128)]), op=ALU.mult),
                         reads=[d_psc, d_DT], writes=[d_Pm])
                yield
                po, d_po = nb()
                for h in range(8):
                    S.op("tensor", lambda e, h=h: e.matmul(po[:, h * 64:(h + 1) * 64], lhsT=sap(qx, 0, 128, h * 128, [(1, 128)]), rhs=vt[:, h * 64:(h + 1) * 64], start=True, stop=False),
                         reads=[d_Pm, d_vt], writes=[d_po])
                    S.op("tensor", lambda e, h=h: e.matmul(po[:, h * 64:(h + 1) * 64], lhsT=qxT[:, h, :], rhs=St_[:, h * 64:(h + 1) * 64], start=False, stop=True),
                         reads=[d_qxT, d_St], writes=[d_po])
                yield
                S.op("scalar", lambda e: e.copy(osb[:], po[:]), reads=[d_po], writes=[d_osb])
                S.op("scalar", lambda e: e.activation(out=osq[:], in_=po[:], func=AF.Square), reads=[d_po], writes=[d_osq])
                S.op("vector", lambda e: e.tensor_reduce(out=st8[:, 0, :], in_=osb[:].rearrange("p (h x) -> p h x", h=8), op=ALU.add, axis=AX.X), reads=[d_osb], writes=[d_st8])
                S.op("vector", lambda e: e.tensor_reduce(out=st8[:, 1, :], in_=osq[:].rearrange("p (h x) -> p h x", h=8), op=ALU.add, axis=AX.X), reads=[d_osq], writes=[d_st8])
                S.op("vector", lambda e: e.tensor_scalar(out=st8[:, 0, :], in0=st8[:, 0, :], scalar1=1.0 / 64, scalar2=None, op0=ALU.mult), reads=[d_st8], writes=[d_st8])
                S.op("vector", lambda e: e.tensor_tensor(out=st8[:, 2, :], in0=st8[:, 0, :], in1=st8[:, 0, :], op=ALU.mult), reads=[d_st8], writes=[d_st8])
                S.op("vector", lambda e: e.scalar_tensor_tensor(out=st8[:, 3, :], in0=st8[:, 1, :], scalar=1.0 / 64, op0=ALU.mult, in1=st8[:, 2, :], op1=ALU.subtract),
                     reads=[d_st8], writes=[d_st8])
                S.op("scalar", lambda e: e.activation(out=st8[:, 3, :], in_=st8[:, 3, :], func=AF.Sqrt, bias=1e-6), reads=[d_st8], writes=[d_st8])
                S.op("vector", lambda e: e.reciprocal(out=st8[:, 3, :], in_=st8[:, 3, :]), reads=[d_st8], writes=[d_st8])
                S.op("vector", lambda e: e.tensor_tensor(out=osb[:].rearrange("p (h x) -> p h x", h=8), in0=osb[:].rearrange("p (h x) -> p h x", h=8),
                                                         in1=sap(st8, 0, 128, 0, [(1, 8), (0, 64)]), op=ALU.subtract), reads=[d_osb, d_st8], writes=[d_osb])
                S.op("vector", lambda e: e.tensor_tensor(out=osb[:].rearrange("p (h x) -> p h x", h=8), in0=osb[:].rearrange("p (h x) -> p h x", h=8),
                                                         in1=sap(st8, 0, 128, 24, [(1, 8), (0, 64)]), op=ALU.mult), reads=[d_osb, d_st8], writes=[d_osb])
                S.op("gpsimd", lambda e: e.tensor_tensor(out=osb[:], in0=osb[:], in1=gnr[:], op=ALU.mult), reads=[d_osb, d_gnr], writes=[d_osb])
                S.op("gpsimd", lambda e: e.tensor_tensor(out=yret[:], in0=osb[:], in1=gt[:], op=ALU.mult), reads=[d_osb, d_gt], writes=[d_yret])
                loads(i + 1)
                yield
                yield
                yield
                yield
                py_, d_py = nbb()
                for hp in range(4):
                    S.op("tensor", lambda e, hp=hp: e.transpose(out=py_[:, hp * 128:(hp + 1) * 128], in_=yret[:, hp * 128:(hp + 1) * 128], identity=identb[:]),
                         reads=[d_yret, d_identb], writes=[d_py])
                S.op("scalar", lambda e: e.copy(mixT[:, 4:8, :].rearrange("p a b -> p (a b)"), py_[:, 0:512]), reads=[d_py], writes=[d_mixT])
                (yc, d_yc), (vxt, d_vxt), (x0t, d_x0t) = hy3
                for ct in range(4):
                    S.op("vector", lambda e, ct=ct: e.scalar_tensor_tensor(out=osq[:, ct * 128:(ct + 1) * 128], in0=vxt[:, ct, :], scalar=hbc[:, ct:ct + 1], op0=ALU.mult,
                                                                          in1=yc[:, ct, :], op1=ALU.add), reads=[d_vxt, d_yc, d_hbc, d_osq], writes=[d_osq])
                S.op("gpsimd", lambda e: e.tensor_tensor(out=mixT[:, 0:4, :].rearrange("p a b -> p (a b)"), in0=osq[:], in1=x0t[:].rearrange("p a b -> p (a b)"), op=ALU.mult),
                     reads=[d_osq, d_x0t], writes=[d_mixT])
                yield
                for nb_ in range(2):
                    pw, d_pw = nb()
                    for k in range(8):
                        S.op("tensor", lambda e, k=k, nb_=nb_, pw=pw: e.matmul(pw[:], lhsT=mixT[:, k, :], rhs=Wo[:, k, nb_ * 512:(nb_ + 1) * 512], start=(k == 0), stop=(k == 7)),
                             reads=[d_mixT, d_Wo], writes=[d_pw])
                    S.op("vector", lambda e, nb_=nb_, pw=pw: e.tensor_tensor(out=osq[:], in0=pw[:], in1=gate2[:, nb_ * 512:(nb_ + 1) * 512], op=ALU.mult),
                         reads=[d_pw, d_gate2], writes=[d_osq])
                    S.op("gpsimd", lambda e, nb_=nb_: e.tensor_tensor(out=xn[:, nb_ * 512:(nb_ + 1) * 512], in0=xn[:, nb_ * 512:(nb_ + 1) * 512], in1=osq[:], op=ALU.add),
                         reads=[d_xn, d_osq], writes=[d_xn])
                yield
                S.op("scalar", lambda e: e.activation(out=qxT[:].rearrange("p a b -> p (a b)"), in_=xn[:], func=AF.Square, accum_out=ssA[:, 0:1]), reads=[d_xn], writes=[d_xm2, d_ssA])
                S.op("scalar", lambda e: e.activation(out=ssA[:, 0:1], in_=ssA[:, 0:1], func=AF.Sqrt, scale=1.0 / D, bias=1e-6), reads=[d_ssA], writes=[d_ssA])
                S.op("vector", lambda e: e.reciprocal(out=ssA[:, 0:1], in_=ssA[:, 0:1]), reads=[d_ssA], writes=[d_ssA])
                S.op("vector", lambda e: e.scalar_tensor_tensor(out=qxT[:].rearrange("p a b -> p (a b)"), in0=xn[:], scalar=ssA[:, 0:1], op0=ALU.mult, in1=gs2[:], op1=ALU.mult),
                     reads=[d_xn, d_ssA, d_gs2], writes=[d_xm2])
                yield
                yield
                yield
                yield
                pt2, d_pt2 = nbb()
                for k in range(8):
                    S.op("tensor", lambda e, k=k: e.transpose(out=pt2[:, k * 128:(k + 1) * 128], in_=qxT[:, k, :], identity=identb[:]),
                         reads=[d_xm2, d_identb], writes=[d_pt2])
                for k in range(8):
                    S.op("scalar", lambda e, k=k: e.activation(out=hx2T[:, k, :], in_=pt2[:, k * 128:(k + 1) * 128], func=AF.Identity, bias=colx[:, 3, k:k + 1]),
                         reads=[d_pt2, d_colx], writes=[d_hx2T])

            def tileB(i):
                xn, d_xn = xnb[i % 2]
                hx2T, d_hx2T = hxb[i % 2]
                for hf in range(4):
                    for f4 in range(2):
                        ph, d_ph = nb()
                        for ff in range(4):
                            ft = hf * 8 + f4 * 4 + ff
                            for k in range(8):
                                S.op("tensor", lambda e, k=k, ft=ft, ff=ff, ph=ph: e.matmul(ph[:, ff * 128:(ff + 1) * 128], lhsT=W1[:, k, ft * 128:(ft + 1) * 128], rhs=hx2T[:, k, :],
                                                                                           start=(k == 0), stop=(k == 7)), reads=[d_W1c[ft // 4], d_hx2T], writes=[d_ph])
                        S.op("scalar", lambda e, ph=ph: e.activation(out=rlb[:], in_=ph[:], func=AF.Relu), reads=[d_ph], writes=[d_rlb])
                        yield
                        S.op("gpsimd", lambda e, f4=f4: e.tensor_tensor(out=hT[:, f4 * 4:(f4 + 1) * 4, :].rearrange("p a b -> p (a b)"), in0=rlb[:], in1=rlb[:], op=ALU.mult),
                             reads=[d_rlb], writes=[d_hT])
                    for nb_ in range(2):
                        pm, d_pm = pmb[nb_]
                        for kk in range(8):
                            k = hf * 8 + kk
                            S.op("tensor", lambda e, k=k, kk=kk, nb_=nb_, pm=pm: e.matmul(pm[:], lhsT=hT[:, kk, :], rhs=W2[:, k, nb_ * 512:(nb_ + 1) * 512], start=(k == 0), stop=(k == 31)),
                                 reads=[d_hT, d_W2k[k // 4]], writes=[d_pm])
                        yield
                for nb_ in range(2):
                    pm, d_pm = pmb[nb_]
                    S.op("vector", lambda e, nb_=nb_, pm=pm: e.tensor_tensor(out=tb[:], in0=pm[:], in1=gate5[:, nb_ * 512:(nb_ + 1) * 512], op=ALU.mult),
                         reads=[d_pm, d_gate5], writes=[d_tb])
                    S.op("gpsimd", lambda e, nb_=nb_: e.tensor_tensor(out=xn[:, nb_ * 512:(nb_ + 1) * 512], in0=xn[:, nb_ * 512:(nb_ + 1) * 512], in1=tb[:], op=ALU.add),
                         reads=[d_xn, d_tb], writes=[d_xn])
                S.op("scalar", lambda e: e.activation(out=hT[:, 0:8, :].rearrange("p a b -> p (a b)"), in_=xn[:], func=AF.Square, accum_out=ssB[:, 0:1]), reads=[d_xn], writes=[d_hT, d_ssB])
                S.op("scalar", lambda e: e.activation(out=ssB[:, 0:1], in_=ssB[:, 0:1], func=AF.Sqrt, scale=1.0 / D, bias=1e-6), reads=[d_ssB], writes=[d_ssB])
                S.op("vector", lambda e: e.reciprocal(out=ssB[:, 0:1], in_=ssB[:, 0:1]), reads=[d_ssB], writes=[d_ssB])
                S.op("vector", lambda e: e.scalar_tensor_tensor(out=xn[:], in0=xn[:], scalar=ssB[:, 0:1], op0=ALU.mult, in1=gF[:], op1=ALU.mult),
                     reads=[d_xn, d_ssB, d_gF], writes=[d_xn])
                final_events.append(store("sync", "r_out%d" % (i % 2), out[i * 128:(i + 1) * 128, :], xn[:], d_xn))


            NTL = NT if stop_after >= 99 else 2
            loads(0)
            for _ in tileA(0):
                pass
            for i in range(NTL):
                gB = tileB(i)
                gA = tileA(i + 1) if i + 1 < NTL else None
                doneA = gA is None
                doneB = False
                while not (doneA and doneB):
                    if not doneB:
                        try:
                            next(gB)
                        except StopIteration:
                            doneB = True
                    if not doneA:
                        try:
                            next(gA)
                        except StopIteration:
                            doneA = True
            S.barrier()
        S.emit()
    return nc


def make_in_map(inputs, b):
    f = lambda a: np.ascontiguousarray(np.asarray(a, dtype=np.float32))
    c = host_consts()
    m = dict(
        x=f(inputs["x"][b]), ctx=f(inputs["ctx"][b]),
        cc=f(np.stack([np.asarray(inputs["c"][b]), np.asarray(inputs["c_ctx"])], axis=0)),
        w_ada=f(inputs["w_ada"][0]), b_ada=f(inputs["b_ada"][0]).reshape(1, -1), norm1_g=f(inputs["norm1_g"][0]).reshape(1, -1),
        w_in=f(inputs["w_in"][0]), hy_conv_w=f(inputs["hy_conv_w"][0]), hy_conv_b=f(inputs["hy_conv_b"][0]).reshape(1, -1),
        hy_f_w1=f(inputs["hy_f_w1"][0]), hy_f_b1=f(inputs["hy_f_b1"][0]).reshape(1, -1), hy_f_freq1=f(inputs["hy_f_freq1"][0]).reshape(1, -1),
        hy_f_w2=f(inputs["hy_f_w2"][0]), hy_f_b2=f(inputs["hy_f_b2"][0]).reshape(1, -1), hy_f_freq2=f(inputs["hy_f_freq2"][0]).reshape(1, -1),
        w3loc=f(np.concatenate([np.asarray(inputs["hy_f_w3"][0])[:, b * 64:(b + 1) * 64],
                                np.asarray(inputs["hy_f_w3"][0])[:, 512 + b * 64:512 + (b + 1) * 64]], axis=1)),
        hy_bias=f(inputs["hy_bias"][0]).reshape(1, -1),
        ret_decay_logit=f(inputs["ret_decay_logit"][0]).reshape(1, 16), ret_gn_g=f(inputs["ret_gn_g"][0]).reshape(1, -1),
        w_out=f(inputs["w_out"][0]), norm2_g=f(inputs["norm2_g"][0]).reshape(1, -1),
        w_mlp1=f(inputs["w_mlp1"][0]), w_mlp2=f(inputs["w_mlp2"][0]), norm_f_g=f(inputs["norm_f_g"]).reshape(1, -1),
    )
    m.update(c)
    m["edec"] = np.ascontiguousarray(c["edec"][b])
    return m


_NC = None


def kernel(**inputs):
    global _NC
    if _NC is None:
        _NC = build()
    in_maps = [make_in_map(inputs, b) for b in range(8)]
    res = run_bass_kernel_spmd(_NC, in_maps, core_ids=list(range(8)))
    return np.stack([np.asarray(r["out"], dtype=np.float32) for r in res.results], axis=0)
```
